# Optimizing a Trainium2 kernel written in Bass

```python
import jax, jax.numpy as jnp
from jax import lax
import numpy as np

D_MODEL = 1024
BATCH = 4
SEQ = 4096
DEPTH = 1

D_MIX = D_MODEL
D_CONV = D_MIX // 2
CONV_GROUPS = 8
CONV_GROUP_DIM = D_CONV // CONV_GROUPS
CONV_WIDTH = 3
GDN_HEADS = 4
GDN_HEAD_DIM = (D_MIX - D_CONV) // GDN_HEADS
GDN_WIDTH = GDN_HEADS * GDN_HEAD_DIM
GDN_CONV_WIDTH = 4
GDN_CHUNK = 64
D_FF = 2816
NORM_EPS = 1e-6
D_IN_PROJ = 3 * D_CONV + 4 * GDN_WIDTH + 2 * GDN_HEADS

kernel_name = 'hybrid_conv_gdn_macaron'


def rmsnorm(x, w, eps=NORM_EPS):
    xf = x.astype(jnp.float32)
    y = xf * lax.rsqrt(jnp.mean(xf * xf, axis=-1, keepdims=True) + eps)
    return (y * w.astype(jnp.float32)).astype(x.dtype)


def l2norm(x, eps=NORM_EPS):
    return x * lax.rsqrt(jnp.sum(x * x, axis=-1, keepdims=True) + eps)


def swiglu_ffn(x, w_in, w_out):
    gate, up = jnp.split(x @ w_in, 2, axis=-1)
    return (jax.nn.silu(gate) * up) @ w_out


def causal_depthwise_conv(x, w):
    K, C = w.shape
    return lax.conv_general_dilated(
        x, w[:, None, :].astype(x.dtype), window_strides=(1,), padding=[(K - 1, 0)],
        dimension_numbers=('NWC', 'WIO', 'NWC'), feature_group_count=C)


def short_conv_mixer(b_gate, c_gate, h, conv_w, out_gain):
    y = b_gate * causal_depthwise_conv(c_gate * h, conv_w)
    bsz, t, _ = y.shape
    y = rmsnorm(y.reshape(bsz, t, CONV_GROUPS, CONV_GROUP_DIM),
                out_gain.reshape(CONV_GROUPS, CONV_GROUP_DIM))
    return y.reshape(bsz, t, D_CONV)


def gated_delta_rule_chunked(q, k, v, g, beta):
    bsz, t, nh, dk = q.shape
    dv = v.shape[-1]
    c = GDN_CHUNK
    n = t // c
    q = q * (dk ** -0.5)
    to_chunks = lambda a: a.reshape(bsz, n, c, nh, a.shape[-1]).transpose(0, 3, 1, 2, 4)
    q, k, v = to_chunks(q), to_chunks(k), to_chunks(v)
    beta = beta.reshape(bsz, n, c, nh).transpose(0, 3, 1, 2)
    g = jnp.cumsum(g.reshape(bsz, n, c, nh).transpose(0, 3, 1, 2), axis=-1)
    k_beta = k * beta[..., None]
    v_beta = v * beta[..., None]
    causal = jnp.tril(jnp.ones((c, c), dtype=bool))
    strict = jnp.tril(jnp.ones((c, c), dtype=bool), -1)
    gdiff = g[..., :, None] - g[..., None, :]
    decay = jnp.exp(jnp.where(causal, gdiff, -jnp.inf))
    lower = jnp.where(strict, jnp.einsum('bhncd,bhnsd->bhncs', k_beta, k) * decay, 0.0)
    rhs = jnp.concatenate([v_beta, k_beta * jnp.exp(g)[..., None]], axis=-1)
    sol = lax.linalg.triangular_solve(lower, rhs, left_side=True, lower=True, unit_diagonal=True)
    u, w = sol[..., :dv], sol[..., dv:]
    attn_intra = jnp.where(causal, jnp.einsum('bhncd,bhnsd->bhncs', q, k) * decay, 0.0)
    q_dec = q * jnp.exp(g)[..., None]
    k_dec = k * jnp.exp(g[..., -1:] - g)[..., None]
    g_last = jnp.exp(g[..., -1])
    xs = tuple(jnp.moveaxis(a, 2, 0) for a in (q_dec, k_dec, u, w, attn_intra, g_last))

    def step(state, inp):
        q_c, k_c, u_c, w_c, a_c, gl = inp
        v_new = u_c - jnp.einsum('bhck,bhkv->bhcv', w_c, state)
        o_c = jnp.einsum('bhck,bhkv->bhcv', q_c, state) + jnp.einsum('bhcs,bhsv->bhcv', a_c, v_new)
        state = state * gl[..., None, None] + jnp.einsum('bhck,bhcv->bhkv', k_c, v_new)
        return state, o_c

    s0 = jnp.zeros((bsz, nh, dk, dv), jnp.float32)
    _, o = lax.scan(step, s0, xs)
    return o.transpose(1, 0, 3, 2, 4).reshape(bsz, t, nh, dv)


def gated_deltanet_mixer(q, k, v, z, b, a, conv_w, A_log, dt_bias, norm_w):
    bsz, t, _ = q.shape
    qkv = jax.nn.silu(causal_depthwise_conv(jnp.concatenate([q, k, v], axis=-1), conv_w))
    q, k, v = jnp.split(qkv.astype(jnp.float32), 3, axis=-1)
    heads = lambda a_: a_.reshape(bsz, t, GDN_HEADS, GDN_HEAD_DIM)
    q, k, v = l2norm(heads(q)), l2norm(heads(k)), heads(v)
    beta = jax.nn.sigmoid(b.astype(jnp.float32))
    g = -jnp.exp(A_log.astype(jnp.float32)) * jax.nn.softplus(a.astype(jnp.float32) + dt_bias.astype(jnp.float32))
    o = gated_delta_rule_chunked(q, k, v, g, beta)
    o = rmsnorm(o, norm_w) * jax.nn.silu(heads(z).astype(jnp.float32))
    return o.reshape(bsz, t, GDN_WIDTH).astype(z.dtype)


def setup_inputs(seed: int = 0) -> dict:
    key = jax.random.key(seed)
    ks = jax.random.split(key, 20)
    nrm = lambda k_, shape, fan_in: jax.random.normal(k_, shape, jnp.float32) * (fan_in ** -0.5)
    gain = lambda k_, shape: 1.0 + 0.05 * jax.random.normal(k_, shape, jnp.float32)
    L = DEPTH
    return {
        'x': jax.random.normal(ks[0], (BATCH, SEQ, D_MODEL), jnp.float32),
        'ffn1_norm': gain(ks[1], (L, D_MODEL)),
        'ffn1_w_in': nrm(ks[2], (L, D_MODEL, 2 * D_FF), D_MODEL),
        'ffn1_w_out': nrm(ks[3], (L, D_FF, D_MODEL), D_FF),
        'mix_norm': gain(ks[4], (L, D_MODEL)),
        'w_mix_in': nrm(ks[5], (L, D_MODEL, D_IN_PROJ), D_MODEL),
        'conv_short_w': nrm(ks[6], (L, CONV_WIDTH, D_CONV), CONV_WIDTH),
        'conv_out_norm': gain(ks[7], (L, D_CONV)),
        'gdn_conv_w': nrm(ks[8], (L, GDN_CONV_WIDTH, 3 * GDN_WIDTH), GDN_CONV_WIDTH),
        'gdn_A_log': jnp.log(jax.random.uniform(ks[9], (L, GDN_HEADS), jnp.float32, 1.0, 16.0)),
        'gdn_dt_bias': 1.0 + 0.1 * jax.random.normal(ks[10], (L, GDN_HEADS), jnp.float32),
        'gdn_out_norm': gain(ks[11], (L, GDN_HEAD_DIM)),
        'w_mix_out': nrm(ks[12], (L, D_MIX, D_MODEL), D_MIX),
        'ffn2_norm': gain(ks[13], (L, D_MODEL)),
        'ffn2_w_in': nrm(ks[14], (L, D_MODEL, 2 * D_FF), D_MODEL),
        'ffn2_w_out': nrm(ks[15], (L, D_FF, D_MODEL), D_FF),
        'final_norm': gain(ks[16], (D_MODEL,)),
    }


def reference(x, ffn1_norm, ffn1_w_in, ffn1_w_out, mix_norm, w_mix_in, conv_short_w, conv_out_norm,
              gdn_conv_w, gdn_A_log, gdn_dt_bias, gdn_out_norm, w_mix_out, ffn2_norm, ffn2_w_in,
              ffn2_w_out, final_norm):
    sizes = [D_CONV, D_CONV, D_CONV, GDN_WIDTH, GDN_WIDTH, GDN_WIDTH, GDN_WIDTH, GDN_HEADS, GDN_HEADS]
    splits = np.cumsum(sizes)[:-1].tolist()
    h = x
    for layer in range(DEPTH):
        h = h + 0.5 * swiglu_ffn(rmsnorm(h, ffn1_norm[layer]), ffn1_w_in[layer], ffn1_w_out[layer])
        u = rmsnorm(h, mix_norm[layer]) @ w_mix_in[layer]
        cb, cc, ch, gq, gk, gv, gz, gb, ga = jnp.split(u, splits, axis=-1)
        y_conv = short_conv_mixer(cb, cc, ch, conv_short_w[layer], conv_out_norm[layer])
        y_gdn = gated_deltanet_mixer(gq, gk, gv, gz, gb, ga, gdn_conv_w[layer], gdn_A_log[layer],
                                     gdn_dt_bias[layer], gdn_out_norm[layer])
        h = h + jnp.concatenate([y_conv, y_gdn], axis=-1) @ w_mix_out[layer]
        h = h + 0.5 * swiglu_ffn(rmsnorm(h, ffn2_norm[layer]), ffn2_w_in[layer], ffn2_w_out[layer])
    return rmsnorm(h, final_norm)
```

```python
import contextlib
import numpy as np
import concourse.bass as bass
import concourse.mybir as mybir
from concourse.bass_utils import run_bass_kernel_spmd

F32 = mybir.dt.float32
BF16 = mybir.dt.bfloat16
AF = mybir.ActivationFunctionType
ALU = mybir.AluOpType

PE, ACT, DVE, POOL, SP = "pe", "act", "dve", "pool", "sp"
COMPUTE = (PE, ACT, DVE, POOL)

D = 1024
DFF = 2816
T = 2048
NT = T // 128
NB = T // 512
EPS = 1e-6
GW0 = 1536
NG = 2056
NEG = -1.0e30


class Buf:
    __slots__ = ("name", "last_w", "readers", "excl")

    def __init__(self, name="", excl=False):
        self.name = name
        self.last_w = None
        self.readers = []
        self.excl = excl


class Op:
    __slots__ = ("eng", "fn", "deps", "needs_inc", "cnt", "is_dma", "key", "grp", "idx")

    def __init__(self, eng, fn, is_dma=False, key=None, grp=None):
        self.eng = eng
        self.fn = fn
        self.deps = []
        self.needs_inc = False
        self.cnt = 0
        self.is_dma = is_dma
        self.key = key
        self.grp = grp


class Sched:
    def __init__(self):
        self.ops = []
        self.grp_ctr = 0

    def new_group(self):
        self.grp_ctr += 1
        return self.grp_ctr

    def _add(self, op, reads, writes):
        if getattr(self, "capture", None) is not None:
            self.capture.append((op, list(reads), list(writes)))
            return op
        return self._add_real(op, reads, writes)

    def replay(self, item):
        return self._add_real(*item)

    def _add_real(self, op, reads, writes):
        op.idx = len(self.ops)
        ex = [b for b in reads if b.excl]
        if ex:
            reads = [b for b in reads if not b.excl]
            writes = list(writes) + ex
        deps = {}
        for b in reads:
            if b.last_w is not None:
                deps[id(b.last_w)] = b.last_w
        for b in writes:
            if b.last_w is not None:
                deps[id(b.last_w)] = b.last_w
            for r in b.readers:
                deps[id(r)] = r
        latest = {}
        for d in deps.values():
            if d is op:
                continue
            if (not d.is_dma) and (not op.is_dma) and d.eng == PE and op.eng == PE:
                continue
            if d.is_dma:
                op.deps.append(d)
            else:
                cur = latest.get(d.eng)
                if cur is None or d.idx > cur.idx:
                    latest[d.eng] = d
        op.deps.extend(latest.values())
        for b in reads:
            if op.is_dma:
                b.readers.append(op)
            else:
                b.readers = [r for r in b.readers if r.is_dma or r.eng != op.eng]
                b.readers.append(op)
        for b in writes:
            b.last_w = op
            b.readers = []
        self.ops.append(op)
        return op

    def op(self, eng, fn, reads=(), writes=()):
        return self._add(Op(eng, fn), reads, writes)

    def dma(self, queue, fn, key, grp, reads=(), writes=()):
        return self._add(Op(queue, fn, is_dma=True, key=key, grp=grp), reads, writes)

    def alias(self, new_bufs, old_bufs):
        acc = {}
        for b in old_bufs:
            if b.last_w is not None:
                acc[id(b.last_w)] = b.last_w
            for r in b.readers:
                acc[id(r)] = r
        for nb in new_bufs:
            nb.readers = list(acc.values())

    def emit(self, nc, final_wait_keys=()):
        ops = self.ops
        for o in ops:
            for d in o.deps:
                d.needs_inc = True
        cnt = {}
        grp_end = {}
        for o in ops:
            if o.is_dma:
                k = ("dma", o.key)
                cnt[k] = cnt.get(k, 0) + 1
                o.cnt = cnt[k]
                grp_end[(o.key, o.grp)] = o.cnt
            elif o.needs_inc:
                cnt[o.eng] = cnt.get(o.eng, 0) + 1
                o.cnt = cnt[o.eng]
        dma_keys = sorted({o.key for o in ops if o.is_dma})
        streams = {e: [o for o in ops if o.eng == e] for e in (PE, ACT, DVE, POOL, SP)}
        self.stats = {e: len(s) for e, s in streams.items()}
        self.stats["incs"] = dict(cnt)

        import os
        SEG = int(os.environ.get('KSEG', 1500))
        with contextlib.ExitStack() as es:
            sems = {}
            for e in COMPUTE:
                nseg = (cnt.get(e, 0) + SEG - 1) // SEG + 1
                sems[e] = [es.enter_context(nc.semaphore("s_%s_%d" % (e, j))) for j in range(nseg)]
            for k in dma_keys:
                sems[("dma", k)] = es.enter_context(nc.semaphore("d_" + str(k)))
            block = es.enter_context(nc.Block())

            def run_stream(engname, eng):
                waited = {}
                for o in streams[engname]:
                    for d in o.deps:
                        if d.is_dma:
                            sk = ("dma", d.key)
                            val = 16 * grp_end[(d.key, d.grp)]
                            sem = sems[sk]
                        else:
                            seg = (d.cnt - 1) // SEG
                            sk = (d.eng, seg)
                            val = (d.cnt - 1) % SEG + 1
                            sem = sems[d.eng][seg]
                            if any(k2[0] == d.eng and k2[1] > seg for k2 in waited if isinstance(k2, tuple) and k2[0] == d.eng):
                                continue
                        if waited.get(sk, 0) >= val:
                            continue
                        waited[sk] = val
                        eng.wait_ge(sem, val)
                    ins = o.fn(eng)
                    if o.is_dma:
                        ins.then_inc(sems[("dma", o.key)], 16)
                    elif o.needs_inc:
                        ins.then_inc(sems[o.eng][(o.cnt - 1) // SEG], 1)
                if engname == SP:
                    for k in final_wait_keys:
                        eng.wait_ge(sems[("dma", k)], 16 * cnt[("dma", k)])

            @block.sync
            def _(e):
                run_stream(SP, e)

            @block.tensor
            def _(e):
                run_stream(PE, e)

            @block.scalar
            def _(e):
                run_stream(ACT, e)

            @block.vector
            def _(e):
                run_stream(DVE, e)

            @block.gpsimd
            def _(e):
                run_stream(POOL, e)


class Builder:
    def __init__(self, debug=False):
        self.debug = debug
        self.nc = bass.Bass("TRN2", target_bir_lowering=False)
        self.S = Sched()
        self.es = contextlib.ExitStack()
        self.dbg_outs = []
        self.dbg_keys = []
        self.rr = 0

    def sb(self, name, cols, dt=F32, es=None):
        return (es or self.es).enter_context(self.nc.sbuf_tensor(name, [128, cols], dt))

    def dram_in(self, name, shape, dt=F32):
        return self.nc.dram_tensor(name, list(shape), dt, kind="ExternalInput").ap()

    def dram_out(self, name, shape, dt=F32):
        return self.nc.dram_tensor(name, list(shape), dt, kind="ExternalOutput").ap()

    def dbg(self, name, ap, cols, bufs, dt=F32):
        if not self.debug:
            return
        o = self.dram_out("dbg_" + name, [128, cols], dt)
        self.dbg_keys.append("dbg_" + name)
        self.S.dma(SP, lambda e: e.dma_start(out=o, in_=ap), "dbg_" + name, self.S.new_group(), reads=bufs)

    def ew(self):
        self.rr += 1
        return ACT if (self.rr & 1) else DVE

    def build(self):
        nc, S = self.nc, self.S
        op = S.op
        x_pre = self.dram_in("x_pre", [T, D])
        x_own = self.dram_in("x_own", [T, D])
        w1_in = self.dram_in("w1_in", [D, 2 * DFF])
        w1_out = self.dram_in("w1_out", [DFF, D])
        w2_in = self.dram_in("w2_in", [D, 2 * DFF])
        w2_out = self.dram_in("w2_out", [DFF, D])
        wm_in = self.dram_in("wm_in", [D, 3592])
        wm_out = self.dram_in("wm_out", [D, D])
        cpk = self.dram_in("cpk", [128, self.CPK_COLS])
        fn_bc = self.dram_in("fn_bc", [128, D])
        out = self.dram_out("out", [T, D])
        self.out_grp = S.new_group()

        h = self.sb("h", NT * D)
        hB = [Buf("h%d" % t) for t in range(NT)]
        stage = [self.sb("stage%d" % i, 2048) for i in range(2)]
        stB = [Buf("st%d" % i) for i in range(2)]
        self.stage, self.stB, self.st_i = stage, stB, 0
        import os
        self.cast_order = os.environ.get('KCAST', 'dve,act,pool,dve,act').split(',')
        cp = self.sb("cp", self.CPK_COLS)
        cpB = Buf("cp")
        hhalo = self.sb("hhalo", D)
        hhB = Buf("hhalo")
        ygT = self.sb("ygT", 4 * T, BF16)
        ygB = [Buf("yg%d" % t) for t in range(NT)]
        pch = self.sb("pch", 36)
        pchB = Buf("pch")
        Sst = [self.sb("Sst%d" % i, 4 * 128) for i in range(2)]
        SsB = [[Buf("S%d_%d" % (i, hh)) for hh in range(4)] for i in range(2)]
        stat = self.sb("stat", 64)
        negA = self.sb("negA", 4)
        negAB = Buf("negA")
        psum = [self.es.enter_context(nc.psum_tensor("ps%d" % i, [128, 512], F32)) for i in range(8)]
        psB = [[Buf("ps%d" % i, excl=True)] * 4 for i in range(8)]
        self.psum, self.psB = psum, psB
        self.h, self.hB = h, hB

        C = self.CP
        g0 = S.new_group()
        S.dma(SP, lambda e: e.dma_start(out=cp[:], in_=cpk), "const", g0, writes=[cpB])
        self.cp, self.cpB = cp, cpB

        def cs(name, n=None):
            a, b = C[name]
            return cp[:, a:b]

        self.cs = cs
        ident = cs("ident")
        op(POOL, lambda e: e.memset(Sst[0][:], 0.0), writes=SsB[0])
        op(POOL, lambda e: e.memset(pch[:], 0.0), writes=[pchB])
        op(ACT, lambda e: e.activation(out=negA[:], in_=cs("alog"), func=AF.Exp), reads=[cpB], writes=[negAB])
        op(DVE, lambda e: e.tensor_scalar(out=negA[:], in0=negA[:], scalar1=-1.0, scalar2=None, op0=ALU.mult),
           reads=[negAB], writes=[negAB])
        self.negA, self.negAB = negA, negAB
        self.pch, self.pchB = pch, pchB
        self.Sst, self.SsB = Sst, SsB
        self.ygT, self.ygB = ygT, ygB
        self.spar = 0
        self.stat = stat
        self.statB = Buf("stat")

        import os
        PH = set(os.environ.get("KPH", "pf,pg,of,og,cv,f2").split(","))
        if "pf" in PH:
            self.load_x(x_pre)
            self.ffn(w1_in, w1_out, "n1", tag="p1")
        op(POOL, lambda e: e.tensor_copy(out=hhalo[:], in_=h[:, (NT - 1) * D:NT * D]), reads=[hB[NT - 1]], writes=[hhB])
        if "pg" in PH:
            self.gdn_phase(wm_in, full=False, tag="pg")
        self.load_x(x_own)
        if "of" in PH:
            self.ffn(w1_in, w1_out, "n1", tag="o1")
        self.dbg("h1", h[:, 0:D], D, [hB[0]])
        if "og" in PH:
            self.gdn_phase(wm_in, full=True, tag="og")
        self.dbg("yg", self.ygT[:, 0:T], T, self.ygB, BF16)
        es_x = contextlib.ExitStack()
        xnT2 = self.sb("xnT2", 8 * T, BF16, es_x)
        xnB2 = [Buf("xn2_%d" % t) for t in range(NT)]
        S.alias(xnB2, getattr(self, "phase_bufs", []))
        keep = list(getattr(self, "phase_bufs", []))
        if "cv" in PH:
            self.conv_phase(wm_in, wm_out, hhalo, hhB, nxt=(xnT2, xnB2))
        self.dbg("h2", h[:, 0:D], D, [hB[0]])
        if "f2" in PH:
            self.ffn(w2_in, w2_out, "n2", tag="o2", pre=(xnT2, xnB2))
        self.phase_bufs = list(self.phase_bufs) + xnB2
        es_x.close()
        self.final(out, fn_bc)
        S.emit(nc, final_wait_keys=["out0", "out1"] + self.dbg_keys)
        self.es.close()
        return nc

    def load_x(self, xd):
        S, h, hB = self.S, self.h, self.hB
        g = S.new_group()
        for t in range(NT):
            S.dma(SP, (lambda t: lambda e: e.dma_start(out=h[:, t * D:(t + 1) * D], in_=xd[t * 128:(t + 1) * 128, :]))(t),
                  "x%d" % (t % 4), g, writes=[hB[t]])

    def load_cast(self, dst_ap, dstB, src_ap, shape3=None, key="w"):
        S = self.S
        n = len(self.stage)
        i = self.st_i % n
        self.st_i += 1
        st, sB = self.stage[i], self.stB[i]
        a, b = shape3
        sview = st[:, 0:a * b].rearrange("p (a b) -> p a b", a=a) if a > 1 else st[:, 0:b]
        sflat = st[:, 0:a * b]
        g = S.new_group()
        S.dma(SP, lambda e: e.dma_start(out=sview, in_=src_ap), "st%d" % i, g, writes=[sB])
        self.cast_i = getattr(self, "cast_i", 0) + 1
        eng = self.cast_order[self.cast_i % len(self.cast_order)]
        if eng == ACT:
            S.op(ACT, lambda e: e.activation(out=dst_ap, in_=sflat, func=AF.Copy), reads=[sB], writes=[dstB])
        else:
            S.op(eng, lambda e: e.tensor_copy(out=dst_ap, in_=sflat), reads=[sB], writes=[dstB])

    def norm_transpose(self, src_ap, srcB, gain_name, dst, dst_off, dst_stride, dstB, xs, xsB, pbanks):
        S, cs = self.S, self.cs
        op = S.op
        stat = self.stat
        stB = self.statB
        op(ACT, lambda e: e.activation(out=xs[:, 0:D], in_=src_ap, func=AF.Square, accum_out=stat[:, 0:1]),
           reads=[srcB], writes=[xsB, stB])
        op(POOL, lambda e: e.tensor_scalar(out=stat[:, 1:2], in0=stat[:, 0:1], scalar1=1.0 / D, scalar2=EPS,
                                           op0=ALU.mult, op1=ALU.add), reads=[stB], writes=[stB])
        op(POOL, lambda e: e.tensor_tensor(out=stat[:, 2:3], in0=stat[:, 1:2], in1=cs("mhalf"), op=ALU.pow),
           reads=[stB, self.cpB], writes=[stB])
        op(DVE, lambda e: e.tensor_scalar(out=xs[:, 0:D], in0=src_ap, scalar1=stat[:, 2:3], scalar2=None, op0=ALU.mult),
           reads=[srcB, stB], writes=[xsB])
        gain = cs(gain_name)
        ident = cs("ident")
        for half in range(2):
            pb = pbanks[half]
            pbuf = self.psum[pb]
            for q in range(4):
                c = half * 4 + q
                op(PE, (lambda c, q, pbuf: lambda e: e.transpose(pbuf[:, q * 128:(q + 1) * 128], xs[:, c * 128:(c + 1) * 128], ident))(c, q, pbuf),
                   reads=[xsB, self.cpB], writes=[self.psB[pb][q]])
            for q in range(4):
                c = half * 4 + q
                eng = self.ew()
                o_ap = dst[:, c * dst_stride + dst_off: c * dst_stride + dst_off + 128]
                i_ap = pbuf[:, q * 128:(q + 1) * 128]
                g_ap = gain[:, c:c + 1]
                if eng == ACT:
                    op(ACT, (lambda o_ap, i_ap, g_ap: lambda e: e.activation(out=o_ap, in_=i_ap, func=AF.Copy, scale=g_ap))(o_ap, i_ap, g_ap),
                       reads=[self.psB[pb][q], self.cpB], writes=[dstB])
                else:
                    op(DVE, (lambda o_ap, i_ap, g_ap: lambda e: e.tensor_scalar(out=o_ap, in0=i_ap, scalar1=g_ap, scalar2=None, op0=ALU.mult))(o_ap, i_ap, g_ap),
                       reads=[self.psB[pb][q], self.cpB], writes=[dstB])

    def ffn(self, w_in, w_out, gain_name, tag, pre=None):
        import os
        nc, S = self.nc, self.S
        op = S.op
        h, hB = self.h, self.hB
        psum, psB = self.psum, self.psB
        es = contextlib.ExitStack()
        if pre is None:
            xnT = self.sb("xnT_" + tag, 8 * T, BF16, es)
            xnB = [Buf("xn%d" % t) for t in range(NT)]
        else:
            xnT, xnB = pre
        CPP = int(os.environ.get("KCPP", 4))
        nsub = CPP // 2
        wbi = [self.sb("wbi%d_%s" % (i, tag), 2 * nsub * 2048, BF16, es) for i in range(2)]
        wbo = [self.sb("wbo%d_%s" % (i, tag), CPP * 1024, BF16, es) for i in range(2)]
        wbiB = [[Buf() for _ in range(2 * nsub)] for _ in range(2)]
        wboB = [[Buf() for _ in range(nsub)] for _ in range(2)]
        hid = [self.sb("hid%d_%s" % (i, tag), CPP * 512, BF16, es) for i in range(2)]
        hidB = [Buf(), Buf()]
        sg0 = self.sb("sg0_%s" % tag, 512, F32, es); sg = [sg0, sg0]
        sgB0 = Buf(); sgB = [sgB0, sgB0]
        ev0 = self.sb("ev0_%s" % tag, 512, F32, es); ev = [ev0, ev0]
        evB0 = Buf(); evB = [evB0, evB0]
        xs0 = self.sb("xs0_%s" % tag, D, F32, es); xs = [xs0, xs0]
        xsB0 = Buf(); xsB = [xsB0, xsB0]
        self.junk = self.sb("junk_" + tag, D, BF16, es)
        self.junkB = Buf()
        base_stage, base_stB = self.stage, self.stB
        nextra = int(os.environ.get("KXST", 0))
        xst = [self.sb("xst%d_%s" % (i, tag), 2048, F32, es) for i in range(nextra)]
        xstB = [Buf() for _ in range(nextra)]
        self.stage, self.stB = base_stage + xst, base_stB + xstB
        new_bufs = (xnB if pre is None else []) + wbiB[0] + wbiB[1] + wboB[0] + wboB[1] + hidB + sgB + xsB + [self.junkB] + evB + xstB
        S.alias(new_bufs, getattr(self, "phase_bufs", []))
        self.phase_bufs = new_bufs

        w_in_v = w_in.rearrange("(c p) n -> p c n", p=128)
        w_out_v = w_out.rearrange("(c p) n -> p c n", p=128)
        pieces = []
        c0 = 0
        while c0 < DFF // 128:
            n = min(CPP, DFF // 128 - c0)
            pieces.append((c0, n))
            c0 += n
        NP = len(pieces)

        def load_piece(p):
            s = p % 2
            ch0, n = pieces[p]
            for sub in range(n // 2):
                col = (ch0 + 2 * sub) * 128
                self.load_cast(wbi[s][:, sub * 2048:(sub + 1) * 2048], wbiB[s][sub], w_in_v[:, :, col:col + 256], (8, 256))
                self.load_cast(wbi[s][:, (nsub + sub) * 2048:(nsub + 1 + sub) * 2048], wbiB[s][nsub + sub], w_in_v[:, :, DFF + col:DFF + col + 256], (8, 256))
                self.load_cast(wbo[s][:, sub * 2048:(sub + 1) * 2048], wboB[s][sub], w_out_v[:, ch0 + 2 * sub:ch0 + 2 * sub + 2, :], (2, 1024))

        load_piece(0)
        for t in range(NT):
            self.norm_transpose(h[:, t * D:(t + 1) * D], hB[t], gain_name, xnT, t * 128, T, xnB[t],
                                xs[t % 2], xsB[t % 2], (6, 7))
        blocks = [(p, tb) for p in range(NP) for tb in range(NB)]
        st = {"gi": 0, "oi": 0}

        def stage1(idx):
            p, tb = blocks[idx]
            s = p % 2
            hs = idx % 2
            n = pieces[p][1]
            for j in range(n):
                gi = st["gi"]
                for which in range(2):
                    pb = (0 if which == 0 else 2) + (gi % 2)
                    sub = which * nsub + j // 2
                    for c in range(8):
                        lhsT = wbi[s][:, sub * 2048 + c * 256 + (j % 2) * 128: sub * 2048 + c * 256 + (j % 2) * 128 + 128]
                        rhs = xnT[:, c * T + tb * 512: c * T + (tb + 1) * 512]
                        op(PE, (lambda pb, lhsT, rhs, c: lambda e: e.matmul(psum[pb][:, :], lhsT=lhsT, rhs=rhs, start=(c == 0), stop=(c == 7)))(pb, lhsT, rhs, c),
                           reads=[wbiB[s][sub]] + xnB[tb * 4:(tb + 1) * 4], writes=psB[pb])
                pg, pu = (gi % 2), 2 + (gi % 2)
                k = gi % 2
                op(ACT, (lambda pg, k: lambda e: e.activation(out=sg[k][:, :], in_=psum[pg][:, :], func=AF.Silu))(pg, k),
                   reads=psB[pg], writes=[sgB[k]])
                op(DVE, (lambda pu, k, hs, j: lambda e: e.tensor_tensor(out=hid[hs][:, j * 512:(j + 1) * 512], in0=sg[k][:, :], in1=psum[pu][:, :], op=ALU.mult))(pu, k, hs, j),
                   reads=[sgB[k]] + psB[pu], writes=[hidB[hs]])
                st["gi"] += 1

        def stage2(idx):
            p, tb = blocks[idx]
            s = p % 2
            hs = idx % 2
            n = pieces[p][1]
            for tt in range(4):
                t = tb * 4 + tt
                for hh in range(2):
                    oi = st["oi"]
                    pb = 4 + (oi % 2)
                    for j in range(n):
                        lhsT = hid[hs][:, j * 512 + tt * 128: j * 512 + (tt + 1) * 128]
                        rhs = wbo[s][:, j * 1024 + hh * 512: j * 1024 + (hh + 1) * 512]
                        op(PE, (lambda pb, lhsT, rhs, j: lambda e: e.matmul(psum[pb][:, :], lhsT=lhsT, rhs=rhs, start=(j == 0), stop=(j == n - 1)))(pb, lhsT, rhs, j),
                           reads=[hidB[hs], wboB[s][j // 2]], writes=psB[pb])
                    hap = h[:, t * D + hh * 512: t * D + (hh + 1) * 512]
                    if oi % 2 == 0:
                        op(DVE, (lambda pb, hap: lambda e: e.scalar_tensor_tensor(out=hap, in0=psum[pb][:, :], scalar=0.5, in1=hap, op0=ALU.mult, op1=ALU.add))(pb, hap),
                           reads=psB[pb] + [hB[t]], writes=[hB[t]])
                    else:
                        k = (oi // 2) % 2
                        op(ACT, (lambda pb, k: lambda e: e.activation(out=ev[k][:, :], in_=psum[pb][:, :], func=AF.Copy, scale=0.5))(pb, k),
                           reads=psB[pb], writes=[evB[k]])
                        op(POOL, (lambda k, hap: lambda e: e.tensor_tensor(out=hap, in0=hap, in1=ev[k][:, :], op=ALU.add))(k, hap),
                           reads=[evB[k], hB[t]], writes=[hB[t]])
                    st["oi"] += 1

        for idx in range(len(blocks)):
            stage1(idx)
            if idx > 0:
                stage2(idx - 1)
            p, tb = blocks[idx]
            if tb == 0 and p + 1 < NP:
                load_piece(p + 1)
        stage2(len(blocks) - 1)
        self.stage, self.stB = base_stage, base_stB
        es.close()

    def gdn_phase(self, wm_in, full, tag):
        import os
        nc, S = self.nc, self.S
        op = S.op
        cs, cpB = self.cs, self.cpB
        h, hB = self.h, self.hB
        psum, psB = self.psum, self.psB
        es = contextlib.ExitStack()
        pc = self.sb("pc_" + tag, 12 * 131, F32, es); pcB = Buf()
        wg = self.sb("wg_" + tag, 8 * NG, BF16, es)
        wgB = [Buf() for _ in range(9)]
        xs = self.sb("xs_" + tag, D, F32, es); xsB = Buf()
        xn = self.sb("xn_" + tag, 8 * 128, BF16, es); xnB = Buf()
        qkv0 = self.sb("qkv_" + tag, 12 * 128, F32, es); qkvB0 = Buf()
        qkv_b = [qkv0, self.stage[0][:, 0:1536]]; qkvB_b = [qkvB0, self.stB[0]]
        cacc = self.sb("cacc_" + tag, 12 * 128, F32, es); caccB = Buf()
        etmp = self.sb("etmp_" + tag, 12 * 128, F32, es); etmpB = Buf()
        rs, rsB = etmp, etmpB
        zT0 = self.sb("zT_" + tag, 4 * 128, F32, es); zB0 = Buf()
        zT_b = [zT0, self.stage[1][:, 0:512]]; zB_b = [zB0, self.stB[1]]
        sm_b = [self.sb("sm%d_%s" % (i, tag), 64, F32, es) for i in range(2)]; smB_b = [Buf(), Buf()]
        Dg = self.sb("Dg_" + tag, 512, F32, es); DgB = Buf()
        glb_b = [self.sb("glb%d_%s" % (i, tag), 8, F32, es) for i in range(2)]; glB_b = [Buf(), Buf()]
        def mk(n, cols=128, dt=F32):
            return self.sb(n + "_" + tag, cols, dt, es), Buf(n)
        HB = []
        for hh in range(4):
            d_ = {}
            for n in ["kbg", "kdec", "vbeta", "tm1", "E1", "Lm", "AT", "X0", "X1", "Y0", "Y1", "P0", "P1", "wT", "um", "vnew"]:
                d_[n] = mk("%s%d" % (n, hh))
            HB.append(d_)
        new_bufs = wgB + [pcB, xsB, xnB, qkvB0, caccB, etmpB, zB0, DgB] + smB_b + glB_b + [b for d_ in HB for _, b in d_.values()]
        S.alias(new_bufs, getattr(self, "phase_bufs", []))
        self.phase_bufs = new_bufs

        wm_v = wm_in.rearrange("(c p) n -> p c n", p=128)
        col = 0
        while col < NG:
            w = min(256, NG - col)
            base = (col // 256) * 2048
            self.load_cast(wg[:, base:base + 8 * w], wgB[col // 256], wm_v[:, :, GW0 + col:GW0 + col + w], (8, w))
            col += w

        ident, ones, triU = cs("ident"), cs("ones"), cs("triU")
        maskL, maskU = cs("maskL"), cs("maskU")
        cwg = cs("cwg")
        pcv0 = pc[:, :].rearrange("p (a b) -> p a b", a=12)
        pchv = self.pch[:, :].rearrange("p (a b) -> p a b", a=12)
        op(DVE, lambda e: e.tensor_copy(out=pcv0[:, :, 0:3], in_=pchv), reads=[self.pchB], writes=[pcB])
        Sst, SsB = self.Sst, self.SsB
        nq = 16 if full else 12

        def pre_ops(t):
            pp = t % 2
            qkv, qkvB = qkv_b[pp], qkvB_b[pp]
            zT, zB = zT_b[pp], zB_b[pp]
            sm, smB = sm_b[pp], smB_b[pp]
            glb, glB = glb_b[pp], glB_b[pp]
            S.capture = []
            self.norm_transpose(h[:, t * D:(t + 1) * D], hB[t], "nm", xn, 0, 128, xnB, xs, xsB, (0, 1))
            for grp in range(nq // 4):
                if (not full) and grp == 0 and t != NT - 1:
                    continue
                pb = 2
                for q in range(4):
                    j = grp * 4 + q
                    for c in range(8):
                        lhsT = wg[:, (j // 2) * 2048 + c * 256 + (j % 2) * 128: (j // 2) * 2048 + c * 256 + (j % 2) * 128 + 128]
                        rhs = xn[:, c * 128:(c + 1) * 128]
                        op(PE, (lambda pb, q, lhsT, rhs, c: lambda e: e.matmul(psum[pb][:, q * 128:(q + 1) * 128], lhsT=lhsT, rhs=rhs, start=(c == 0), stop=(c == 7)))(pb, q, lhsT, rhs, c),
                           reads=[wgB[j // 2], xnB], writes=[psB[pb][q]])
                if grp < 3:
                    dstv = pc[:, grp * 4 * 131:(grp + 1) * 4 * 131].rearrange("p (a b) -> p a b", a=4)[:, :, 3:131]
                    srcv = psum[pb][:, :].rearrange("p (a b) -> p a b", a=4)
                    eng = self.ew()
                    if eng == ACT:
                        op(ACT, (lambda dstv, srcv: lambda e: e.activation(out=dstv, in_=srcv, func=AF.Copy))(dstv, srcv), reads=psB[pb], writes=[pcB])
                    else:
                        op(DVE, (lambda dstv, srcv: lambda e: e.tensor_copy(out=dstv, in_=srcv))(dstv, srcv), reads=psB[pb], writes=[pcB])
                else:
                    op(ACT, (lambda pb: lambda e: e.activation(out=zT[:, :], in_=psum[pb][:, :], func=AF.Copy))(pb), reads=psB[pb], writes=[zB])
            i3 = len(S.capture)
            for c in range(8):
                lhsT = xn[:, c * 128:(c + 1) * 128]
                rhs = wg[:, 8 * 2048 + c * 8: 8 * 2048 + c * 8 + 8]
                op(PE, (lambda lhsT, rhs, c: lambda e: e.matmul(psum[2][:, 0:8], lhsT=lhsT, rhs=rhs, start=(c == 0), stop=(c == 7)))(lhsT, rhs, c),
                   reads=[wgB[8], xnB], writes=[psB[2][0]])
            op(DVE, lambda e: e.tensor_copy(out=sm[:, 0:8], in_=psum[2][:, 0:8]), reads=[psB[2][0]], writes=[smB])
            op(ACT, lambda e: e.activation(out=sm[:, 8:12], in_=sm[:, 0:4], func=AF.Exp, scale=-1.0), reads=[smB], writes=[smB])
            op(DVE, lambda e: e.tensor_scalar(out=sm[:, 8:12], in0=sm[:, 8:12], scalar1=1.0, scalar2=None, op0=ALU.add), reads=[smB], writes=[smB])
            op(DVE, lambda e: e.reciprocal(out=sm[:, 8:12], in_=sm[:, 8:12]), reads=[smB], writes=[smB])
            op(DVE, lambda e: e.tensor_tensor(out=sm[:, 12:16], in0=sm[:, 4:8], in1=cs("dtb"), op=ALU.add), reads=[smB, cpB], writes=[smB])
            op(ACT, lambda e: e.activation(out=sm[:, 12:16], in_=sm[:, 12:16], func=AF.Exp), reads=[smB], writes=[smB])
            op(ACT, lambda e: e.activation(out=sm[:, 12:16], in_=sm[:, 12:16], func=AF.Ln, bias=1.0), reads=[smB], writes=[smB])
            op(DVE, lambda e: e.tensor_tensor(out=sm[:, 12:16], in0=sm[:, 12:16], in1=self.negA[:, :], op=ALU.mult), reads=[smB, self.negAB], writes=[smB])
            op(PE, lambda e: e.matmul(psum[2][:, 8:12], lhsT=triU, rhs=sm[:, 12:16], start=True, stop=True), reads=[smB, cpB], writes=[psB[2][0]])
            op(DVE, lambda e: e.tensor_copy(out=sm[:, 16:20], in_=psum[2][:, 8:12]), reads=[psB[2][0]], writes=[smB])
            op(ACT, lambda e: e.activation(out=sm[:, 20:24], in_=sm[:, 16:20], func=AF.Exp), reads=[smB], writes=[smB])
            op(DVE, lambda e: e.tensor_scalar(out=sm[:, 24:28], in0=sm[:, 16:20], scalar1=-1.0, scalar2=None, op0=ALU.mult), reads=[smB], writes=[smB])
            op(DVE, lambda e: e.tensor_tensor(out=sm[:, 28:32], in0=sm[:, 8:12], in1=sm[:, 20:24], op=ALU.mult), reads=[smB], writes=[smB])
            for hh in range(4):
                op(DVE, (lambda hh: lambda e: e.tensor_scalar(out=Dg[:, hh * 128:(hh + 1) * 128], in0=ident, scalar1=sm[:, 16 + hh:17 + hh], scalar2=None, op0=ALU.mult))(hh),
                   reads=[smB, cpB], writes=[DgB])
            op(PE, lambda e: e.matmul(psum[3][:, :], lhsT=ones, rhs=Dg[:, 0:512], start=True, stop=True), reads=[DgB, cpB], writes=psB[3])
            Gv = psum[3][:, :].rearrange("p (a b) -> p a b", a=4)
            op(ACT, lambda e: e.activation(out=glb[:, 0:8].rearrange("p (a b) -> p a b", a=4), in_=Gv[:, :, 63:128:64], func=AF.Exp), reads=psB[3], writes=[glB])
            op(DVE, lambda e: e.tensor_tensor(out=sm[0:64, 32:36], in0=Gv[0:64, :, 63], in1=sm[0:64, 16:20], op=ALU.subtract), reads=psB[3] + [smB], writes=[smB])
            op(DVE, lambda e: e.tensor_tensor(out=sm[64:128, 32:36], in0=Gv[64:128, :, 127], in1=sm[64:128, 16:20], op=ALU.subtract), reads=psB[3] + [smB], writes=[smB])
            op(ACT, lambda e: e.activation(out=sm[:, 32:36], in_=sm[:, 32:36], func=AF.Exp), reads=[smB], writes=[smB])
            i4 = len(S.capture)
            pcv = pc[:, :].rearrange("p (a b) -> p a b", a=12)
            for j in range(0 if full else 4, 12):
                op(DVE, (lambda j: lambda e: e.tensor_scalar(out=cacc[:, j * 128:(j + 1) * 128], in0=pc[:, j * 131:j * 131 + 128], scalar1=cwg[:, j * 4:j * 4 + 1], scalar2=None, op0=ALU.mult))(j),
                   reads=[pcB, cpB], writes=[caccB])
                for k in range(1, 4):
                    op(DVE, (lambda j, k: lambda e: e.scalar_tensor_tensor(out=cacc[:, j * 128:(j + 1) * 128], in0=pc[:, j * 131 + k:j * 131 + k + 128], scalar=cwg[:, j * 4 + k:j * 4 + k + 1], in1=cacc[:, j * 128:(j + 1) * 128], op0=ALU.mult, op1=ALU.add))(j, k),
                       reads=[pcB, cpB, caccB], writes=[caccB])
            op(DVE, lambda e: e.tensor_copy(out=pcv[:, :, 0:3], in_=pcv[:, :, 128:131]), reads=[pcB], writes=[pcB])
            i5 = len(S.capture)
            c_lo = 0 if full else 512
            op(ACT, lambda e: e.activation(out=etmp[:, c_lo:1536], in_=cacc[:, c_lo:1536], func=AF.Exp, scale=-1.0), reads=[caccB], writes=[etmpB])
            op(ACT, lambda e: e.activation(out=etmp[:, c_lo:1536], in_=etmp[:, c_lo:1536], func=AF.Ln, bias=1.0), reads=[etmpB], writes=[etmpB])
            op(ACT, lambda e: e.activation(out=etmp[:, c_lo:1536], in_=etmp[:, c_lo:1536], func=AF.Exp, scale=-1.0), reads=[etmpB], writes=[etmpB])
            op(DVE, lambda e: e.tensor_tensor(out=qkv[:, c_lo:1536], in0=cacc[:, c_lo:1536], in1=etmp[:, c_lo:1536], op=ALU.mult), reads=[etmpB, caccB], writes=[qkvB])
            op(ACT, lambda e: e.activation(out=etmp[:, c_lo:1024], in_=qkv[:, c_lo:1024], func=AF.Square), reads=[qkvB, etmpB], writes=[etmpB])
            for half in range(0 if full else 1, 2):
                op(PE, (lambda half: lambda e: e.matmul(psum[half][:, :], lhsT=ones, rhs=etmp[:, half * 512:(half + 1) * 512], start=True, stop=True))(half),
                   reads=[etmpB, cpB], writes=psB[half])
                op(ACT, (lambda half: lambda e: e.activation(out=rs[:, half * 512:(half + 1) * 512], in_=psum[half][:, :], func=AF.Ln, bias=cs("eps")))(half),
                   reads=psB[half] + [cpB], writes=[rsB])
            op(ACT, lambda e: e.activation(out=rs[:, c_lo:1024], in_=rs[:, c_lo:1024], func=AF.Exp, scale=-0.5), reads=[rsB], writes=[rsB])
            if full:
                op(DVE, lambda e: e.scalar_tensor_tensor(out=qkv[:, 0:512], in0=qkv[:, 0:512], scalar=128.0 ** -0.5, in1=rs[:, 0:512], op0=ALU.mult, op1=ALU.mult), reads=[qkvB, rsB], writes=[qkvB])
            op(DVE, lambda e: e.tensor_tensor(out=qkv[:, 512:1024], in0=qkv[:, 512:1024], in1=rs[:, 512:1024], op=ALU.mult), reads=[qkvB, rsB], writes=[qkvB])
            if full:
                op(ACT, lambda e: e.activation(out=etmp[:, 0:512], in_=zT[:, :], func=AF.Exp, scale=-1.0), reads=[zB, etmpB], writes=[etmpB])
                op(ACT, lambda e: e.activation(out=etmp[:, 0:512], in_=etmp[:, 0:512], func=AF.Ln, bias=1.0), reads=[etmpB], writes=[etmpB])
                op(ACT, lambda e: e.activation(out=etmp[:, 0:512], in_=etmp[:, 0:512], func=AF.Exp, scale=-1.0), reads=[etmpB], writes=[etmpB])
                op(DVE, lambda e: e.tensor_tensor(out=zT[:, :], in0=zT[:, :], in1=etmp[:, 0:512], op=ALU.mult), reads=[etmpB, zB], writes=[zB])
            cap_ = S.capture
            S.capture = None
            s3, s4 = cap_[i3:i4], cap_[i4:i5]
            mer = []
            i_, j_ = 0, 0
            while i_ < len(s3) or j_ < len(s4):
                if j_ < len(s4) and (i_ >= len(s3) or j_ * max(len(s3), 1) <= i_ * len(s4)):
                    mer.append(s4[j_]); j_ += 1
                else:
                    mer.append(s3[i_]); i_ += 1
            return cap_[:i3] + mer + cap_[i5:]

        def chain(hh, t):
            pp = t % 2
            qkv, qkvB = qkv_b[pp], qkvB_b[pp]
            zT, zB = zT_b[pp], zB_b[pp]
            sm, smB = sm_b[pp], smB_b[pp]
            glb, glB = glb_b[pp], glB_b[pp]
            B_ = HB[hh]
            kbg, kbgB = B_["kbg"]; kdec, kdecB = B_["kdec"]; vbeta, vbetaB = B_["vbeta"]
            tm1, tm1B = B_["tm1"]; E1, E1B = B_["E1"]; Lm, LmB = B_["Lm"]; AT, ATB = B_["AT"]
            X = [B_["X0"], B_["X1"]]; Y = [B_["Y0"], B_["Y1"]]; Pm = [B_["P0"], B_["P1"]]
            wT, wTB = B_["wT"]; um, umB = B_["um"]; vnew, vnewB = B_["vnew"]
            o1s, o1sB = tm1, tm1B
            om, omB = E1, E1B
            on, onB = Lm, LmB
            qT = qkv[:, hh * 128:(hh + 1) * 128]
            kT = qkv[:, 512 + hh * 128:512 + (hh + 1) * 128]
            vT = qkv[:, 1024 + hh * 128:1024 + (hh + 1) * 128]
            PH = psum[4 + hh]
            PB = psB[4 + hh][0]
            Q0, Q1, Q2, Q3 = PH[:, 0:128], PH[:, 128:256], PH[:, 256:384], PH[:, 384:512]
            Gh = psum[3][:, hh * 128:(hh + 1) * 128]
            GhB = psB[3]
            op(PE, lambda e: e.transpose(Q0, kT, ident), reads=[qkvB, cpB], writes=[PB])
            op(PE, lambda e: e.transpose(Q1, vT, ident), reads=[qkvB, cpB], writes=[PB])
            op(PE, lambda e: e.matmul(Q2, lhsT=kT, rhs=kT, start=True, stop=True), reads=[qkvB], writes=[PB])
            if full:
                op(PE, lambda e: e.matmul(Q3, lhsT=kT, rhs=qT, start=True, stop=True), reads=[qkvB], writes=[PB])
            yield
            op(DVE, lambda e: e.tensor_tensor(out=tm1[:, :], in0=maskL, in1=Gh, op=ALU.subtract), reads=GhB + [cpB], writes=[tm1B])
            yield
            op(ACT, lambda e: e.activation(out=E1[:, :], in_=tm1[:, :], func=AF.Exp, bias=sm[:, 16 + hh:17 + hh]), reads=[tm1B, smB], writes=[E1B])
            yield
            op(ACT, lambda e: e.activation(out=kbg[:, :], in_=Q0, func=AF.Copy, scale=sm[:, 28 + hh:29 + hh]), reads=[PB, smB], writes=[kbgB])
            op(ACT, lambda e: e.activation(out=vbeta[:, :], in_=Q1, func=AF.Copy, scale=sm[:, 8 + hh:9 + hh]), reads=[PB, smB], writes=[vbetaB])
            yield
            op(DVE, lambda e: e.tensor_scalar(out=kdec[:, :], in0=Q0, scalar1=sm[:, 32 + hh:33 + hh], scalar2=None, op0=ALU.mult), reads=[PB, smB], writes=[kdecB])
            op(DVE, lambda e: e.scalar_tensor_tensor(out=Lm[:, :], in0=Q2, scalar=sm[:, 8 + hh:9 + hh], in1=E1[:, :], op0=ALU.mult, op1=ALU.mult), reads=[PB, smB, E1B], writes=[LmB])
            yield
            if full:
                op(DVE, lambda e: e.tensor_tensor(out=tm1[:, :], in0=maskU, in1=Gh, op=ALU.add), reads=GhB + [cpB, tm1B], writes=[tm1B])
                yield
                op(ACT, lambda e: e.activation(out=E1[:, :], in_=tm1[:, :], func=AF.Exp, bias=sm[:, 24 + hh:25 + hh]), reads=[tm1B, smB, E1B], writes=[E1B])
                yield
                op(DVE, lambda e: e.tensor_tensor(out=AT[:, :], in0=Q3, in1=E1[:, :], op=ALU.mult), reads=[PB, E1B], writes=[ATB])
                yield
            op(PE, lambda e: e.transpose(Q0, Lm[:, :], ident), reads=[LmB, cpB], writes=[PB])
            yield
            X0, X0B = X[0]
            P0, P0B = Pm[0]
            op(ACT, lambda e: e.activation(out=X0[:, :], in_=Q0, func=AF.Copy), reads=[PB], writes=[X0B])
            op(DVE, lambda e: e.tensor_tensor(out=P0[:, :], in0=ident, in1=Q0, op=ALU.subtract), reads=[PB, cpB], writes=[P0B])
            yield
            Xc, XcB = X0, X0B
            Yc, YcB = Lm, LmB
            Pc, PcB = P0, P0B
            for k in range(1, 6):
                Yn, YnB = Y[k % 2]
                Xn, XnB = X[k % 2]
                Pn, PnB = Pm[k % 2]
                op(PE, (lambda Xc, Yc: lambda e: e.matmul(Q1, lhsT=Xc[:, :], rhs=Yc[:, :], start=True, stop=True))(Xc, Yc), reads=[XcB, YcB], writes=[PB])
                if k < 5:
                    op(PE, (lambda Xc, Yc: lambda e: e.matmul(Q2, lhsT=Yc[:, :], rhs=Xc[:, :], start=True, stop=True))(Xc, Yc), reads=[XcB, YcB], writes=[PB])
                yield
                op(ACT, (lambda Yn: lambda e: e.activation(out=Yn[:, :], in_=Q1, func=AF.Copy))(Yn), reads=[PB], writes=[YnB])
                if k < 5:
                    op(ACT, (lambda Xn: lambda e: e.activation(out=Xn[:, :], in_=Q2, func=AF.Copy))(Xn), reads=[PB], writes=[XnB])
                yield
                op(PE, (lambda Yn, Pc: lambda e: e.matmul(Q3, lhsT=Yn[:, :], rhs=Pc[:, :], start=True, stop=True))(Yn, Pc), reads=[YnB, PcB], writes=[PB])
                yield
                op(DVE, (lambda Pn, Pc: lambda e: e.tensor_tensor(out=Pn[:, :], in0=Pc[:, :], in1=Q3, op=ALU.add))(Pn, Pc), reads=[PcB, PB], writes=[PnB])
                yield
                Xc, XcB, Yc, YcB, Pc, PcB = Xn, XnB, Yn, YnB, Pn, PnB
            op(PE, (lambda Pc: lambda e: e.matmul(Q0, lhsT=kbg[:, :], rhs=Pc[:, :], start=True, stop=True))(Pc), reads=[kbgB, PcB], writes=[PB])
            op(PE, (lambda Pc: lambda e: e.matmul(Q1, lhsT=Pc[:, :], rhs=vbeta[:, :], start=True, stop=True))(Pc), reads=[vbetaB, PcB], writes=[PB])
            yield
            op(ACT, lambda e: e.activation(out=wT[:, :], in_=Q0, func=AF.Copy), reads=[PB], writes=[wTB])
            op(ACT, lambda e: e.activation(out=um[:, :], in_=Q1, func=AF.Copy), reads=[PB], writes=[umB])
            yield
            for half in range(2):
                r0, r1 = half * 64, half * 64 + 64
                sp_ = self.spar_h[hh]
                Scur = Sst[sp_][:, hh * 128:(hh + 1) * 128]; ScurB = SsB[sp_][hh]
                Snew = Sst[1 - sp_][:, hh * 128:(hh + 1) * 128]; SnewB = SsB[1 - sp_][hh]
                self.spar_h[hh] = 1 - sp_
                op(PE, (lambda r0, r1, Scur: lambda e: e.matmul(PH[r0:r1, 256:384], lhsT=wT[:, r0:r1], rhs=Scur, start=True, stop=True))(r0, r1, Scur), reads=[wTB, ScurB], writes=[PB])
                yield
                op(DVE, (lambda r0, r1: lambda e: e.tensor_tensor(out=vnew[r0:r1, :], in0=um[r0:r1, :], in1=PH[r0:r1, 256:384], op=ALU.subtract))(r0, r1), reads=[umB, PB], writes=[vnewB])
                yield
                if full:
                    op(PE, (lambda r0, r1, Scur: lambda e: e.matmul(PH[r0:r1, 0:128], lhsT=qT[:, r0:r1], rhs=Scur, start=True, stop=True))(r0, r1, Scur), reads=[qkvB, ScurB], writes=[PB])
                    op(PE, (lambda r0, r1: lambda e: e.matmul(PH[r0:r1, 128:256], lhsT=AT[r0:r1, r0:r1], rhs=vnew[r0:r1, :], start=True, stop=True))(r0, r1), reads=[ATB, vnewB], writes=[PB])
                op(PE, (lambda r0, r1: lambda e: e.matmul(Q3, lhsT=kdec[r0:r1, :], rhs=vnew[r0:r1, :], start=True, stop=True))(r0, r1), reads=[kdecB, vnewB], writes=[PB])
                yield
                op(DVE, (lambda Snew, Scur, half: lambda e: e.scalar_tensor_tensor(out=Snew, in0=Scur, scalar=glb[:, hh * 2 + half:hh * 2 + half + 1], in1=Q3, op0=ALU.mult, op1=ALU.add))(Snew, Scur, half),
                   reads=[ScurB, glB, PB], writes=[SnewB])
                yield
            if full:
                c0 = 40 + hh * 3
                op(ACT, lambda e: e.activation(out=o1s[:, :], in_=Q0, func=AF.Copy, scale=sm[:, 20 + hh:21 + hh]), reads=[PB, smB], writes=[o1sB])
                yield
                op(DVE, lambda e: e.tensor_tensor(out=om[:, :], in0=o1s[:, :], in1=Q1, op=ALU.add), reads=[o1sB, PB], writes=[omB])
                yield
                op(ACT, lambda e: e.activation(out=on[:, :], in_=om[:, :], func=AF.Square, accum_out=sm[:, c0:c0 + 1]), reads=[omB], writes=[onB, smB])
                yield
                op(POOL, lambda e: e.tensor_scalar(out=sm[:, c0 + 1:c0 + 2], in0=sm[:, c0:c0 + 1], scalar1=1.0 / 128, scalar2=EPS, op0=ALU.mult, op1=ALU.add), reads=[smB], writes=[smB])
                op(POOL, lambda e: e.tensor_tensor(out=sm[:, c0 + 2:c0 + 3], in0=sm[:, c0 + 1:c0 + 2], in1=cs("mhalf"), op=ALU.pow), reads=[smB, cpB], writes=[smB])
                yield
                op(DVE, lambda e: e.scalar_tensor_tensor(out=on[:, :], in0=om[:, :], scalar=sm[:, c0 + 2:c0 + 3], in1=cs("gon"), op0=ALU.mult, op1=ALU.mult), reads=[omB, smB, cpB], writes=[onB])
                yield
                op(PE, lambda e: e.transpose(Q2, on[:, :], ident), reads=[onB, cpB], writes=[PB])
                yield
                yg_ap = self.ygT[:, hh * T + t * 128: hh * T + (t + 1) * 128]
                op(DVE, lambda e: e.tensor_tensor(out=yg_ap, in0=Q2, in1=zT[:, hh * 128:(hh + 1) * 128], op=ALU.mult), reads=[PB, zB], writes=[self.ygB[t]])
                yield

        for it in pre_ops(0):
            S.replay(it)
        for t in range(NT):
            pend = pre_ops(t + 1) if t + 1 < NT else []
            per = (len(pend) + 39) // 40
            S0 = int(os.environ.get("KSTAG", 4))
            per = (len(pend) + 39 + 3 * S0) // (40 + 3 * S0)
            alive = [(hh, chain(hh, t)) for hh in range(4)]
            pi = 0
            rnd = 0
            while alive or pi < len(pend):
                nxt = []
                for hh, g_ in alive:
                    if rnd < hh * S0:
                        nxt.append((hh, g_))
                        continue
                    try:
                        next(g_)
                        nxt.append((hh, g_))
                    except StopIteration:
                        pass
                alive = nxt
                for it in pend[pi:pi + per]:
                    S.replay(it)
                pi += per
                rnd += 1
        op(DVE, lambda e: e.tensor_copy(out=pchv, in_=pcv0[:, :, 0:3]), reads=[pcB], writes=[self.pchB])
        es.close()

    def conv_phase(self, wm_in, wm_out, hhalo, hhB, nxt=None):
        import os
        nc, S = self.nc, self.S
        op = S.op
        cs, cpB = self.cs, self.cpB
        h, hB = self.h, self.hB
        psum, psB = self.psum, self.psB
        es = contextlib.ExitStack()
        tag = "cv"
        wc = self.sb("wc", 8 * 1536, BF16, es); wcB = [Buf() for _ in range(6)]
        wo = self.sb("wo", 8 * 1024, BF16, es); woB = [Buf() for _ in range(4)]
        xs = self.sb("xs_cv", D, F32, es); xsB = Buf()
        xn = self.sb("xn_cv", 8 * 128, BF16, es); xnB = Buf()
        mpc = self.sb("mpc", 4 * 130, F32, es); mpcB = Buf()
        mpc2 = self.sb("mpc2", 4 * 130, F32, es); mpc2B = Buf()
        cbs0 = self.sb("cbs0", 512, F32, es); cbs0B = Buf()
        cbs1 = self.sb("cbs1", 512, F32, es); cbs1B = Buf()
        cct = self.sb("cct", 512, F32, es); cctB = Buf()
        cacc = self.sb("cacc_cv", 512, F32, es); caccB = Buf()
        yv = self.sb("yv", 512, F32, es); yvB = Buf()
        sq, sqB = cacc, caccB
        rs = self.sb("rs_cv", 512, F32, es); rsB = Buf()
        ycT = self.sb("ycT", 512, BF16, es); ycB = Buf()
        xs2, xs2B = xs, xsB
        motmp = [self.sb("motmp%d" % i, 512, F32, es) for i in range(2)]; motB = [Buf(), Buf()]
        new_bufs = wcB + woB + [mpc2B, cbs0B, cbs1B, xsB, xnB, mpcB, cctB, caccB, yvB, rsB, ycB] + motB
        S.alias(new_bufs, getattr(self, "phase_bufs", []))
        self.phase_bufs = new_bufs
        wm_v = wm_in.rearrange("(c p) n -> p c n", p=128)
        for col in range(0, 1536, 256):
            base = (col // 256) * 2048
            self.load_cast(wc[:, base:base + 2048], wcB[col // 256], wm_v[:, :, col:col + 256], (8, 256))
        wo_v = wm_out.rearrange("(c p) n -> p c n", p=128)
        for i in range(4):
            self.load_cast(wo[:, i * 2048:(i + 1) * 2048], woB[i], wo_v[:, 2 * i:2 * i + 2, :], (2, 1024))
        op(POOL, lambda e: e.memset(mpc[:, :], 0.0), writes=[mpcB])
        op(POOL, lambda e: e.memset(mpc2[:, :], 0.0), writes=[mpc2B])
        ident, blk64 = cs("ident"), cs("blk64")
        csw, cgain = cs("csw"), cs("cgain")
        ygT, ygB = self.ygT, self.ygB
        mpc_b = [mpc, mpc2]; mpcB_b = [mpcB, mpc2B]
        cbs_b = [cbs0, cbs1]; cbsB_b = [cbs0B, cbs1B]

        def capA(t):
            p = t % 2
            mp, mpB = mpc_b[p], mpcB_b[p]
            mo, moB = mpc_b[1 - p], mpcB_b[1 - p]
            mpv = mp[:, :].rearrange("p (a b) -> p a b", a=4)
            mov = mo[:, :].rearrange("p (a b) -> p a b", a=4)
            S.capture = []
            if t < 0:
                src, srcB = hhalo[:, :], hhB
            else:
                src, srcB = h[:, t * D:(t + 1) * D], hB[t]
            self.norm_transpose(src, srcB, "nm", xn, 0, 128, xnB, xs, xsB, (0, 1))
            for grp in range(3):
                if t < 0 and grp == 0:
                    continue
                pb = 2 + grp
                for q in range(4):
                    j = grp * 4 + q
                    for c in range(8):
                        lhsT = wc[:, (j // 2) * 2048 + c * 256 + (j % 2) * 128: (j // 2) * 2048 + c * 256 + (j % 2) * 128 + 128]
                        rhs = xn[:, c * 128:(c + 1) * 128]
                        op(PE, (lambda pb, q, lhsT, rhs, c: lambda e: e.matmul(psum[pb][:, q * 128:(q + 1) * 128], lhsT=lhsT, rhs=rhs, start=(c == 0), stop=(c == 7)))(pb, q, lhsT, rhs, c),
                           reads=[wcB[j // 2], xnB], writes=[psB[pb][q]])
                if grp == 0:
                    op(ACT, (lambda p: lambda e: e.activation(out=cbs_b[p][:, :], in_=psum[2][:, :], func=AF.Copy))(p), reads=psB[2], writes=[cbsB_b[p]])
                if grp == 1:
                    op(ACT, lambda e: e.activation(out=cct[:, :], in_=psum[3][:, :], func=AF.Copy), reads=psB[3], writes=[cctB])
            op(DVE, lambda e: e.tensor_tensor(out=mpv[:, :, 2:130], in0=cct[:, :].rearrange("p (a b) -> p a b", a=4), in1=psum[4][:, :].rearrange("p (a b) -> p a b", a=4), op=ALU.mult),
               reads=[cctB, mpB] + psB[4], writes=[mpB])
            op(DVE, lambda e: e.tensor_copy(out=mpv[:, :, 0:2], in_=mov[:, :, 128:130]), reads=[moB, mpB], writes=[mpB])
            ops_ = S.capture
            S.capture = None
            return ops_

        def capB(t):
            p = t % 2
            mp, mpB = mpc_b[p], mpcB_b[p]
            cbs, cbsB = cbs_b[p], cbsB_b[p]
            S.capture = []
            for j in range(4):
                op(DVE, (lambda j: lambda e: e.tensor_scalar(out=cacc[:, j * 128:(j + 1) * 128], in0=mp[:, j * 130:j * 130 + 128], scalar1=csw[:, j * 3:j * 3 + 1], scalar2=None, op0=ALU.mult))(j),
                   reads=[mpB, cpB], writes=[caccB])
                for k in range(1, 3):
                    op(DVE, (lambda j, k: lambda e: e.scalar_tensor_tensor(out=cacc[:, j * 128:(j + 1) * 128], in0=mp[:, j * 130 + k:j * 130 + k + 128], scalar=csw[:, j * 3 + k:j * 3 + k + 1], in1=cacc[:, j * 128:(j + 1) * 128], op0=ALU.mult, op1=ALU.add))(j, k),
                       reads=[mpB, cpB, caccB], writes=[caccB])
            op(DVE, lambda e: e.tensor_tensor(out=yv[:, :], in0=cacc[:, :], in1=cbs[:, :], op=ALU.mult), reads=[caccB, cbsB], writes=[yvB])
            op(ACT, lambda e: e.activation(out=sq[:, :], in_=yv[:, :], func=AF.Square), reads=[yvB], writes=[sqB])
            op(PE, lambda e: e.matmul(psum[5][:, :], lhsT=blk64, rhs=sq[:, :], start=True, stop=True), reads=[sqB, cpB], writes=psB[5])
            op(ACT, lambda e: e.activation(out=rs[:, :], in_=psum[5][:, :], func=AF.Ln, bias=cs("eps")), reads=psB[5] + [cpB], writes=[rsB])
            op(ACT, lambda e: e.activation(out=rs[:, :], in_=rs[:, :], func=AF.Exp, scale=-0.5), reads=[rsB], writes=[rsB])
            for j in range(4):
                op(DVE, (lambda j: lambda e: e.scalar_tensor_tensor(out=ycT[:, j * 128:(j + 1) * 128], in0=yv[:, j * 128:(j + 1) * 128], scalar=cgain[:, j:j + 1], in1=rs[:, j * 128:(j + 1) * 128], op0=ALU.mult, op1=ALU.mult))(j),
                   reads=[yvB, rsB, cpB], writes=[ycB])
            for hh in range(2):
                pb = 6 + hh
                for j in range(8):
                    if j < 4:
                        lhsT = ycT[:, j * 128:(j + 1) * 128]
                        rd = [ycB]
                    else:
                        lhsT = ygT[:, (j - 4) * T + t * 128:(j - 4) * T + (t + 1) * 128]
                        rd = [ygB[t]]
                    rhs = wo[:, j * 1024 + hh * 512: j * 1024 + (hh + 1) * 512]
                    op(PE, (lambda pb, lhsT, rhs, j: lambda e: e.matmul(psum[pb][:, :], lhsT=lhsT, rhs=rhs, start=(j == 0), stop=(j == 7)))(pb, lhsT, rhs, j),
                       reads=rd + [woB[j // 2]], writes=psB[pb])
                hap = h[:, t * D + hh * 512: t * D + (hh + 1) * 512]
                op(ACT, (lambda pb, hh: lambda e: e.activation(out=motmp[hh][:, :], in_=psum[pb][:, :], func=AF.Copy))(pb, hh), reads=psB[pb], writes=[motB[hh]])
                op(DVE, (lambda hh, hap: lambda e: e.tensor_tensor(out=hap, in0=hap, in1=motmp[hh][:, :], op=ALU.add))(hh, hap), reads=[motB[hh], hB[t]], writes=[hB[t]])
            if nxt is not None:
                self.norm_transpose(h[:, t * D:(t + 1) * D], hB[t], "n2", nxt[0], t * 128, T, nxt[1][t], xs2, xs2B, (0, 1))
            ops_ = S.capture
            S.capture = None
            return ops_

        for it in capA(-1):
            S.replay(it)
        for it in capA(0):
            S.replay(it)
        for t in range(NT):
            A = capA(t + 1) if t + 1 < NT else []
            B_ = capB(t)
            na, nb = len(A), len(B_)
            ia = ib = 0
            while ia < na or ib < nb:
                if ib < nb and (ia >= na or ib * max(na, 1) <= ia * nb):
                    S.replay(B_[ib]); ib += 1
                else:
                    S.replay(A[ia]); ia += 1
        es.close()

    def final(self, out, fn_bc):
        S = self.S
        op = S.op
        cs, cpB = self.cs, self.cpB
        h, hB = self.h, self.hB
        es = contextlib.ExitStack()
        ot = [self.sb("ot%d" % i, D, F32, es) for i in range(2)]
        otB = [Buf(), Buf()]
        fs = self.sb("fs", 64, F32, es); fsB = Buf()
        junk = self.sb("junk_f", D, BF16, es); junkB = Buf()
        fnb = self.sb("fnb", D, F32, es); fnbB = Buf()
        new_bufs = otB + [fsB, junkB, fnbB]
        S.alias(new_bufs, getattr(self, "phase_bufs", []))
        self.phase_bufs = new_bufs
        S.dma(SP, lambda e: e.dma_start(out=fnb[:], in_=fn_bc), "const2", S.new_group(), writes=[fnbB])
        import os
        if os.environ.get("KRAWOUT"):
            for t in range(NT):
                S.dma(SP, (lambda t: lambda e: e.dma_start(out=out[t * 128:(t + 1) * 128, :], in_=h[:, t * D:(t + 1) * D]))(t), "out%d" % (t % 2), S.new_group(), reads=[hB[t]])
            es.close()
            return
        for t in range(NT):
            k = t % 2
            c0 = (t % 16) * 3
            hs = h[:, t * D:(t + 1) * D]
            op(ACT, (lambda hs, c0: lambda e: e.activation(out=junk[:, :], in_=hs, func=AF.Square, accum_out=fs[:, c0:c0 + 1]))(hs, c0), reads=[hB[t]], writes=[junkB, fsB])
            op(POOL, (lambda c0: lambda e: e.tensor_scalar(out=fs[:, c0 + 1:c0 + 2], in0=fs[:, c0:c0 + 1], scalar1=1.0 / D, scalar2=EPS, op0=ALU.mult, op1=ALU.add))(c0), reads=[fsB], writes=[fsB])
            op(POOL, (lambda c0: lambda e: e.tensor_tensor(out=fs[:, c0 + 2:c0 + 3], in0=fs[:, c0 + 1:c0 + 2], in1=cs("mhalf"), op=ALU.pow))(c0), reads=[fsB, cpB], writes=[fsB])
            op(DVE, (lambda hs, c0, k: lambda e: e.scalar_tensor_tensor(out=ot[k][:, :], in0=hs, scalar=fs[:, c0 + 2:c0 + 3], in1=fnb[:, :], op0=ALU.mult, op1=ALU.mult))(hs, c0, k),
               reads=[hB[t], fsB, fnbB], writes=[otB[k]])
            S.dma(SP, (lambda t, k: lambda e: e.dma_start(out=out[t * 128:(t + 1) * 128, :], in_=ot[k][:, :]))(t, k), "out%d" % k, S.new_group(), reads=[otB[k]])
        es.close()


def _pack_layout():
    names = [("ident", 128), ("ones", 128), ("triU", 128), ("maskL", 128), ("maskU", 128), ("blk64", 128),
             ("gon", 128), ("n1", 8), ("nm", 8), ("n2", 8), ("cwg", 48), ("csw", 12), ("cgain", 4),
             ("alog", 4), ("dtb", 4), ("mhalf", 1), ("eps", 1)]
    lay = {}
    off = 0
    for n, w in names:
        lay[n] = (off, off + w)
        off += w
    return lay, off


_CP, _CPK_COLS = _pack_layout()
Builder.CP = _CP
Builder.CPK_COLS = _CPK_COLS


def _pack_consts(inp):
    f = np.float32
    cp = np.zeros((128, _CPK_COLS), f)

    def put(name, arr):
        a, b = _CP[name]
        cp[:, a:b] = np.asarray(arr, f).reshape(128, b - a)

    idx = np.arange(128)
    same = (idx[:, None] // 64) == (idx[None, :] // 64)
    put("ident", np.eye(128))
    put("ones", np.ones((128, 128)))
    put("triU", (same & (idx[:, None] <= idx[None, :])))
    put("maskL", np.where(same & (idx[:, None] > idx[None, :]), 0.0, NEG))
    put("maskU", np.where(same & (idx[:, None] <= idx[None, :]), 0.0, NEG))
    put("blk64", same.astype(f) / 64.0)
    put("gon", np.broadcast_to(inp["gdn_out_norm"].reshape(1, 128), (128, 128)))
    put("n1", inp["ffn1_norm"].reshape(8, 128).T)
    put("nm", inp["mix_norm"].reshape(8, 128).T)
    put("n2", inp["ffn2_norm"].reshape(8, 128).T)
    put("cwg", inp["gdn_conv_w"].reshape(4, 12, 128).transpose(2, 1, 0).reshape(128, 48))
    put("csw", inp["conv_short_w"].reshape(3, 4, 128).transpose(2, 1, 0).reshape(128, 12))
    put("cgain", inp["conv_out_norm"].reshape(4, 128).T)
    put("alog", np.broadcast_to(inp["gdn_A_log"].reshape(1, 4), (128, 4)))
    put("dtb", np.broadcast_to(inp["gdn_dt_bias"].reshape(1, 4), (128, 4)))
    put("mhalf", np.full((128, 1), -0.5))
    put("eps", np.full((128, 1), EPS))
    return cp


_NC_CACHE = {}


def _get_nc(debug=False):
    if debug not in _NC_CACHE:
        b = Builder(debug=debug)
        b.spar_h = [0, 0, 0, 0]
        _NC_CACHE[debug] = (b.build(), b)
    return _NC_CACHE[debug]


def kernel(debug=False, **inputs):
    inp = {k: np.asarray(v) for k, v in inputs.items()}
    x = inp["x"].astype(np.float32, copy=False)
    nc, b = _get_nc(debug)
    cp = _pack_consts(inp)
    fn_bc = np.ascontiguousarray(np.broadcast_to(inp["final_norm"].reshape(1, D).astype(np.float32), (128, D)))
    shared = {
        "w1_in": np.ascontiguousarray(inp["ffn1_w_in"][0]), "w1_out": np.ascontiguousarray(inp["ffn1_w_out"][0]),
        "w2_in": np.ascontiguousarray(inp["ffn2_w_in"][0]), "w2_out": np.ascontiguousarray(inp["ffn2_w_out"][0]),
        "wm_in": np.ascontiguousarray(inp["w_mix_in"][0]), "wm_out": np.ascontiguousarray(inp["w_mix_out"][0]),
        "cpk": cp, "fn_bc": fn_bc,
    }
    zeros = np.zeros((T, D), np.float32)
    in_maps = []
    for c in range(8):
        bi, half = c // 2, c % 2
        m = dict(shared)
        m["x_own"] = np.ascontiguousarray(x[bi, half * T:(half + 1) * T])
        m["x_pre"] = zeros if half == 0 else np.ascontiguousarray(x[bi, 0:T])
        in_maps.append(m)
    import os
    ncores = int(os.environ.get("KCORES", 8))
    res = run_bass_kernel_spmd(nc, in_maps[:ncores], core_ids=list(range(ncores)))
    outp = np.zeros((4, 2 * T, D), np.float32)
    for c in range(ncores):
        outp[c // 2, (c % 2) * T:(c % 2 + 1) * T] = res.results[c]["out"]
    if debug:
        return outp, res.results
    return outp
```

```python
import contextlib
import numpy as np
import concourse.bass as bass
import concourse.mybir as mybir
from concourse.bass_utils import run_bass_kernel_spmd

F32 = mybir.dt.float32
BF16 = mybir.dt.bfloat16
AF = mybir.ActivationFunctionType
ALU = mybir.AluOpType

PE, ACT, DVE, POOL, SP = "pe", "act", "dve", "pool", "sp"
COMPUTE = (PE, ACT, DVE, POOL)

D = 1024
DFF = 2816
T = 2048
NT = T // 128
NB = T // 512
EPS = 1e-6
GW0 = 1536
NG = 2056
NEG = -1.0e30


class Buf:
    __slots__ = ("name", "last_w", "readers", "excl")

    def __init__(self, name="", excl=False):
        self.name = name
        self.last_w = None
        self.readers = []
        self.excl = excl


class Op:
    __slots__ = ("eng", "fn", "deps", "needs_inc", "cnt", "is_dma", "key", "grp", "idx")

    def __init__(self, eng, fn, is_dma=False, key=None, grp=None):
        self.eng = eng
        self.fn = fn
        self.deps = []
        self.needs_inc = False
        self.cnt = 0
        self.is_dma = is_dma
        self.key = key
        self.grp = grp


class Sched:
    def __init__(self):
        self.ops = []
        self.grp_ctr = 0

    def new_group(self):
        self.grp_ctr += 1
        return self.grp_ctr

    def _add(self, op, reads, writes):
        if getattr(self, "capture", None) is not None:
            self.capture.append((op, list(reads), list(writes)))
            return op
        return self._add_real(op, reads, writes)

    def replay(self, item):
        return self._add_real(*item)

    def _add_real(self, op, reads, writes):
        op.idx = len(self.ops)
        ex = [b for b in reads if b.excl]
        if ex:
            reads = [b for b in reads if not b.excl]
            writes = list(writes) + ex
        deps = {}
        for b in reads:
            if b.last_w is not None:
                deps[id(b.last_w)] = b.last_w
        for b in writes:
            if b.last_w is not None:
                deps[id(b.last_w)] = b.last_w
            for r in b.readers:
                deps[id(r)] = r
        latest = {}
        for d in deps.values():
            if d is op:
                continue
            if (not d.is_dma) and (not op.is_dma) and d.eng == PE and op.eng == PE:
                continue
            if d.is_dma:
                op.deps.append(d)
            else:
                cur = latest.get(d.eng)
                if cur is None or d.idx > cur.idx:
                    latest[d.eng] = d
        op.deps.extend(latest.values())
        for b in reads:
            if op.is_dma:
                b.readers.append(op)
            else:
                b.readers = [r for r in b.readers if r.is_dma or r.eng != op.eng]
                b.readers.append(op)
        for b in writes:
            b.last_w = op
            b.readers = []
        self.ops.append(op)
        return op

    def op(self, eng, fn, reads=(), writes=()):
        return self._add(Op(eng, fn), reads, writes)

    def dma(self, queue, fn, key, grp, reads=(), writes=()):
        return self._add(Op(queue, fn, is_dma=True, key=key, grp=grp), reads, writes)

    def alias(self, new_bufs, old_bufs):
        acc = {}
        for b in old_bufs:
            if b.last_w is not None:
                acc[id(b.last_w)] = b.last_w
            for r in b.readers:
                acc[id(r)] = r
        for nb in new_bufs:
            nb.readers = list(acc.values())

    def emit(self, nc, final_wait_keys=()):
        ops = self.ops
        for o in ops:
            for d in o.deps:
                d.needs_inc = True
        cnt = {}
        grp_end = {}
        for o in ops:
            if o.is_dma:
                k = ("dma", o.key)
                cnt[k] = cnt.get(k, 0) + 1
                o.cnt = cnt[k]
                grp_end[(o.key, o.grp)] = o.cnt
            elif o.needs_inc:
                cnt[o.eng] = cnt.get(o.eng, 0) + 1
                o.cnt = cnt[o.eng]
        dma_keys = sorted({o.key for o in ops if o.is_dma})
        streams = {e: [o for o in ops if o.eng == e] for e in (PE, ACT, DVE, POOL, SP)}
        self.stats = {e: len(s) for e, s in streams.items()}
        self.stats["incs"] = dict(cnt)

        import os
        SEG = int(os.environ.get('KSEG', 1500))
        with contextlib.ExitStack() as es:
            sems = {}
            for e in COMPUTE:
                nseg = (cnt.get(e, 0) + SEG - 1) // SEG + 1
                sems[e] = [es.enter_context(nc.semaphore("s_%s_%d" % (e, j))) for j in range(nseg)]
            for k in dma_keys:
                sems[("dma", k)] = es.enter_context(nc.semaphore("d_" + str(k)))
            block = es.enter_context(nc.Block())

            def run_stream(engname, eng):
                waited = {}
                for o in streams[engname]:
                    for d in o.deps:
                        if d.is_dma:
                            sk = ("dma", d.key)
                            val = 16 * grp_end[(d.key, d.grp)]
                            sem = sems[sk]
                        else:
                            seg = (d.cnt - 1) // SEG
                            sk = (d.eng, seg)
                            val = (d.cnt - 1) % SEG + 1
                            sem = sems[d.eng][seg]
                            if any(k2[0] == d.eng and k2[1] > seg for k2 in waited if isinstance(k2, tuple) and k2[0] == d.eng):
                                continue
                        if waited.get(sk, 0) >= val:
                            continue
                        waited[sk] = val
                        eng.wait_ge(sem, val)
                    ins = o.fn(eng)
                    if o.is_dma:
                        ins.then_inc(sems[("dma", o.key)], 16)
                    elif o.needs_inc:
                        ins.then_inc(sems[o.eng][(o.cnt - 1) // SEG], 1)
                if engname == SP:
                    for k in final_wait_keys:
                        eng.wait_ge(sems[("dma", k)], 16 * cnt[("dma", k)])

            @block.sync
            def _(e):
                run_stream(SP, e)

            @block.tensor
            def _(e):
                run_stream(PE, e)

            @block.scalar
            def _(e):
                run_stream(ACT, e)

            @block.vector
            def _(e):
                run_stream(DVE, e)

            @block.gpsimd
            def _(e):
                run_stream(POOL, e)


class Builder:
    def __init__(self, debug=False):
        self.debug = debug
        self.nc = bass.Bass("TRN2", target_bir_lowering=False)
        self.S = Sched()
        self.es = contextlib.ExitStack()
        self.dbg_outs = []
        self.dbg_keys = []
        self.rr = 0

    def sb(self, name, cols, dt=F32, es=None):
        return (es or self.es).enter_context(self.nc.sbuf_tensor(name, [128, cols], dt))

    def dram_in(self, name, shape, dt=F32):
        return self.nc.dram_tensor(name, list(shape), dt, kind="ExternalInput").ap()

    def dram_out(self, name, shape, dt=F32):
        return self.nc.dram_tensor(name, list(shape), dt, kind="ExternalOutput").ap()

    def dbg(self, name, ap, cols, bufs, dt=F32):
        if not self.debug:
            return
        o = self.dram_out("dbg_" + name, [128, cols], dt)
        self.dbg_keys.append("dbg_" + name)
        self.S.dma(SP, lambda e: e.dma_start(out=o, in_=ap), "dbg_" + name, self.S.new_group(), reads=bufs)

    def ew(self):
        self.rr += 1
        return ACT if (self.rr & 1) else DVE

    def build(self):
        nc, S = self.nc, self.S
        op = S.op
        x_pre = self.dram_in("x_pre", [T, D])
        x_own = self.dram_in("x_own", [T, D])
        w1_in = self.dram_in("w1_in", [D, 2 * DFF])
        w1_out = self.dram_in("w1_out", [DFF, D])
        w2_in = self.dram_in("w2_in", [D, 2 * DFF])
        w2_out = self.dram_in("w2_out", [DFF, D])
        wm_in = self.dram_in("wm_in", [D, 3592])
        wm_out = self.dram_in("wm_out", [D, D])
        cpk = self.dram_in("cpk", [128, self.CPK_COLS])
        fn_bc = self.dram_in("fn_bc", [128, D])
        out = self.dram_out("out", [T, D])
        self.out_grp = S.new_group()

        h = self.sb("h", NT * D)
        hB = [Buf("h%d" % t) for t in range(NT)]
        stage = [self.sb("stage%d" % i, 2048) for i in range(2)]
        stB = [Buf("st%d" % i) for i in range(2)]
        self.stage, self.stB, self.st_i = stage, stB, 0
        import os
        self.cast_order = os.environ.get('KCAST', 'dve,act,pool,dve,act').split(',')
        cp = self.sb("cp", self.CPK_COLS)
        cpB = Buf("cp")
        hhalo = self.sb("hhalo", D)
        hhB = Buf("hhalo")
        ygT = self.sb("ygT", 4 * T, BF16)
        ygB = [Buf("yg%d" % t) for t in range(NT)]
        pch = self.sb("pch", 36)
        pchB = Buf("pch")
        Sst = [self.sb("Sst%d" % i, 4 * 128) for i in range(2)]
        SsB = [[Buf("S%d_%d" % (i, hh)) for hh in range(4)] for i in range(2)]
        stat = self.sb("stat", 64)
        negA = self.sb("negA", 4)
        negAB = Buf("negA")
        psum = [self.es.enter_context(nc.psum_tensor("ps%d" % i, [128, 512], F32)) for i in range(8)]
        psB = [[Buf("ps%d" % i, excl=True)] * 4 for i in range(8)]
        self.psum, self.psB = psum, psB
        self.h, self.hB = h, hB

        C = self.CP
        g0 = S.new_group()
        S.dma(SP, lambda e: e.dma_start(out=cp[:], in_=cpk), "const", g0, writes=[cpB])
        self.cp, self.cpB = cp, cpB

        def cs(name, n=None):
            a, b = C[name]
            return cp[:, a:b]

        self.cs = cs
        ident = cs("ident")
        op(POOL, lambda e: e.memset(Sst[0][:], 0.0), writes=SsB[0])
        op(POOL, lambda e: e.memset(pch[:], 0.0), writes=[pchB])
        op(ACT, lambda e: e.activation(out=negA[:], in_=cs("alog"), func=AF.Exp), reads=[cpB], writes=[negAB])
        op(DVE, lambda e: e.tensor_scalar(out=negA[:], in0=negA[:], scalar1=-1.0, scalar2=None, op0=ALU.mult),
           reads=[negAB], writes=[negAB])
        self.negA, self.negAB = negA, negAB
        self.pch, self.pchB = pch, pchB
        self.Sst, self.SsB = Sst, SsB
        self.ygT, self.ygB = ygT, ygB
        self.spar = 0
        self.stat = stat
        self.statB = Buf("stat")

        import os
        PH = set(os.environ.get("KPH", "pf,pg,of,og,cv,f2").split(","))
        if "pf" in PH:
            self.load_x(x_pre)
            self.ffn(w1_in, w1_out, "n1", tag="p1")
        op(POOL, lambda e: e.tensor_copy(out=hhalo[:], in_=h[:, (NT - 1) * D:NT * D]), reads=[hB[NT - 1]], writes=[hhB])
        if "pg" in PH:
            self.gdn_phase(wm_in, full=False, tag="pg")
        self.load_x(x_own)
        if "of" in PH:
            self.ffn(w1_in, w1_out, "n1", tag="o1")
        self.dbg("h1", h[:, 0:D], D, [hB[0]])
        if "og" in PH:
            self.gdn_phase(wm_in, full=True, tag="og")
        self.dbg("yg", self.ygT[:, 0:T], T, self.ygB, BF16)
        if "cv" in PH:
            self.conv_phase(wm_in, wm_out, hhalo, hhB)
        self.dbg("h2", h[:, 0:D], D, [hB[0]])
        if "f2" in PH:
            self.ffn(w2_in, w2_out, "n2", tag="o2")
        self.final(out, fn_bc)
        S.emit(nc, final_wait_keys=["out0", "out1"] + self.dbg_keys)
        self.es.close()
        return nc

    def load_x(self, xd):
        S, h, hB = self.S, self.h, self.hB
        g = S.new_group()
        for t in range(NT):
            S.dma(SP, (lambda t: lambda e: e.dma_start(out=h[:, t * D:(t + 1) * D], in_=xd[t * 128:(t + 1) * 128, :]))(t),
                  "x%d" % (t % 4), g, writes=[hB[t]])

    def load_cast(self, dst_ap, dstB, src_ap, shape3=None, key="w"):
        S = self.S
        n = len(self.stage)
        i = self.st_i % n
        self.st_i += 1
        st, sB = self.stage[i], self.stB[i]
        a, b = shape3
        sview = st[:, 0:a * b].rearrange("p (a b) -> p a b", a=a) if a > 1 else st[:, 0:b]
        sflat = st[:, 0:a * b]
        g = S.new_group()
        S.dma(SP, lambda e: e.dma_start(out=sview, in_=src_ap), "st%d" % i, g, writes=[sB])
        self.cast_i = getattr(self, "cast_i", 0) + 1
        eng = self.cast_order[self.cast_i % len(self.cast_order)]
        if eng == ACT:
            S.op(ACT, lambda e: e.activation(out=dst_ap, in_=sflat, func=AF.Copy), reads=[sB], writes=[dstB])
        else:
            S.op(eng, lambda e: e.tensor_copy(out=dst_ap, in_=sflat), reads=[sB], writes=[dstB])

    def norm_transpose(self, src_ap, srcB, gain_name, dst, dst_off, dst_stride, dstB, xs, xsB, pbanks):
        S, cs = self.S, self.cs
        op = S.op
        stat = self.stat
        stB = self.statB
        op(ACT, lambda e: e.activation(out=xs[:, 0:D], in_=src_ap, func=AF.Square, accum_out=stat[:, 0:1]),
           reads=[srcB], writes=[xsB, stB])
        op(POOL, lambda e: e.tensor_scalar(out=stat[:, 1:2], in0=stat[:, 0:1], scalar1=1.0 / D, scalar2=EPS,
                                           op0=ALU.mult, op1=ALU.add), reads=[stB], writes=[stB])
        op(POOL, lambda e: e.tensor_tensor(out=stat[:, 2:3], in0=stat[:, 1:2], in1=cs("mhalf"), op=ALU.pow),
           reads=[stB, self.cpB], writes=[stB])
        op(DVE, lambda e: e.tensor_scalar(out=xs[:, 0:D], in0=src_ap, scalar1=stat[:, 2:3], scalar2=None, op0=ALU.mult),
           reads=[srcB, stB], writes=[xsB])
        gain = cs(gain_name)
        ident = cs("ident")
        for half in range(2):
            pb = pbanks[half]
            pbuf = self.psum[pb]
            for q in range(4):
                c = half * 4 + q
                op(PE, (lambda c, q, pbuf: lambda e: e.transpose(pbuf[:, q * 128:(q + 1) * 128], xs[:, c * 128:(c + 1) * 128], ident))(c, q, pbuf),
                   reads=[xsB, self.cpB], writes=[self.psB[pb][q]])
            for q in range(4):
                c = half * 4 + q
                eng = self.ew()
                o_ap = dst[:, c * dst_stride + dst_off: c * dst_stride + dst_off + 128]
                i_ap = pbuf[:, q * 128:(q + 1) * 128]
                g_ap = gain[:, c:c + 1]
                if eng == ACT:
                    op(ACT, (lambda o_ap, i_ap, g_ap: lambda e: e.activation(out=o_ap, in_=i_ap, func=AF.Copy, scale=g_ap))(o_ap, i_ap, g_ap),
                       reads=[self.psB[pb][q], self.cpB], writes=[dstB])
                else:
                    op(DVE, (lambda o_ap, i_ap, g_ap: lambda e: e.tensor_scalar(out=o_ap, in0=i_ap, scalar1=g_ap, scalar2=None, op0=ALU.mult))(o_ap, i_ap, g_ap),
                       reads=[self.psB[pb][q], self.cpB], writes=[dstB])

    def ffn(self, w_in, w_out, gain_name, tag):
        import os
        nc, S = self.nc, self.S
        op = S.op
        h, hB = self.h, self.hB
        psum, psB = self.psum, self.psB
        es = contextlib.ExitStack()
        xnT = self.sb("xnT_" + tag, 8 * T, BF16, es)
        xnB = [Buf("xn%d" % t) for t in range(NT)]
        CPP = int(os.environ.get("KCPP", 4))
        nsub = CPP // 2
        wbi = [self.sb("wbi%d_%s" % (i, tag), 2 * nsub * 2048, BF16, es) for i in range(2)]
        wbo = [self.sb("wbo%d_%s" % (i, tag), CPP * 1024, BF16, es) for i in range(2)]
        assert CPP == 4
        wbiB = [[Buf() for _ in range(4)] for _ in range(2)]
        wboB = [[Buf() for _ in range(nsub)] for _ in range(2)]
        hid = [self.sb("hid%d_%s" % (i, tag), CPP * 512, BF16, es) for i in range(2)]
        hidB = [Buf(), Buf()]
        sg0 = self.sb("sg0_%s" % tag, 512, F32, es); sg = [sg0, sg0]
        sgB0 = Buf(); sgB = [sgB0, sgB0]
        ev0 = self.sb("ev0_%s" % tag, 512, F32, es); ev = [ev0, ev0]
        evB0 = Buf(); evB = [evB0, evB0]
        xs0 = self.sb("xs0_%s" % tag, D, F32, es); xs = [xs0, xs0]
        xsB0 = Buf(); xsB = [xsB0, xsB0]
        self.junk = self.sb("junk_" + tag, D, BF16, es)
        self.junkB = Buf()
        base_stage, base_stB = self.stage, self.stB
        nextra = int(os.environ.get("KXST", 0))
        xst = [self.sb("xst%d_%s" % (i, tag), 2048, F32, es) for i in range(nextra)]
        xstB = [Buf() for _ in range(nextra)]
        self.stage, self.stB = base_stage + xst, base_stB + xstB
        new_bufs = xnB + wbiB[0] + wbiB[1] + wboB[0] + wboB[1] + hidB + sgB + xsB + [self.junkB] + evB + xstB
        S.alias(new_bufs, getattr(self, "phase_bufs", []))
        self.phase_bufs = new_bufs

        w_in_v = w_in.rearrange("(c p) n -> p c n", p=128)
        w_out_v = w_out.rearrange("(c p) n -> p c n", p=128)
        pieces = []
        c0 = 0
        while c0 < DFF // 128:
            n = min(CPP, DFF // 128 - c0)
            pieces.append((c0, n))
            c0 += n
        NP = len(pieces)

        def load_piece(p):
            s = p % 2
            ch0, n = pieces[p]
            W = n * 128
            col = ch0 * 128
            for which in range(2):
                for csub in range(2):
                    base = which * 4096 + csub * 2048
                    self.load_cast(wbi[s][:, base:base + 4 * W], wbiB[s][which * 2 + csub],
                                   w_in_v[:, csub * 4:csub * 4 + 4, which * DFF + col:which * DFF + col + W], (4, W))
            for sub in range(n // 2):
                self.load_cast(wbo[s][:, sub * 2048:(sub + 1) * 2048], wboB[s][sub], w_out_v[:, ch0 + 2 * sub:ch0 + 2 * sub + 2, :], (2, 1024))

        load_piece(0)
        for t in range(NT):
            self.norm_transpose(h[:, t * D:(t + 1) * D], hB[t], gain_name, xnT, t * 128, T, xnB[t],
                                xs[t % 2], xsB[t % 2], (6, 7))
        blocks = [(p, tb) for p in range(NP) for tb in range(NB)]
        st = {"gi": 0, "oi": 0}

        def stage1(idx):
            p, tb = blocks[idx]
            s = p % 2
            hs = idx % 2
            n = pieces[p][1]
            for j in range(n):
                gi = st["gi"]
                for which in range(2):
                    pb = (0 if which == 0 else 2) + (gi % 2)
                    W = n * 128
                    for c in range(8):
                        off = which * 4096 + (c // 4) * 2048 + (c % 4) * W + j * 128
                        lhsT = wbi[s][:, off:off + 128]
                        rhs = xnT[:, c * T + tb * 512: c * T + (tb + 1) * 512]
                        op(PE, (lambda pb, lhsT, rhs, c: lambda e: e.matmul(psum[pb][:, :], lhsT=lhsT, rhs=rhs, start=(c == 0), stop=(c == 7)))(pb, lhsT, rhs, c),
                           reads=[wbiB[s][which * 2 + c // 4]] + xnB[tb * 4:(tb + 1) * 4], writes=psB[pb])
                pg, pu = (gi % 2), 2 + (gi % 2)
                k = gi % 2
                op(ACT, (lambda pg, k: lambda e: e.activation(out=sg[k][:, :], in_=psum[pg][:, :], func=AF.Silu))(pg, k),
                   reads=psB[pg], writes=[sgB[k]])
                op(DVE, (lambda pu, k, hs, j: lambda e: e.tensor_tensor(out=hid[hs][:, j * 512:(j + 1) * 512], in0=sg[k][:, :], in1=psum[pu][:, :], op=ALU.mult))(pu, k, hs, j),
                   reads=[sgB[k]] + psB[pu], writes=[hidB[hs]])
                st["gi"] += 1

        def stage2(idx):
            p, tb = blocks[idx]
            s = p % 2
            hs = idx % 2
            n = pieces[p][1]
            for tt in range(4):
                t = tb * 4 + tt
                for hh in range(2):
                    oi = st["oi"]
                    pb = 4 + (oi % 2)
                    for j in range(n):
                        lhsT = hid[hs][:, j * 512 + tt * 128: j * 512 + (tt + 1) * 128]
                        rhs = wbo[s][:, j * 1024 + hh * 512: j * 1024 + (hh + 1) * 512]
                        op(PE, (lambda pb, lhsT, rhs, j: lambda e: e.matmul(psum[pb][:, :], lhsT=lhsT, rhs=rhs, start=(j == 0), stop=(j == n - 1)))(pb, lhsT, rhs, j),
                           reads=[hidB[hs], wboB[s][j // 2]], writes=psB[pb])
                    hap = h[:, t * D + hh * 512: t * D + (hh + 1) * 512]
                    if oi % 2 == 0:
                        op(DVE, (lambda pb, hap: lambda e: e.scalar_tensor_tensor(out=hap, in0=psum[pb][:, :], scalar=0.5, in1=hap, op0=ALU.mult, op1=ALU.add))(pb, hap),
                           reads=psB[pb] + [hB[t]], writes=[hB[t]])
                    else:
                        k = (oi // 2) % 2
                        op(ACT, (lambda pb, k: lambda e: e.activation(out=ev[k][:, :], in_=psum[pb][:, :], func=AF.Copy, scale=0.5))(pb, k),
                           reads=psB[pb], writes=[evB[k]])
                        op(POOL, (lambda k, hap: lambda e: e.tensor_tensor(out=hap, in0=hap, in1=ev[k][:, :], op=ALU.add))(k, hap),
                           reads=[evB[k], hB[t]], writes=[hB[t]])
                    st["oi"] += 1

        for idx in range(len(blocks)):
            stage1(idx)
            if idx > 0:
                stage2(idx - 1)
            p, tb = blocks[idx]
            if tb == 0 and p + 1 < NP:
                load_piece(p + 1)
        stage2(len(blocks) - 1)
        self.stage, self.stB = base_stage, base_stB
        es.close()

    def gdn_phase(self, wm_in, full, tag):
        import os
        nc, S = self.nc, self.S
        op = S.op
        cs, cpB = self.cs, self.cpB
        h, hB = self.h, self.hB
        psum, psB = self.psum, self.psB
        es = contextlib.ExitStack()
        pc = self.sb("pc_" + tag, 12 * 131, F32, es); pcB = Buf()
        wg = self.sb("wg_" + tag, 8 * NG, BF16, es)
        wgB = [Buf() for _ in range(9)]
        xs = self.sb("xs_" + tag, D, F32, es); xsB = Buf()
        xn = self.sb("xn_" + tag, 8 * 128, BF16, es); xnB = Buf()
        qkv0 = self.sb("qkv_" + tag, 12 * 128, F32, es); qkvB0 = Buf()
        qkv_b = [qkv0, self.stage[0][:, 0:1536]]; qkvB_b = [qkvB0, self.stB[0]]
        cacc = self.sb("cacc_" + tag, 12 * 128, F32, es); caccB = Buf()
        etmp = self.sb("etmp_" + tag, 12 * 128, F32, es); etmpB = Buf()
        rs, rsB = etmp, etmpB
        zT0 = self.sb("zT_" + tag, 4 * 128, F32, es); zB0 = Buf()
        zT_b = [zT0, self.stage[1][:, 0:512]]; zB_b = [zB0, self.stB[1]]
        sm_b = [self.sb("sm%d_%s" % (i, tag), 64, F32, es) for i in range(2)]; smB_b = [Buf(), Buf()]
        Dg = self.sb("Dg_" + tag, 512, F32, es); DgB = Buf()
        glb_b = [self.sb("glb%d_%s" % (i, tag), 8, F32, es) for i in range(2)]; glB_b = [Buf(), Buf()]
        def mk(n, cols=128, dt=F32):
            return self.sb(n + "_" + tag, cols, dt, es), Buf(n)
        HB = []
        for hh in range(4):
            d_ = {}
            for n in ["kbg", "kdec", "vbeta", "tm1", "E1", "Lm", "AT", "X0", "X1", "Y0", "Y1", "P0", "P1", "wT", "um", "vnew"]:
                d_[n] = mk("%s%d" % (n, hh))
            HB.append(d_)
        new_bufs = wgB + [pcB, xsB, xnB, qkvB0, caccB, etmpB, zB0, DgB] + smB_b + glB_b + [b for d_ in HB for _, b in d_.values()]
        S.alias(new_bufs, getattr(self, "phase_bufs", []))
        self.phase_bufs = new_bufs

        wm_v = wm_in.rearrange("(c p) n -> p c n", p=128)
        col = 0
        while col < NG:
            w = min(256, NG - col)
            base = (col // 256) * 2048
            self.load_cast(wg[:, base:base + 8 * w], wgB[col // 256], wm_v[:, :, GW0 + col:GW0 + col + w], (8, w))
            col += w

        ident, ones, triU = cs("ident"), cs("ones"), cs("triU")
        maskL, maskU = cs("maskL"), cs("maskU")
        cwg = cs("cwg")
        pcv0 = pc[:, :].rearrange("p (a b) -> p a b", a=12)
        pchv = self.pch[:, :].rearrange("p (a b) -> p a b", a=12)
        op(DVE, lambda e: e.tensor_copy(out=pcv0[:, :, 0:3], in_=pchv), reads=[self.pchB], writes=[pcB])
        Sst, SsB = self.Sst, self.SsB
        nq = 16 if full else 12

        def pre_ops(t):
            pp = t % 2
            qkv, qkvB = qkv_b[pp], qkvB_b[pp]
            zT, zB = zT_b[pp], zB_b[pp]
            sm, smB = sm_b[pp], smB_b[pp]
            glb, glB = glb_b[pp], glB_b[pp]
            S.capture = []
            self.norm_transpose(h[:, t * D:(t + 1) * D], hB[t], "nm", xn, 0, 128, xnB, xs, xsB, (0, 1))
            for grp in range(nq // 4):
                if (not full) and grp == 0 and t != NT - 1:
                    continue
                pb = 2
                for q in range(4):
                    j = grp * 4 + q
                    for c in range(8):
                        lhsT = wg[:, (j // 2) * 2048 + c * 256 + (j % 2) * 128: (j // 2) * 2048 + c * 256 + (j % 2) * 128 + 128]
                        rhs = xn[:, c * 128:(c + 1) * 128]
                        op(PE, (lambda pb, q, lhsT, rhs, c: lambda e: e.matmul(psum[pb][:, q * 128:(q + 1) * 128], lhsT=lhsT, rhs=rhs, start=(c == 0), stop=(c == 7)))(pb, q, lhsT, rhs, c),
                           reads=[wgB[j // 2], xnB], writes=[psB[pb][q]])
                if grp < 3:
                    dstv = pc[:, grp * 4 * 131:(grp + 1) * 4 * 131].rearrange("p (a b) -> p a b", a=4)[:, :, 3:131]
                    srcv = psum[pb][:, :].rearrange("p (a b) -> p a b", a=4)
                    eng = self.ew()
                    if eng == ACT:
                        op(ACT, (lambda dstv, srcv: lambda e: e.activation(out=dstv, in_=srcv, func=AF.Copy))(dstv, srcv), reads=psB[pb], writes=[pcB])
                    else:
                        op(DVE, (lambda dstv, srcv: lambda e: e.tensor_copy(out=dstv, in_=srcv))(dstv, srcv), reads=psB[pb], writes=[pcB])
                else:
                    op(ACT, (lambda pb: lambda e: e.activation(out=zT[:, :], in_=psum[pb][:, :], func=AF.Copy))(pb), reads=psB[pb], writes=[zB])
            i3 = len(S.capture)
            for c in range(8):
                lhsT = xn[:, c * 128:(c + 1) * 128]
                rhs = wg[:, 8 * 2048 + c * 8: 8 * 2048 + c * 8 + 8]
                op(PE, (lambda lhsT, rhs, c: lambda e: e.matmul(psum[2][:, 0:8], lhsT=lhsT, rhs=rhs, start=(c == 0), stop=(c == 7)))(lhsT, rhs, c),
                   reads=[wgB[8], xnB], writes=[psB[2][0]])
            op(DVE, lambda e: e.tensor_copy(out=sm[:, 0:8], in_=psum[2][:, 0:8]), reads=[psB[2][0]], writes=[smB])
            op(ACT, lambda e: e.activation(out=sm[:, 8:12], in_=sm[:, 0:4], func=AF.Exp, scale=-1.0), reads=[smB], writes=[smB])
            op(DVE, lambda e: e.tensor_scalar(out=sm[:, 8:12], in0=sm[:, 8:12], scalar1=1.0, scalar2=None, op0=ALU.add), reads=[smB], writes=[smB])
            op(DVE, lambda e: e.reciprocal(out=sm[:, 8:12], in_=sm[:, 8:12]), reads=[smB], writes=[smB])
            op(DVE, lambda e: e.tensor_tensor(out=sm[:, 12:16], in0=sm[:, 4:8], in1=cs("dtb"), op=ALU.add), reads=[smB, cpB], writes=[smB])
            op(ACT, lambda e: e.activation(out=sm[:, 12:16], in_=sm[:, 12:16], func=AF.Exp), reads=[smB], writes=[smB])
            op(ACT, lambda e: e.activation(out=sm[:, 12:16], in_=sm[:, 12:16], func=AF.Ln, bias=1.0), reads=[smB], writes=[smB])
            op(DVE, lambda e: e.tensor_tensor(out=sm[:, 12:16], in0=sm[:, 12:16], in1=self.negA[:, :], op=ALU.mult), reads=[smB, self.negAB], writes=[smB])
            op(PE, lambda e: e.matmul(psum[2][:, 8:12], lhsT=triU, rhs=sm[:, 12:16], start=True, stop=True), reads=[smB, cpB], writes=[psB[2][0]])
            op(DVE, lambda e: e.tensor_copy(out=sm[:, 16:20], in_=psum[2][:, 8:12]), reads=[psB[2][0]], writes=[smB])
            op(ACT, lambda e: e.activation(out=sm[:, 20:24], in_=sm[:, 16:20], func=AF.Exp), reads=[smB], writes=[smB])
            op(DVE, lambda e: e.tensor_scalar(out=sm[:, 24:28], in0=sm[:, 16:20], scalar1=-1.0, scalar2=None, op0=ALU.mult), reads=[smB], writes=[smB])
            op(DVE, lambda e: e.tensor_tensor(out=sm[:, 28:32], in0=sm[:, 8:12], in1=sm[:, 20:24], op=ALU.mult), reads=[smB], writes=[smB])
            for hh in range(4):
                op(DVE, (lambda hh: lambda e: e.tensor_scalar(out=Dg[:, hh * 128:(hh + 1) * 128], in0=ident, scalar1=sm[:, 16 + hh:17 + hh], scalar2=None, op0=ALU.mult))(hh),
                   reads=[smB, cpB], writes=[DgB])
            op(PE, lambda e: e.matmul(psum[3][:, :], lhsT=ones, rhs=Dg[:, 0:512], start=True, stop=True), reads=[DgB, cpB], writes=psB[3])
            Gv = psum[3][:, :].rearrange("p (a b) -> p a b", a=4)
            op(ACT, lambda e: e.activation(out=glb[:, 0:8].rearrange("p (a b) -> p a b", a=4), in_=Gv[:, :, 63:128:64], func=AF.Exp), reads=psB[3], writes=[glB])
            op(DVE, lambda e: e.tensor_tensor(out=sm[0:64, 32:36], in0=Gv[0:64, :, 63], in1=sm[0:64, 16:20], op=ALU.subtract), reads=psB[3] + [smB], writes=[smB])
            op(DVE, lambda e: e.tensor_tensor(out=sm[64:128, 32:36], in0=Gv[64:128, :, 127], in1=sm[64:128, 16:20], op=ALU.subtract), reads=psB[3] + [smB], writes=[smB])
            op(ACT, lambda e: e.activation(out=sm[:, 32:36], in_=sm[:, 32:36], func=AF.Exp), reads=[smB], writes=[smB])
            i4 = len(S.capture)
            pcv = pc[:, :].rearrange("p (a b) -> p a b", a=12)
            for j in range(0 if full else 4, 12):
                op(DVE, (lambda j: lambda e: e.tensor_scalar(out=cacc[:, j * 128:(j + 1) * 128], in0=pc[:, j * 131:j * 131 + 128], scalar1=cwg[:, j * 4:j * 4 + 1], scalar2=None, op0=ALU.mult))(j),
                   reads=[pcB, cpB], writes=[caccB])
                for k in range(1, 4):
                    op(DVE, (lambda j, k: lambda e: e.scalar_tensor_tensor(out=cacc[:, j * 128:(j + 1) * 128], in0=pc[:, j * 131 + k:j * 131 + k + 128], scalar=cwg[:, j * 4 + k:j * 4 + k + 1], in1=cacc[:, j * 128:(j + 1) * 128], op0=ALU.mult, op1=ALU.add))(j, k),
                       reads=[pcB, cpB, caccB], writes=[caccB])
            op(DVE, lambda e: e.tensor_copy(out=pcv[:, :, 0:3], in_=pcv[:, :, 128:131]), reads=[pcB], writes=[pcB])
            i5 = len(S.capture)
            c_lo = 0 if full else 512
            op(ACT, lambda e: e.activation(out=etmp[:, c_lo:1536], in_=cacc[:, c_lo:1536], func=AF.Exp, scale=-1.0), reads=[caccB], writes=[etmpB])
            op(ACT, lambda e: e.activation(out=etmp[:, c_lo:1536], in_=etmp[:, c_lo:1536], func=AF.Ln, bias=1.0), reads=[etmpB], writes=[etmpB])
            op(ACT, lambda e: e.activation(out=etmp[:, c_lo:1536], in_=etmp[:, c_lo:1536], func=AF.Exp, scale=-1.0), reads=[etmpB], writes=[etmpB])
            op(DVE, lambda e: e.tensor_tensor(out=qkv[:, c_lo:1536], in0=cacc[:, c_lo:1536], in1=etmp[:, c_lo:1536], op=ALU.mult), reads=[etmpB, caccB], writes=[qkvB])
            op(ACT, lambda e: e.activation(out=etmp[:, c_lo:1024], in_=qkv[:, c_lo:1024], func=AF.Square), reads=[qkvB, etmpB], writes=[etmpB])
            for half in range(0 if full else 1, 2):
                op(PE, (lambda half: lambda e: e.matmul(psum[half][:, :], lhsT=ones, rhs=etmp[:, half * 512:(half + 1) * 512], start=True, stop=True))(half),
                   reads=[etmpB, cpB], writes=psB[half])
                op(ACT, (lambda half: lambda e: e.activation(out=rs[:, half * 512:(half + 1) * 512], in_=psum[half][:, :], func=AF.Ln, bias=cs("eps")))(half),
                   reads=psB[half] + [cpB], writes=[rsB])
            op(ACT, lambda e: e.activation(out=rs[:, c_lo:1024], in_=rs[:, c_lo:1024], func=AF.Exp, scale=-0.5), reads=[rsB], writes=[rsB])
            if full:
                op(DVE, lambda e: e.scalar_tensor_tensor(out=qkv[:, 0:512], in0=qkv[:, 0:512], scalar=128.0 ** -0.5, in1=rs[:, 0:512], op0=ALU.mult, op1=ALU.mult), reads=[qkvB, rsB], writes=[qkvB])
            op(DVE, lambda e: e.tensor_tensor(out=qkv[:, 512:1024], in0=qkv[:, 512:1024], in1=rs[:, 512:1024], op=ALU.mult), reads=[qkvB, rsB], writes=[qkvB])
            if full:
                op(ACT, lambda e: e.activation(out=etmp[:, 0:512], in_=zT[:, :], func=AF.Exp, scale=-1.0), reads=[zB, etmpB], writes=[etmpB])
                op(ACT, lambda e: e.activation(out=etmp[:, 0:512], in_=etmp[:, 0:512], func=AF.Ln, bias=1.0), reads=[etmpB], writes=[etmpB])
                op(ACT, lambda e: e.activation(out=etmp[:, 0:512], in_=etmp[:, 0:512], func=AF.Exp, scale=-1.0), reads=[etmpB], writes=[etmpB])
                op(DVE, lambda e: e.tensor_tensor(out=zT[:, :], in0=zT[:, :], in1=etmp[:, 0:512], op=ALU.mult), reads=[etmpB, zB], writes=[zB])
            cap_ = S.capture
            S.capture = None
            s3, s4 = cap_[i3:i4], cap_[i4:i5]
            mer = []
            i_, j_ = 0, 0
            while i_ < len(s3) or j_ < len(s4):
                if j_ < len(s4) and (i_ >= len(s3) or j_ * max(len(s3), 1) <= i_ * len(s4)):
                    mer.append(s4[j_]); j_ += 1
                else:
                    mer.append(s3[i_]); i_ += 1
            return cap_[:i3] + mer + cap_[i5:]

        def chain(hh, t):
            pp = t % 2
            qkv, qkvB = qkv_b[pp], qkvB_b[pp]
            zT, zB = zT_b[pp], zB_b[pp]
            sm, smB = sm_b[pp], smB_b[pp]
            glb, glB = glb_b[pp], glB_b[pp]
            B_ = HB[hh]
            kbg, kbgB = B_["kbg"]; kdec, kdecB = B_["kdec"]; vbeta, vbetaB = B_["vbeta"]
            tm1, tm1B = B_["tm1"]; E1, E1B = B_["E1"]; Lm, LmB = B_["Lm"]; AT, ATB = B_["AT"]
            X = [B_["X0"], B_["X1"]]; Y = [B_["Y0"], B_["Y1"]]; Pm = [B_["P0"], B_["P1"]]
            wT, wTB = B_["wT"]; um, umB = B_["um"]; vnew, vnewB = B_["vnew"]
            o1s, o1sB = tm1, tm1B
            om, omB = E1, E1B
            on, onB = Lm, LmB
            qT = qkv[:, hh * 128:(hh + 1) * 128]
            kT = qkv[:, 512 + hh * 128:512 + (hh + 1) * 128]
            vT = qkv[:, 1024 + hh * 128:1024 + (hh + 1) * 128]
            PH = psum[4 + hh]
            PB = psB[4 + hh][0]
            Q0, Q1, Q2, Q3 = PH[:, 0:128], PH[:, 128:256], PH[:, 256:384], PH[:, 384:512]
            Gh = psum[3][:, hh * 128:(hh + 1) * 128]
            GhB = psB[3]
            op(PE, lambda e: e.transpose(Q0, kT, ident), reads=[qkvB, cpB], writes=[PB])
            op(PE, lambda e: e.transpose(Q1, vT, ident), reads=[qkvB, cpB], writes=[PB])
            op(PE, lambda e: e.matmul(Q2, lhsT=kT, rhs=kT, start=True, stop=True), reads=[qkvB], writes=[PB])
            if full:
                op(PE, lambda e: e.matmul(Q3, lhsT=kT, rhs=qT, start=True, stop=True), reads=[qkvB], writes=[PB])
            yield
            op(DVE, lambda e: e.tensor_tensor(out=tm1[:, :], in0=maskL, in1=Gh, op=ALU.subtract), reads=GhB + [cpB], writes=[tm1B])
            yield
            op(ACT, lambda e: e.activation(out=E1[:, :], in_=tm1[:, :], func=AF.Exp, bias=sm[:, 16 + hh:17 + hh]), reads=[tm1B, smB], writes=[E1B])
            yield
            op(ACT, lambda e: e.activation(out=kbg[:, :], in_=Q0, func=AF.Copy, scale=sm[:, 28 + hh:29 + hh]), reads=[PB, smB], writes=[kbgB])
            op(ACT, lambda e: e.activation(out=vbeta[:, :], in_=Q1, func=AF.Copy, scale=sm[:, 8 + hh:9 + hh]), reads=[PB, smB], writes=[vbetaB])
            yield
            op(DVE, lambda e: e.tensor_scalar(out=kdec[:, :], in0=Q0, scalar1=sm[:, 32 + hh:33 + hh], scalar2=None, op0=ALU.mult), reads=[PB, smB], writes=[kdecB])
            op(DVE, lambda e: e.scalar_tensor_tensor(out=Lm[:, :], in0=Q2, scalar=sm[:, 8 + hh:9 + hh], in1=E1[:, :], op0=ALU.mult, op1=ALU.mult), reads=[PB, smB, E1B], writes=[LmB])
            yield
            if full:
                op(DVE, lambda e: e.tensor_tensor(out=tm1[:, :], in0=maskU, in1=Gh, op=ALU.add), reads=GhB + [cpB, tm1B], writes=[tm1B])
                yield
                op(ACT, lambda e: e.activation(out=E1[:, :], in_=tm1[:, :], func=AF.Exp, bias=sm[:, 24 + hh:25 + hh]), reads=[tm1B, smB, E1B], writes=[E1B])
                yield
                op(DVE, lambda e: e.tensor_tensor(out=AT[:, :], in0=Q3, in1=E1[:, :], op=ALU.mult), reads=[PB, E1B], writes=[ATB])
                yield
            op(PE, lambda e: e.transpose(Q0, Lm[:, :], ident), reads=[LmB, cpB], writes=[PB])
            yield
            X0, X0B = X[0]
            P0, P0B = Pm[0]
            op(ACT, lambda e: e.activation(out=X0[:, :], in_=Q0, func=AF.Copy), reads=[PB], writes=[X0B])
            op(DVE, lambda e: e.tensor_tensor(out=P0[:, :], in0=ident, in1=Q0, op=ALU.subtract), reads=[PB, cpB], writes=[P0B])
            yield
            Xc, XcB = X0, X0B
            Yc, YcB = Lm, LmB
            Pc, PcB = P0, P0B
            for k in range(1, 6):
                Yn, YnB = Y[k % 2]
                Xn, XnB = X[k % 2]
                Pn, PnB = Pm[k % 2]
                op(PE, (lambda Xc, Yc: lambda e: e.matmul(Q1, lhsT=Xc[:, :], rhs=Yc[:, :], start=True, stop=True))(Xc, Yc), reads=[XcB, YcB], writes=[PB])
                if k < 5:
                    op(PE, (lambda Xc, Yc: lambda e: e.matmul(Q2, lhsT=Yc[:, :], rhs=Xc[:, :], start=True, stop=True))(Xc, Yc), reads=[XcB, YcB], writes=[PB])
                yield
                op(ACT, (lambda Yn: lambda e: e.activation(out=Yn[:, :], in_=Q1, func=AF.Copy))(Yn), reads=[PB], writes=[YnB])
                if k < 5:
                    op(ACT, (lambda Xn: lambda e: e.activation(out=Xn[:, :], in_=Q2, func=AF.Copy))(Xn), reads=[PB], writes=[XnB])
                yield
                op(PE, (lambda Yn, Pc: lambda e: e.matmul(Q3, lhsT=Yn[:, :], rhs=Pc[:, :], start=True, stop=True))(Yn, Pc), reads=[YnB, PcB], writes=[PB])
                yield
                op(DVE, (lambda Pn, Pc: lambda e: e.tensor_tensor(out=Pn[:, :], in0=Pc[:, :], in1=Q3, op=ALU.add))(Pn, Pc), reads=[PcB, PB], writes=[PnB])
                yield
                Xc, XcB, Yc, YcB, Pc, PcB = Xn, XnB, Yn, YnB, Pn, PnB
            op(PE, (lambda Pc: lambda e: e.matmul(Q0, lhsT=kbg[:, :], rhs=Pc[:, :], start=True, stop=True))(Pc), reads=[kbgB, PcB], writes=[PB])
            op(PE, (lambda Pc: lambda e: e.matmul(Q1, lhsT=Pc[:, :], rhs=vbeta[:, :], start=True, stop=True))(Pc), reads=[vbetaB, PcB], writes=[PB])
            yield
            op(ACT, lambda e: e.activation(out=wT[:, :], in_=Q0, func=AF.Copy), reads=[PB], writes=[wTB])
            op(ACT, lambda e: e.activation(out=um[:, :], in_=Q1, func=AF.Copy), reads=[PB], writes=[umB])
            yield
            for half in range(2):
                r0, r1 = half * 64, half * 64 + 64
                sp_ = self.spar_h[hh]
                Scur = Sst[sp_][:, hh * 128:(hh + 1) * 128]; ScurB = SsB[sp_][hh]
                Snew = Sst[1 - sp_][:, hh * 128:(hh + 1) * 128]; SnewB = SsB[1 - sp_][hh]
                self.spar_h[hh] = 1 - sp_
                op(PE, (lambda r0, r1, Scur: lambda e: e.matmul(PH[r0:r1, 256:384], lhsT=wT[:, r0:r1], rhs=Scur, start=True, stop=True))(r0, r1, Scur), reads=[wTB, ScurB], writes=[PB])
                yield
                op(DVE, (lambda r0, r1: lambda e: e.tensor_tensor(out=vnew[r0:r1, :], in0=um[r0:r1, :], in1=PH[r0:r1, 256:384], op=ALU.subtract))(r0, r1), reads=[umB, PB], writes=[vnewB])
                yield
                if full:
                    op(PE, (lambda r0, r1, Scur: lambda e: e.matmul(PH[r0:r1, 0:128], lhsT=qT[:, r0:r1], rhs=Scur, start=True, stop=True))(r0, r1, Scur), reads=[qkvB, ScurB], writes=[PB])
                    op(PE, (lambda r0, r1: lambda e: e.matmul(PH[r0:r1, 128:256], lhsT=AT[r0:r1, r0:r1], rhs=vnew[r0:r1, :], start=True, stop=True))(r0, r1), reads=[ATB, vnewB], writes=[PB])
                op(PE, (lambda r0, r1: lambda e: e.matmul(Q3, lhsT=kdec[r0:r1, :], rhs=vnew[r0:r1, :], start=True, stop=True))(r0, r1), reads=[kdecB, vnewB], writes=[PB])
                yield
                op(DVE, (lambda Snew, Scur, half: lambda e: e.scalar_tensor_tensor(out=Snew, in0=Scur, scalar=glb[:, hh * 2 + half:hh * 2 + half + 1], in1=Q3, op0=ALU.mult, op1=ALU.add))(Snew, Scur, half),
                   reads=[ScurB, glB, PB], writes=[SnewB])
                yield
            if full:
                c0 = 40 + hh * 3
                op(ACT, lambda e: e.activation(out=o1s[:, :], in_=Q0, func=AF.Copy, scale=sm[:, 20 + hh:21 + hh]), reads=[PB, smB], writes=[o1sB])
                yield
                op(DVE, lambda e: e.tensor_tensor(out=om[:, :], in0=o1s[:, :], in1=Q1, op=ALU.add), reads=[o1sB, PB], writes=[omB])
                yield
                op(ACT, lambda e: e.activation(out=on[:, :], in_=om[:, :], func=AF.Square, accum_out=sm[:, c0:c0 + 1]), reads=[omB], writes=[onB, smB])
                yield
                op(POOL, lambda e: e.tensor_scalar(out=sm[:, c0 + 1:c0 + 2], in0=sm[:, c0:c0 + 1], scalar1=1.0 / 128, scalar2=EPS, op0=ALU.mult, op1=ALU.add), reads=[smB], writes=[smB])
                op(POOL, lambda e: e.tensor_tensor(out=sm[:, c0 + 2:c0 + 3], in0=sm[:, c0 + 1:c0 + 2], in1=cs("mhalf"), op=ALU.pow), reads=[smB, cpB], writes=[smB])
                yield
                op(DVE, lambda e: e.scalar_tensor_tensor(out=on[:, :], in0=om[:, :], scalar=sm[:, c0 + 2:c0 + 3], in1=cs("gon"), op0=ALU.mult, op1=ALU.mult), reads=[omB, smB, cpB], writes=[onB])
                yield
                op(PE, lambda e: e.transpose(Q2, on[:, :], ident), reads=[onB, cpB], writes=[PB])
                yield
                yg_ap = self.ygT[:, hh * T + t * 128: hh * T + (t + 1) * 128]
                op(DVE, lambda e: e.tensor_tensor(out=yg_ap, in0=Q2, in1=zT[:, hh * 128:(hh + 1) * 128], op=ALU.mult), reads=[PB, zB], writes=[self.ygB[t]])
                yield

        for it in pre_ops(0):
            S.replay(it)
        for t in range(NT):
            pend = pre_ops(t + 1) if t + 1 < NT else []
            per = (len(pend) + 39) // 40
            S0 = int(os.environ.get("KSTAG", 4))
            per = (len(pend) + 39 + 3 * S0) // (40 + 3 * S0)
            alive = [(hh, chain(hh, t)) for hh in range(4)]
            pi = 0
            rnd = 0
            while alive or pi < len(pend):
                nxt = []
                for hh, g_ in alive:
                    if rnd < hh * S0:
                        nxt.append((hh, g_))
                        continue
                    try:
                        next(g_)
                        nxt.append((hh, g_))
                    except StopIteration:
                        pass
                alive = nxt
                for it in pend[pi:pi + per]:
                    S.replay(it)
                pi += per
                rnd += 1
        op(DVE, lambda e: e.tensor_copy(out=pchv, in_=pcv0[:, :, 0:3]), reads=[pcB], writes=[self.pchB])
        es.close()

    def conv_phase(self, wm_in, wm_out, hhalo, hhB):
        import os
        nc, S = self.nc, self.S
        op = S.op
        cs, cpB = self.cs, self.cpB
        h, hB = self.h, self.hB
        psum, psB = self.psum, self.psB
        es = contextlib.ExitStack()
        tag = "cv"
        wc = self.sb("wc", 8 * 1536, BF16, es); wcB = [Buf() for _ in range(6)]
        wo = self.sb("wo", 8 * 1024, BF16, es); woB = [Buf() for _ in range(4)]
        xs = self.sb("xs_cv", D, F32, es); xsB = Buf()
        self.junk = self.sb("junk_cv", D, BF16, es); self.junkB = Buf()
        xn = self.sb("xn_cv", 8 * 128, BF16, es); xnB = Buf()
        mpc = self.sb("mpc", 4 * 130, F32, es); mpcB = Buf()
        mpc2 = self.sb("mpc2", 4 * 130, F32, es); mpc2B = Buf()
        cbs0 = self.sb("cbs0", 512, F32, es); cbs0B = Buf()
        cbs1 = self.sb("cbs1", 512, F32, es); cbs1B = Buf()
        cct = self.sb("cct", 512, F32, es); cctB = Buf()
        cacc = self.sb("cacc_cv", 512, F32, es); caccB = Buf()
        yv = self.sb("yv", 512, F32, es); yvB = Buf()
        sq = self.sb("sq_cv", 512, F32, es); sqB = Buf()
        rs = self.sb("rs_cv", 512, F32, es); rsB = Buf()
        ycT = self.sb("ycT", 512, BF16, es); ycB = Buf()
        motmp = [self.sb("motmp%d" % i, 512, F32, es) for i in range(2)]; motB = [Buf(), Buf()]
        new_bufs = wcB + woB + [mpc2B, cbs0B, cbs1B, xsB, self.junkB, xnB, mpcB, cctB, caccB, yvB, sqB, rsB, ycB] + motB
        S.alias(new_bufs, getattr(self, "phase_bufs", []))
        self.phase_bufs = new_bufs
        wm_v = wm_in.rearrange("(c p) n -> p c n", p=128)
        for col in range(0, 1536, 256):
            base = (col // 256) * 2048
            self.load_cast(wc[:, base:base + 2048], wcB[col // 256], wm_v[:, :, col:col + 256], (8, 256))
        wo_v = wm_out.rearrange("(c p) n -> p c n", p=128)
        for i in range(4):
            self.load_cast(wo[:, i * 2048:(i + 1) * 2048], woB[i], wo_v[:, 2 * i:2 * i + 2, :], (2, 1024))
        op(POOL, lambda e: e.memset(mpc[:, :], 0.0), writes=[mpcB])
        op(POOL, lambda e: e.memset(mpc2[:, :], 0.0), writes=[mpc2B])
        ident, blk64 = cs("ident"), cs("blk64")
        csw, cgain = cs("csw"), cs("cgain")
        ygT, ygB = self.ygT, self.ygB
        mpc_b = [mpc, mpc2]; mpcB_b = [mpcB, mpc2B]
        cbs_b = [cbs0, cbs1]; cbsB_b = [cbs0B, cbs1B]

        def capA(t):
            p = t % 2
            mp, mpB = mpc_b[p], mpcB_b[p]
            mo, moB = mpc_b[1 - p], mpcB_b[1 - p]
            mpv = mp[:, :].rearrange("p (a b) -> p a b", a=4)
            mov = mo[:, :].rearrange("p (a b) -> p a b", a=4)
            S.capture = []
            if t < 0:
                src, srcB = hhalo[:, :], hhB
            else:
                src, srcB = h[:, t * D:(t + 1) * D], hB[t]
            self.norm_transpose(src, srcB, "nm", xn, 0, 128, xnB, xs, xsB, (0, 1))
            for grp in range(3):
                if t < 0 and grp == 0:
                    continue
                pb = 2 + grp
                for q in range(4):
                    j = grp * 4 + q
                    for c in range(8):
                        lhsT = wc[:, (j // 2) * 2048 + c * 256 + (j % 2) * 128: (j // 2) * 2048 + c * 256 + (j % 2) * 128 + 128]
                        rhs = xn[:, c * 128:(c + 1) * 128]
                        op(PE, (lambda pb, q, lhsT, rhs, c: lambda e: e.matmul(psum[pb][:, q * 128:(q + 1) * 128], lhsT=lhsT, rhs=rhs, start=(c == 0), stop=(c == 7)))(pb, q, lhsT, rhs, c),
                           reads=[wcB[j // 2], xnB], writes=[psB[pb][q]])
                if grp == 0:
                    op(ACT, (lambda p: lambda e: e.activation(out=cbs_b[p][:, :], in_=psum[2][:, :], func=AF.Copy))(p), reads=psB[2], writes=[cbsB_b[p]])
                if grp == 1:
                    op(ACT, lambda e: e.activation(out=cct[:, :], in_=psum[3][:, :], func=AF.Copy), reads=psB[3], writes=[cctB])
            op(DVE, lambda e: e.tensor_tensor(out=mpv[:, :, 2:130], in0=cct[:, :].rearrange("p (a b) -> p a b", a=4), in1=psum[4][:, :].rearrange("p (a b) -> p a b", a=4), op=ALU.mult),
               reads=[cctB, mpB] + psB[4], writes=[mpB])
            op(DVE, lambda e: e.tensor_copy(out=mpv[:, :, 0:2], in_=mov[:, :, 128:130]), reads=[moB, mpB], writes=[mpB])
            ops_ = S.capture
            S.capture = None
            return ops_

        def capB(t):
            p = t % 2
            mp, mpB = mpc_b[p], mpcB_b[p]
            cbs, cbsB = cbs_b[p], cbsB_b[p]
            S.capture = []
            for j in range(4):
                op(DVE, (lambda j: lambda e: e.tensor_scalar(out=cacc[:, j * 128:(j + 1) * 128], in0=mp[:, j * 130:j * 130 + 128], scalar1=csw[:, j * 3:j * 3 + 1], scalar2=None, op0=ALU.mult))(j),
                   reads=[mpB, cpB], writes=[caccB])
                for k in range(1, 3):
                    op(DVE, (lambda j, k: lambda e: e.scalar_tensor_tensor(out=cacc[:, j * 128:(j + 1) * 128], in0=mp[:, j * 130 + k:j * 130 + k + 128], scalar=csw[:, j * 3 + k:j * 3 + k + 1], in1=cacc[:, j * 128:(j + 1) * 128], op0=ALU.mult, op1=ALU.add))(j, k),
                       reads=[mpB, cpB, caccB], writes=[caccB])
            op(DVE, lambda e: e.tensor_tensor(out=yv[:, :], in0=cacc[:, :], in1=cbs[:, :], op=ALU.mult), reads=[caccB, cbsB], writes=[yvB])
            op(ACT, lambda e: e.activation(out=sq[:, :], in_=yv[:, :], func=AF.Square), reads=[yvB], writes=[sqB])
            op(PE, lambda e: e.matmul(psum[5][:, :], lhsT=blk64, rhs=sq[:, :], start=True, stop=True), reads=[sqB, cpB], writes=psB[5])
            op(ACT, lambda e: e.activation(out=rs[:, :], in_=psum[5][:, :], func=AF.Ln, bias=cs("eps")), reads=psB[5] + [cpB], writes=[rsB])
            op(ACT, lambda e: e.activation(out=rs[:, :], in_=rs[:, :], func=AF.Exp, scale=-0.5), reads=[rsB], writes=[rsB])
            for j in range(4):
                op(DVE, (lambda j: lambda e: e.scalar_tensor_tensor(out=ycT[:, j * 128:(j + 1) * 128], in0=yv[:, j * 128:(j + 1) * 128], scalar=cgain[:, j:j + 1], in1=rs[:, j * 128:(j + 1) * 128], op0=ALU.mult, op1=ALU.mult))(j),
                   reads=[yvB, rsB, cpB], writes=[ycB])
            for hh in range(2):
                pb = 6 + hh
                for j in range(8):
                    if j < 4:
                        lhsT = ycT[:, j * 128:(j + 1) * 128]
                        rd = [ycB]
                    else:
                        lhsT = ygT[:, (j - 4) * T + t * 128:(j - 4) * T + (t + 1) * 128]
                        rd = [ygB[t]]
                    rhs = wo[:, j * 1024 + hh * 512: j * 1024 + (hh + 1) * 512]
                    op(PE, (lambda pb, lhsT, rhs, j: lambda e: e.matmul(psum[pb][:, :], lhsT=lhsT, rhs=rhs, start=(j == 0), stop=(j == 7)))(pb, lhsT, rhs, j),
                       reads=rd + [woB[j // 2]], writes=psB[pb])
                hap = h[:, t * D + hh * 512: t * D + (hh + 1) * 512]
                op(ACT, (lambda pb, hh: lambda e: e.activation(out=motmp[hh][:, :], in_=psum[pb][:, :], func=AF.Copy))(pb, hh), reads=psB[pb], writes=[motB[hh]])
                op(DVE, (lambda hh, hap: lambda e: e.tensor_tensor(out=hap, in0=hap, in1=motmp[hh][:, :], op=ALU.add))(hh, hap), reads=[motB[hh], hB[t]], writes=[hB[t]])
            ops_ = S.capture
            S.capture = None
            return ops_

        for it in capA(-1):
            S.replay(it)
        for it in capA(0):
            S.replay(it)
        for t in range(NT):
            A = capA(t + 1) if t + 1 < NT else []
            B_ = capB(t)
            na, nb = len(A), len(B_)
            ia = ib = 0
            while ia < na or ib < nb:
                if ib < nb and (ia >= na or ib * max(na, 1) <= ia * nb):
                    S.replay(B_[ib]); ib += 1
                else:
                    S.replay(A[ia]); ia += 1
        es.close()

    def final(self, out, fn_bc):
        S = self.S
        op = S.op
        cs, cpB = self.cs, self.cpB
        h, hB = self.h, self.hB
        es = contextlib.ExitStack()
        ot = [self.sb("ot%d" % i, D, F32, es) for i in range(2)]
        otB = [Buf(), Buf()]
        fs = self.sb("fs", 64, F32, es); fsB = Buf()
        junk = self.sb("junk_f", D, BF16, es); junkB = Buf()
        fnb = self.sb("fnb", D, F32, es); fnbB = Buf()
        new_bufs = otB + [fsB, junkB, fnbB]
        S.alias(new_bufs, getattr(self, "phase_bufs", []))
        self.phase_bufs = new_bufs
        S.dma(SP, lambda e: e.dma_start(out=fnb[:], in_=fn_bc), "const2", S.new_group(), writes=[fnbB])
        import os
        if os.environ.get("KRAWOUT"):
            for t in range(NT):
                S.dma(SP, (lambda t: lambda e: e.dma_start(out=out[t * 128:(t + 1) * 128, :], in_=h[:, t * D:(t + 1) * D]))(t), "out%d" % (t % 2), S.new_group(), reads=[hB[t]])
            es.close()
            return
        for t in range(NT):
            k = t % 2
            c0 = (t % 16) * 3
            hs = h[:, t * D:(t + 1) * D]
            op(ACT, (lambda hs, c0: lambda e: e.activation(out=junk[:, :], in_=hs, func=AF.Square, accum_out=fs[:, c0:c0 + 1]))(hs, c0), reads=[hB[t]], writes=[junkB, fsB])
            op(POOL, (lambda c0: lambda e: e.tensor_scalar(out=fs[:, c0 + 1:c0 + 2], in0=fs[:, c0:c0 + 1], scalar1=1.0 / D, scalar2=EPS, op0=ALU.mult, op1=ALU.add))(c0), reads=[fsB], writes=[fsB])
            op(POOL, (lambda c0: lambda e: e.tensor_tensor(out=fs[:, c0 + 2:c0 + 3], in0=fs[:, c0 + 1:c0 + 2], in1=cs("mhalf"), op=ALU.pow))(c0), reads=[fsB, cpB], writes=[fsB])
            op(DVE, (lambda hs, c0, k: lambda e: e.scalar_tensor_tensor(out=ot[k][:, :], in0=hs, scalar=fs[:, c0 + 2:c0 + 3], in1=fnb[:, :], op0=ALU.mult, op1=ALU.mult))(hs, c0, k),
               reads=[hB[t], fsB, fnbB], writes=[otB[k]])
            S.dma(SP, (lambda t, k: lambda e: e.dma_start(out=out[t * 128:(t + 1) * 128, :], in_=ot[k][:, :]))(t, k), "out%d" % k, S.new_group(), reads=[otB[k]])
        es.close()


def _pack_layout():
    names = [("ident", 128), ("ones", 128), ("triU", 128), ("maskL", 128), ("maskU", 128), ("blk64", 128),
             ("gon", 128), ("n1", 8), ("nm", 8), ("n2", 8), ("cwg", 48), ("csw", 12), ("cgain", 4),
             ("alog", 4), ("dtb", 4), ("mhalf", 1), ("eps", 1)]
    lay = {}
    off = 0
    for n, w in names:
        lay[n] = (off, off + w)
        off += w
    return lay, off


_CP, _CPK_COLS = _pack_layout()
Builder.CP = _CP
Builder.CPK_COLS = _CPK_COLS


def _pack_consts(inp):
    f = np.float32
    cp = np.zeros((128, _CPK_COLS), f)

    def put(name, arr):
        a, b = _CP[name]
        cp[:, a:b] = np.asarray(arr, f).reshape(128, b - a)

    idx = np.arange(128)
    same = (idx[:, None] // 64) == (idx[None, :] // 64)
    put("ident", np.eye(128))
    put("ones", np.ones((128, 128)))
    put("triU", (same & (idx[:, None] <= idx[None, :])))
    put("maskL", np.where(same & (idx[:, None] > idx[None, :]), 0.0, NEG))
    put("maskU", np.where(same & (idx[:, None] <= idx[None, :]), 0.0, NEG))
    put("blk64", same.astype(f) / 64.0)
    put("gon", np.broadcast_to(inp["gdn_out_norm"].reshape(1, 128), (128, 128)))
    put("n1", inp["ffn1_norm"].reshape(8, 128).T)
    put("nm", inp["mix_norm"].reshape(8, 128).T)
    put("n2", inp["ffn2_norm"].reshape(8, 128).T)
    put("cwg", inp["gdn_conv_w"].reshape(4, 12, 128).transpose(2, 1, 0).reshape(128, 48))
    put("csw", inp["conv_short_w"].reshape(3, 4, 128).transpose(2, 1, 0).reshape(128, 12))
    put("cgain", inp["conv_out_norm"].reshape(4, 128).T)
    put("alog", np.broadcast_to(inp["gdn_A_log"].reshape(1, 4), (128, 4)))
    put("dtb", np.broadcast_to(inp["gdn_dt_bias"].reshape(1, 4), (128, 4)))
    put("mhalf", np.full((128, 1), -0.5))
    put("eps", np.full((128, 1), EPS))
    return cp


_NC_CACHE = {}


def _get_nc(debug=False):
    if debug not in _NC_CACHE:
        b = Builder(debug=debug)
        b.spar_h = [0, 0, 0, 0]
        _NC_CACHE[debug] = (b.build(), b)
    return _NC_CACHE[debug]


def kernel(debug=False, **inputs):
    inp = {k: np.asarray(v) for k, v in inputs.items()}
    x = inp["x"].astype(np.float32, copy=False)
    nc, b = _get_nc(debug)
    cp = _pack_consts(inp)
    fn_bc = np.ascontiguousarray(np.broadcast_to(inp["final_norm"].reshape(1, D).astype(np.float32), (128, D)))
    shared = {
        "w1_in": np.ascontiguousarray(inp["ffn1_w_in"][0]), "w1_out": np.ascontiguousarray(inp["ffn1_w_out"][0]),
        "w2_in": np.ascontiguousarray(inp["ffn2_w_in"][0]), "w2_out": np.ascontiguousarray(inp["ffn2_w_out"][0]),
        "wm_in": np.ascontiguousarray(inp["w_mix_in"][0]), "wm_out": np.ascontiguousarray(inp["w_mix_out"][0]),
        "cpk": cp, "fn_bc": fn_bc,
    }
    zeros = np.zeros((T, D), np.float32)
    in_maps = []
    for c in range(8):
        bi, half = c // 2, c % 2
        m = dict(shared)
        m["x_own"] = np.ascontiguousarray(x[bi, half * T:(half + 1) * T])
        m["x_pre"] = zeros if half == 0 else np.ascontiguousarray(x[bi, 0:T])
        in_maps.append(m)
    import os
    ncores = int(os.environ.get("KCORES", 8))
    res = run_bass_kernel_spmd(nc, in_maps[:ncores], core_ids=list(range(ncores)))
    outp = np.zeros((4, 2 * T, D), np.float32)
    for c in range(ncores):
        outp[c // 2, (c % 2) * T:(c % 2 + 1) * T] = res.results[c]["out"]
    if debug:
        return outp, res.results
    return outp
```

```python
import contextlib
import numpy as np
import concourse.bass as bass
import concourse.mybir as mybir
from concourse.bass_utils import run_bass_kernel_spmd

F32 = mybir.dt.float32
BF16 = mybir.dt.bfloat16
AF = mybir.ActivationFunctionType
ALU = mybir.AluOpType

PE, ACT, DVE, POOL, SP = "pe", "act", "dve", "pool", "sp"
COMPUTE = (PE, ACT, DVE, POOL)

D = 1024
DFF = 2816
T = 2048
NT = T // 128
NB = T // 512
EPS = 1e-6
GW0 = 1536
NG = 2056
NEG = -1.0e30


class Buf:
    __slots__ = ("name", "last_w", "readers", "excl")

    def __init__(self, name="", excl=False):
        self.name = name
        self.last_w = None
        self.readers = []
        self.excl = excl


class Op:
    __slots__ = ("eng", "fn", "deps", "needs_inc", "cnt", "is_dma", "key", "grp", "idx")

    def __init__(self, eng, fn, is_dma=False, key=None, grp=None):
        self.eng = eng
        self.fn = fn
        self.deps = []
        self.needs_inc = False
        self.cnt = 0
        self.is_dma = is_dma
        self.key = key
        self.grp = grp


class Sched:
    def __init__(self):
        self.ops = []
        self.grp_ctr = 0

    def new_group(self):
        self.grp_ctr += 1
        return self.grp_ctr

    def _add(self, op, reads, writes):
        if getattr(self, "capture", None) is not None:
            self.capture.append((op, list(reads), list(writes)))
            return op
        return self._add_real(op, reads, writes)

    def replay(self, item):
        return self._add_real(*item)

    def _add_real(self, op, reads, writes):
        op.idx = len(self.ops)
        ex = [b for b in reads if b.excl]
        if ex:
            reads = [b for b in reads if not b.excl]
            writes = list(writes) + ex
        deps = {}
        for b in reads:
            if b.last_w is not None:
                deps[id(b.last_w)] = b.last_w
        for b in writes:
            if b.last_w is not None:
                deps[id(b.last_w)] = b.last_w
            for r in b.readers:
                deps[id(r)] = r
        latest = {}
        for d in deps.values():
            if d is op:
                continue
            if (not d.is_dma) and (not op.is_dma) and d.eng == PE and op.eng == PE:
                continue
            if d.is_dma:
                op.deps.append(d)
            else:
                cur = latest.get(d.eng)
                if cur is None or d.idx > cur.idx:
                    latest[d.eng] = d
        op.deps.extend(latest.values())
        for b in reads:
            if op.is_dma:
                b.readers.append(op)
            else:
                b.readers = [r for r in b.readers if r.is_dma or r.eng != op.eng]
                b.readers.append(op)
        for b in writes:
            b.last_w = op
            b.readers = []
        self.ops.append(op)
        return op

    def op(self, eng, fn, reads=(), writes=()):
        return self._add(Op(eng, fn), reads, writes)

    def dma(self, queue, fn, key, grp, reads=(), writes=()):
        return self._add(Op(queue, fn, is_dma=True, key=key, grp=grp), reads, writes)

    def alias(self, new_bufs, old_bufs):
        acc = {}
        for b in old_bufs:
            if b.last_w is not None:
                acc[id(b.last_w)] = b.last_w
            for r in b.readers:
                acc[id(r)] = r
        for nb in new_bufs:
            nb.readers = list(acc.values())

    def emit(self, nc, final_wait_keys=()):
        ops = self.ops
        for o in ops:
            for d in o.deps:
                d.needs_inc = True
        cnt = {}
        grp_end = {}
        for o in ops:
            if o.is_dma:
                k = ("dma", o.key)
                cnt[k] = cnt.get(k, 0) + 1
                o.cnt = cnt[k]
                grp_end[(o.key, o.grp)] = o.cnt
            elif o.needs_inc:
                cnt[o.eng] = cnt.get(o.eng, 0) + 1
                o.cnt = cnt[o.eng]
        dma_keys = sorted({o.key for o in ops if o.is_dma})
        streams = {e: [o for o in ops if o.eng == e] for e in (PE, ACT, DVE, POOL, SP)}
        self.stats = {e: len(s) for e, s in streams.items()}
        self.stats["incs"] = dict(cnt)

        import os
        SEG = int(os.environ.get('KSEG', 1500))
        with contextlib.ExitStack() as es:
            sems = {}
            for e in COMPUTE:
                nseg = (cnt.get(e, 0) + SEG - 1) // SEG + 1
                sems[e] = [es.enter_context(nc.semaphore("s_%s_%d" % (e, j))) for j in range(nseg)]
            for k in dma_keys:
                sems[("dma", k)] = es.enter_context(nc.semaphore("d_" + str(k)))
            block = es.enter_context(nc.Block())

            def run_stream(engname, eng):
                waited = {}
                for o in streams[engname]:
                    for d in o.deps:
                        if d.is_dma:
                            sk = ("dma", d.key)
                            val = 16 * grp_end[(d.key, d.grp)]
                            sem = sems[sk]
                        else:
                            seg = (d.cnt - 1) // SEG
                            sk = (d.eng, seg)
                            val = (d.cnt - 1) % SEG + 1
                            sem = sems[d.eng][seg]
                            if any(k2[0] == d.eng and k2[1] > seg for k2 in waited if isinstance(k2, tuple) and k2[0] == d.eng):
                                continue
                        if waited.get(sk, 0) >= val:
                            continue
                        waited[sk] = val
                        eng.wait_ge(sem, val)
                    ins = o.fn(eng)
                    if o.is_dma:
                        ins.then_inc(sems[("dma", o.key)], 16)
                    elif o.needs_inc:
                        ins.then_inc(sems[o.eng][(o.cnt - 1) // SEG], 1)
                if engname == SP:
                    for k in final_wait_keys:
                        eng.wait_ge(sems[("dma", k)], 16 * cnt[("dma", k)])

            @block.sync
            def _(e):
                run_stream(SP, e)

            @block.tensor
            def _(e):
                run_stream(PE, e)

            @block.scalar
            def _(e):
                run_stream(ACT, e)

            @block.vector
            def _(e):
                run_stream(DVE, e)

            @block.gpsimd
            def _(e):
                run_stream(POOL, e)


class Builder:
    def __init__(self, debug=False):
        self.debug = debug
        self.nc = bass.Bass("TRN2", target_bir_lowering=False)
        self.S = Sched()
        self.es = contextlib.ExitStack()
        self.dbg_outs = []
        self.dbg_keys = []
        self.rr = 0

    def sb(self, name, cols, dt=F32, es=None):
        return (es or self.es).enter_context(self.nc.sbuf_tensor(name, [128, cols], dt))

    def dram_in(self, name, shape, dt=F32):
        return self.nc.dram_tensor(name, list(shape), dt, kind="ExternalInput").ap()

    def dram_out(self, name, shape, dt=F32):
        return self.nc.dram_tensor(name, list(shape), dt, kind="ExternalOutput").ap()

    def dbg(self, name, ap, cols, bufs, dt=F32):
        if not self.debug:
            return
        o = self.dram_out("dbg_" + name, [128, cols], dt)
        self.dbg_keys.append("dbg_" + name)
        self.S.dma(SP, lambda e: e.dma_start(out=o, in_=ap), "dbg_" + name, self.S.new_group(), reads=bufs)

    def ew(self):
        self.rr += 1
        return ACT if (self.rr & 1) else DVE

    def build(self):
        nc, S = self.nc, self.S
        op = S.op
        x_pre = self.dram_in("x_pre", [T, D])
        x_own = self.dram_in("x_own", [T, D])
        w1_in = self.dram_in("w1_in", [D, 2 * DFF])
        w1_out = self.dram_in("w1_out", [DFF, D])
        w2_in = self.dram_in("w2_in", [D, 2 * DFF])
        w2_out = self.dram_in("w2_out", [DFF, D])
        wm_in = self.dram_in("wm_in", [D, 3592])
        wm_out = self.dram_in("wm_out", [D, D])
        cpk = self.dram_in("cpk", [128, self.CPK_COLS])
        fn_bc = self.dram_in("fn_bc", [128, D])
        out = self.dram_out("out", [T, D])
        self.out_grp = S.new_group()

        h = self.sb("h", NT * D)
        hB = [Buf("h%d" % t) for t in range(NT)]
        stage = [self.sb("stage%d" % i, 2048) for i in range(2)]
        stB = [Buf("st%d" % i) for i in range(2)]
        self.stage, self.stB, self.st_i = stage, stB, 0
        import os
        self.cast_order = os.environ.get('KCAST', 'dve,act,pool,dve,act').split(',')
        cp = self.sb("cp", self.CPK_COLS)
        cpB = Buf("cp")
        hhalo = self.sb("hhalo", D)
        hhB = Buf("hhalo")
        ygT = self.sb("ygT", 4 * T, BF16)
        ygB = [Buf("yg%d" % t) for t in range(NT)]
        pch = self.sb("pch", 36)
        pchB = Buf("pch")
        Sst = [self.sb("Sst%d" % i, 4 * 128) for i in range(2)]
        SsB = [[Buf("S%d_%d" % (i, hh)) for hh in range(4)] for i in range(2)]
        stat = self.sb("stat", 64)
        negA = self.sb("negA", 4)
        negAB = Buf("negA")
        psum = [self.es.enter_context(nc.psum_tensor("ps%d" % i, [128, 512], F32)) for i in range(8)]
        psB = [[Buf("ps%d" % i, excl=True)] * 4 for i in range(8)]
        self.psum, self.psB = psum, psB
        self.h, self.hB = h, hB

        C = self.CP
        g0 = S.new_group()
        S.dma(SP, lambda e: e.dma_start(out=cp[:], in_=cpk), "const", g0, writes=[cpB])
        self.cp, self.cpB = cp, cpB

        def cs(name, n=None):
            a, b = C[name]
            return cp[:, a:b]

        self.cs = cs
        ident = cs("ident")
        op(POOL, lambda e: e.memset(Sst[0][:], 0.0), writes=SsB[0])
        op(POOL, lambda e: e.memset(pch[:], 0.0), writes=[pchB])
        op(ACT, lambda e: e.activation(out=negA[:], in_=cs("alog"), func=AF.Exp), reads=[cpB], writes=[negAB])
        op(DVE, lambda e: e.tensor_scalar(out=negA[:], in0=negA[:], scalar1=-1.0, scalar2=None, op0=ALU.mult),
           reads=[negAB], writes=[negAB])
        self.negA, self.negAB = negA, negAB
        self.pch, self.pchB = pch, pchB
        self.Sst, self.SsB = Sst, SsB
        self.ygT, self.ygB = ygT, ygB
        self.spar = 0
        self.stat = stat
        self.statB = Buf("stat")

        import os
        PH = set(os.environ.get("KPH", "pf,pg,of,og,cv,f2").split(","))
        if "pf" in PH:
            self.load_x(x_pre)
            self.ffn(w1_in, w1_out, "n1", tag="p1")
        op(POOL, lambda e: e.tensor_copy(out=hhalo[:], in_=h[:, (NT - 1) * D:NT * D]), reads=[hB[NT - 1]], writes=[hhB])
        if "pg" in PH:
            self.gdn_phase(wm_in, full=False, tag="pg")
        self.load_x(x_own)
        if "of" in PH:
            self.ffn(w1_in, w1_out, "n1", tag="o1")
        self.dbg("h1", h[:, 0:D], D, [hB[0]])
        if "og" in PH:
            self.gdn_phase(wm_in, full=True, tag="og")
        self.dbg("yg", self.ygT[:, 0:T], T, self.ygB, BF16)
        if "cv" in PH:
            self.conv_phase(wm_in, wm_out, hhalo, hhB)
        self.dbg("h2", h[:, 0:D], D, [hB[0]])
        if "f2" in PH:
            self.ffn(w2_in, w2_out, "n2", tag="o2")
        self.final(out, fn_bc)
        S.emit(nc, final_wait_keys=["out0", "out1"] + self.dbg_keys)
        self.es.close()
        return nc

    def load_x(self, xd):
        S, h, hB = self.S, self.h, self.hB
        g = S.new_group()
        for t in range(NT):
            S.dma(SP, (lambda t: lambda e: e.dma_start(out=h[:, t * D:(t + 1) * D], in_=xd[t * 128:(t + 1) * 128, :]))(t),
                  "x%d" % (t % 4), g, writes=[hB[t]])

    def load_dma(self, dst_ap, dstB, src_ap, shape3):
        S = self.S
        n = len(self.stage)
        i = self.st_i % n
        self.st_i += 1
        st, sB = self.stage[i], self.stB[i]
        a, b = shape3
        sview = st[:, 0:a * b].rearrange("p (a b) -> p a b", a=a) if a > 1 else st[:, 0:b]
        sflat = st[:, 0:a * b]
        g = S.new_group()
        S.dma(SP, lambda e: e.dma_start(out=sview, in_=src_ap), "st%d" % i, g, writes=[sB])
        return (dst_ap, dstB, sflat, sB)

    def load_cast_do(self, hnd):
        S = self.S
        dst_ap, dstB, sflat, sB = hnd
        self.cast_i = getattr(self, "cast_i", 0) + 1
        eng = self.cast_order[self.cast_i % len(self.cast_order)]
        if eng == ACT:
            S.op(ACT, lambda e: e.activation(out=dst_ap, in_=sflat, func=AF.Copy), reads=[sB], writes=[dstB])
        else:
            S.op(eng, lambda e: e.tensor_copy(out=dst_ap, in_=sflat), reads=[sB], writes=[dstB])

    def load_cast(self, dst_ap, dstB, src_ap, shape3=None, key="w"):
        self.load_cast_do(self.load_dma(dst_ap, dstB, src_ap, shape3))

    def norm_transpose(self, src_ap, srcB, gain_name, dst, dst_off, dst_stride, dstB, xs, xsB, pbanks):
        S, cs = self.S, self.cs
        op = S.op
        stat = self.stat
        stB = self.statB
        op(ACT, lambda e: e.activation(out=xs[:, 0:D], in_=src_ap, func=AF.Square, accum_out=stat[:, 0:1]),
           reads=[srcB], writes=[xsB, stB])
        op(POOL, lambda e: e.tensor_scalar(out=stat[:, 1:2], in0=stat[:, 0:1], scalar1=1.0 / D, scalar2=EPS,
                                           op0=ALU.mult, op1=ALU.add), reads=[stB], writes=[stB])
        op(POOL, lambda e: e.tensor_tensor(out=stat[:, 2:3], in0=stat[:, 1:2], in1=cs("mhalf"), op=ALU.pow),
           reads=[stB, self.cpB], writes=[stB])
        op(DVE, lambda e: e.tensor_scalar(out=xs[:, 0:D], in0=src_ap, scalar1=stat[:, 2:3], scalar2=None, op0=ALU.mult),
           reads=[srcB, stB], writes=[xsB])
        gain = cs(gain_name)
        ident = cs("ident")
        for half in range(2):
            pb = pbanks[half]
            pbuf = self.psum[pb]
            for q in range(4):
                c = half * 4 + q
                op(PE, (lambda c, q, pbuf: lambda e: e.transpose(pbuf[:, q * 128:(q + 1) * 128], xs[:, c * 128:(c + 1) * 128], ident))(c, q, pbuf),
                   reads=[xsB, self.cpB], writes=[self.psB[pb][q]])
            for q in range(4):
                c = half * 4 + q
                eng = self.ew()
                o_ap = dst[:, c * dst_stride + dst_off: c * dst_stride + dst_off + 128]
                i_ap = pbuf[:, q * 128:(q + 1) * 128]
                g_ap = gain[:, c:c + 1]
                if eng == ACT:
                    op(ACT, (lambda o_ap, i_ap, g_ap: lambda e: e.activation(out=o_ap, in_=i_ap, func=AF.Copy, scale=g_ap))(o_ap, i_ap, g_ap),
                       reads=[self.psB[pb][q], self.cpB], writes=[dstB])
                else:
                    op(DVE, (lambda o_ap, i_ap, g_ap: lambda e: e.tensor_scalar(out=o_ap, in0=i_ap, scalar1=g_ap, scalar2=None, op0=ALU.mult))(o_ap, i_ap, g_ap),
                       reads=[self.psB[pb][q], self.cpB], writes=[dstB])

    def ffn(self, w_in, w_out, gain_name, tag):
        import os
        nc, S = self.nc, self.S
        op = S.op
        h, hB = self.h, self.hB
        psum, psB = self.psum, self.psB
        es = contextlib.ExitStack()
        xnT = self.sb("xnT_" + tag, 8 * T, BF16, es)
        xnB = [Buf("xn%d" % t) for t in range(NT)]
        CPP = int(os.environ.get("KCPP", 4))
        nsub = CPP // 2
        wbi = [self.sb("wbi%d_%s" % (i, tag), 2 * nsub * 2048, BF16, es) for i in range(2)]
        wbo = [self.sb("wbo%d_%s" % (i, tag), CPP * 1024, BF16, es) for i in range(2)]
        assert CPP == 4
        wbiB = [[Buf() for _ in range(4)] for _ in range(2)]
        wboB = [[Buf() for _ in range(nsub)] for _ in range(2)]
        hid = [self.sb("hid%d_%s" % (i, tag), CPP * 512, BF16, es) for i in range(2)]
        hidB = [Buf(), Buf()]
        sg0 = self.sb("sg0_%s" % tag, 512, F32, es); sg = [sg0, sg0]
        sgB0 = Buf(); sgB = [sgB0, sgB0]
        ev0 = self.sb("ev0_%s" % tag, 512, F32, es); ev = [ev0, ev0]
        evB0 = Buf(); evB = [evB0, evB0]
        xs0 = self.sb("xs0_%s" % tag, D, F32, es); xs = [xs0, xs0]
        xsB0 = Buf(); xsB = [xsB0, xsB0]
        self.junk = self.sb("junk_" + tag, D, BF16, es)
        self.junkB = Buf()
        base_stage, base_stB = self.stage, self.stB
        nextra = int(os.environ.get("KXST", 0))
        xst = [self.sb("xst%d_%s" % (i, tag), 2048, F32, es) for i in range(nextra)]
        xstB = [Buf() for _ in range(nextra)]
        self.stage, self.stB = base_stage + xst, base_stB + xstB
        new_bufs = xnB + wbiB[0] + wbiB[1] + wboB[0] + wboB[1] + hidB + sgB + xsB + [self.junkB] + evB + xstB
        S.alias(new_bufs, getattr(self, "phase_bufs", []))
        self.phase_bufs = new_bufs

        w_in_v = w_in.rearrange("(c p) n -> p c n", p=128)
        w_out_v = w_out.rearrange("(c p) n -> p c n", p=128)
        pieces = []
        c0 = 0
        while c0 < DFF // 128:
            n = min(CPP, DFF // 128 - c0)
            pieces.append((c0, n))
            c0 += n
        NP = len(pieces)

        def piece_specs(p):
            s = p % 2
            ch0, n = pieces[p]
            W = n * 128
            col = ch0 * 128
            specs = []
            for which in range(2):
                for csub in range(2):
                    base = which * 4096 + csub * 2048
                    specs.append((wbi[s][:, base:base + 4 * W], wbiB[s][which * 2 + csub],
                                  w_in_v[:, csub * 4:csub * 4 + 4, which * DFF + col:which * DFF + col + W], (4, W)))
            for sub in range(n // 2):
                specs.append((wbo[s][:, sub * 2048:(sub + 1) * 2048], wboB[s][sub], w_out_v[:, ch0 + 2 * sub:ch0 + 2 * sub + 2, :], (2, 1024)))
            return specs

        def load_piece(p):
            for sp_ in piece_specs(p):
                self.load_cast(*sp_)

        load_piece(0)
        for t in range(NT):
            self.norm_transpose(h[:, t * D:(t + 1) * D], hB[t], gain_name, xnT, t * 128, T, xnB[t],
                                xs[t % 2], xsB[t % 2], (6, 7))
        blocks = [(p, tb) for p in range(NP) for tb in range(NB)]
        st = {"gi": 0, "oi": 0}

        def stage1(idx):
            p, tb = blocks[idx]
            s = p % 2
            hs = idx % 2
            n = pieces[p][1]
            for j in range(n):
                gi = st["gi"]
                for which in range(2):
                    pb = (0 if which == 0 else 2) + (gi % 2)
                    W = n * 128
                    for c in range(8):
                        off = which * 4096 + (c // 4) * 2048 + (c % 4) * W + j * 128
                        lhsT = wbi[s][:, off:off + 128]
                        rhs = xnT[:, c * T + tb * 512: c * T + (tb + 1) * 512]
                        op(PE, (lambda pb, lhsT, rhs, c: lambda e: e.matmul(psum[pb][:, :], lhsT=lhsT, rhs=rhs, start=(c == 0), stop=(c == 7)))(pb, lhsT, rhs, c),
                           reads=[wbiB[s][which * 2 + c // 4]] + xnB[tb * 4:(tb + 1) * 4], writes=psB[pb])
                pg, pu = (gi % 2), 2 + (gi % 2)
                k = gi % 2
                op(ACT, (lambda pg, k: lambda e: e.activation(out=sg[k][:, :], in_=psum[pg][:, :], func=AF.Silu))(pg, k),
                   reads=psB[pg], writes=[sgB[k]])
                op(DVE, (lambda pu, k, hs, j: lambda e: e.tensor_tensor(out=hid[hs][:, j * 512:(j + 1) * 512], in0=sg[k][:, :], in1=psum[pu][:, :], op=ALU.mult))(pu, k, hs, j),
                   reads=[sgB[k]] + psB[pu], writes=[hidB[hs]])
                st["gi"] += 1

        def stage2(idx):
            p, tb = blocks[idx]
            s = p % 2
            hs = idx % 2
            n = pieces[p][1]
            for tt in range(4):
                t = tb * 4 + tt
                for hh in range(2):
                    oi = st["oi"]
                    pb = 4 + (oi % 2)
                    for j in range(n):
                        lhsT = hid[hs][:, j * 512 + tt * 128: j * 512 + (tt + 1) * 128]
                        rhs = wbo[s][:, j * 1024 + hh * 512: j * 1024 + (hh + 1) * 512]
                        op(PE, (lambda pb, lhsT, rhs, j: lambda e: e.matmul(psum[pb][:, :], lhsT=lhsT, rhs=rhs, start=(j == 0), stop=(j == n - 1)))(pb, lhsT, rhs, j),
                           reads=[hidB[hs], wboB[s][j // 2]], writes=psB[pb])
                    hap = h[:, t * D + hh * 512: t * D + (hh + 1) * 512]
                    if oi % 2 == 0:
                        op(DVE, (lambda pb, hap: lambda e: e.scalar_tensor_tensor(out=hap, in0=psum[pb][:, :], scalar=0.5, in1=hap, op0=ALU.mult, op1=ALU.add))(pb, hap),
                           reads=psB[pb] + [hB[t]], writes=[hB[t]])
                    else:
                        k = (oi // 2) % 2
                        op(ACT, (lambda pb, k: lambda e: e.activation(out=ev[k][:, :], in_=psum[pb][:, :], func=AF.Copy, scale=0.5))(pb, k),
                           reads=psB[pb], writes=[evB[k]])
                        op(POOL, (lambda k, hap: lambda e: e.tensor_tensor(out=hap, in0=hap, in1=ev[k][:, :], op=ALU.add))(k, hap),
                           reads=[evB[k], hB[t]], writes=[hB[t]])
                    st["oi"] += 1

        pend_specs = []
        inflight = []
        for idx in range(len(blocks)):
            stage1(idx)
            if idx > 0:
                stage2(idx - 1)
            p, tb = blocks[idx]
            if tb == 0 and p + 1 < NP:
                pend_specs = piece_specs(p + 1)
            for hnd in inflight:
                self.load_cast_do(hnd)
            inflight = []
            if tb == NB - 1:
                for sp_ in pend_specs:
                    self.load_cast(*sp_)
                pend_specs = []
            else:
                for sp_ in pend_specs[:2]:
                    inflight.append(self.load_dma(*sp_))
                pend_specs = pend_specs[2:]
        stage2(len(blocks) - 1)
        self.stage, self.stB = base_stage, base_stB
        es.close()

    def gdn_phase(self, wm_in, full, tag):
        import os
        nc, S = self.nc, self.S
        op = S.op
        cs, cpB = self.cs, self.cpB
        h, hB = self.h, self.hB
        psum, psB = self.psum, self.psB
        es = contextlib.ExitStack()
        pc = self.sb("pc_" + tag, 12 * 131, F32, es); pcB = Buf()
        wg = self.sb("wg_" + tag, 8 * NG, BF16, es)
        wgB = [Buf() for _ in range(9)]
        xs = self.sb("xs_" + tag, D, F32, es); xsB = Buf()
        xn = self.sb("xn_" + tag, 8 * 128, BF16, es); xnB = Buf()
        qkv0 = self.sb("qkv_" + tag, 12 * 128, F32, es); qkvB0 = Buf()
        qkv_b = [qkv0, self.stage[0][:, 0:1536]]; qkvB_b = [qkvB0, self.stB[0]]
        cacc = self.sb("cacc_" + tag, 12 * 128, F32, es); caccB = Buf()
        etmp = self.sb("etmp_" + tag, 12 * 128, F32, es); etmpB = Buf()
        rs, rsB = etmp, etmpB
        zT0 = self.sb("zT_" + tag, 4 * 128, F32, es); zB0 = Buf()
        zT_b = [zT0, self.stage[1][:, 0:512]]; zB_b = [zB0, self.stB[1]]
        sm_b = [self.sb("sm%d_%s" % (i, tag), 64, F32, es) for i in range(2)]; smB_b = [Buf(), Buf()]
        Dg = self.sb("Dg_" + tag, 512, F32, es); DgB = Buf()
        glb_b = [self.sb("glb%d_%s" % (i, tag), 8, F32, es) for i in range(2)]; glB_b = [Buf(), Buf()]
        def mk(n, cols=128, dt=F32):
            return self.sb(n + "_" + tag, cols, dt, es), Buf(n)
        HB = []
        for hh in range(4):
            d_ = {}
            for n in ["kbg", "kdec", "vbeta", "tm1", "E1", "Lm", "AT", "X0", "X1", "Y0", "Y1", "P0", "P1", "wT", "um", "vnew"]:
                d_[n] = mk("%s%d" % (n, hh))
            HB.append(d_)
        new_bufs = wgB + [pcB, xsB, xnB, qkvB0, caccB, etmpB, zB0, DgB] + smB_b + glB_b + [b for d_ in HB for _, b in d_.values()]
        S.alias(new_bufs, getattr(self, "phase_bufs", []))
        self.phase_bufs = new_bufs

        wm_v = wm_in.rearrange("(c p) n -> p c n", p=128)
        col = 0
        while col < NG:
            w = min(256, NG - col)
            base = (col // 256) * 2048
            self.load_cast(wg[:, base:base + 8 * w], wgB[col // 256], wm_v[:, :, GW0 + col:GW0 + col + w], (8, w))
            col += w

        ident, ones, triU = cs("ident"), cs("ones"), cs("triU")
        maskL, maskU = cs("maskL"), cs("maskU")
        cwg = cs("cwg")
        pcv0 = pc[:, :].rearrange("p (a b) -> p a b", a=12)
        pchv = self.pch[:, :].rearrange("p (a b) -> p a b", a=12)
        op(DVE, lambda e: e.tensor_copy(out=pcv0[:, :, 0:3], in_=pchv), reads=[self.pchB], writes=[pcB])
        Sst, SsB = self.Sst, self.SsB
        nq = 16 if full else 12

        def pre_ops(t):
            pp = t % 2
            qkv, qkvB = qkv_b[pp], qkvB_b[pp]
            zT, zB = zT_b[pp], zB_b[pp]
            sm, smB = sm_b[pp], smB_b[pp]
            glb, glB = glb_b[pp], glB_b[pp]
            S.capture = []
            self.norm_transpose(h[:, t * D:(t + 1) * D], hB[t], "nm", xn, 0, 128, xnB, xs, xsB, (0, 1))
            for grp in range(nq // 4):
                if (not full) and grp == 0 and t != NT - 1:
                    continue
                pb = 2
                for q in range(4):
                    j = grp * 4 + q
                    for c in range(8):
                        lhsT = wg[:, (j // 2) * 2048 + c * 256 + (j % 2) * 128: (j // 2) * 2048 + c * 256 + (j % 2) * 128 + 128]
                        rhs = xn[:, c * 128:(c + 1) * 128]
                        op(PE, (lambda pb, q, lhsT, rhs, c: lambda e: e.matmul(psum[pb][:, q * 128:(q + 1) * 128], lhsT=lhsT, rhs=rhs, start=(c == 0), stop=(c == 7)))(pb, q, lhsT, rhs, c),
                           reads=[wgB[j // 2], xnB], writes=[psB[pb][q]])
                if grp < 3:
                    dstv = pc[:, grp * 4 * 131:(grp + 1) * 4 * 131].rearrange("p (a b) -> p a b", a=4)[:, :, 3:131]
                    srcv = psum[pb][:, :].rearrange("p (a b) -> p a b", a=4)
                    eng = self.ew()
                    if eng == ACT:
                        op(ACT, (lambda dstv, srcv: lambda e: e.activation(out=dstv, in_=srcv, func=AF.Copy))(dstv, srcv), reads=psB[pb], writes=[pcB])
                    else:
                        op(DVE, (lambda dstv, srcv: lambda e: e.tensor_copy(out=dstv, in_=srcv))(dstv, srcv), reads=psB[pb], writes=[pcB])
                else:
                    op(ACT, (lambda pb: lambda e: e.activation(out=zT[:, :], in_=psum[pb][:, :], func=AF.Copy))(pb), reads=psB[pb], writes=[zB])
            i3 = len(S.capture)
            for c in range(8):
                lhsT = xn[:, c * 128:(c + 1) * 128]
                rhs = wg[:, 8 * 2048 + c * 8: 8 * 2048 + c * 8 + 8]
                op(PE, (lambda lhsT, rhs, c: lambda e: e.matmul(psum[2][:, 0:8], lhsT=lhsT, rhs=rhs, start=(c == 0), stop=(c == 7)))(lhsT, rhs, c),
                   reads=[wgB[8], xnB], writes=[psB[2][0]])
            op(DVE, lambda e: e.tensor_copy(out=sm[:, 0:8], in_=psum[2][:, 0:8]), reads=[psB[2][0]], writes=[smB])
            op(ACT, lambda e: e.activation(out=sm[:, 8:12], in_=sm[:, 0:4], func=AF.Exp, scale=-1.0), reads=[smB], writes=[smB])
            op(DVE, lambda e: e.tensor_scalar(out=sm[:, 8:12], in0=sm[:, 8:12], scalar1=1.0, scalar2=None, op0=ALU.add), reads=[smB], writes=[smB])
            op(DVE, lambda e: e.reciprocal(out=sm[:, 8:12], in_=sm[:, 8:12]), reads=[smB], writes=[smB])
            op(DVE, lambda e: e.tensor_tensor(out=sm[:, 12:16], in0=sm[:, 4:8], in1=cs("dtb"), op=ALU.add), reads=[smB, cpB], writes=[smB])
            op(ACT, lambda e: e.activation(out=sm[:, 12:16], in_=sm[:, 12:16], func=AF.Exp), reads=[smB], writes=[smB])
            op(ACT, lambda e: e.activation(out=sm[:, 12:16], in_=sm[:, 12:16], func=AF.Ln, bias=1.0), reads=[smB], writes=[smB])
            op(DVE, lambda e: e.tensor_tensor(out=sm[:, 12:16], in0=sm[:, 12:16], in1=self.negA[:, :], op=ALU.mult), reads=[smB, self.negAB], writes=[smB])
            op(PE, lambda e: e.matmul(psum[2][:, 8:12], lhsT=triU, rhs=sm[:, 12:16], start=True, stop=True), reads=[smB, cpB], writes=[psB[2][0]])
            op(DVE, lambda e: e.tensor_copy(out=sm[:, 16:20], in_=psum[2][:, 8:12]), reads=[psB[2][0]], writes=[smB])
            op(ACT, lambda e: e.activation(out=sm[:, 20:24], in_=sm[:, 16:20], func=AF.Exp), reads=[smB], writes=[smB])
            op(DVE, lambda e: e.tensor_scalar(out=sm[:, 24:28], in0=sm[:, 16:20], scalar1=-1.0, scalar2=None, op0=ALU.mult), reads=[smB], writes=[smB])
            op(DVE, lambda e: e.tensor_tensor(out=sm[:, 28:32], in0=sm[:, 8:12], in1=sm[:, 20:24], op=ALU.mult), reads=[smB], writes=[smB])
            for hh in range(4):
                op(DVE, (lambda hh: lambda e: e.tensor_scalar(out=Dg[:, hh * 128:(hh + 1) * 128], in0=ident, scalar1=sm[:, 16 + hh:17 + hh], scalar2=None, op0=ALU.mult))(hh),
                   reads=[smB, cpB], writes=[DgB])
            op(PE, lambda e: e.matmul(psum[3][:, :], lhsT=ones, rhs=Dg[:, 0:512], start=True, stop=True), reads=[DgB, cpB], writes=psB[3])
            Gv = psum[3][:, :].rearrange("p (a b) -> p a b", a=4)
            op(ACT, lambda e: e.activation(out=glb[:, 0:8].rearrange("p (a b) -> p a b", a=4), in_=Gv[:, :, 63:128:64], func=AF.Exp), reads=psB[3], writes=[glB])
            op(DVE, lambda e: e.tensor_tensor(out=sm[0:64, 32:36], in0=Gv[0:64, :, 63], in1=sm[0:64, 16:20], op=ALU.subtract), reads=psB[3] + [smB], writes=[smB])
            op(DVE, lambda e: e.tensor_tensor(out=sm[64:128, 32:36], in0=Gv[64:128, :, 127], in1=sm[64:128, 16:20], op=ALU.subtract), reads=psB[3] + [smB], writes=[smB])
            op(ACT, lambda e: e.activation(out=sm[:, 32:36], in_=sm[:, 32:36], func=AF.Exp), reads=[smB], writes=[smB])
            i4 = len(S.capture)
            pcv = pc[:, :].rearrange("p (a b) -> p a b", a=12)
            for j in range(0 if full else 4, 12):
                op(DVE, (lambda j: lambda e: e.tensor_scalar(out=cacc[:, j * 128:(j + 1) * 128], in0=pc[:, j * 131:j * 131 + 128], scalar1=cwg[:, j * 4:j * 4 + 1], scalar2=None, op0=ALU.mult))(j),
                   reads=[pcB, cpB], writes=[caccB])
                for k in range(1, 4):
                    op(DVE, (lambda j, k: lambda e: e.scalar_tensor_tensor(out=cacc[:, j * 128:(j + 1) * 128], in0=pc[:, j * 131 + k:j * 131 + k + 128], scalar=cwg[:, j * 4 + k:j * 4 + k + 1], in1=cacc[:, j * 128:(j + 1) * 128], op0=ALU.mult, op1=ALU.add))(j, k),
                       reads=[pcB, cpB, caccB], writes=[caccB])
            op(DVE, lambda e: e.tensor_copy(out=pcv[:, :, 0:3], in_=pcv[:, :, 128:131]), reads=[pcB], writes=[pcB])
            i5 = len(S.capture)
            c_lo = 0 if full else 512
            op(ACT, lambda e: e.activation(out=etmp[:, c_lo:1536], in_=cacc[:, c_lo:1536], func=AF.Exp, scale=-1.0), reads=[caccB], writes=[etmpB])
            op(ACT, lambda e: e.activation(out=etmp[:, c_lo:1536], in_=etmp[:, c_lo:1536], func=AF.Ln, bias=1.0), reads=[etmpB], writes=[etmpB])
            op(ACT, lambda e: e.activation(out=etmp[:, c_lo:1536], in_=etmp[:, c_lo:1536], func=AF.Exp, scale=-1.0), reads=[etmpB], writes=[etmpB])
            op(DVE, lambda e: e.tensor_tensor(out=qkv[:, c_lo:1536], in0=cacc[:, c_lo:1536], in1=etmp[:, c_lo:1536], op=ALU.mult), reads=[etmpB, caccB], writes=[qkvB])
            op(ACT, lambda e: e.activation(out=etmp[:, c_lo:1024], in_=qkv[:, c_lo:1024], func=AF.Square), reads=[qkvB, etmpB], writes=[etmpB])
            for half in range(0 if full else 1, 2):
                op(PE, (lambda half: lambda e: e.matmul(psum[half][:, :], lhsT=ones, rhs=etmp[:, half * 512:(half + 1) * 512], start=True, stop=True))(half),
                   reads=[etmpB, cpB], writes=psB[half])
                op(ACT, (lambda half: lambda e: e.activation(out=rs[:, half * 512:(half + 1) * 512], in_=psum[half][:, :], func=AF.Ln, bias=cs("eps")))(half),
                   reads=psB[half] + [cpB], writes=[rsB])
            op(ACT, lambda e: e.activation(out=rs[:, c_lo:1024], in_=rs[:, c_lo:1024], func=AF.Exp, scale=-0.5), reads=[rsB], writes=[rsB])
            if full:
                op(DVE, lambda e: e.scalar_tensor_tensor(out=qkv[:, 0:512], in0=qkv[:, 0:512], scalar=128.0 ** -0.5, in1=rs[:, 0:512], op0=ALU.mult, op1=ALU.mult), reads=[qkvB, rsB], writes=[qkvB])
            op(DVE, lambda e: e.tensor_tensor(out=qkv[:, 512:1024], in0=qkv[:, 512:1024], in1=rs[:, 512:1024], op=ALU.mult), reads=[qkvB, rsB], writes=[qkvB])
            if full:
                op(ACT, lambda e: e.activation(out=etmp[:, 0:512], in_=zT[:, :], func=AF.Exp, scale=-1.0), reads=[zB, etmpB], writes=[etmpB])
                op(ACT, lambda e: e.activation(out=etmp[:, 0:512], in_=etmp[:, 0:512], func=AF.Ln, bias=1.0), reads=[etmpB], writes=[etmpB])
                op(ACT, lambda e: e.activation(out=etmp[:, 0:512], in_=etmp[:, 0:512], func=AF.Exp, scale=-1.0), reads=[etmpB], writes=[etmpB])
                op(DVE, lambda e: e.tensor_tensor(out=zT[:, :], in0=zT[:, :], in1=etmp[:, 0:512], op=ALU.mult), reads=[etmpB, zB], writes=[zB])
            cap_ = S.capture
            S.capture = None
            s3, s4 = cap_[i3:i4], cap_[i4:i5]
            mer = []
            i_, j_ = 0, 0
            while i_ < len(s3) or j_ < len(s4):
                if j_ < len(s4) and (i_ >= len(s3) or j_ * max(len(s3), 1) <= i_ * len(s4)):
                    mer.append(s4[j_]); j_ += 1
                else:
                    mer.append(s3[i_]); i_ += 1
            return cap_[:i3] + mer + cap_[i5:]

        def chain(hh, t):
            pp = t % 2
            qkv, qkvB = qkv_b[pp], qkvB_b[pp]
            zT, zB = zT_b[pp], zB_b[pp]
            sm, smB = sm_b[pp], smB_b[pp]
            glb, glB = glb_b[pp], glB_b[pp]
            B_ = HB[hh]
            kbg, kbgB = B_["kbg"]; kdec, kdecB = B_["kdec"]; vbeta, vbetaB = B_["vbeta"]
            tm1, tm1B = B_["tm1"]; E1, E1B = B_["E1"]; Lm, LmB = B_["Lm"]; AT, ATB = B_["AT"]
            X = [B_["X0"], B_["X1"]]; Y = [B_["Y0"], B_["Y1"]]; Pm = [B_["P0"], B_["P1"]]
            wT, wTB = B_["wT"]; um, umB = B_["um"]; vnew, vnewB = B_["vnew"]
            o1s, o1sB = tm1, tm1B
            om, omB = E1, E1B
            on, onB = Lm, LmB
            qT = qkv[:, hh * 128:(hh + 1) * 128]
            kT = qkv[:, 512 + hh * 128:512 + (hh + 1) * 128]
            vT = qkv[:, 1024 + hh * 128:1024 + (hh + 1) * 128]
            PH = psum[4 + hh]
            PB = psB[4 + hh][0]
            Q0, Q1, Q2, Q3 = PH[:, 0:128], PH[:, 128:256], PH[:, 256:384], PH[:, 384:512]
            Gh = psum[3][:, hh * 128:(hh + 1) * 128]
            GhB = psB[3]
            op(PE, lambda e: e.transpose(Q0, kT, ident), reads=[qkvB, cpB], writes=[PB])
            op(PE, lambda e: e.transpose(Q1, vT, ident), reads=[qkvB, cpB], writes=[PB])
            op(PE, lambda e: e.matmul(Q2, lhsT=kT, rhs=kT, start=True, stop=True), reads=[qkvB], writes=[PB])
            if full:
                op(PE, lambda e: e.matmul(Q3, lhsT=kT, rhs=qT, start=True, stop=True), reads=[qkvB], writes=[PB])
            yield
            op(DVE, lambda e: e.tensor_tensor(out=tm1[:, :], in0=maskL, in1=Gh, op=ALU.subtract), reads=GhB + [cpB], writes=[tm1B])
            yield
            op(ACT, lambda e: e.activation(out=E1[:, :], in_=tm1[:, :], func=AF.Exp, bias=sm[:, 16 + hh:17 + hh]), reads=[tm1B, smB], writes=[E1B])
            yield
            op(ACT, lambda e: e.activation(out=kbg[:, :], in_=Q0, func=AF.Copy, scale=sm[:, 28 + hh:29 + hh]), reads=[PB, smB], writes=[kbgB])
            op(ACT, lambda e: e.activation(out=vbeta[:, :], in_=Q1, func=AF.Copy, scale=sm[:, 8 + hh:9 + hh]), reads=[PB, smB], writes=[vbetaB])
            yield
            op(DVE, lambda e: e.tensor_scalar(out=kdec[:, :], in0=Q0, scalar1=sm[:, 32 + hh:33 + hh], scalar2=None, op0=ALU.mult), reads=[PB, smB], writes=[kdecB])
            op(DVE, lambda e: e.scalar_tensor_tensor(out=Lm[:, :], in0=Q2, scalar=sm[:, 8 + hh:9 + hh], in1=E1[:, :], op0=ALU.mult, op1=ALU.mult), reads=[PB, smB, E1B], writes=[LmB])
            yield
            if full:
                op(DVE, lambda e: e.tensor_tensor(out=tm1[:, :], in0=maskU, in1=Gh, op=ALU.add), reads=GhB + [cpB, tm1B], writes=[tm1B])
                yield
                op(ACT, lambda e: e.activation(out=E1[:, :], in_=tm1[:, :], func=AF.Exp, bias=sm[:, 24 + hh:25 + hh]), reads=[tm1B, smB, E1B], writes=[E1B])
                yield
                op(DVE, lambda e: e.tensor_tensor(out=AT[:, :], in0=Q3, in1=E1[:, :], op=ALU.mult), reads=[PB, E1B], writes=[ATB])
                yield
            op(PE, lambda e: e.transpose(Q0, Lm[:, :], ident), reads=[LmB, cpB], writes=[PB])
            yield
            X0, X0B = X[0]
            P0, P0B = Pm[0]
            op(ACT, lambda e: e.activation(out=X0[:, :], in_=Q0, func=AF.Copy), reads=[PB], writes=[X0B])
            op(DVE, lambda e: e.tensor_tensor(out=P0[:, :], in0=ident, in1=Q0, op=ALU.subtract), reads=[PB, cpB], writes=[P0B])
            yield
            Xc, XcB = X0, X0B
            Yc, YcB = Lm, LmB
            Pc, PcB = P0, P0B
            for k in range(1, 6):
                Yn, YnB = Y[k % 2]
                Xn, XnB = X[k % 2]
                Pn, PnB = Pm[k % 2]
                op(PE, (lambda Xc, Yc: lambda e: e.matmul(Q1, lhsT=Xc[:, :], rhs=Yc[:, :], start=True, stop=True))(Xc, Yc), reads=[XcB, YcB], writes=[PB])
                if k < 5:
                    op(PE, (lambda Xc, Yc: lambda e: e.matmul(Q2, lhsT=Yc[:, :], rhs=Xc[:, :], start=True, stop=True))(Xc, Yc), reads=[XcB, YcB], writes=[PB])
                yield
                op(ACT, (lambda Yn: lambda e: e.activation(out=Yn[:, :], in_=Q1, func=AF.Copy))(Yn), reads=[PB], writes=[YnB])
                if k < 5:
                    op(ACT, (lambda Xn: lambda e: e.activation(out=Xn[:, :], in_=Q2, func=AF.Copy))(Xn), reads=[PB], writes=[XnB])
                yield
                op(PE, (lambda Yn, Pc: lambda e: e.matmul(Q3, lhsT=Yn[:, :], rhs=Pc[:, :], start=True, stop=True))(Yn, Pc), reads=[YnB, PcB], writes=[PB])
                yield
                op(DVE, (lambda Pn, Pc: lambda e: e.tensor_tensor(out=Pn[:, :], in0=Pc[:, :], in1=Q3, op=ALU.add))(Pn, Pc), reads=[PcB, PB], writes=[PnB])
                yield
                Xc, XcB, Yc, YcB, Pc, PcB = Xn, XnB, Yn, YnB, Pn, PnB
            op(PE, (lambda Pc: lambda e: e.matmul(Q0, lhsT=kbg[:, :], rhs=Pc[:, :], start=True, stop=True))(Pc), reads=[kbgB, PcB], writes=[PB])
            op(PE, (lambda Pc: lambda e: e.matmul(Q1, lhsT=Pc[:, :], rhs=vbeta[:, :], start=True, stop=True))(Pc), reads=[vbetaB, PcB], writes=[PB])
            yield
            op(ACT, lambda e: e.activation(out=wT[:, :], in_=Q0, func=AF.Copy), reads=[PB], writes=[wTB])
            op(ACT, lambda e: e.activation(out=um[:, :], in_=Q1, func=AF.Copy), reads=[PB], writes=[umB])
            yield
            for half in range(2):
                r0, r1 = half * 64, half * 64 + 64
                sp_ = self.spar_h[hh]
                Scur = Sst[sp_][:, hh * 128:(hh + 1) * 128]; ScurB = SsB[sp_][hh]
                Snew = Sst[1 - sp_][:, hh * 128:(hh + 1) * 128]; SnewB = SsB[1 - sp_][hh]
                self.spar_h[hh] = 1 - sp_
                op(PE, (lambda r0, r1, Scur: lambda e: e.matmul(PH[r0:r1, 256:384], lhsT=wT[:, r0:r1], rhs=Scur, start=True, stop=True))(r0, r1, Scur), reads=[wTB, ScurB], writes=[PB])
                yield
                op(DVE, (lambda r0, r1: lambda e: e.tensor_tensor(out=vnew[r0:r1, :], in0=um[r0:r1, :], in1=PH[r0:r1, 256:384], op=ALU.subtract))(r0, r1), reads=[umB, PB], writes=[vnewB])
                yield
                if full:
                    op(PE, (lambda r0, r1, Scur: lambda e: e.matmul(PH[r0:r1, 0:128], lhsT=qT[:, r0:r1], rhs=Scur, start=True, stop=True))(r0, r1, Scur), reads=[qkvB, ScurB], writes=[PB])
                    op(PE, (lambda r0, r1: lambda e: e.matmul(PH[r0:r1, 128:256], lhsT=AT[r0:r1, r0:r1], rhs=vnew[r0:r1, :], start=True, stop=True))(r0, r1), reads=[ATB, vnewB], writes=[PB])
                op(PE, (lambda r0, r1: lambda e: e.matmul(Q3, lhsT=kdec[r0:r1, :], rhs=vnew[r0:r1, :], start=True, stop=True))(r0, r1), reads=[kdecB, vnewB], writes=[PB])
                yield
                op(DVE, (lambda Snew, Scur, half: lambda e: e.scalar_tensor_tensor(out=Snew, in0=Scur, scalar=glb[:, hh * 2 + half:hh * 2 + half + 1], in1=Q3, op0=ALU.mult, op1=ALU.add))(Snew, Scur, half),
                   reads=[ScurB, glB, PB], writes=[SnewB])
                yield
            if full:
                c0 = 40 + hh * 3
                op(ACT, lambda e: e.activation(out=o1s[:, :], in_=Q0, func=AF.Copy, scale=sm[:, 20 + hh:21 + hh]), reads=[PB, smB], writes=[o1sB])
                yield
                op(DVE, lambda e: e.tensor_tensor(out=om[:, :], in0=o1s[:, :], in1=Q1, op=ALU.add), reads=[o1sB, PB], writes=[omB])
                yield
                op(ACT, lambda e: e.activation(out=on[:, :], in_=om[:, :], func=AF.Square, accum_out=sm[:, c0:c0 + 1]), reads=[omB], writes=[onB, smB])
                yield
                op(POOL, lambda e: e.tensor_scalar(out=sm[:, c0 + 1:c0 + 2], in0=sm[:, c0:c0 + 1], scalar1=1.0 / 128, scalar2=EPS, op0=ALU.mult, op1=ALU.add), reads=[smB], writes=[smB])
                op(POOL, lambda e: e.tensor_tensor(out=sm[:, c0 + 2:c0 + 3], in0=sm[:, c0 + 1:c0 + 2], in1=cs("mhalf"), op=ALU.pow), reads=[smB, cpB], writes=[smB])
                yield
                op(DVE, lambda e: e.scalar_tensor_tensor(out=on[:, :], in0=om[:, :], scalar=sm[:, c0 + 2:c0 + 3], in1=cs("gon"), op0=ALU.mult, op1=ALU.mult), reads=[omB, smB, cpB], writes=[onB])
                yield
                op(PE, lambda e: e.transpose(Q2, on[:, :], ident), reads=[onB, cpB], writes=[PB])
                yield
                yg_ap = self.ygT[:, hh * T + t * 128: hh * T + (t + 1) * 128]
                op(DVE, lambda e: e.tensor_tensor(out=yg_ap, in0=Q2, in1=zT[:, hh * 128:(hh + 1) * 128], op=ALU.mult), reads=[PB, zB], writes=[self.ygB[t]])
                yield

        for it in pre_ops(0):
            S.replay(it)
        for t in range(NT):
            pend = pre_ops(t + 1) if t + 1 < NT else []
            per = (len(pend) + 39) // 40
            S0 = int(os.environ.get("KSTAG", 4))
            per = (len(pend) + 39 + 3 * S0) // (40 + 3 * S0)
            alive = [(hh, chain(hh, t)) for hh in range(4)]
            pi = 0
            rnd = 0
            while alive or pi < len(pend):
                nxt = []
                for hh, g_ in alive:
                    if rnd < hh * S0:
                        nxt.append((hh, g_))
                        continue
                    try:
                        next(g_)
                        nxt.append((hh, g_))
                    except StopIteration:
                        pass
                alive = nxt
                for it in pend[pi:pi + per]:
                    S.replay(it)
                pi += per
                rnd += 1
        op(DVE, lambda e: e.tensor_copy(out=pchv, in_=pcv0[:, :, 0:3]), reads=[pcB], writes=[self.pchB])
        es.close()

    def conv_phase(self, wm_in, wm_out, hhalo, hhB):
        import os
        nc, S = self.nc, self.S
        op = S.op
        cs, cpB = self.cs, self.cpB
        h, hB = self.h, self.hB
        psum, psB = self.psum, self.psB
        es = contextlib.ExitStack()
        tag = "cv"
        wc = self.sb("wc", 8 * 1536, BF16, es); wcB = [Buf() for _ in range(6)]
        wo = self.sb("wo", 8 * 1024, BF16, es); woB = [Buf() for _ in range(4)]
        xs = self.sb("xs_cv", D, F32, es); xsB = Buf()
        self.junk = self.sb("junk_cv", D, BF16, es); self.junkB = Buf()
        xn = self.sb("xn_cv", 8 * 128, BF16, es); xnB = Buf()
        mpc = self.sb("mpc", 4 * 130, F32, es); mpcB = Buf()
        mpc2 = self.sb("mpc2", 4 * 130, F32, es); mpc2B = Buf()
        cbs0 = self.sb("cbs0", 512, F32, es); cbs0B = Buf()
        cbs1 = self.sb("cbs1", 512, F32, es); cbs1B = Buf()
        cct = self.sb("cct", 512, F32, es); cctB = Buf()
        cacc = self.sb("cacc_cv", 512, F32, es); caccB = Buf()
        yv = self.sb("yv", 512, F32, es); yvB = Buf()
        sq = self.sb("sq_cv", 512, F32, es); sqB = Buf()
        rs = self.sb("rs_cv", 512, F32, es); rsB = Buf()
        ycT = self.sb("ycT", 512, BF16, es); ycB = Buf()
        motmp = [self.sb("motmp%d" % i, 512, F32, es) for i in range(2)]; motB = [Buf(), Buf()]
        new_bufs = wcB + woB + [mpc2B, cbs0B, cbs1B, xsB, self.junkB, xnB, mpcB, cctB, caccB, yvB, sqB, rsB, ycB] + motB
        S.alias(new_bufs, getattr(self, "phase_bufs", []))
        self.phase_bufs = new_bufs
        wm_v = wm_in.rearrange("(c p) n -> p c n", p=128)
        for col in range(0, 1536, 256):
            base = (col // 256) * 2048
            self.load_cast(wc[:, base:base + 2048], wcB[col // 256], wm_v[:, :, col:col + 256], (8, 256))
        wo_v = wm_out.rearrange("(c p) n -> p c n", p=128)
        for i in range(4):
            self.load_cast(wo[:, i * 2048:(i + 1) * 2048], woB[i], wo_v[:, 2 * i:2 * i + 2, :], (2, 1024))
        op(POOL, lambda e: e.memset(mpc[:, :], 0.0), writes=[mpcB])
        op(POOL, lambda e: e.memset(mpc2[:, :], 0.0), writes=[mpc2B])
        ident, blk64 = cs("ident"), cs("blk64")
        csw, cgain = cs("csw"), cs("cgain")
        ygT, ygB = self.ygT, self.ygB
        mpc_b = [mpc, mpc2]; mpcB_b = [mpcB, mpc2B]
        cbs_b = [cbs0, cbs1]; cbsB_b = [cbs0B, cbs1B]

        def capA(t):
            p = t % 2
            mp, mpB = mpc_b[p], mpcB_b[p]
            mo, moB = mpc_b[1 - p], mpcB_b[1 - p]
            mpv = mp[:, :].rearrange("p (a b) -> p a b", a=4)
            mov = mo[:, :].rearrange("p (a b) -> p a b", a=4)
            S.capture = []
            if t < 0:
                src, srcB = hhalo[:, :], hhB
            else:
                src, srcB = h[:, t * D:(t + 1) * D], hB[t]
            self.norm_transpose(src, srcB, "nm", xn, 0, 128, xnB, xs, xsB, (0, 1))
            for grp in range(3):
                if t < 0 and grp == 0:
                    continue
                pb = 2 + grp
                for q in range(4):
                    j = grp * 4 + q
                    for c in range(8):
                        lhsT = wc[:, (j // 2) * 2048 + c * 256 + (j % 2) * 128: (j // 2) * 2048 + c * 256 + (j % 2) * 128 + 128]
                        rhs = xn[:, c * 128:(c + 1) * 128]
                        op(PE, (lambda pb, q, lhsT, rhs, c: lambda e: e.matmul(psum[pb][:, q * 128:(q + 1) * 128], lhsT=lhsT, rhs=rhs, start=(c == 0), stop=(c == 7)))(pb, q, lhsT, rhs, c),
                           reads=[wcB[j // 2], xnB], writes=[psB[pb][q]])
                if grp == 0:
                    op(ACT, (lambda p: lambda e: e.activation(out=cbs_b[p][:, :], in_=psum[2][:, :], func=AF.Copy))(p), reads=psB[2], writes=[cbsB_b[p]])
                if grp == 1:
                    op(ACT, lambda e: e.activation(out=cct[:, :], in_=psum[3][:, :], func=AF.Copy), reads=psB[3], writes=[cctB])
            op(DVE, lambda e: e.tensor_tensor(out=mpv[:, :, 2:130], in0=cct[:, :].rearrange("p (a b) -> p a b", a=4), in1=psum[4][:, :].rearrange("p (a b) -> p a b", a=4), op=ALU.mult),
               reads=[cctB, mpB] + psB[4], writes=[mpB])
            op(DVE, lambda e: e.tensor_copy(out=mpv[:, :, 0:2], in_=mov[:, :, 128:130]), reads=[moB, mpB], writes=[mpB])
            ops_ = S.capture
            S.capture = None
            return ops_

        def capB(t):
            p = t % 2
            mp, mpB = mpc_b[p], mpcB_b[p]
            cbs, cbsB = cbs_b[p], cbsB_b[p]
            S.capture = []
            for j in range(4):
                op(DVE, (lambda j: lambda e: e.tensor_scalar(out=cacc[:, j * 128:(j + 1) * 128], in0=mp[:, j * 130:j * 130 + 128], scalar1=csw[:, j * 3:j * 3 + 1], scalar2=None, op0=ALU.mult))(j),
                   reads=[mpB, cpB], writes=[caccB])
                for k in range(1, 3):
                    op(DVE, (lambda j, k: lambda e: e.scalar_tensor_tensor(out=cacc[:, j * 128:(j + 1) * 128], in0=mp[:, j * 130 + k:j * 130 + k + 128], scalar=csw[:, j * 3 + k:j * 3 + k + 1], in1=cacc[:, j * 128:(j + 1) * 128], op0=ALU.mult, op1=ALU.add))(j, k),
                       reads=[mpB, cpB, caccB], writes=[caccB])
            op(DVE, lambda e: e.tensor_tensor(out=yv[:, :], in0=cacc[:, :], in1=cbs[:, :], op=ALU.mult), reads=[caccB, cbsB], writes=[yvB])
            op(ACT, lambda e: e.activation(out=sq[:, :], in_=yv[:, :], func=AF.Square), reads=[yvB], writes=[sqB])
            op(PE, lambda e: e.matmul(psum[5][:, :], lhsT=blk64, rhs=sq[:, :], start=True, stop=True), reads=[sqB, cpB], writes=psB[5])
            op(ACT, lambda e: e.activation(out=rs[:, :], in_=psum[5][:, :], func=AF.Ln, bias=cs("eps")), reads=psB[5] + [cpB], writes=[rsB])
            op(ACT, lambda e: e.activation(out=rs[:, :], in_=rs[:, :], func=AF.Exp, scale=-0.5), reads=[rsB], writes=[rsB])
            for j in range(4):
                op(DVE, (lambda j: lambda e: e.scalar_tensor_tensor(out=ycT[:, j * 128:(j + 1) * 128], in0=yv[:, j * 128:(j + 1) * 128], scalar=cgain[:, j:j + 1], in1=rs[:, j * 128:(j + 1) * 128], op0=ALU.mult, op1=ALU.mult))(j),
                   reads=[yvB, rsB, cpB], writes=[ycB])
            for hh in range(2):
                pb = 6 + hh
                for j in range(8):
                    if j < 4:
                        lhsT = ycT[:, j * 128:(j + 1) * 128]
                        rd = [ycB]
                    else:
                        lhsT = ygT[:, (j - 4) * T + t * 128:(j - 4) * T + (t + 1) * 128]
                        rd = [ygB[t]]
                    rhs = wo[:, j * 1024 + hh * 512: j * 1024 + (hh + 1) * 512]
                    op(PE, (lambda pb, lhsT, rhs, j: lambda e: e.matmul(psum[pb][:, :], lhsT=lhsT, rhs=rhs, start=(j == 0), stop=(j == 7)))(pb, lhsT, rhs, j),
                       reads=rd + [woB[j // 2]], writes=psB[pb])
                hap = h[:, t * D + hh * 512: t * D + (hh + 1) * 512]
                op(ACT, (lambda pb, hh: lambda e: e.activation(out=motmp[hh][:, :], in_=psum[pb][:, :], func=AF.Copy))(pb, hh), reads=psB[pb], writes=[motB[hh]])
                op(DVE, (lambda hh, hap: lambda e: e.tensor_tensor(out=hap, in0=hap, in1=motmp[hh][:, :], op=ALU.add))(hh, hap), reads=[motB[hh], hB[t]], writes=[hB[t]])
            ops_ = S.capture
            S.capture = None
            return ops_

        for it in capA(-1):
            S.replay(it)
        for it in capA(0):
            S.replay(it)
        for t in range(NT):
            A = capA(t + 1) if t + 1 < NT else []
            B_ = capB(t)
            na, nb = len(A), len(B_)
            ia = ib = 0
            while ia < na or ib < nb:
                if ib < nb and (ia >= na or ib * max(na, 1) <= ia * nb):
                    S.replay(B_[ib]); ib += 1
                else:
                    S.replay(A[ia]); ia += 1
        es.close()

    def final(self, out, fn_bc):
        S = self.S
        op = S.op
        cs, cpB = self.cs, self.cpB
        h, hB = self.h, self.hB
        es = contextlib.ExitStack()
        ot = [self.sb("ot%d" % i, D, F32, es) for i in range(2)]
        otB = [Buf(), Buf()]
        fs = self.sb("fs", 64, F32, es); fsB = Buf()
        junk = self.sb("junk_f", D, BF16, es); junkB = Buf()
        fnb = self.sb("fnb", D, F32, es); fnbB = Buf()
        new_bufs = otB + [fsB, junkB, fnbB]
        S.alias(new_bufs, getattr(self, "phase_bufs", []))
        self.phase_bufs = new_bufs
        S.dma(SP, lambda e: e.dma_start(out=fnb[:], in_=fn_bc), "const2", S.new_group(), writes=[fnbB])
        import os
        if os.environ.get("KRAWOUT"):
            for t in range(NT):
                S.dma(SP, (lambda t: lambda e: e.dma_start(out=out[t * 128:(t + 1) * 128, :], in_=h[:, t * D:(t + 1) * D]))(t), "out%d" % (t % 2), S.new_group(), reads=[hB[t]])
            es.close()
            return
        for t in range(NT):
            k = t % 2
            c0 = (t % 16) * 3
            hs = h[:, t * D:(t + 1) * D]
            op(ACT, (lambda hs, c0: lambda e: e.activation(out=junk[:, :], in_=hs, func=AF.Square, accum_out=fs[:, c0:c0 + 1]))(hs, c0), reads=[hB[t]], writes=[junkB, fsB])
            op(POOL, (lambda c0: lambda e: e.tensor_scalar(out=fs[:, c0 + 1:c0 + 2], in0=fs[:, c0:c0 + 1], scalar1=1.0 / D, scalar2=EPS, op0=ALU.mult, op1=ALU.add))(c0), reads=[fsB], writes=[fsB])
            op(POOL, (lambda c0: lambda e: e.tensor_tensor(out=fs[:, c0 + 2:c0 + 3], in0=fs[:, c0 + 1:c0 + 2], in1=cs("mhalf"), op=ALU.pow))(c0), reads=[fsB, cpB], writes=[fsB])
            op(DVE, (lambda hs, c0, k: lambda e: e.scalar_tensor_tensor(out=ot[k][:, :], in0=hs, scalar=fs[:, c0 + 2:c0 + 3], in1=fnb[:, :], op0=ALU.mult, op1=ALU.mult))(hs, c0, k),
               reads=[hB[t], fsB, fnbB], writes=[otB[k]])
            S.dma(SP, (lambda t, k: lambda e: e.dma_start(out=out[t * 128:(t + 1) * 128, :], in_=ot[k][:, :]))(t, k), "out%d" % k, S.new_group(), reads=[otB[k]])
        es.close()


def _pack_layout():
    names = [("ident", 128), ("ones", 128), ("triU", 128), ("maskL", 128), ("maskU", 128), ("blk64", 128),
             ("gon", 128), ("n1", 8), ("nm", 8), ("n2", 8), ("cwg", 48), ("csw", 12), ("cgain", 4),
             ("alog", 4), ("dtb", 4), ("mhalf", 1), ("eps", 1)]
    lay = {}
    off = 0
    for n, w in names:
        lay[n] = (off, off + w)
        off += w
    return lay, off


_CP, _CPK_COLS = _pack_layout()
Builder.CP = _CP
Builder.CPK_COLS = _CPK_COLS


def _pack_consts(inp):
    f = np.float32
    cp = np.zeros((128, _CPK_COLS), f)

    def put(name, arr):
        a, b = _CP[name]
        cp[:, a:b] = np.asarray(arr, f).reshape(128, b - a)

    idx = np.arange(128)
    same = (idx[:, None] // 64) == (idx[None, :] // 64)
    put("ident", np.eye(128))
    put("ones", np.ones((128, 128)))
    put("triU", (same & (idx[:, None] <= idx[None, :])))
    put("maskL", np.where(same & (idx[:, None] > idx[None, :]), 0.0, NEG))
    put("maskU", np.where(same & (idx[:, None] <= idx[None, :]), 0.0, NEG))
    put("blk64", same.astype(f) / 64.0)
    put("gon", np.broadcast_to(inp["gdn_out_norm"].reshape(1, 128), (128, 128)))
    put("n1", inp["ffn1_norm"].reshape(8, 128).T)
    put("nm", inp["mix_norm"].reshape(8, 128).T)
    put("n2", inp["ffn2_norm"].reshape(8, 128).T)
    put("cwg", inp["gdn_conv_w"].reshape(4, 12, 128).transpose(2, 1, 0).reshape(128, 48))
    put("csw", inp["conv_short_w"].reshape(3, 4, 128).transpose(2, 1, 0).reshape(128, 12))
    put("cgain", inp["conv_out_norm"].reshape(4, 128).T)
    put("alog", np.broadcast_to(inp["gdn_A_log"].reshape(1, 4), (128, 4)))
    put("dtb", np.broadcast_to(inp["gdn_dt_bias"].reshape(1, 4), (128, 4)))
    put("mhalf", np.full((128, 1), -0.5))
    put("eps", np.full((128, 1), EPS))
    return cp


_NC_CACHE = {}


def _get_nc(debug=False):
    if debug not in _NC_CACHE:
        b = Builder(debug=debug)
        b.spar_h = [0, 0, 0, 0]
        _NC_CACHE[debug] = (b.build(), b)
    return _NC_CACHE[debug]


def kernel(debug=False, **inputs):
    inp = {k: np.asarray(v) for k, v in inputs.items()}
    x = inp["x"].astype(np.float32, copy=False)
    nc, b = _get_nc(debug)
    cp = _pack_consts(inp)
    fn_bc = np.ascontiguousarray(np.broadcast_to(inp["final_norm"].reshape(1, D).astype(np.float32), (128, D)))
    shared = {
        "w1_in": np.ascontiguousarray(inp["ffn1_w_in"][0]), "w1_out": np.ascontiguousarray(inp["ffn1_w_out"][0]),
        "w2_in": np.ascontiguousarray(inp["ffn2_w_in"][0]), "w2_out": np.ascontiguousarray(inp["ffn2_w_out"][0]),
        "wm_in": np.ascontiguousarray(inp["w_mix_in"][0]), "wm_out": np.ascontiguousarray(inp["w_mix_out"][0]),
        "cpk": cp, "fn_bc": fn_bc,
    }
    zeros = np.zeros((T, D), np.float32)
    in_maps = []
    for c in range(8):
        bi, half = c // 2, c % 2
        m = dict(shared)
        m["x_own"] = np.ascontiguousarray(x[bi, half * T:(half + 1) * T])
        m["x_pre"] = zeros if half == 0 else np.ascontiguousarray(x[bi, 0:T])
        in_maps.append(m)
    import os
    ncores = int(os.environ.get("KCORES", 8))
    res = run_bass_kernel_spmd(nc, in_maps[:ncores], core_ids=list(range(ncores)))
    outp = np.zeros((4, 2 * T, D), np.float32)
    for c in range(ncores):
        outp[c // 2, (c % 2) * T:(c % 2 + 1) * T] = res.results[c]["out"]
    if debug:
        return outp, res.results
    return outp
```

```python
import contextlib
import numpy as np
import concourse.bass as bass
import concourse.mybir as mybir
from concourse.bass_utils import run_bass_kernel_spmd

F32 = mybir.dt.float32
BF16 = mybir.dt.bfloat16
AF = mybir.ActivationFunctionType
ALU = mybir.AluOpType

PE, ACT, DVE, POOL, SP = "pe", "act", "dve", "pool", "sp"
COMPUTE = (PE, ACT, DVE, POOL)

D = 1024
DFF = 2816
T = 2048
NT = T // 128
NB = T // 512
EPS = 1e-6
GW0 = 1536
NG = 2056
NEG = -1.0e30


class Buf:
    __slots__ = ("name", "last_w", "readers", "excl")

    def __init__(self, name="", excl=False):
        self.name = name
        self.last_w = None
        self.readers = []
        self.excl = excl


class Op:
    __slots__ = ("eng", "fn", "deps", "needs_inc", "cnt", "is_dma", "key", "grp", "idx")

    def __init__(self, eng, fn, is_dma=False, key=None, grp=None):
        self.eng = eng
        self.fn = fn
        self.deps = []
        self.needs_inc = False
        self.cnt = 0
        self.is_dma = is_dma
        self.key = key
        self.grp = grp


class Sched:
    def __init__(self):
        self.ops = []
        self.grp_ctr = 0

    def new_group(self):
        self.grp_ctr += 1
        return self.grp_ctr

    def _add(self, op, reads, writes):
        if getattr(self, "capture", None) is not None:
            self.capture.append((op, list(reads), list(writes)))
            return op
        return self._add_real(op, reads, writes)

    def replay(self, item):
        return self._add_real(*item)

    def _add_real(self, op, reads, writes):
        op.idx = len(self.ops)
        ex = [b for b in reads if b.excl]
        if ex:
            reads = [b for b in reads if not b.excl]
            writes = list(writes) + ex
        deps = {}
        for b in reads:
            if b.last_w is not None:
                deps[id(b.last_w)] = b.last_w
        for b in writes:
            if b.last_w is not None:
                deps[id(b.last_w)] = b.last_w
            for r in b.readers:
                deps[id(r)] = r
        latest = {}
        for d in deps.values():
            if d is op:
                continue
            if (not d.is_dma) and (not op.is_dma) and d.eng == PE and op.eng == PE:
                continue
            if d.is_dma:
                op.deps.append(d)
            else:
                cur = latest.get(d.eng)
                if cur is None or d.idx > cur.idx:
                    latest[d.eng] = d
        op.deps.extend(latest.values())
        for b in reads:
            if op.is_dma:
                b.readers.append(op)
            else:
                b.readers = [r for r in b.readers if r.is_dma or r.eng != op.eng]
                b.readers.append(op)
        for b in writes:
            b.last_w = op
            b.readers = []
        self.ops.append(op)
        return op

    def op(self, eng, fn, reads=(), writes=()):
        return self._add(Op(eng, fn), reads, writes)

    def dma(self, queue, fn, key, grp, reads=(), writes=()):
        return self._add(Op(queue, fn, is_dma=True, key=key, grp=grp), reads, writes)

    def alias(self, new_bufs, old_bufs):
        acc = {}
        for b in old_bufs:
            if b.last_w is not None:
                acc[id(b.last_w)] = b.last_w
            for r in b.readers:
                acc[id(r)] = r
        for nb in new_bufs:
            nb.readers = list(acc.values())

    def emit(self, nc, final_wait_keys=()):
        ops = self.ops
        for o in ops:
            for d in o.deps:
                d.needs_inc = True
        cnt = {}
        grp_end = {}
        for o in ops:
            if o.is_dma:
                k = ("dma", o.key)
                cnt[k] = cnt.get(k, 0) + 1
                o.cnt = cnt[k]
                grp_end[(o.key, o.grp)] = o.cnt
            elif o.needs_inc:
                cnt[o.eng] = cnt.get(o.eng, 0) + 1
                o.cnt = cnt[o.eng]
        dma_keys = sorted({o.key for o in ops if o.is_dma})
        streams = {e: [o for o in ops if o.eng == e] for e in (PE, ACT, DVE, POOL, SP)}
        self.stats = {e: len(s) for e, s in streams.items()}
        self.stats["incs"] = dict(cnt)

        import os
        SEG = int(os.environ.get('KSEG', 1500))
        with contextlib.ExitStack() as es:
            sems = {}
            for e in COMPUTE:
                nseg = (cnt.get(e, 0) + SEG - 1) // SEG + 1
                sems[e] = [es.enter_context(nc.semaphore("s_%s_%d" % (e, j))) for j in range(nseg)]
            for k in dma_keys:
                sems[("dma", k)] = es.enter_context(nc.semaphore("d_" + str(k)))
            block = es.enter_context(nc.Block())

            def run_stream(engname, eng):
                waited = {}
                for o in streams[engname]:
                    for d in o.deps:
                        if d.is_dma:
                            sk = ("dma", d.key)
                            val = 16 * grp_end[(d.key, d.grp)]
                            sem = sems[sk]
                        else:
                            seg = (d.cnt - 1) // SEG
                            sk = (d.eng, seg)
                            val = (d.cnt - 1) % SEG + 1
                            sem = sems[d.eng][seg]
                            if any(k2[0] == d.eng and k2[1] > seg for k2 in waited if isinstance(k2, tuple) and k2[0] == d.eng):
                                continue
                        if waited.get(sk, 0) >= val:
                            continue
                        waited[sk] = val
                        eng.wait_ge(sem, val)
                    ins = o.fn(eng)
                    if o.is_dma:
                        ins.then_inc(sems[("dma", o.key)], 16)
                    elif o.needs_inc:
                        ins.then_inc(sems[o.eng][(o.cnt - 1) // SEG], 1)
                if engname == SP:
                    for k in final_wait_keys:
                        eng.wait_ge(sems[("dma", k)], 16 * cnt[("dma", k)])

            @block.sync
            def _(e):
                run_stream(SP, e)

            @block.tensor
            def _(e):
                run_stream(PE, e)

            @block.scalar
            def _(e):
                run_stream(ACT, e)

            @block.vector
            def _(e):
                run_stream(DVE, e)

            @block.gpsimd
            def _(e):
                run_stream(POOL, e)


class Builder:
    def __init__(self, debug=False):
        self.debug = debug
        self.nc = bass.Bass("TRN2", target_bir_lowering=False)
        self.S = Sched()
        self.es = contextlib.ExitStack()
        self.dbg_outs = []
        self.dbg_keys = []
        self.rr = 0

    def sb(self, name, cols, dt=F32, es=None):
        return (es or self.es).enter_context(self.nc.sbuf_tensor(name, [128, cols], dt))

    def dram_in(self, name, shape, dt=F32):
        return self.nc.dram_tensor(name, list(shape), dt, kind="ExternalInput").ap()

    def dram_out(self, name, shape, dt=F32):
        return self.nc.dram_tensor(name, list(shape), dt, kind="ExternalOutput").ap()

    def dbg(self, name, ap, cols, bufs, dt=F32):
        if not self.debug:
            return
        o = self.dram_out("dbg_" + name, [128, cols], dt)
        self.dbg_keys.append("dbg_" + name)
        self.S.dma(SP, lambda e: e.dma_start(out=o, in_=ap), "dbg_" + name, self.S.new_group(), reads=bufs)

    def ew(self):
        self.rr += 1
        return ACT if (self.rr & 1) else DVE

    def build(self):
        nc, S = self.nc, self.S
        op = S.op
        x_pre = self.dram_in("x_pre", [T, D])
        x_own = self.dram_in("x_own", [T, D])
        w1_in = self.dram_in("w1_in", [D, 2 * DFF])
        w1_out = self.dram_in("w1_out", [DFF, D])
        w2_in = self.dram_in("w2_in", [D, 2 * DFF])
        w2_out = self.dram_in("w2_out", [DFF, D])
        wm_in = self.dram_in("wm_in", [D, 3592])
        wm_out = self.dram_in("wm_out", [D, D])
        cpk = self.dram_in("cpk", [128, self.CPK_COLS])
        fn_bc = self.dram_in("fn_bc", [128, D])
        out = self.dram_out("out", [T, D])
        self.out_grp = S.new_group()

        h = self.sb("h", NT * D)
        hB = [Buf("h%d" % t) for t in range(NT)]
        stage = [self.sb("stage%d" % i, 2048) for i in range(2)]
        stB = [Buf("st%d" % i) for i in range(2)]
        self.stage, self.stB, self.st_i = stage, stB, 0
        import os
        self.cast_order = os.environ.get('KCAST', 'dve,act,pool,dve,act').split(',')
        cp = self.sb("cp", self.CPK_COLS)
        cpB = Buf("cp")
        hhalo = self.sb("hhalo", D)
        hhB = Buf("hhalo")
        ygT = self.sb("ygT", 4 * T, BF16)
        ygB = [Buf("yg%d" % t) for t in range(NT)]
        pch = self.sb("pch", 36)
        pchB = Buf("pch")
        Sst = [self.sb("Sst%d" % i, 4 * 128) for i in range(2)]
        SsB = [[Buf("S%d_%d" % (i, hh)) for hh in range(4)] for i in range(2)]
        stat = self.sb("stat", 64)
        negA = self.sb("negA", 4)
        negAB = Buf("negA")
        psum = [self.es.enter_context(nc.psum_tensor("ps%d" % i, [128, 512], F32)) for i in range(8)]
        psB = [[Buf("ps%d" % i, excl=True)] * 4 for i in range(8)]
        self.psum, self.psB = psum, psB
        self.h, self.hB = h, hB

        C = self.CP
        g0 = S.new_group()
        S.dma(SP, lambda e: e.dma_start(out=cp[:], in_=cpk), "const", g0, writes=[cpB])
        self.cp, self.cpB = cp, cpB

        def cs(name, n=None):
            a, b = C[name]
            return cp[:, a:b]

        self.cs = cs
        ident = cs("ident")
        op(POOL, lambda e: e.memset(Sst[0][:], 0.0), writes=SsB[0])
        op(POOL, lambda e: e.memset(pch[:], 0.0), writes=[pchB])
        op(ACT, lambda e: e.activation(out=negA[:], in_=cs("alog"), func=AF.Exp), reads=[cpB], writes=[negAB])
        op(DVE, lambda e: e.tensor_scalar(out=negA[:], in0=negA[:], scalar1=-1.0, scalar2=None, op0=ALU.mult),
           reads=[negAB], writes=[negAB])
        self.negA, self.negAB = negA, negAB
        self.pch, self.pchB = pch, pchB
        self.Sst, self.SsB = Sst, SsB
        self.ygT, self.ygB = ygT, ygB
        self.spar = 0
        self.stat = stat
        self.statB = Buf("stat")

        import os
        PH = set(os.environ.get("KPH", "pf,pg,of,og,cv,f2").split(","))
        if "pf" in PH:
            self.load_x(x_pre)
            self.ffn(w1_in, w1_out, "n1", tag="p1")
        op(POOL, lambda e: e.tensor_copy(out=hhalo[:], in_=h[:, (NT - 1) * D:NT * D]), reads=[hB[NT - 1]], writes=[hhB])
        if "pg" in PH:
            self.gdn_phase(wm_in, full=False, tag="pg")
        self.load_x(x_own)
        if "of" in PH:
            self.ffn(w1_in, w1_out, "n1", tag="o1")
        self.dbg("h1", h[:, 0:D], D, [hB[0]])
        if "og" in PH:
            self.gdn_phase(wm_in, full=True, tag="og")
        self.dbg("yg", self.ygT[:, 0:T], T, self.ygB, BF16)
        if "cv" in PH:
            self.conv_phase(wm_in, wm_out, hhalo, hhB)
        self.dbg("h2", h[:, 0:D], D, [hB[0]])
        if "f2" in PH:
            self.ffn(w2_in, w2_out, "n2", tag="o2")
        self.final(out, fn_bc)
        S.emit(nc, final_wait_keys=["out0", "out1"] + self.dbg_keys)
        self.es.close()
        return nc

    def load_x(self, xd):
        S, h, hB = self.S, self.h, self.hB
        g = S.new_group()
        for t in range(NT):
            S.dma(SP, (lambda t: lambda e: e.dma_start(out=h[:, t * D:(t + 1) * D], in_=xd[t * 128:(t + 1) * 128, :]))(t),
                  "x%d" % (t % 4), g, writes=[hB[t]])

    def load_dma(self, dst_ap, dstB, src_ap, shape3):
        S = self.S
        n = len(self.stage)
        i = self.st_i % n
        self.st_i += 1
        st, sB = self.stage[i], self.stB[i]
        a, b = shape3
        sview = st[:, 0:a * b].rearrange("p (a b) -> p a b", a=a) if a > 1 else st[:, 0:b]
        sflat = st[:, 0:a * b]
        g = S.new_group()
        S.dma(SP, lambda e: e.dma_start(out=sview, in_=src_ap), "st%d" % i, g, writes=[sB])
        return (dst_ap, dstB, sflat, sB)

    def load_cast_do(self, hnd):
        S = self.S
        dst_ap, dstB, sflat, sB = hnd
        self.cast_i = getattr(self, "cast_i", 0) + 1
        eng = self.cast_order[self.cast_i % len(self.cast_order)]
        if eng == ACT:
            S.op(ACT, lambda e: e.activation(out=dst_ap, in_=sflat, func=AF.Copy), reads=[sB], writes=[dstB])
        else:
            S.op(eng, lambda e: e.tensor_copy(out=dst_ap, in_=sflat), reads=[sB], writes=[dstB])

    def load_cast(self, dst_ap, dstB, src_ap, shape3=None, key="w"):
        self.load_cast_do(self.load_dma(dst_ap, dstB, src_ap, shape3))

    def norm_transpose(self, src_ap, srcB, gain_name, dst, dst_off, dst_stride, dstB, xs, xsB, pbanks):
        S, cs = self.S, self.cs
        op = S.op
        stat = self.stat
        stB = self.statB
        op(ACT, lambda e: e.activation(out=xs[:, 0:D], in_=src_ap, func=AF.Square, accum_out=stat[:, 0:1]),
           reads=[srcB], writes=[xsB, stB])
        op(POOL, lambda e: e.tensor_scalar(out=stat[:, 1:2], in0=stat[:, 0:1], scalar1=1.0 / D, scalar2=EPS,
                                           op0=ALU.mult, op1=ALU.add), reads=[stB], writes=[stB])
        op(POOL, lambda e: e.tensor_tensor(out=stat[:, 2:3], in0=stat[:, 1:2], in1=cs("mhalf"), op=ALU.pow),
           reads=[stB, self.cpB], writes=[stB])
        op(DVE, lambda e: e.tensor_scalar(out=xs[:, 0:D], in0=src_ap, scalar1=stat[:, 2:3], scalar2=None, op0=ALU.mult),
           reads=[srcB, stB], writes=[xsB])
        gain = cs(gain_name)
        ident = cs("ident")
        for half in range(2):
            pb = pbanks[half]
            pbuf = self.psum[pb]
            for q in range(4):
                c = half * 4 + q
                op(PE, (lambda c, q, pbuf: lambda e: e.transpose(pbuf[:, q * 128:(q + 1) * 128], xs[:, c * 128:(c + 1) * 128], ident))(c, q, pbuf),
                   reads=[xsB, self.cpB], writes=[self.psB[pb][q]])
            for q in range(4):
                c = half * 4 + q
                eng = self.ew()
                o_ap = dst[:, c * dst_stride + dst_off: c * dst_stride + dst_off + 128]
                i_ap = pbuf[:, q * 128:(q + 1) * 128]
                g_ap = gain[:, c:c + 1]
                if eng == ACT:
                    op(ACT, (lambda o_ap, i_ap, g_ap: lambda e: e.activation(out=o_ap, in_=i_ap, func=AF.Copy, scale=g_ap))(o_ap, i_ap, g_ap),
                       reads=[self.psB[pb][q], self.cpB], writes=[dstB])
                else:
                    op(DVE, (lambda o_ap, i_ap, g_ap: lambda e: e.tensor_scalar(out=o_ap, in0=i_ap, scalar1=g_ap, scalar2=None, op0=ALU.mult))(o_ap, i_ap, g_ap),
                       reads=[self.psB[pb][q], self.cpB], writes=[dstB])

    def ffn(self, w_in, w_out, gain_name, tag):
        import os
        nc, S = self.nc, self.S
        op = S.op
        h, hB = self.h, self.hB
        psum, psB = self.psum, self.psB
        es = contextlib.ExitStack()
        xnT = self.sb("xnT_" + tag, 8 * T, BF16, es)
        xnB = [Buf("xn%d" % t) for t in range(NT)]
        CPP = int(os.environ.get("KCPP", 4))
        nsub = CPP // 2
        wbi = [self.sb("wbi%d_%s" % (i, tag), 2 * nsub * 2048, BF16, es) for i in range(2)]
        wbo = [self.sb("wbo%d_%s" % (i, tag), CPP * 1024, BF16, es) for i in range(2)]
        assert CPP == 4
        wbiB = [[Buf() for _ in range(4)] for _ in range(2)]
        wboB = [[Buf() for _ in range(nsub)] for _ in range(2)]
        hid = [self.sb("hid%d_%s" % (i, tag), CPP * 512, BF16, es) for i in range(2)]
        hidB = [Buf(), Buf()]
        sg0 = self.sb("sg0_%s" % tag, 512, F32, es); sg = [sg0, sg0]
        sgB0 = Buf(); sgB = [sgB0, sgB0]
        xs = [self.sb("xs%d_%s" % (i, tag), D, F32, es) for i in range(2)]
        xsB = [Buf(), Buf()]
        ev = [xs[1][:, 0:512], xs[1][:, 512:1024]]
        evB = [xsB[1], xsB[1]]
        self.junkB = Buf()
        base_stage, base_stB = self.stage, self.stB
        nextra = int(os.environ.get("KXST", 0))
        xst = [self.sb("xst%d_%s" % (i, tag), 2048, F32, es) for i in range(nextra)]
        xstB = [Buf() for _ in range(nextra)]
        self.stage, self.stB = base_stage + xst, base_stB + xstB
        new_bufs = xnB + wbiB[0] + wbiB[1] + wboB[0] + wboB[1] + hidB + sgB + xsB + [self.junkB] + evB + xstB
        S.alias(new_bufs, getattr(self, "phase_bufs", []))
        self.phase_bufs = new_bufs

        w_in_v = w_in.rearrange("(c p) n -> p c n", p=128)
        w_out_v = w_out.rearrange("(c p) n -> p c n", p=128)
        pieces = []
        c0 = 0
        while c0 < DFF // 128:
            n = min(CPP, DFF // 128 - c0)
            pieces.append((c0, n))
            c0 += n
        NP = len(pieces)

        def piece_specs(p):
            s = p % 2
            ch0, n = pieces[p]
            W = n * 128
            col = ch0 * 128
            specs = []
            for which in range(2):
                for csub in range(2):
                    base = which * 4096 + csub * 2048
                    specs.append((wbi[s][:, base:base + 4 * W], wbiB[s][which * 2 + csub],
                                  w_in_v[:, csub * 4:csub * 4 + 4, which * DFF + col:which * DFF + col + W], (4, W)))
            for sub in range(n // 2):
                specs.append((wbo[s][:, sub * 2048:(sub + 1) * 2048], wboB[s][sub], w_out_v[:, ch0 + 2 * sub:ch0 + 2 * sub + 2, :], (2, 1024)))
            return specs

        def load_piece(p):
            for sp_ in piece_specs(p):
                self.load_cast(*sp_)

        load_piece(0)
        for t in range(NT):
            self.norm_transpose(h[:, t * D:(t + 1) * D], hB[t], gain_name, xnT, t * 128, T, xnB[t],
                                xs[t % 2], xsB[t % 2], [(0, 1), (2, 3), (4, 5), (6, 7)][t % 4])
        blocks = [(p, tb) for p in range(NP) for tb in range(NB)]
        st = {"gi": 0, "oi": 0}

        def stage1(idx):
            p, tb = blocks[idx]
            s = p % 2
            hs = idx % 2
            n = pieces[p][1]
            for j in range(n):
                gi = st["gi"]
                for which in range(2):
                    pb = (0 if which == 0 else 2) + (gi % 2)
                    W = n * 128
                    for c in range(8):
                        off = which * 4096 + (c // 4) * 2048 + (c % 4) * W + j * 128
                        lhsT = wbi[s][:, off:off + 128]
                        rhs = xnT[:, c * T + tb * 512: c * T + (tb + 1) * 512]
                        op(PE, (lambda pb, lhsT, rhs, c: lambda e: e.matmul(psum[pb][:, :], lhsT=lhsT, rhs=rhs, start=(c == 0), stop=(c == 7)))(pb, lhsT, rhs, c),
                           reads=[wbiB[s][which * 2 + c // 4]] + xnB[tb * 4:(tb + 1) * 4], writes=psB[pb])
                pg, pu = (gi % 2), 2 + (gi % 2)
                k = gi % 2
                op(ACT, (lambda pg, k: lambda e: e.activation(out=sg[k][:, :], in_=psum[pg][:, :], func=AF.Silu))(pg, k),
                   reads=psB[pg], writes=[sgB[k]])
                op(DVE, (lambda pu, k, hs, j: lambda e: e.tensor_tensor(out=hid[hs][:, j * 512:(j + 1) * 512], in0=sg[k][:, :], in1=psum[pu][:, :], op=ALU.mult))(pu, k, hs, j),
                   reads=[sgB[k]] + psB[pu], writes=[hidB[hs]])
                st["gi"] += 1

        def stage2(idx):
            p, tb = blocks[idx]
            s = p % 2
            hs = idx % 2
            n = pieces[p][1]
            for tt in range(4):
                t = tb * 4 + tt
                for hh in range(2):
                    oi = st["oi"]
                    pb = 4 + (oi % 2)
                    for j in range(n):
                        lhsT = hid[hs][:, j * 512 + tt * 128: j * 512 + (tt + 1) * 128]
                        rhs = wbo[s][:, j * 1024 + hh * 512: j * 1024 + (hh + 1) * 512]
                        op(PE, (lambda pb, lhsT, rhs, j: lambda e: e.matmul(psum[pb][:, :], lhsT=lhsT, rhs=rhs, start=(j == 0), stop=(j == n - 1)))(pb, lhsT, rhs, j),
                           reads=[hidB[hs], wboB[s][j // 2]], writes=psB[pb])
                    hap = h[:, t * D + hh * 512: t * D + (hh + 1) * 512]
                    if oi % 2 == 0:
                        op(DVE, (lambda pb, hap: lambda e: e.scalar_tensor_tensor(out=hap, in0=psum[pb][:, :], scalar=0.5, in1=hap, op0=ALU.mult, op1=ALU.add))(pb, hap),
                           reads=psB[pb] + [hB[t]], writes=[hB[t]])
                    else:
                        k = (oi // 2) % 2
                        op(ACT, (lambda pb, k: lambda e: e.activation(out=ev[k][:, :], in_=psum[pb][:, :], func=AF.Copy, scale=0.5))(pb, k),
                           reads=psB[pb], writes=[evB[k]])
                        op(POOL, (lambda k, hap: lambda e: e.tensor_tensor(out=hap, in0=hap, in1=ev[k][:, :], op=ALU.add))(k, hap),
                           reads=[evB[k], hB[t]], writes=[hB[t]])
                    st["oi"] += 1

        pend_specs = []
        inflight = []
        for idx in range(len(blocks)):
            stage1(idx)
            if idx > 0:
                stage2(idx - 1)
            p, tb = blocks[idx]
            if tb == 0 and p + 1 < NP:
                pend_specs = piece_specs(p + 1)
            for hnd in inflight:
                self.load_cast_do(hnd)
            inflight = []
            if tb == NB - 1:
                for sp_ in pend_specs:
                    self.load_cast(*sp_)
                pend_specs = []
            else:
                for sp_ in pend_specs[:2]:
                    inflight.append(self.load_dma(*sp_))
                pend_specs = pend_specs[2:]
        stage2(len(blocks) - 1)
        self.stage, self.stB = base_stage, base_stB
        es.close()

    def gdn_phase(self, wm_in, full, tag):
        import os
        nc, S = self.nc, self.S
        op = S.op
        cs, cpB = self.cs, self.cpB
        h, hB = self.h, self.hB
        psum, psB = self.psum, self.psB
        es = contextlib.ExitStack()
        pc = self.sb("pc_" + tag, 12 * 131, F32, es); pcB = Buf()
        wg = self.sb("wg_" + tag, 8 * NG, BF16, es)
        wgB = [Buf() for _ in range(9)]
        xs = self.sb("xs_" + tag, D, F32, es); xsB = Buf()
        xn = self.sb("xn_" + tag, 8 * 128, BF16, es); xnB = Buf()
        qkv0 = self.sb("qkv_" + tag, 12 * 128, F32, es); qkvB0 = Buf()
        qkv_b = [qkv0, self.stage[0][:, 0:1536]]; qkvB_b = [qkvB0, self.stB[0]]
        cacc = self.sb("cacc_" + tag, 12 * 128, F32, es); caccB = Buf()
        etmp = self.sb("etmp_" + tag, 12 * 128, F32, es); etmpB = Buf()
        rs, rsB = etmp, etmpB
        zT0 = self.sb("zT_" + tag, 4 * 128, F32, es); zB0 = Buf()
        zT_b = [zT0, self.stage[1][:, 0:512]]; zB_b = [zB0, self.stB[1]]
        sm_b = [self.sb("sm%d_%s" % (i, tag), 64, F32, es) for i in range(2)]; smB_b = [Buf(), Buf()]
        Dg = self.sb("Dg_" + tag, 512, F32, es); DgB = Buf()
        glb_b = [self.sb("glb%d_%s" % (i, tag), 8, F32, es) for i in range(2)]; glB_b = [Buf(), Buf()]
        def mk(n, cols=128, dt=F32):
            return self.sb(n + "_" + tag, cols, dt, es), Buf(n)
        HB = []
        for hh in range(4):
            d_ = {}
            for n in ["kbg", "kdec", "vbeta", "tm1", "E1", "Lm", "AT", "X0", "X1", "Y0", "Y1", "P0", "P1", "wT", "um", "vnew"]:
                d_[n] = mk("%s%d" % (n, hh))
            HB.append(d_)
        new_bufs = wgB + [pcB, xsB, xnB, qkvB0, caccB, etmpB, zB0, DgB] + smB_b + glB_b + [b for d_ in HB for _, b in d_.values()]
        S.alias(new_bufs, getattr(self, "phase_bufs", []))
        self.phase_bufs = new_bufs

        wm_v = wm_in.rearrange("(c p) n -> p c n", p=128)
        col = 0
        while col < NG:
            w = min(256, NG - col)
            base = (col // 256) * 2048
            self.load_cast(wg[:, base:base + 8 * w], wgB[col // 256], wm_v[:, :, GW0 + col:GW0 + col + w], (8, w))
            col += w

        ident, ones, triU = cs("ident"), cs("ones"), cs("triU")
        maskL, maskU = cs("maskL"), cs("maskU")
        cwg = cs("cwg")
        pcv0 = pc[:, :].rearrange("p (a b) -> p a b", a=12)
        pchv = self.pch[:, :].rearrange("p (a b) -> p a b", a=12)
        op(DVE, lambda e: e.tensor_copy(out=pcv0[:, :, 0:3], in_=pchv), reads=[self.pchB], writes=[pcB])
        Sst, SsB = self.Sst, self.SsB
        nq = 16 if full else 12

        def pre_ops(t):
            pp = t % 2
            qkv, qkvB = qkv_b[pp], qkvB_b[pp]
            zT, zB = zT_b[pp], zB_b[pp]
            sm, smB = sm_b[pp], smB_b[pp]
            glb, glB = glb_b[pp], glB_b[pp]
            S.capture = []
            self.norm_transpose(h[:, t * D:(t + 1) * D], hB[t], "nm", xn, 0, 128, xnB, xs, xsB, (0, 1))
            for grp in range(nq // 4):
                if (not full) and grp == 0 and t != NT - 1:
                    continue
                pb = 2
                for q in range(4):
                    j = grp * 4 + q
                    for c in range(8):
                        lhsT = wg[:, (j // 2) * 2048 + c * 256 + (j % 2) * 128: (j // 2) * 2048 + c * 256 + (j % 2) * 128 + 128]
                        rhs = xn[:, c * 128:(c + 1) * 128]
                        op(PE, (lambda pb, q, lhsT, rhs, c: lambda e: e.matmul(psum[pb][:, q * 128:(q + 1) * 128], lhsT=lhsT, rhs=rhs, start=(c == 0), stop=(c == 7)))(pb, q, lhsT, rhs, c),
                           reads=[wgB[j // 2], xnB], writes=[psB[pb][q]])
                if grp < 3:
                    dstv = pc[:, grp * 4 * 131:(grp + 1) * 4 * 131].rearrange("p (a b) -> p a b", a=4)[:, :, 3:131]
                    srcv = psum[pb][:, :].rearrange("p (a b) -> p a b", a=4)
                    eng = self.ew()
                    if eng == ACT:
                        op(ACT, (lambda dstv, srcv: lambda e: e.activation(out=dstv, in_=srcv, func=AF.Copy))(dstv, srcv), reads=psB[pb], writes=[pcB])
                    else:
                        op(DVE, (lambda dstv, srcv: lambda e: e.tensor_copy(out=dstv, in_=srcv))(dstv, srcv), reads=psB[pb], writes=[pcB])
                else:
                    op(ACT, (lambda pb: lambda e: e.activation(out=zT[:, :], in_=psum[pb][:, :], func=AF.Copy))(pb), reads=psB[pb], writes=[zB])
            i3 = len(S.capture)
            for c in range(8):
                lhsT = xn[:, c * 128:(c + 1) * 128]
                rhs = wg[:, 8 * 2048 + c * 8: 8 * 2048 + c * 8 + 8]
                op(PE, (lambda lhsT, rhs, c: lambda e: e.matmul(psum[2][:, 0:8], lhsT=lhsT, rhs=rhs, start=(c == 0), stop=(c == 7)))(lhsT, rhs, c),
                   reads=[wgB[8], xnB], writes=[psB[2][0]])
            op(DVE, lambda e: e.tensor_copy(out=sm[:, 0:8], in_=psum[2][:, 0:8]), reads=[psB[2][0]], writes=[smB])
            op(ACT, lambda e: e.activation(out=sm[:, 8:12], in_=sm[:, 0:4], func=AF.Exp, scale=-1.0), reads=[smB], writes=[smB])
            op(DVE, lambda e: e.tensor_scalar(out=sm[:, 8:12], in0=sm[:, 8:12], scalar1=1.0, scalar2=None, op0=ALU.add), reads=[smB], writes=[smB])
            op(DVE, lambda e: e.reciprocal(out=sm[:, 8:12], in_=sm[:, 8:12]), reads=[smB], writes=[smB])
            op(DVE, lambda e: e.tensor_tensor(out=sm[:, 12:16], in0=sm[:, 4:8], in1=cs("dtb"), op=ALU.add), reads=[smB, cpB], writes=[smB])
            op(ACT, lambda e: e.activation(out=sm[:, 12:16], in_=sm[:, 12:16], func=AF.Exp), reads=[smB], writes=[smB])
            op(ACT, lambda e: e.activation(out=sm[:, 12:16], in_=sm[:, 12:16], func=AF.Ln, bias=1.0), reads=[smB], writes=[smB])
            op(DVE, lambda e: e.tensor_tensor(out=sm[:, 12:16], in0=sm[:, 12:16], in1=self.negA[:, :], op=ALU.mult), reads=[smB, self.negAB], writes=[smB])
            op(PE, lambda e: e.matmul(psum[2][:, 8:12], lhsT=triU, rhs=sm[:, 12:16], start=True, stop=True), reads=[smB, cpB], writes=[psB[2][0]])
            op(DVE, lambda e: e.tensor_copy(out=sm[:, 16:20], in_=psum[2][:, 8:12]), reads=[psB[2][0]], writes=[smB])
            op(ACT, lambda e: e.activation(out=sm[:, 20:24], in_=sm[:, 16:20], func=AF.Exp), reads=[smB], writes=[smB])
            op(DVE, lambda e: e.tensor_scalar(out=sm[:, 24:28], in0=sm[:, 16:20], scalar1=-1.0, scalar2=None, op0=ALU.mult), reads=[smB], writes=[smB])
            op(DVE, lambda e: e.tensor_tensor(out=sm[:, 28:32], in0=sm[:, 8:12], in1=sm[:, 20:24], op=ALU.mult), reads=[smB], writes=[smB])
            for hh in range(4):
                op(DVE, (lambda hh: lambda e: e.tensor_scalar(out=Dg[:, hh * 128:(hh + 1) * 128], in0=ident, scalar1=sm[:, 16 + hh:17 + hh], scalar2=None, op0=ALU.mult))(hh),
                   reads=[smB, cpB], writes=[DgB])
            op(PE, lambda e: e.matmul(psum[3][:, :], lhsT=ones, rhs=Dg[:, 0:512], start=True, stop=True), reads=[DgB, cpB], writes=psB[3])
            Gv = psum[3][:, :].rearrange("p (a b) -> p a b", a=4)
            op(ACT, lambda e: e.activation(out=glb[:, 0:8].rearrange("p (a b) -> p a b", a=4), in_=Gv[:, :, 63:128:64], func=AF.Exp), reads=psB[3], writes=[glB])
            op(DVE, lambda e: e.tensor_tensor(out=sm[0:64, 32:36], in0=Gv[0:64, :, 63], in1=sm[0:64, 16:20], op=ALU.subtract), reads=psB[3] + [smB], writes=[smB])
            op(DVE, lambda e: e.tensor_tensor(out=sm[64:128, 32:36], in0=Gv[64:128, :, 127], in1=sm[64:128, 16:20], op=ALU.subtract), reads=psB[3] + [smB], writes=[smB])
            op(ACT, lambda e: e.activation(out=sm[:, 32:36], in_=sm[:, 32:36], func=AF.Exp), reads=[smB], writes=[smB])
            i4 = len(S.capture)
            pcv = pc[:, :].rearrange("p (a b) -> p a b", a=12)
            for j in range(0 if full else 4, 12):
                op(DVE, (lambda j: lambda e: e.tensor_scalar(out=cacc[:, j * 128:(j + 1) * 128], in0=pc[:, j * 131:j * 131 + 128], scalar1=cwg[:, j * 4:j * 4 + 1], scalar2=None, op0=ALU.mult))(j),
                   reads=[pcB, cpB], writes=[caccB])
                for k in range(1, 4):
                    op(DVE, (lambda j, k: lambda e: e.scalar_tensor_tensor(out=cacc[:, j * 128:(j + 1) * 128], in0=pc[:, j * 131 + k:j * 131 + k + 128], scalar=cwg[:, j * 4 + k:j * 4 + k + 1], in1=cacc[:, j * 128:(j + 1) * 128], op0=ALU.mult, op1=ALU.add))(j, k),
                       reads=[pcB, cpB, caccB], writes=[caccB])
            op(DVE, lambda e: e.tensor_copy(out=pcv[:, :, 0:3], in_=pcv[:, :, 128:131]), reads=[pcB], writes=[pcB])
            i5 = len(S.capture)
            c_lo = 0 if full else 512
            op(ACT, lambda e: e.activation(out=etmp[:, c_lo:1536], in_=cacc[:, c_lo:1536], func=AF.Exp, scale=-1.0), reads=[caccB], writes=[etmpB])
            op(ACT, lambda e: e.activation(out=etmp[:, c_lo:1536], in_=etmp[:, c_lo:1536], func=AF.Ln, bias=1.0), reads=[etmpB], writes=[etmpB])
            op(ACT, lambda e: e.activation(out=etmp[:, c_lo:1536], in_=etmp[:, c_lo:1536], func=AF.Exp, scale=-1.0), reads=[etmpB], writes=[etmpB])
            op(DVE, lambda e: e.tensor_tensor(out=qkv[:, c_lo:1536], in0=cacc[:, c_lo:1536], in1=etmp[:, c_lo:1536], op=ALU.mult), reads=[etmpB, caccB], writes=[qkvB])
            op(ACT, lambda e: e.activation(out=etmp[:, c_lo:1024], in_=qkv[:, c_lo:1024], func=AF.Square), reads=[qkvB, etmpB], writes=[etmpB])
            for half in range(0 if full else 1, 2):
                op(PE, (lambda half: lambda e: e.matmul(psum[half][:, :], lhsT=ones, rhs=etmp[:, half * 512:(half + 1) * 512], start=True, stop=True))(half),
                   reads=[etmpB, cpB], writes=psB[half])
                op(ACT, (lambda half: lambda e: e.activation(out=rs[:, half * 512:(half + 1) * 512], in_=psum[half][:, :], func=AF.Ln, bias=cs("eps")))(half),
                   reads=psB[half] + [cpB], writes=[rsB])
            op(ACT, lambda e: e.activation(out=rs[:, c_lo:1024], in_=rs[:, c_lo:1024], func=AF.Exp, scale=-0.5), reads=[rsB], writes=[rsB])
            if full:
                op(DVE, lambda e: e.scalar_tensor_tensor(out=qkv[:, 0:512], in0=qkv[:, 0:512], scalar=128.0 ** -0.5, in1=rs[:, 0:512], op0=ALU.mult, op1=ALU.mult), reads=[qkvB, rsB], writes=[qkvB])
            op(DVE, lambda e: e.tensor_tensor(out=qkv[:, 512:1024], in0=qkv[:, 512:1024], in1=rs[:, 512:1024], op=ALU.mult), reads=[qkvB, rsB], writes=[qkvB])
            if full:
                op(ACT, lambda e: e.activation(out=etmp[:, 0:512], in_=zT[:, :], func=AF.Exp, scale=-1.0), reads=[zB, etmpB], writes=[etmpB])
                op(ACT, lambda e: e.activation(out=etmp[:, 0:512], in_=etmp[:, 0:512], func=AF.Ln, bias=1.0), reads=[etmpB], writes=[etmpB])
                op(ACT, lambda e: e.activation(out=etmp[:, 0:512], in_=etmp[:, 0:512], func=AF.Exp, scale=-1.0), reads=[etmpB], writes=[etmpB])
                op(DVE, lambda e: e.tensor_tensor(out=zT[:, :], in0=zT[:, :], in1=etmp[:, 0:512], op=ALU.mult), reads=[etmpB, zB], writes=[zB])
            cap_ = S.capture
            S.capture = None
            s3, s4 = cap_[i3:i4], cap_[i4:i5]
            mer = []
            i_, j_ = 0, 0
            while i_ < len(s3) or j_ < len(s4):
                if j_ < len(s4) and (i_ >= len(s3) or j_ * max(len(s3), 1) <= i_ * len(s4)):
                    mer.append(s4[j_]); j_ += 1
                else:
                    mer.append(s3[i_]); i_ += 1
            return cap_[:i3] + mer + cap_[i5:]

        def chain(hh, t):
            pp = t % 2
            qkv, qkvB = qkv_b[pp], qkvB_b[pp]
            zT, zB = zT_b[pp], zB_b[pp]
            sm, smB = sm_b[pp], smB_b[pp]
            glb, glB = glb_b[pp], glB_b[pp]
            B_ = HB[hh]
            kbg, kbgB = B_["kbg"]; kdec, kdecB = B_["kdec"]; vbeta, vbetaB = B_["vbeta"]
            tm1, tm1B = B_["tm1"]; E1, E1B = B_["E1"]; Lm, LmB = B_["Lm"]; AT, ATB = B_["AT"]
            X = [B_["X0"], B_["X1"]]; Y = [B_["Y0"], B_["Y1"]]; Pm = [B_["P0"], B_["P1"]]
            wT, wTB = B_["wT"]; um, umB = B_["um"]; vnew, vnewB = B_["vnew"]
            o1s, o1sB = tm1, tm1B
            om, omB = E1, E1B
            on, onB = Lm, LmB
            qT = qkv[:, hh * 128:(hh + 1) * 128]
            kT = qkv[:, 512 + hh * 128:512 + (hh + 1) * 128]
            vT = qkv[:, 1024 + hh * 128:1024 + (hh + 1) * 128]
            PH = psum[4 + hh]
            PB = psB[4 + hh][0]
            Q0, Q1, Q2, Q3 = PH[:, 0:128], PH[:, 128:256], PH[:, 256:384], PH[:, 384:512]
            Gh = psum[3][:, hh * 128:(hh + 1) * 128]
            GhB = psB[3]
            op(PE, lambda e: e.transpose(Q0, kT, ident), reads=[qkvB, cpB], writes=[PB])
            op(PE, lambda e: e.transpose(Q1, vT, ident), reads=[qkvB, cpB], writes=[PB])
            op(PE, lambda e: e.matmul(Q2, lhsT=kT, rhs=kT, start=True, stop=True), reads=[qkvB], writes=[PB])
            if full:
                op(PE, lambda e: e.matmul(Q3, lhsT=kT, rhs=qT, start=True, stop=True), reads=[qkvB], writes=[PB])
            yield
            op(DVE, lambda e: e.tensor_tensor(out=tm1[:, :], in0=maskL, in1=Gh, op=ALU.subtract), reads=GhB + [cpB], writes=[tm1B])
            yield
            op(ACT, lambda e: e.activation(out=E1[:, :], in_=tm1[:, :], func=AF.Exp, bias=sm[:, 16 + hh:17 + hh]), reads=[tm1B, smB], writes=[E1B])
            yield
            op(ACT, lambda e: e.activation(out=kbg[:, :], in_=Q0, func=AF.Copy, scale=sm[:, 28 + hh:29 + hh]), reads=[PB, smB], writes=[kbgB])
            op(ACT, lambda e: e.activation(out=vbeta[:, :], in_=Q1, func=AF.Copy, scale=sm[:, 8 + hh:9 + hh]), reads=[PB, smB], writes=[vbetaB])
            yield
            op(DVE, lambda e: e.tensor_scalar(out=kdec[:, :], in0=Q0, scalar1=sm[:, 32 + hh:33 + hh], scalar2=None, op0=ALU.mult), reads=[PB, smB], writes=[kdecB])
            op(DVE, lambda e: e.scalar_tensor_tensor(out=Lm[:, :], in0=Q2, scalar=sm[:, 8 + hh:9 + hh], in1=E1[:, :], op0=ALU.mult, op1=ALU.mult), reads=[PB, smB, E1B], writes=[LmB])
            yield
            if full:
                op(DVE, lambda e: e.tensor_tensor(out=tm1[:, :], in0=maskU, in1=Gh, op=ALU.add), reads=GhB + [cpB, tm1B], writes=[tm1B])
                yield
                op(ACT, lambda e: e.activation(out=E1[:, :], in_=tm1[:, :], func=AF.Exp, bias=sm[:, 24 + hh:25 + hh]), reads=[tm1B, smB, E1B], writes=[E1B])
                yield
                op(DVE, lambda e: e.tensor_tensor(out=AT[:, :], in0=Q3, in1=E1[:, :], op=ALU.mult), reads=[PB, E1B], writes=[ATB])
                yield
            op(PE, lambda e: e.transpose(Q0, Lm[:, :], ident), reads=[LmB, cpB], writes=[PB])
            yield
            X0, X0B = X[0]
            P0, P0B = Pm[0]
            op(ACT, lambda e: e.activation(out=X0[:, :], in_=Q0, func=AF.Copy), reads=[PB], writes=[X0B])
            op(DVE, lambda e: e.tensor_tensor(out=P0[:, :], in0=ident, in1=Q0, op=ALU.subtract), reads=[PB, cpB], writes=[P0B])
            yield
            Xc, XcB = X0, X0B
            Yc, YcB = Lm, LmB
            Pc, PcB = P0, P0B
            for k in range(1, 6):
                Yn, YnB = Y[k % 2]
                Xn, XnB = X[k % 2]
                Pn, PnB = Pm[k % 2]
                op(PE, (lambda Xc, Yc: lambda e: e.matmul(Q1, lhsT=Xc[:, :], rhs=Yc[:, :], start=True, stop=True))(Xc, Yc), reads=[XcB, YcB], writes=[PB])
                if k < 5:
                    op(PE, (lambda Xc, Yc: lambda e: e.matmul(Q2, lhsT=Yc[:, :], rhs=Xc[:, :], start=True, stop=True))(Xc, Yc), reads=[XcB, YcB], writes=[PB])
                yield
                op(ACT, (lambda Yn: lambda e: e.activation(out=Yn[:, :], in_=Q1, func=AF.Copy))(Yn), reads=[PB], writes=[YnB])
                if k < 5:
                    op(ACT, (lambda Xn: lambda e: e.activation(out=Xn[:, :], in_=Q2, func=AF.Copy))(Xn), reads=[PB], writes=[XnB])
                yield
                op(PE, (lambda Yn, Pc: lambda e: e.matmul(Q3, lhsT=Yn[:, :], rhs=Pc[:, :], start=True, stop=True))(Yn, Pc), reads=[YnB, PcB], writes=[PB])
                yield
                op(DVE, (lambda Pn, Pc: lambda e: e.tensor_tensor(out=Pn[:, :], in0=Pc[:, :], in1=Q3, op=ALU.add))(Pn, Pc), reads=[PcB, PB], writes=[PnB])
                yield
                Xc, XcB, Yc, YcB, Pc, PcB = Xn, XnB, Yn, YnB, Pn, PnB
            op(PE, (lambda Pc: lambda e: e.matmul(Q0, lhsT=kbg[:, :], rhs=Pc[:, :], start=True, stop=True))(Pc), reads=[kbgB, PcB], writes=[PB])
            op(PE, (lambda Pc: lambda e: e.matmul(Q1, lhsT=Pc[:, :], rhs=vbeta[:, :], start=True, stop=True))(Pc), reads=[vbetaB, PcB], writes=[PB])
            yield
            op(ACT, lambda e: e.activation(out=wT[:, :], in_=Q0, func=AF.Copy), reads=[PB], writes=[wTB])
            op(ACT, lambda e: e.activation(out=um[:, :], in_=Q1, func=AF.Copy), reads=[PB], writes=[umB])
            yield
            for half in range(2):
                r0, r1 = half * 64, half * 64 + 64
                sp_ = self.spar_h[hh]
                Scur = Sst[sp_][:, hh * 128:(hh + 1) * 128]; ScurB = SsB[sp_][hh]
                Snew = Sst[1 - sp_][:, hh * 128:(hh + 1) * 128]; SnewB = SsB[1 - sp_][hh]
                self.spar_h[hh] = 1 - sp_
                op(PE, (lambda r0, r1, Scur: lambda e: e.matmul(PH[r0:r1, 256:384], lhsT=wT[:, r0:r1], rhs=Scur, start=True, stop=True))(r0, r1, Scur), reads=[wTB, ScurB], writes=[PB])
                yield
                op(DVE, (lambda r0, r1: lambda e: e.tensor_tensor(out=vnew[r0:r1, :], in0=um[r0:r1, :], in1=PH[r0:r1, 256:384], op=ALU.subtract))(r0, r1), reads=[umB, PB], writes=[vnewB])
                yield
                if full:
                    op(PE, (lambda r0, r1, Scur: lambda e: e.matmul(PH[r0:r1, 0:128], lhsT=qT[:, r0:r1], rhs=Scur, start=True, stop=True))(r0, r1, Scur), reads=[qkvB, ScurB], writes=[PB])
                    op(PE, (lambda r0, r1: lambda e: e.matmul(PH[r0:r1, 128:256], lhsT=AT[r0:r1, r0:r1], rhs=vnew[r0:r1, :], start=True, stop=True))(r0, r1), reads=[ATB, vnewB], writes=[PB])
                op(PE, (lambda r0, r1: lambda e: e.matmul(Q3, lhsT=kdec[r0:r1, :], rhs=vnew[r0:r1, :], start=True, stop=True))(r0, r1), reads=[kdecB, vnewB], writes=[PB])
                yield
                op(DVE, (lambda Snew, Scur, half: lambda e: e.scalar_tensor_tensor(out=Snew, in0=Scur, scalar=glb[:, hh * 2 + half:hh * 2 + half + 1], in1=Q3, op0=ALU.mult, op1=ALU.add))(Snew, Scur, half),
                   reads=[ScurB, glB, PB], writes=[SnewB])
                yield
            if full:
                c0 = 40 + hh * 3
                op(ACT, lambda e: e.activation(out=o1s[:, :], in_=Q0, func=AF.Copy, scale=sm[:, 20 + hh:21 + hh]), reads=[PB, smB], writes=[o1sB])
                yield
                op(DVE, lambda e: e.tensor_tensor(out=om[:, :], in0=o1s[:, :], in1=Q1, op=ALU.add), reads=[o1sB, PB], writes=[omB])
                yield
                op(ACT, lambda e: e.activation(out=on[:, :], in_=om[:, :], func=AF.Square, accum_out=sm[:, c0:c0 + 1]), reads=[omB], writes=[onB, smB])
                yield
                op(POOL, lambda e: e.tensor_scalar(out=sm[:, c0 + 1:c0 + 2], in0=sm[:, c0:c0 + 1], scalar1=1.0 / 128, scalar2=EPS, op0=ALU.mult, op1=ALU.add), reads=[smB], writes=[smB])
                op(POOL, lambda e: e.tensor_tensor(out=sm[:, c0 + 2:c0 + 3], in0=sm[:, c0 + 1:c0 + 2], in1=cs("mhalf"), op=ALU.pow), reads=[smB, cpB], writes=[smB])
                yield
                op(DVE, lambda e: e.scalar_tensor_tensor(out=on[:, :], in0=om[:, :], scalar=sm[:, c0 + 2:c0 + 3], in1=cs("gon"), op0=ALU.mult, op1=ALU.mult), reads=[omB, smB, cpB], writes=[onB])
                yield
                op(PE, lambda e: e.transpose(Q2, on[:, :], ident), reads=[onB, cpB], writes=[PB])
                yield
                yg_ap = self.ygT[:, hh * T + t * 128: hh * T + (t + 1) * 128]
                op(DVE, lambda e: e.tensor_tensor(out=yg_ap, in0=Q2, in1=zT[:, hh * 128:(hh + 1) * 128], op=ALU.mult), reads=[PB, zB], writes=[self.ygB[t]])
                yield

        for it in pre_ops(0):
            S.replay(it)
        for t in range(NT):
            pend = pre_ops(t + 1) if t + 1 < NT else []
            per = (len(pend) + 39) // 40
            S0 = int(os.environ.get("KSTAG", 4))
            per = (len(pend) + 39 + 3 * S0) // (40 + 3 * S0)
            alive = [(hh, chain(hh, t)) for hh in range(4)]
            pi = 0
            rnd = 0
            while alive or pi < len(pend):
                nxt = []
                for hh, g_ in alive:
                    if rnd < hh * S0:
                        nxt.append((hh, g_))
                        continue
                    try:
                        next(g_)
                        nxt.append((hh, g_))
                    except StopIteration:
                        pass
                alive = nxt
                for it in pend[pi:pi + per]:
                    S.replay(it)
                pi += per
                rnd += 1
        op(DVE, lambda e: e.tensor_copy(out=pchv, in_=pcv0[:, :, 0:3]), reads=[pcB], writes=[self.pchB])
        es.close()

    def conv_phase(self, wm_in, wm_out, hhalo, hhB):
        import os
        nc, S = self.nc, self.S
        op = S.op
        cs, cpB = self.cs, self.cpB
        h, hB = self.h, self.hB
        psum, psB = self.psum, self.psB
        es = contextlib.ExitStack()
        tag = "cv"
        wc = self.sb("wc", 8 * 1536, BF16, es); wcB = [Buf() for _ in range(6)]
        wo = self.sb("wo", 8 * 1024, BF16, es); woB = [Buf() for _ in range(4)]
        xs = self.sb("xs_cv", D, F32, es); xsB = Buf()
        self.junk = self.sb("junk_cv", D, BF16, es); self.junkB = Buf()
        xn = self.sb("xn_cv", 8 * 128, BF16, es); xnB = Buf()
        mpc = self.sb("mpc", 4 * 130, F32, es); mpcB = Buf()
        mpc2 = self.sb("mpc2", 4 * 130, F32, es); mpc2B = Buf()
        cbs0 = self.sb("cbs0", 512, F32, es); cbs0B = Buf()
        cbs1 = self.sb("cbs1", 512, F32, es); cbs1B = Buf()
        cct = self.sb("cct", 512, F32, es); cctB = Buf()
        cacc = self.sb("cacc_cv", 512, F32, es); caccB = Buf()
        yv = self.sb("yv", 512, F32, es); yvB = Buf()
        sq = self.sb("sq_cv", 512, F32, es); sqB = Buf()
        rs = self.sb("rs_cv", 512, F32, es); rsB = Buf()
        ycT = self.sb("ycT", 512, BF16, es); ycB = Buf()
        motmp = [self.sb("motmp%d" % i, 512, F32, es) for i in range(2)]; motB = [Buf(), Buf()]
        new_bufs = wcB + woB + [mpc2B, cbs0B, cbs1B, xsB, self.junkB, xnB, mpcB, cctB, caccB, yvB, sqB, rsB, ycB] + motB
        S.alias(new_bufs, getattr(self, "phase_bufs", []))
        self.phase_bufs = new_bufs
        wm_v = wm_in.rearrange("(c p) n -> p c n", p=128)
        for col in range(0, 1536, 256):
            base = (col // 256) * 2048
            self.load_cast(wc[:, base:base + 2048], wcB[col // 256], wm_v[:, :, col:col + 256], (8, 256))
        wo_v = wm_out.rearrange("(c p) n -> p c n", p=128)
        for i in range(4):
            self.load_cast(wo[:, i * 2048:(i + 1) * 2048], woB[i], wo_v[:, 2 * i:2 * i + 2, :], (2, 1024))
        op(POOL, lambda e: e.memset(mpc[:, :], 0.0), writes=[mpcB])
        op(POOL, lambda e: e.memset(mpc2[:, :], 0.0), writes=[mpc2B])
        ident, blk64 = cs("ident"), cs("blk64")
        csw, cgain = cs("csw"), cs("cgain")
        ygT, ygB = self.ygT, self.ygB
        mpc_b = [mpc, mpc2]; mpcB_b = [mpcB, mpc2B]
        cbs_b = [cbs0, cbs1]; cbsB_b = [cbs0B, cbs1B]

        def capA(t):
            p = t % 2
            mp, mpB = mpc_b[p], mpcB_b[p]
            mo, moB = mpc_b[1 - p], mpcB_b[1 - p]
            mpv = mp[:, :].rearrange("p (a b) -> p a b", a=4)
            mov = mo[:, :].rearrange("p (a b) -> p a b", a=4)
            S.capture = []
            if t < 0:
                src, srcB = hhalo[:, :], hhB
            else:
                src, srcB = h[:, t * D:(t + 1) * D], hB[t]
            self.norm_transpose(src, srcB, "nm", xn, 0, 128, xnB, xs, xsB, (0, 1))
            for grp in range(3):
                if t < 0 and grp == 0:
                    continue
                pb = 2 + grp
                for q in range(4):
                    j = grp * 4 + q
                    for c in range(8):
                        lhsT = wc[:, (j // 2) * 2048 + c * 256 + (j % 2) * 128: (j // 2) * 2048 + c * 256 + (j % 2) * 128 + 128]
                        rhs = xn[:, c * 128:(c + 1) * 128]
                        op(PE, (lambda pb, q, lhsT, rhs, c: lambda e: e.matmul(psum[pb][:, q * 128:(q + 1) * 128], lhsT=lhsT, rhs=rhs, start=(c == 0), stop=(c == 7)))(pb, q, lhsT, rhs, c),
                           reads=[wcB[j // 2], xnB], writes=[psB[pb][q]])
                if grp == 0:
                    op(ACT, (lambda p: lambda e: e.activation(out=cbs_b[p][:, :], in_=psum[2][:, :], func=AF.Copy))(p), reads=psB[2], writes=[cbsB_b[p]])
                if grp == 1:
                    op(ACT, lambda e: e.activation(out=cct[:, :], in_=psum[3][:, :], func=AF.Copy), reads=psB[3], writes=[cctB])
            op(DVE, lambda e: e.tensor_tensor(out=mpv[:, :, 2:130], in0=cct[:, :].rearrange("p (a b) -> p a b", a=4), in1=psum[4][:, :].rearrange("p (a b) -> p a b", a=4), op=ALU.mult),
               reads=[cctB, mpB] + psB[4], writes=[mpB])
            op(DVE, lambda e: e.tensor_copy(out=mpv[:, :, 0:2], in_=mov[:, :, 128:130]), reads=[moB, mpB], writes=[mpB])
            ops_ = S.capture
            S.capture = None
            return ops_

        def capB(t):
            p = t % 2
            mp, mpB = mpc_b[p], mpcB_b[p]
            cbs, cbsB = cbs_b[p], cbsB_b[p]
            S.capture = []
            for j in range(4):
                op(DVE, (lambda j: lambda e: e.tensor_scalar(out=cacc[:, j * 128:(j + 1) * 128], in0=mp[:, j * 130:j * 130 + 128], scalar1=csw[:, j * 3:j * 3 + 1], scalar2=None, op0=ALU.mult))(j),
                   reads=[mpB, cpB], writes=[caccB])
                for k in range(1, 3):
                    op(DVE, (lambda j, k: lambda e: e.scalar_tensor_tensor(out=cacc[:, j * 128:(j + 1) * 128], in0=mp[:, j * 130 + k:j * 130 + k + 128], scalar=csw[:, j * 3 + k:j * 3 + k + 1], in1=cacc[:, j * 128:(j + 1) * 128], op0=ALU.mult, op1=ALU.add))(j, k),
                       reads=[mpB, cpB, caccB], writes=[caccB])
            op(DVE, lambda e: e.tensor_tensor(out=yv[:, :], in0=cacc[:, :], in1=cbs[:, :], op=ALU.mult), reads=[caccB, cbsB], writes=[yvB])
            op(ACT, lambda e: e.activation(out=sq[:, :], in_=yv[:, :], func=AF.Square), reads=[yvB], writes=[sqB])
            op(PE, lambda e: e.matmul(psum[5][:, :], lhsT=blk64, rhs=sq[:, :], start=True, stop=True), reads=[sqB, cpB], writes=psB[5])
            op(ACT, lambda e: e.activation(out=rs[:, :], in_=psum[5][:, :], func=AF.Ln, bias=cs("eps")), reads=psB[5] + [cpB], writes=[rsB])
            op(ACT, lambda e: e.activation(out=rs[:, :], in_=rs[:, :], func=AF.Exp, scale=-0.5), reads=[rsB], writes=[rsB])
            for j in range(4):
                op(DVE, (lambda j: lambda e: e.scalar_tensor_tensor(out=ycT[:, j * 128:(j + 1) * 128], in0=yv[:, j * 128:(j + 1) * 128], scalar=cgain[:, j:j + 1], in1=rs[:, j * 128:(j + 1) * 128], op0=ALU.mult, op1=ALU.mult))(j),
                   reads=[yvB, rsB, cpB], writes=[ycB])
            for hh in range(2):
                pb = 6 + hh
                for j in range(8):
                    if j < 4:
                        lhsT = ycT[:, j * 128:(j + 1) * 128]
                        rd = [ycB]
                    else:
                        lhsT = ygT[:, (j - 4) * T + t * 128:(j - 4) * T + (t + 1) * 128]
                        rd = [ygB[t]]
                    rhs = wo[:, j * 1024 + hh * 512: j * 1024 + (hh + 1) * 512]
                    op(PE, (lambda pb, lhsT, rhs, j: lambda e: e.matmul(psum[pb][:, :], lhsT=lhsT, rhs=rhs, start=(j == 0), stop=(j == 7)))(pb, lhsT, rhs, j),
                       reads=rd + [woB[j // 2]], writes=psB[pb])
                hap = h[:, t * D + hh * 512: t * D + (hh + 1) * 512]
                op(ACT, (lambda pb, hh: lambda e: e.activation(out=motmp[hh][:, :], in_=psum[pb][:, :], func=AF.Copy))(pb, hh), reads=psB[pb], writes=[motB[hh]])
                op(DVE, (lambda hh, hap: lambda e: e.tensor_tensor(out=hap, in0=hap, in1=motmp[hh][:, :], op=ALU.add))(hh, hap), reads=[motB[hh], hB[t]], writes=[hB[t]])
            ops_ = S.capture
            S.capture = None
            return ops_

        for it in capA(-1):
            S.replay(it)
        for it in capA(0):
            S.replay(it)
        for t in range(NT):
            A = capA(t + 1) if t + 1 < NT else []
            B_ = capB(t)
            na, nb = len(A), len(B_)
            ia = ib = 0
            while ia < na or ib < nb:
                if ib < nb and (ia >= na or ib * max(na, 1) <= ia * nb):
                    S.replay(B_[ib]); ib += 1
                else:
                    S.replay(A[ia]); ia += 1
        es.close()

    def final(self, out, fn_bc):
        S = self.S
        op = S.op
        cs, cpB = self.cs, self.cpB
        h, hB = self.h, self.hB
        es = contextlib.ExitStack()
        ot = [self.sb("ot%d" % i, D, F32, es) for i in range(2)]
        otB = [Buf(), Buf()]
        fs = self.sb("fs", 64, F32, es); fsB = Buf()
        junk = self.sb("junk_f", D, BF16, es); junkB = Buf()
        fnb = self.sb("fnb", D, F32, es); fnbB = Buf()
        new_bufs = otB + [fsB, junkB, fnbB]
        S.alias(new_bufs, getattr(self, "phase_bufs", []))
        self.phase_bufs = new_bufs
        S.dma(SP, lambda e: e.dma_start(out=fnb[:], in_=fn_bc), "const2", S.new_group(), writes=[fnbB])
        import os
        if os.environ.get("KRAWOUT"):
            for t in range(NT):
                S.dma(SP, (lambda t: lambda e: e.dma_start(out=out[t * 128:(t + 1) * 128, :], in_=h[:, t * D:(t + 1) * D]))(t), "out%d" % (t % 2), S.new_group(), reads=[hB[t]])
            es.close()
            return
        for t in range(NT):
            k = t % 2
            c0 = (t % 16) * 3
            hs = h[:, t * D:(t + 1) * D]
            op(ACT, (lambda hs, c0: lambda e: e.activation(out=junk[:, :], in_=hs, func=AF.Square, accum_out=fs[:, c0:c0 + 1]))(hs, c0), reads=[hB[t]], writes=[junkB, fsB])
            op(POOL, (lambda c0: lambda e: e.tensor_scalar(out=fs[:, c0 + 1:c0 + 2], in0=fs[:, c0:c0 + 1], scalar1=1.0 / D, scalar2=EPS, op0=ALU.mult, op1=ALU.add))(c0), reads=[fsB], writes=[fsB])
            op(POOL, (lambda c0: lambda e: e.tensor_tensor(out=fs[:, c0 + 2:c0 + 3], in0=fs[:, c0 + 1:c0 + 2], in1=cs("mhalf"), op=ALU.pow))(c0), reads=[fsB, cpB], writes=[fsB])
            op(DVE, (lambda hs, c0, k: lambda e: e.scalar_tensor_tensor(out=ot[k][:, :], in0=hs, scalar=fs[:, c0 + 2:c0 + 3], in1=fnb[:, :], op0=ALU.mult, op1=ALU.mult))(hs, c0, k),
               reads=[hB[t], fsB, fnbB], writes=[otB[k]])
            S.dma(SP, (lambda t, k: lambda e: e.dma_start(out=out[t * 128:(t + 1) * 128, :], in_=ot[k][:, :]))(t, k), "out%d" % k, S.new_group(), reads=[otB[k]])
        es.close()


def _pack_layout():
    names = [("ident", 128), ("ones", 128), ("triU", 128), ("maskL", 128), ("maskU", 128), ("blk64", 128),
             ("gon", 128), ("n1", 8), ("nm", 8), ("n2", 8), ("cwg", 48), ("csw", 12), ("cgain", 4),
             ("alog", 4), ("dtb", 4), ("mhalf", 1), ("eps", 1)]
    lay = {}
    off = 0
    for n, w in names:
        lay[n] = (off, off + w)
        off += w
    return lay, off


_CP, _CPK_COLS = _pack_layout()
Builder.CP = _CP
Builder.CPK_COLS = _CPK_COLS


def _pack_consts(inp):
    f = np.float32
    cp = np.zeros((128, _CPK_COLS), f)

    def put(name, arr):
        a, b = _CP[name]
        cp[:, a:b] = np.asarray(arr, f).reshape(128, b - a)

    idx = np.arange(128)
    same = (idx[:, None] // 64) == (idx[None, :] // 64)
    put("ident", np.eye(128))
    put("ones", np.ones((128, 128)))
    put("triU", (same & (idx[:, None] <= idx[None, :])))
    put("maskL", np.where(same & (idx[:, None] > idx[None, :]), 0.0, NEG))
    put("maskU", np.where(same & (idx[:, None] <= idx[None, :]), 0.0, NEG))
    put("blk64", same.astype(f) / 64.0)
    put("gon", np.broadcast_to(inp["gdn_out_norm"].reshape(1, 128), (128, 128)))
    put("n1", inp["ffn1_norm"].reshape(8, 128).T)
    put("nm", inp["mix_norm"].reshape(8, 128).T)
    put("n2", inp["ffn2_norm"].reshape(8, 128).T)
    put("cwg", inp["gdn_conv_w"].reshape(4, 12, 128).transpose(2, 1, 0).reshape(128, 48))
    put("csw", inp["conv_short_w"].reshape(3, 4, 128).transpose(2, 1, 0).reshape(128, 12))
    put("cgain", inp["conv_out_norm"].reshape(4, 128).T)
    put("alog", np.broadcast_to(inp["gdn_A_log"].reshape(1, 4), (128, 4)))
    put("dtb", np.broadcast_to(inp["gdn_dt_bias"].reshape(1, 4), (128, 4)))
    put("mhalf", np.full((128, 1), -0.5))
    put("eps", np.full((128, 1), EPS))
    return cp


_NC_CACHE = {}


def _get_nc(debug=False):
    if debug not in _NC_CACHE:
        b = Builder(debug=debug)
        b.spar_h = [0, 0, 0, 0]
        _NC_CACHE[debug] = (b.build(), b)
    return _NC_CACHE[debug]


def kernel(debug=False, **inputs):
    inp = {k: np.asarray(v) for k, v in inputs.items()}
    x = inp["x"].astype(np.float32, copy=False)
    nc, b = _get_nc(debug)
    cp = _pack_consts(inp)
    fn_bc = np.ascontiguousarray(np.broadcast_to(inp["final_norm"].reshape(1, D).astype(np.float32), (128, D)))
    shared = {
        "w1_in": np.ascontiguousarray(inp["ffn1_w_in"][0]), "w1_out": np.ascontiguousarray(inp["ffn1_w_out"][0]),
        "w2_in": np.ascontiguousarray(inp["ffn2_w_in"][0]), "w2_out": np.ascontiguousarray(inp["ffn2_w_out"][0]),
        "wm_in": np.ascontiguousarray(inp["w_mix_in"][0]), "wm_out": np.ascontiguousarray(inp["w_mix_out"][0]),
        "cpk": cp, "fn_bc": fn_bc,
    }
    zeros = np.zeros((T, D), np.float32)
    in_maps = []
    for c in range(8):
        bi, half = c // 2, c % 2
        m = dict(shared)
        m["x_own"] = np.ascontiguousarray(x[bi, half * T:(half + 1) * T])
        m["x_pre"] = zeros if half == 0 else np.ascontiguousarray(x[bi, 0:T])
        in_maps.append(m)
    import os
    ncores = int(os.environ.get("KCORES", 8))
    res = run_bass_kernel_spmd(nc, in_maps[:ncores], core_ids=list(range(ncores)))
    outp = np.zeros((4, 2 * T, D), np.float32)
    for c in range(ncores):
        outp[c // 2, (c % 2) * T:(c % 2 + 1) * T] = res.results[c]["out"]
    if debug:
        return outp, res.results
    return outp
```

```python
import contextlib
import numpy as np
import concourse.bass as bass
import concourse.mybir as mybir
from concourse.bass_utils import run_bass_kernel_spmd

F32 = mybir.dt.float32
BF16 = mybir.dt.bfloat16
AF = mybir.ActivationFunctionType
ALU = mybir.AluOpType

PE, ACT, DVE, POOL, SP = "pe", "act", "dve", "pool", "sp"
COMPUTE = (PE, ACT, DVE, POOL)

D = 1024
DFF = 2816
T = 2048
NT = T // 128
NB = T // 512
EPS = 1e-6
GW0 = 1536
NG = 2056
NEG = -1.0e30


class Buf:
    __slots__ = ("name", "last_w", "readers", "excl")

    def __init__(self, name="", excl=False):
        self.name = name
        self.last_w = None
        self.readers = []
        self.excl = excl


class Op:
    __slots__ = ("eng", "fn", "deps", "needs_inc", "cnt", "is_dma", "key", "grp", "idx")

    def __init__(self, eng, fn, is_dma=False, key=None, grp=None):
        self.eng = eng
        self.fn = fn
        self.deps = []
        self.needs_inc = False
        self.cnt = 0
        self.is_dma = is_dma
        self.key = key
        self.grp = grp


class Sched:
    def __init__(self):
        self.ops = []
        self.grp_ctr = 0

    def new_group(self):
        self.grp_ctr += 1
        return self.grp_ctr

    def _add(self, op, reads, writes):
        if getattr(self, "capture", None) is not None:
            self.capture.append((op, list(reads), list(writes)))
            return op
        return self._add_real(op, reads, writes)

    def replay(self, item):
        return self._add_real(*item)

    def _add_real(self, op, reads, writes):
        op.idx = len(self.ops)
        ex = [b for b in reads if b.excl]
        if ex:
            reads = [b for b in reads if not b.excl]
            writes = list(writes) + ex
        deps = {}
        for b in reads:
            if b.last_w is not None:
                deps[id(b.last_w)] = b.last_w
        for b in writes:
            if b.last_w is not None:
                deps[id(b.last_w)] = b.last_w
            for r in b.readers:
                deps[id(r)] = r
        latest = {}
        for d in deps.values():
            if d is op:
                continue
            if (not d.is_dma) and (not op.is_dma) and d.eng == PE and op.eng == PE:
                continue
            if d.is_dma:
                op.deps.append(d)
            else:
                cur = latest.get(d.eng)
                if cur is None or d.idx > cur.idx:
                    latest[d.eng] = d
        op.deps.extend(latest.values())
        for b in reads:
            if op.is_dma:
                b.readers.append(op)
            else:
                b.readers = [r for r in b.readers if r.is_dma or r.eng != op.eng]
                b.readers.append(op)
        for b in writes:
            b.last_w = op
            b.readers = []
        self.ops.append(op)
        return op

    def op(self, eng, fn, reads=(), writes=()):
        return self._add(Op(eng, fn), reads, writes)

    def dma(self, queue, fn, key, grp, reads=(), writes=()):
        return self._add(Op(queue, fn, is_dma=True, key=key, grp=grp), reads, writes)

    def alias(self, new_bufs, old_bufs):
        acc = {}
        for b in old_bufs:
            if b.last_w is not None:
                acc[id(b.last_w)] = b.last_w
            for r in b.readers:
                acc[id(r)] = r
        for nb in new_bufs:
            nb.readers = list(acc.values())

    def emit(self, nc, final_wait_keys=()):
        ops = self.ops
        for o in ops:
            for d in o.deps:
                d.needs_inc = True
        cnt = {}
        grp_end = {}
        for o in ops:
            if o.is_dma:
                k = ("dma", o.key)
                cnt[k] = cnt.get(k, 0) + 1
                o.cnt = cnt[k]
                grp_end[(o.key, o.grp)] = o.cnt
            elif o.needs_inc:
                cnt[o.eng] = cnt.get(o.eng, 0) + 1
                o.cnt = cnt[o.eng]
        dma_keys = sorted({o.key for o in ops if o.is_dma})
        streams = {e: [o for o in ops if o.eng == e] for e in (PE, ACT, DVE, POOL, SP)}
        self.stats = {e: len(s) for e, s in streams.items()}
        self.stats["incs"] = dict(cnt)

        import os
        SEG = int(os.environ.get('KSEG', 1500))
        with contextlib.ExitStack() as es:
            sems = {}
            for e in COMPUTE:
                nseg = (cnt.get(e, 0) + SEG - 1) // SEG + 1
                sems[e] = [es.enter_context(nc.semaphore("s_%s_%d" % (e, j))) for j in range(nseg)]
            for k in dma_keys:
                sems[("dma", k)] = es.enter_context(nc.semaphore("d_" + str(k)))
            block = es.enter_context(nc.Block())

            def run_stream(engname, eng):
                waited = {}
                for o in streams[engname]:
                    for d in o.deps:
                        if d.is_dma:
                            sk = ("dma", d.key)
                            val = 16 * grp_end[(d.key, d.grp)]
                            sem = sems[sk]
                        else:
                            seg = (d.cnt - 1) // SEG
                            sk = (d.eng, seg)
                            val = (d.cnt - 1) % SEG + 1
                            sem = sems[d.eng][seg]
                            if any(k2[0] == d.eng and k2[1] > seg for k2 in waited if isinstance(k2, tuple) and k2[0] == d.eng):
                                continue
                        if waited.get(sk, 0) >= val:
                            continue
                        waited[sk] = val
                        eng.wait_ge(sem, val)
                    ins = o.fn(eng)
                    if o.is_dma:
                        ins.then_inc(sems[("dma", o.key)], 16)
                    elif o.needs_inc:
                        ins.then_inc(sems[o.eng][(o.cnt - 1) // SEG], 1)
                if engname == SP:
                    for k in final_wait_keys:
                        eng.wait_ge(sems[("dma", k)], 16 * cnt[("dma", k)])

            @block.sync
            def _(e):
                run_stream(SP, e)

            @block.tensor
            def _(e):
                run_stream(PE, e)

            @block.scalar
            def _(e):
                run_stream(ACT, e)

            @block.vector
            def _(e):
                run_stream(DVE, e)

            @block.gpsimd
            def _(e):
                run_stream(POOL, e)


class Builder:
    def __init__(self, debug=False):
        self.debug = debug
        self.nc = bass.Bass("TRN2", target_bir_lowering=False)
        self.S = Sched()
        self.es = contextlib.ExitStack()
        self.dbg_outs = []
        self.dbg_keys = []
        self.rr = 0

    def sb(self, name, cols, dt=F32, es=None):
        return (es or self.es).enter_context(self.nc.sbuf_tensor(name, [128, cols], dt))

    def dram_in(self, name, shape, dt=F32):
        return self.nc.dram_tensor(name, list(shape), dt, kind="ExternalInput").ap()

    def dram_out(self, name, shape, dt=F32):
        return self.nc.dram_tensor(name, list(shape), dt, kind="ExternalOutput").ap()

    def dbg(self, name, ap, cols, bufs, dt=F32):
        if not self.debug:
            return
        o = self.dram_out("dbg_" + name, [128, cols], dt)
        self.dbg_keys.append("dbg_" + name)
        self.S.dma(SP, lambda e: e.dma_start(out=o, in_=ap), "dbg_" + name, self.S.new_group(), reads=bufs)

    def ew(self):
        self.rr += 1
        return ACT if (self.rr & 1) else DVE

    def build(self):
        nc, S = self.nc, self.S
        op = S.op
        x_pre = self.dram_in("x_pre", [T, D])
        x_own = self.dram_in("x_own", [T, D])
        w1_in = self.dram_in("w1_in", [D, 2 * DFF])
        w1_out = self.dram_in("w1_out", [DFF, D])
        w2_in = self.dram_in("w2_in", [D, 2 * DFF])
        w2_out = self.dram_in("w2_out", [DFF, D])
        wm_in = self.dram_in("wm_in", [D, 3592])
        wm_out = self.dram_in("wm_out", [D, D])
        cpk = self.dram_in("cpk", [128, self.CPK_COLS])
        fn_bc = self.dram_in("fn_bc", [128, D])
        out = self.dram_out("out", [T, D])
        self.out_grp = S.new_group()

        h = self.sb("h", NT * D)
        hB = [Buf("h%d" % t) for t in range(NT)]
        stage = [self.sb("stage%d" % i, 2048) for i in range(2)]
        stB = [Buf("st%d" % i) for i in range(2)]
        self.stage, self.stB, self.st_i = stage, stB, 0
        import os
        self.cast_order = os.environ.get('KCAST', 'dve,act,pool,dve,act').split(',')
        cp = self.sb("cp", self.CPK_COLS)
        cpB = Buf("cp")
        hhalo = self.sb("hhalo", D)
        hhB = Buf("hhalo")
        ygT = self.sb("ygT", 4 * T, BF16)
        ygB = [Buf("yg%d" % t) for t in range(NT)]
        pch = self.sb("pch", 36)
        pchB = Buf("pch")
        Sst = [self.sb("Sst%d" % i, 4 * 128) for i in range(2)]
        SsB = [[Buf("S%d_%d" % (i, hh)) for hh in range(4)] for i in range(2)]
        stat = self.sb("stat", 64)
        negA = self.sb("negA", 4)
        negAB = Buf("negA")
        psum = [self.es.enter_context(nc.psum_tensor("ps%d" % i, [128, 512], F32)) for i in range(8)]
        psB = [[Buf("ps%d" % i, excl=True)] * 4 for i in range(8)]
        self.psum, self.psB = psum, psB
        self.h, self.hB = h, hB

        C = self.CP
        g0 = S.new_group()
        S.dma(SP, lambda e: e.dma_start(out=cp[:], in_=cpk), "const", g0, writes=[cpB])
        self.cp, self.cpB = cp, cpB

        def cs(name, n=None):
            a, b = C[name]
            return cp[:, a:b]

        self.cs = cs
        ident = cs("ident")
        op(POOL, lambda e: e.memset(Sst[0][:], 0.0), writes=SsB[0])
        op(POOL, lambda e: e.memset(pch[:], 0.0), writes=[pchB])
        op(ACT, lambda e: e.activation(out=negA[:], in_=cs("alog"), func=AF.Exp), reads=[cpB], writes=[negAB])
        op(DVE, lambda e: e.tensor_scalar(out=negA[:], in0=negA[:], scalar1=-1.0, scalar2=None, op0=ALU.mult),
           reads=[negAB], writes=[negAB])
        self.negA, self.negAB = negA, negAB
        self.pch, self.pchB = pch, pchB
        self.Sst, self.SsB = Sst, SsB
        self.ygT, self.ygB = ygT, ygB
        self.spar = 0
        self.stat = stat
        self.statB = Buf("stat")

        import os
        PH = set(os.environ.get("KPH", "pf,pg,of,og,cv,f2").split(","))
        if "pf" in PH:
            self.load_x(x_pre)
            self.ffn(w1_in, w1_out, "n1", tag="p1")
        op(POOL, lambda e: e.tensor_copy(out=hhalo[:], in_=h[:, (NT - 1) * D:NT * D]), reads=[hB[NT - 1]], writes=[hhB])
        if "pg" in PH:
            self.gdn_phase(wm_in, full=False, tag="pg")
        self.load_x(x_own)
        if "of" in PH:
            self.ffn(w1_in, w1_out, "n1", tag="o1")
        self.dbg("h1", h[:, 0:D], D, [hB[0]])
        if "og" in PH:
            self.gdn_phase(wm_in, full=True, tag="og")
        self.dbg("yg", self.ygT[:, 0:T], T, self.ygB, BF16)
        if "cv" in PH:
            self.conv_phase(wm_in, wm_out, hhalo, hhB)
        self.dbg("h2", h[:, 0:D], D, [hB[0]])
        if "f2" in PH:
            self.ffn(w2_in, w2_out, "n2", tag="o2")
        self.final(out, fn_bc)
        S.emit(nc, final_wait_keys=["out0", "out1"] + self.dbg_keys)
        self.es.close()
        return nc

    def load_x(self, xd):
        S, h, hB = self.S, self.h, self.hB
        g = S.new_group()
        for t in range(NT):
            S.dma(SP, (lambda t: lambda e: e.dma_start(out=h[:, t * D:(t + 1) * D], in_=xd[t * 128:(t + 1) * 128, :]))(t),
                  "x%d" % (t % 4), g, writes=[hB[t]])

    def load_dma(self, dst_ap, dstB, src_ap, shape3):
        S = self.S
        n = len(self.stage)
        i = self.st_i % n
        self.st_i += 1
        st, sB = self.stage[i], self.stB[i]
        a, b = shape3
        sview = st[:, 0:a * b].rearrange("p (a b) -> p a b", a=a) if a > 1 else st[:, 0:b]
        sflat = st[:, 0:a * b]
        g = S.new_group()
        S.dma(SP, lambda e: e.dma_start(out=sview, in_=src_ap), "st%d" % i, g, writes=[sB])
        return (dst_ap, dstB, sflat, sB)

    def load_cast_do(self, hnd):
        S = self.S
        dst_ap, dstB, sflat, sB = hnd
        self.cast_i = getattr(self, "cast_i", 0) + 1
        eng = self.cast_order[self.cast_i % len(self.cast_order)]
        if eng == ACT:
            S.op(ACT, lambda e: e.activation(out=dst_ap, in_=sflat, func=AF.Copy), reads=[sB], writes=[dstB])
        else:
            S.op(eng, lambda e: e.tensor_copy(out=dst_ap, in_=sflat), reads=[sB], writes=[dstB])

    def load_cast(self, dst_ap, dstB, src_ap, shape3=None, key="w"):
        self.load_cast_do(self.load_dma(dst_ap, dstB, src_ap, shape3))

    def norm_transpose(self, src_ap, srcB, gain_name, dst, dst_off, dst_stride, dstB, xs, xsB, pbanks, only=None):
        S, cs = self.S, self.cs
        op = S.op
        stat = self.stat
        stB = self.statB
        op(ACT, lambda e: e.activation(out=xs[:, 0:D], in_=src_ap, func=AF.Square, accum_out=stat[:, 0:1]),
           reads=[srcB], writes=[xsB, stB])
        op(POOL, lambda e: e.tensor_scalar(out=stat[:, 1:2], in0=stat[:, 0:1], scalar1=1.0 / D, scalar2=EPS,
                                           op0=ALU.mult, op1=ALU.add), reads=[stB], writes=[stB])
        op(POOL, lambda e: e.tensor_tensor(out=stat[:, 2:3], in0=stat[:, 1:2], in1=cs("mhalf"), op=ALU.pow),
           reads=[stB, self.cpB], writes=[stB])
        op(DVE, lambda e: e.tensor_scalar(out=xs[:, 0:D], in0=src_ap, scalar1=stat[:, 2:3], scalar2=None, op0=ALU.mult),
           reads=[srcB, stB], writes=[xsB])
        if only == "front":
            return
        self._nt_back(gain_name, dst, dst_off, dst_stride, dstB, xs, xsB, pbanks)

    def _nt_back(self, gain_name, dst, dst_off, dst_stride, dstB, xs, xsB, pbanks):
        S, cs = self.S, self.cs
        op = S.op
        gain = cs(gain_name)
        ident = cs("ident")
        for half in range(2):
            pb = pbanks[half]
            pbuf = self.psum[pb]
            for q in range(4):
                c = half * 4 + q
                op(PE, (lambda c, q, pbuf: lambda e: e.transpose(pbuf[:, q * 128:(q + 1) * 128], xs[:, c * 128:(c + 1) * 128], ident))(c, q, pbuf),
                   reads=[xsB, self.cpB], writes=[self.psB[pb][q]])
            for q in range(4):
                c = half * 4 + q
                eng = self.ew()
                o_ap = dst[:, c * dst_stride + dst_off: c * dst_stride + dst_off + 128]
                i_ap = pbuf[:, q * 128:(q + 1) * 128]
                g_ap = gain[:, c:c + 1]
                if eng == ACT:
                    op(ACT, (lambda o_ap, i_ap, g_ap: lambda e: e.activation(out=o_ap, in_=i_ap, func=AF.Copy, scale=g_ap))(o_ap, i_ap, g_ap),
                       reads=[self.psB[pb][q], self.cpB], writes=[dstB])
                else:
                    op(DVE, (lambda o_ap, i_ap, g_ap: lambda e: e.tensor_scalar(out=o_ap, in0=i_ap, scalar1=g_ap, scalar2=None, op0=ALU.mult))(o_ap, i_ap, g_ap),
                       reads=[self.psB[pb][q], self.cpB], writes=[dstB])

    def ffn(self, w_in, w_out, gain_name, tag):
        import os
        nc, S = self.nc, self.S
        op = S.op
        h, hB = self.h, self.hB
        psum, psB = self.psum, self.psB
        es = contextlib.ExitStack()
        xnT = self.sb("xnT_" + tag, 8 * T, BF16, es)
        xnB = [Buf("xn%d" % t) for t in range(NT)]
        CPP = int(os.environ.get("KCPP", 4))
        nsub = CPP // 2
        wbi = [self.sb("wbi%d_%s" % (i, tag), 2 * nsub * 2048, BF16, es) for i in range(2)]
        wbo = [self.sb("wbo%d_%s" % (i, tag), CPP * 1024, BF16, es) for i in range(2)]
        assert CPP == 4
        wbiB = [[Buf() for _ in range(4)] for _ in range(2)]
        wboB = [[Buf() for _ in range(nsub)] for _ in range(2)]
        hid = [self.sb("hid%d_%s" % (i, tag), CPP * 512, BF16, es) for i in range(2)]
        hidB = [Buf(), Buf()]
        sg0 = self.sb("sg0_%s" % tag, 512, F32, es); sg = [sg0, sg0]
        sgB0 = Buf(); sgB = [sgB0, sgB0]
        xs = [self.sb("xs%d_%s" % (i, tag), D, F32, es) for i in range(2)]
        xsB = [Buf(), Buf()]
        ev = [xs[1][:, 0:512], xs[1][:, 512:1024]]
        evB = [xsB[1], xsB[1]]
        self.junkB = Buf()
        base_stage, base_stB = self.stage, self.stB
        nextra = int(os.environ.get("KXST", 0))
        xst = [self.sb("xst%d_%s" % (i, tag), 2048, F32, es) for i in range(nextra)]
        xstB = [Buf() for _ in range(nextra)]
        self.stage, self.stB = base_stage + xst, base_stB + xstB
        new_bufs = xnB + wbiB[0] + wbiB[1] + wboB[0] + wboB[1] + hidB + sgB + xsB + [self.junkB] + evB + xstB
        S.alias(new_bufs, getattr(self, "phase_bufs", []))
        self.phase_bufs = new_bufs

        w_in_v = w_in.rearrange("(c p) n -> p c n", p=128)
        w_out_v = w_out.rearrange("(c p) n -> p c n", p=128)
        pieces = []
        c0 = 0
        while c0 < DFF // 128:
            n = min(CPP, DFF // 128 - c0)
            pieces.append((c0, n))
            c0 += n
        NP = len(pieces)

        def piece_specs(p):
            s = p % 2
            ch0, n = pieces[p]
            W = n * 128
            col = ch0 * 128
            specs = []
            for which in range(2):
                for csub in range(2):
                    base = which * 4096 + csub * 2048
                    specs.append((wbi[s][:, base:base + 4 * W], wbiB[s][which * 2 + csub],
                                  w_in_v[:, csub * 4:csub * 4 + 4, which * DFF + col:which * DFF + col + W], (4, W)))
            for sub in range(n // 2):
                specs.append((wbo[s][:, sub * 2048:(sub + 1) * 2048], wboB[s][sub], w_out_v[:, ch0 + 2 * sub:ch0 + 2 * sub + 2, :], (2, 1024)))
            return specs

        def load_piece(p):
            for sp_ in piece_specs(p):
                self.load_cast(*sp_)

        load_piece(0)
        PBK = [(0, 1), (2, 3), (4, 5), (6, 7)]
        self.norm_transpose(h[:, 0:D], hB[0], gain_name, xnT, 0, T, xnB[0], xs[0], xsB[0], PBK[0], only="front")
        for t in range(NT):
            if t + 1 < NT:
                self.norm_transpose(h[:, (t + 1) * D:(t + 2) * D], hB[t + 1], gain_name, xnT, (t + 1) * 128, T, xnB[t + 1],
                                    xs[(t + 1) % 2], xsB[(t + 1) % 2], PBK[(t + 1) % 4], only="front")
            self._nt_back(gain_name, xnT, t * 128, T, xnB[t], xs[t % 2], xsB[t % 2], PBK[t % 4])
        blocks = [(p, tb) for p in range(NP) for tb in range(NB)]
        st = {"gi": 0, "oi": 0}

        def stage1(idx):
            p, tb = blocks[idx]
            s = p % 2
            hs = idx % 2
            n = pieces[p][1]
            for j in range(n):
                gi = st["gi"]
                for which in range(2):
                    pb = (0 if which == 0 else 2) + (gi % 2)
                    W = n * 128
                    for c in range(8):
                        off = which * 4096 + (c // 4) * 2048 + (c % 4) * W + j * 128
                        lhsT = wbi[s][:, off:off + 128]
                        rhs = xnT[:, c * T + tb * 512: c * T + (tb + 1) * 512]
                        op(PE, (lambda pb, lhsT, rhs, c: lambda e: e.matmul(psum[pb][:, :], lhsT=lhsT, rhs=rhs, start=(c == 0), stop=(c == 7)))(pb, lhsT, rhs, c),
                           reads=[wbiB[s][which * 2 + c // 4]] + xnB[tb * 4:(tb + 1) * 4], writes=psB[pb])
                pg, pu = (gi % 2), 2 + (gi % 2)
                k = gi % 2
                op(ACT, (lambda pg, k: lambda e: e.activation(out=sg[k][:, :], in_=psum[pg][:, :], func=AF.Silu))(pg, k),
                   reads=psB[pg], writes=[sgB[k]])
                op(DVE, (lambda pu, k, hs, j: lambda e: e.tensor_tensor(out=hid[hs][:, j * 512:(j + 1) * 512], in0=sg[k][:, :], in1=psum[pu][:, :], op=ALU.mult))(pu, k, hs, j),
                   reads=[sgB[k]] + psB[pu], writes=[hidB[hs]])
                st["gi"] += 1

        def stage2(idx):
            p, tb = blocks[idx]
            s = p % 2
            hs = idx % 2
            n = pieces[p][1]
            for tt in range(4):
                t = tb * 4 + tt
                for hh in range(2):
                    oi = st["oi"]
                    pb = 4 + (oi % 2)
                    for j in range(n):
                        lhsT = hid[hs][:, j * 512 + tt * 128: j * 512 + (tt + 1) * 128]
                        rhs = wbo[s][:, j * 1024 + hh * 512: j * 1024 + (hh + 1) * 512]
                        op(PE, (lambda pb, lhsT, rhs, j: lambda e: e.matmul(psum[pb][:, :], lhsT=lhsT, rhs=rhs, start=(j == 0), stop=(j == n - 1)))(pb, lhsT, rhs, j),
                           reads=[hidB[hs], wboB[s][j // 2]], writes=psB[pb])
                    hap = h[:, t * D + hh * 512: t * D + (hh + 1) * 512]
                    if oi % 2 == 0:
                        op(DVE, (lambda pb, hap: lambda e: e.scalar_tensor_tensor(out=hap, in0=psum[pb][:, :], scalar=0.5, in1=hap, op0=ALU.mult, op1=ALU.add))(pb, hap),
                           reads=psB[pb] + [hB[t]], writes=[hB[t]])
                    else:
                        k = (oi // 2) % 2
                        op(ACT, (lambda pb, k: lambda e: e.activation(out=ev[k][:, :], in_=psum[pb][:, :], func=AF.Copy, scale=0.5))(pb, k),
                           reads=psB[pb], writes=[evB[k]])
                        op(POOL, (lambda k, hap: lambda e: e.tensor_tensor(out=hap, in0=hap, in1=ev[k][:, :], op=ALU.add))(k, hap),
                           reads=[evB[k], hB[t]], writes=[hB[t]])
                    st["oi"] += 1

        pend_specs = []
        inflight = []
        for idx in range(len(blocks)):
            stage1(idx)
            if idx > 0:
                stage2(idx - 1)
            p, tb = blocks[idx]
            if tb == 0 and p + 1 < NP:
                pend_specs = piece_specs(p + 1)
            for hnd in inflight:
                self.load_cast_do(hnd)
            inflight = []
            if tb == NB - 1:
                for sp_ in pend_specs:
                    self.load_cast(*sp_)
                pend_specs = []
            else:
                for sp_ in pend_specs[:2]:
                    inflight.append(self.load_dma(*sp_))
                pend_specs = pend_specs[2:]
        stage2(len(blocks) - 1)
        self.stage, self.stB = base_stage, base_stB
        es.close()

    def gdn_phase(self, wm_in, full, tag):
        import os
        nc, S = self.nc, self.S
        op = S.op
        cs, cpB = self.cs, self.cpB
        h, hB = self.h, self.hB
        psum, psB = self.psum, self.psB
        es = contextlib.ExitStack()
        pc = self.sb("pc_" + tag, 12 * 131, F32, es); pcB = Buf()
        wg = self.sb("wg_" + tag, 8 * NG, BF16, es)
        wgB = [Buf() for _ in range(9)]
        xs = self.sb("xs_" + tag, D, F32, es); xsB = Buf()
        xn = self.sb("xn_" + tag, 8 * 128, BF16, es); xnB = Buf()
        qkv0 = self.sb("qkv_" + tag, 12 * 128, F32, es); qkvB0 = Buf()
        qkv_b = [qkv0, self.stage[0][:, 0:1536]]; qkvB_b = [qkvB0, self.stB[0]]
        cacc = self.sb("cacc_" + tag, 12 * 128, F32, es); caccB = Buf()
        etmp = self.sb("etmp_" + tag, 12 * 128, F32, es); etmpB = Buf()
        rs, rsB = etmp, etmpB
        zT0 = self.sb("zT_" + tag, 4 * 128, F32, es); zB0 = Buf()
        zT_b = [zT0, self.stage[1][:, 0:512]]; zB_b = [zB0, self.stB[1]]
        sm_b = [self.sb("sm%d_%s" % (i, tag), 64, F32, es) for i in range(2)]; smB_b = [Buf(), Buf()]
        Dg = self.sb("Dg_" + tag, 512, F32, es); DgB = Buf()
        glb_b = [self.sb("glb%d_%s" % (i, tag), 8, F32, es) for i in range(2)]; glB_b = [Buf(), Buf()]
        def mk(n, cols=128, dt=F32):
            return self.sb(n + "_" + tag, cols, dt, es), Buf(n)
        HB = []
        for hh in range(4):
            d_ = {}
            for n in ["kbg", "kdec", "vbeta", "tm1", "E1", "Lm", "AT", "X0", "X1", "Y0", "Y1", "P0", "P1", "wT", "um", "vnew"]:
                d_[n] = mk("%s%d" % (n, hh))
            HB.append(d_)
        new_bufs = wgB + [pcB, xsB, xnB, qkvB0, caccB, etmpB, zB0, DgB] + smB_b + glB_b + [b for d_ in HB for _, b in d_.values()]
        S.alias(new_bufs, getattr(self, "phase_bufs", []))
        self.phase_bufs = new_bufs

        wm_v = wm_in.rearrange("(c p) n -> p c n", p=128)
        col = 0
        while col < NG:
            w = min(256, NG - col)
            base = (col // 256) * 2048
            self.load_cast(wg[:, base:base + 8 * w], wgB[col // 256], wm_v[:, :, GW0 + col:GW0 + col + w], (8, w))
            col += w

        ident, ones, triU = cs("ident"), cs("ones"), cs("triU")
        maskL, maskU = cs("maskL"), cs("maskU")
        cwg = cs("cwg")
        pcv0 = pc[:, :].rearrange("p (a b) -> p a b", a=12)
        pchv = self.pch[:, :].rearrange("p (a b) -> p a b", a=12)
        op(DVE, lambda e: e.tensor_copy(out=pcv0[:, :, 0:3], in_=pchv), reads=[self.pchB], writes=[pcB])
        Sst, SsB = self.Sst, self.SsB
        nq = 16 if full else 12

        def pre_ops(t):
            pp = t % 2
            qkv, qkvB = qkv_b[pp], qkvB_b[pp]
            zT, zB = zT_b[pp], zB_b[pp]
            sm, smB = sm_b[pp], smB_b[pp]
            glb, glB = glb_b[pp], glB_b[pp]
            S.capture = []
            self.norm_transpose(h[:, t * D:(t + 1) * D], hB[t], "nm", xn, 0, 128, xnB, xs, xsB, (0, 1))
            for grp in range(nq // 4):
                if (not full) and grp == 0 and t != NT - 1:
                    continue
                pb = 2
                for q in range(4):
                    j = grp * 4 + q
                    for c in range(8):
                        lhsT = wg[:, (j // 2) * 2048 + c * 256 + (j % 2) * 128: (j // 2) * 2048 + c * 256 + (j % 2) * 128 + 128]
                        rhs = xn[:, c * 128:(c + 1) * 128]
                        op(PE, (lambda pb, q, lhsT, rhs, c: lambda e: e.matmul(psum[pb][:, q * 128:(q + 1) * 128], lhsT=lhsT, rhs=rhs, start=(c == 0), stop=(c == 7)))(pb, q, lhsT, rhs, c),
                           reads=[wgB[j // 2], xnB], writes=[psB[pb][q]])
                if grp < 3:
                    dstv = pc[:, grp * 4 * 131:(grp + 1) * 4 * 131].rearrange("p (a b) -> p a b", a=4)[:, :, 3:131]
                    srcv = psum[pb][:, :].rearrange("p (a b) -> p a b", a=4)
                    eng = self.ew()
                    if eng == ACT:
                        op(ACT, (lambda dstv, srcv: lambda e: e.activation(out=dstv, in_=srcv, func=AF.Copy))(dstv, srcv), reads=psB[pb], writes=[pcB])
                    else:
                        op(DVE, (lambda dstv, srcv: lambda e: e.tensor_copy(out=dstv, in_=srcv))(dstv, srcv), reads=psB[pb], writes=[pcB])
                else:
                    op(ACT, (lambda pb: lambda e: e.activation(out=zT[:, :], in_=psum[pb][:, :], func=AF.Copy))(pb), reads=psB[pb], writes=[zB])
            i3 = len(S.capture)
            for c in range(8):
                lhsT = xn[:, c * 128:(c + 1) * 128]
                rhs = wg[:, 8 * 2048 + c * 8: 8 * 2048 + c * 8 + 8]
                op(PE, (lambda lhsT, rhs, c: lambda e: e.matmul(psum[2][:, 0:8], lhsT=lhsT, rhs=rhs, start=(c == 0), stop=(c == 7)))(lhsT, rhs, c),
                   reads=[wgB[8], xnB], writes=[psB[2][0]])
            op(DVE, lambda e: e.tensor_copy(out=sm[:, 0:8], in_=psum[2][:, 0:8]), reads=[psB[2][0]], writes=[smB])
            op(ACT, lambda e: e.activation(out=sm[:, 8:12], in_=sm[:, 0:4], func=AF.Exp, scale=-1.0), reads=[smB], writes=[smB])
            op(DVE, lambda e: e.tensor_scalar(out=sm[:, 8:12], in0=sm[:, 8:12], scalar1=1.0, scalar2=None, op0=ALU.add), reads=[smB], writes=[smB])
            op(DVE, lambda e: e.reciprocal(out=sm[:, 8:12], in_=sm[:, 8:12]), reads=[smB], writes=[smB])
            op(DVE, lambda e: e.tensor_tensor(out=sm[:, 12:16], in0=sm[:, 4:8], in1=cs("dtb"), op=ALU.add), reads=[smB, cpB], writes=[smB])
            op(ACT, lambda e: e.activation(out=sm[:, 12:16], in_=sm[:, 12:16], func=AF.Exp), reads=[smB], writes=[smB])
            op(ACT, lambda e: e.activation(out=sm[:, 12:16], in_=sm[:, 12:16], func=AF.Ln, bias=1.0), reads=[smB], writes=[smB])
            op(DVE, lambda e: e.tensor_tensor(out=sm[:, 12:16], in0=sm[:, 12:16], in1=self.negA[:, :], op=ALU.mult), reads=[smB, self.negAB], writes=[smB])
            op(PE, lambda e: e.matmul(psum[2][:, 8:12], lhsT=triU, rhs=sm[:, 12:16], start=True, stop=True), reads=[smB, cpB], writes=[psB[2][0]])
            op(DVE, lambda e: e.tensor_copy(out=sm[:, 16:20], in_=psum[2][:, 8:12]), reads=[psB[2][0]], writes=[smB])
            op(ACT, lambda e: e.activation(out=sm[:, 20:24], in_=sm[:, 16:20], func=AF.Exp), reads=[smB], writes=[smB])
            op(DVE, lambda e: e.tensor_scalar(out=sm[:, 24:28], in0=sm[:, 16:20], scalar1=-1.0, scalar2=None, op0=ALU.mult), reads=[smB], writes=[smB])
            op(DVE, lambda e: e.tensor_tensor(out=sm[:, 28:32], in0=sm[:, 8:12], in1=sm[:, 20:24], op=ALU.mult), reads=[smB], writes=[smB])
            for hh in range(4):
                op(DVE, (lambda hh: lambda e: e.tensor_scalar(out=Dg[:, hh * 128:(hh + 1) * 128], in0=ident, scalar1=sm[:, 16 + hh:17 + hh], scalar2=None, op0=ALU.mult))(hh),
                   reads=[smB, cpB], writes=[DgB])
            op(PE, lambda e: e.matmul(psum[3][:, :], lhsT=ones, rhs=Dg[:, 0:512], start=True, stop=True), reads=[DgB, cpB], writes=psB[3])
            Gv = psum[3][:, :].rearrange("p (a b) -> p a b", a=4)
            op(ACT, lambda e: e.activation(out=glb[:, 0:8].rearrange("p (a b) -> p a b", a=4), in_=Gv[:, :, 63:128:64], func=AF.Exp), reads=psB[3], writes=[glB])
            op(DVE, lambda e: e.tensor_tensor(out=sm[0:64, 32:36], in0=Gv[0:64, :, 63], in1=sm[0:64, 16:20], op=ALU.subtract), reads=psB[3] + [smB], writes=[smB])
            op(DVE, lambda e: e.tensor_tensor(out=sm[64:128, 32:36], in0=Gv[64:128, :, 127], in1=sm[64:128, 16:20], op=ALU.subtract), reads=psB[3] + [smB], writes=[smB])
            op(ACT, lambda e: e.activation(out=sm[:, 32:36], in_=sm[:, 32:36], func=AF.Exp), reads=[smB], writes=[smB])
            i4 = len(S.capture)
            pcv = pc[:, :].rearrange("p (a b) -> p a b", a=12)
            for j in range(0 if full else 4, 12):
                op(DVE, (lambda j: lambda e: e.tensor_scalar(out=cacc[:, j * 128:(j + 1) * 128], in0=pc[:, j * 131:j * 131 + 128], scalar1=cwg[:, j * 4:j * 4 + 1], scalar2=None, op0=ALU.mult))(j),
                   reads=[pcB, cpB], writes=[caccB])
                for k in range(1, 4):
                    op(DVE, (lambda j, k: lambda e: e.scalar_tensor_tensor(out=cacc[:, j * 128:(j + 1) * 128], in0=pc[:, j * 131 + k:j * 131 + k + 128], scalar=cwg[:, j * 4 + k:j * 4 + k + 1], in1=cacc[:, j * 128:(j + 1) * 128], op0=ALU.mult, op1=ALU.add))(j, k),
                       reads=[pcB, cpB, caccB], writes=[caccB])
            op(DVE, lambda e: e.tensor_copy(out=pcv[:, :, 0:3], in_=pcv[:, :, 128:131]), reads=[pcB], writes=[pcB])
            i5 = len(S.capture)
            c_lo = 0 if full else 512
            op(ACT, lambda e: e.activation(out=etmp[:, c_lo:1536], in_=cacc[:, c_lo:1536], func=AF.Exp, scale=-1.0), reads=[caccB], writes=[etmpB])
            op(ACT, lambda e: e.activation(out=etmp[:, c_lo:1536], in_=etmp[:, c_lo:1536], func=AF.Ln, bias=1.0), reads=[etmpB], writes=[etmpB])
            op(ACT, lambda e: e.activation(out=etmp[:, c_lo:1536], in_=etmp[:, c_lo:1536], func=AF.Exp, scale=-1.0), reads=[etmpB], writes=[etmpB])
            op(DVE, lambda e: e.tensor_tensor(out=qkv[:, c_lo:1536], in0=cacc[:, c_lo:1536], in1=etmp[:, c_lo:1536], op=ALU.mult), reads=[etmpB, caccB], writes=[qkvB])
            op(ACT, lambda e: e.activation(out=etmp[:, c_lo:1024], in_=qkv[:, c_lo:1024], func=AF.Square), reads=[qkvB, etmpB], writes=[etmpB])
            for half in range(0 if full else 1, 2):
                op(PE, (lambda half: lambda e: e.matmul(psum[half][:, :], lhsT=ones, rhs=etmp[:, half * 512:(half + 1) * 512], start=True, stop=True))(half),
                   reads=[etmpB, cpB], writes=psB[half])
                op(ACT, (lambda half: lambda e: e.activation(out=rs[:, half * 512:(half + 1) * 512], in_=psum[half][:, :], func=AF.Ln, bias=cs("eps")))(half),
                   reads=psB[half] + [cpB], writes=[rsB])
            op(ACT, lambda e: e.activation(out=rs[:, c_lo:1024], in_=rs[:, c_lo:1024], func=AF.Exp, scale=-0.5), reads=[rsB], writes=[rsB])
            if full:
                op(DVE, lambda e: e.scalar_tensor_tensor(out=qkv[:, 0:512], in0=qkv[:, 0:512], scalar=128.0 ** -0.5, in1=rs[:, 0:512], op0=ALU.mult, op1=ALU.mult), reads=[qkvB, rsB], writes=[qkvB])
            op(DVE, lambda e: e.tensor_tensor(out=qkv[:, 512:1024], in0=qkv[:, 512:1024], in1=rs[:, 512:1024], op=ALU.mult), reads=[qkvB, rsB], writes=[qkvB])
            if full:
                op(ACT, lambda e: e.activation(out=etmp[:, 0:512], in_=zT[:, :], func=AF.Exp, scale=-1.0), reads=[zB, etmpB], writes=[etmpB])
                op(ACT, lambda e: e.activation(out=etmp[:, 0:512], in_=etmp[:, 0:512], func=AF.Ln, bias=1.0), reads=[etmpB], writes=[etmpB])
                op(ACT, lambda e: e.activation(out=etmp[:, 0:512], in_=etmp[:, 0:512], func=AF.Exp, scale=-1.0), reads=[etmpB], writes=[etmpB])
                op(DVE, lambda e: e.tensor_tensor(out=zT[:, :], in0=zT[:, :], in1=etmp[:, 0:512], op=ALU.mult), reads=[etmpB, zB], writes=[zB])
            cap_ = S.capture
            S.capture = None
            s3, s4 = cap_[i3:i4], cap_[i4:i5]
            mer = []
            i_, j_ = 0, 0
            while i_ < len(s3) or j_ < len(s4):
                if j_ < len(s4) and (i_ >= len(s3) or j_ * max(len(s3), 1) <= i_ * len(s4)):
                    mer.append(s4[j_]); j_ += 1
                else:
                    mer.append(s3[i_]); i_ += 1
            return cap_[:i3] + mer + cap_[i5:]

        def chain(hh, t):
            pp = t % 2
            qkv, qkvB = qkv_b[pp], qkvB_b[pp]
            zT, zB = zT_b[pp], zB_b[pp]
            sm, smB = sm_b[pp], smB_b[pp]
            glb, glB = glb_b[pp], glB_b[pp]
            B_ = HB[hh]
            kbg, kbgB = B_["kbg"]; kdec, kdecB = B_["kdec"]; vbeta, vbetaB = B_["vbeta"]
            tm1, tm1B = B_["tm1"]; E1, E1B = B_["E1"]; Lm, LmB = B_["Lm"]; AT, ATB = B_["AT"]
            X = [B_["X0"], B_["X1"]]; Y = [B_["Y0"], B_["Y1"]]; Pm = [B_["P0"], B_["P1"]]
            wT, wTB = B_["wT"]; um, umB = B_["um"]; vnew, vnewB = B_["vnew"]
            o1s, o1sB = tm1, tm1B
            om, omB = E1, E1B
            on, onB = Lm, LmB
            qT = qkv[:, hh * 128:(hh + 1) * 128]
            kT = qkv[:, 512 + hh * 128:512 + (hh + 1) * 128]
            vT = qkv[:, 1024 + hh * 128:1024 + (hh + 1) * 128]
            PH = psum[4 + hh]
            PB = psB[4 + hh][0]
            Q0, Q1, Q2, Q3 = PH[:, 0:128], PH[:, 128:256], PH[:, 256:384], PH[:, 384:512]
            Gh = psum[3][:, hh * 128:(hh + 1) * 128]
            GhB = psB[3]
            op(PE, lambda e: e.transpose(Q0, kT, ident), reads=[qkvB, cpB], writes=[PB])
            op(PE, lambda e: e.transpose(Q1, vT, ident), reads=[qkvB, cpB], writes=[PB])
            op(PE, lambda e: e.matmul(Q2, lhsT=kT, rhs=kT, start=True, stop=True), reads=[qkvB], writes=[PB])
            if full:
                op(PE, lambda e: e.matmul(Q3, lhsT=kT, rhs=qT, start=True, stop=True), reads=[qkvB], writes=[PB])
            yield
            op(DVE, lambda e: e.tensor_tensor(out=tm1[:, :], in0=maskL, in1=Gh, op=ALU.subtract), reads=GhB + [cpB], writes=[tm1B])
            yield
            op(ACT, lambda e: e.activation(out=E1[:, :], in_=tm1[:, :], func=AF.Exp, bias=sm[:, 16 + hh:17 + hh]), reads=[tm1B, smB], writes=[E1B])
            yield
            op(ACT, lambda e: e.activation(out=kbg[:, :], in_=Q0, func=AF.Copy, scale=sm[:, 28 + hh:29 + hh]), reads=[PB, smB], writes=[kbgB])
            op(ACT, lambda e: e.activation(out=vbeta[:, :], in_=Q1, func=AF.Copy, scale=sm[:, 8 + hh:9 + hh]), reads=[PB, smB], writes=[vbetaB])
            yield
            op(DVE, lambda e: e.tensor_scalar(out=kdec[:, :], in0=Q0, scalar1=sm[:, 32 + hh:33 + hh], scalar2=None, op0=ALU.mult), reads=[PB, smB], writes=[kdecB])
            op(DVE, lambda e: e.scalar_tensor_tensor(out=Lm[:, :], in0=Q2, scalar=sm[:, 8 + hh:9 + hh], in1=E1[:, :], op0=ALU.mult, op1=ALU.mult), reads=[PB, smB, E1B], writes=[LmB])
            yield
            if full:
                op(DVE, lambda e: e.tensor_tensor(out=tm1[:, :], in0=maskU, in1=Gh, op=ALU.add), reads=GhB + [cpB, tm1B], writes=[tm1B])
                yield
                op(ACT, lambda e: e.activation(out=E1[:, :], in_=tm1[:, :], func=AF.Exp, bias=sm[:, 24 + hh:25 + hh]), reads=[tm1B, smB, E1B], writes=[E1B])
                yield
                op(DVE, lambda e: e.tensor_tensor(out=AT[:, :], in0=Q3, in1=E1[:, :], op=ALU.mult), reads=[PB, E1B], writes=[ATB])
                yield
            op(PE, lambda e: e.transpose(Q0, Lm[:, :], ident), reads=[LmB, cpB], writes=[PB])
            yield
            X0, X0B = X[0]
            P0, P0B = Pm[0]
            op(ACT, lambda e: e.activation(out=X0[:, :], in_=Q0, func=AF.Copy), reads=[PB], writes=[X0B])
            op(DVE, lambda e: e.tensor_tensor(out=P0[:, :], in0=ident, in1=Q0, op=ALU.subtract), reads=[PB, cpB], writes=[P0B])
            yield
            Xc, XcB = X0, X0B
            Yc, YcB = Lm, LmB
            Pc, PcB = P0, P0B
            for k in range(1, 6):
                Yn, YnB = Y[k % 2]
                Xn, XnB = X[k % 2]
                Pn, PnB = Pm[k % 2]
                op(PE, (lambda Xc, Yc: lambda e: e.matmul(Q1, lhsT=Xc[:, :], rhs=Yc[:, :], start=True, stop=True))(Xc, Yc), reads=[XcB, YcB], writes=[PB])
                if k < 5:
                    op(PE, (lambda Xc, Yc: lambda e: e.matmul(Q2, lhsT=Yc[:, :], rhs=Xc[:, :], start=True, stop=True))(Xc, Yc), reads=[XcB, YcB], writes=[PB])
                yield
                op(ACT, (lambda Yn: lambda e: e.activation(out=Yn[:, :], in_=Q1, func=AF.Copy))(Yn), reads=[PB], writes=[YnB])
                if k < 5:
                    op(ACT, (lambda Xn: lambda e: e.activation(out=Xn[:, :], in_=Q2, func=AF.Copy))(Xn), reads=[PB], writes=[XnB])
                yield
                op(PE, (lambda Yn, Pc: lambda e: e.matmul(Q3, lhsT=Yn[:, :], rhs=Pc[:, :], start=True, stop=True))(Yn, Pc), reads=[YnB, PcB], writes=[PB])
                yield
                op(DVE, (lambda Pn, Pc: lambda e: e.tensor_tensor(out=Pn[:, :], in0=Pc[:, :], in1=Q3, op=ALU.add))(Pn, Pc), reads=[PcB, PB], writes=[PnB])
                yield
                Xc, XcB, Yc, YcB, Pc, PcB = Xn, XnB, Yn, YnB, Pn, PnB
            op(PE, (lambda Pc: lambda e: e.matmul(Q0, lhsT=kbg[:, :], rhs=Pc[:, :], start=True, stop=True))(Pc), reads=[kbgB, PcB], writes=[PB])
            op(PE, (lambda Pc: lambda e: e.matmul(Q1, lhsT=Pc[:, :], rhs=vbeta[:, :], start=True, stop=True))(Pc), reads=[vbetaB, PcB], writes=[PB])
            yield
            op(ACT, lambda e: e.activation(out=wT[:, :], in_=Q0, func=AF.Copy), reads=[PB], writes=[wTB])
            op(ACT, lambda e: e.activation(out=um[:, :], in_=Q1, func=AF.Copy), reads=[PB], writes=[umB])
            yield
            for half in range(2):
                r0, r1 = half * 64, half * 64 + 64
                sp_ = self.spar_h[hh]
                Scur = Sst[sp_][:, hh * 128:(hh + 1) * 128]; ScurB = SsB[sp_][hh]
                Snew = Sst[1 - sp_][:, hh * 128:(hh + 1) * 128]; SnewB = SsB[1 - sp_][hh]
                self.spar_h[hh] = 1 - sp_
                op(PE, (lambda r0, r1, Scur: lambda e: e.matmul(PH[r0:r1, 256:384], lhsT=wT[:, r0:r1], rhs=Scur, start=True, stop=True))(r0, r1, Scur), reads=[wTB, ScurB], writes=[PB])
                yield
                op(DVE, (lambda r0, r1: lambda e: e.tensor_tensor(out=vnew[r0:r1, :], in0=um[r0:r1, :], in1=PH[r0:r1, 256:384], op=ALU.subtract))(r0, r1), reads=[umB, PB], writes=[vnewB])
                yield
                if full:
                    op(PE, (lambda r0, r1, Scur: lambda e: e.matmul(PH[r0:r1, 0:128], lhsT=qT[:, r0:r1], rhs=Scur, start=True, stop=True))(r0, r1, Scur), reads=[qkvB, ScurB], writes=[PB])
                    op(PE, (lambda r0, r1: lambda e: e.matmul(PH[r0:r1, 128:256], lhsT=AT[r0:r1, r0:r1], rhs=vnew[r0:r1, :], start=True, stop=True))(r0, r1), reads=[ATB, vnewB], writes=[PB])
                op(PE, (lambda r0, r1: lambda e: e.matmul(Q3, lhsT=kdec[r0:r1, :], rhs=vnew[r0:r1, :], start=True, stop=True))(r0, r1), reads=[kdecB, vnewB], writes=[PB])
                yield
                op(DVE, (lambda Snew, Scur, half: lambda e: e.scalar_tensor_tensor(out=Snew, in0=Scur, scalar=glb[:, hh * 2 + half:hh * 2 + half + 1], in1=Q3, op0=ALU.mult, op1=ALU.add))(Snew, Scur, half),
                   reads=[ScurB, glB, PB], writes=[SnewB])
                yield
            if full:
                c0 = 40 + hh * 3
                op(ACT, lambda e: e.activation(out=o1s[:, :], in_=Q0, func=AF.Copy, scale=sm[:, 20 + hh:21 + hh]), reads=[PB, smB], writes=[o1sB])
                yield
                op(DVE, lambda e: e.tensor_tensor(out=om[:, :], in0=o1s[:, :], in1=Q1, op=ALU.add), reads=[o1sB, PB], writes=[omB])
                yield
                op(ACT, lambda e: e.activation(out=on[:, :], in_=om[:, :], func=AF.Square, accum_out=sm[:, c0:c0 + 1]), reads=[omB], writes=[onB, smB])
                yield
                op(POOL, lambda e: e.tensor_scalar(out=sm[:, c0 + 1:c0 + 2], in0=sm[:, c0:c0 + 1], scalar1=1.0 / 128, scalar2=EPS, op0=ALU.mult, op1=ALU.add), reads=[smB], writes=[smB])
                op(POOL, lambda e: e.tensor_tensor(out=sm[:, c0 + 2:c0 + 3], in0=sm[:, c0 + 1:c0 + 2], in1=cs("mhalf"), op=ALU.pow), reads=[smB, cpB], writes=[smB])
                yield
                op(DVE, lambda e: e.scalar_tensor_tensor(out=on[:, :], in0=om[:, :], scalar=sm[:, c0 + 2:c0 + 3], in1=cs("gon"), op0=ALU.mult, op1=ALU.mult), reads=[omB, smB, cpB], writes=[onB])
                yield
                op(PE, lambda e: e.transpose(Q2, on[:, :], ident), reads=[onB, cpB], writes=[PB])
                yield
                yg_ap = self.ygT[:, hh * T + t * 128: hh * T + (t + 1) * 128]
                op(DVE, lambda e: e.tensor_tensor(out=yg_ap, in0=Q2, in1=zT[:, hh * 128:(hh + 1) * 128], op=ALU.mult), reads=[PB, zB], writes=[self.ygB[t]])
                yield

        for it in pre_ops(0):
            S.replay(it)
        for t in range(NT):
            pend = pre_ops(t + 1) if t + 1 < NT else []
            per = (len(pend) + 39) // 40
            S0 = int(os.environ.get("KSTAG", 4))
            per = (len(pend) + 39 + 3 * S0) // (40 + 3 * S0)
            alive = [(hh, chain(hh, t)) for hh in range(4)]
            pi = 0
            rnd = 0
            while alive or pi < len(pend):
                nxt = []
                for hh, g_ in alive:
                    if rnd < hh * S0:
                        nxt.append((hh, g_))
                        continue
                    try:
                        next(g_)
                        nxt.append((hh, g_))
                    except StopIteration:
                        pass
                alive = nxt
                for it in pend[pi:pi + per]:
                    S.replay(it)
                pi += per
                rnd += 1
        op(DVE, lambda e: e.tensor_copy(out=pchv, in_=pcv0[:, :, 0:3]), reads=[pcB], writes=[self.pchB])
        es.close()

    def conv_phase(self, wm_in, wm_out, hhalo, hhB):
        import os
        nc, S = self.nc, self.S
        op = S.op
        cs, cpB = self.cs, self.cpB
        h, hB = self.h, self.hB
        psum, psB = self.psum, self.psB
        es = contextlib.ExitStack()
        tag = "cv"
        wc = self.sb("wc", 8 * 1536, BF16, es); wcB = [Buf() for _ in range(6)]
        wo = self.sb("wo", 8 * 1024, BF16, es); woB = [Buf() for _ in range(4)]
        xs = self.sb("xs_cv", D, F32, es); xsB = Buf()
        self.junk = self.sb("junk_cv", D, BF16, es); self.junkB = Buf()
        xn = self.sb("xn_cv", 8 * 128, BF16, es); xnB = Buf()
        mpc = self.sb("mpc", 4 * 130, F32, es); mpcB = Buf()
        mpc2 = self.sb("mpc2", 4 * 130, F32, es); mpc2B = Buf()
        cbs0 = self.sb("cbs0", 512, F32, es); cbs0B = Buf()
        cbs1 = self.sb("cbs1", 512, F32, es); cbs1B = Buf()
        cct = self.sb("cct", 512, F32, es); cctB = Buf()
        cacc = self.sb("cacc_cv", 512, F32, es); caccB = Buf()
        yv = self.sb("yv", 512, F32, es); yvB = Buf()
        sq = self.sb("sq_cv", 512, F32, es); sqB = Buf()
        rs = self.sb("rs_cv", 512, F32, es); rsB = Buf()
        ycT = self.sb("ycT", 512, BF16, es); ycB = Buf()
        motmp = [self.sb("motmp%d" % i, 512, F32, es) for i in range(2)]; motB = [Buf(), Buf()]
        new_bufs = wcB + woB + [mpc2B, cbs0B, cbs1B, xsB, self.junkB, xnB, mpcB, cctB, caccB, yvB, sqB, rsB, ycB] + motB
        S.alias(new_bufs, getattr(self, "phase_bufs", []))
        self.phase_bufs = new_bufs
        wm_v = wm_in.rearrange("(c p) n -> p c n", p=128)
        for col in range(0, 1536, 256):
            base = (col // 256) * 2048
            self.load_cast(wc[:, base:base + 2048], wcB[col // 256], wm_v[:, :, col:col + 256], (8, 256))
        wo_v = wm_out.rearrange("(c p) n -> p c n", p=128)
        for i in range(4):
            self.load_cast(wo[:, i * 2048:(i + 1) * 2048], woB[i], wo_v[:, 2 * i:2 * i + 2, :], (2, 1024))
        op(POOL, lambda e: e.memset(mpc[:, :], 0.0), writes=[mpcB])
        op(POOL, lambda e: e.memset(mpc2[:, :], 0.0), writes=[mpc2B])
        ident, blk64 = cs("ident"), cs("blk64")
        csw, cgain = cs("csw"), cs("cgain")
        ygT, ygB = self.ygT, self.ygB
        mpc_b = [mpc, mpc2]; mpcB_b = [mpcB, mpc2B]
        cbs_b = [cbs0, cbs1]; cbsB_b = [cbs0B, cbs1B]

        def capA(t):
            p = t % 2
            mp, mpB = mpc_b[p], mpcB_b[p]
            mo, moB = mpc_b[1 - p], mpcB_b[1 - p]
            mpv = mp[:, :].rearrange("p (a b) -> p a b", a=4)
            mov = mo[:, :].rearrange("p (a b) -> p a b", a=4)
            S.capture = []
            if t < 0:
                src, srcB = hhalo[:, :], hhB
            else:
                src, srcB = h[:, t * D:(t + 1) * D], hB[t]
            self.norm_transpose(src, srcB, "nm", xn, 0, 128, xnB, xs, xsB, (0, 1))
            for grp in range(3):
                if t < 0 and grp == 0:
                    continue
                pb = 2 + grp
                for q in range(4):
                    j = grp * 4 + q
                    for c in range(8):
                        lhsT = wc[:, (j // 2) * 2048 + c * 256 + (j % 2) * 128: (j // 2) * 2048 + c * 256 + (j % 2) * 128 + 128]
                        rhs = xn[:, c * 128:(c + 1) * 128]
                        op(PE, (lambda pb, q, lhsT, rhs, c: lambda e: e.matmul(psum[pb][:, q * 128:(q + 1) * 128], lhsT=lhsT, rhs=rhs, start=(c == 0), stop=(c == 7)))(pb, q, lhsT, rhs, c),
                           reads=[wcB[j // 2], xnB], writes=[psB[pb][q]])
                if grp == 0:
                    op(ACT, (lambda p: lambda e: e.activation(out=cbs_b[p][:, :], in_=psum[2][:, :], func=AF.Copy))(p), reads=psB[2], writes=[cbsB_b[p]])
                if grp == 1:
                    op(ACT, lambda e: e.activation(out=cct[:, :], in_=psum[3][:, :], func=AF.Copy), reads=psB[3], writes=[cctB])
            op(DVE, lambda e: e.tensor_tensor(out=mpv[:, :, 2:130], in0=cct[:, :].rearrange("p (a b) -> p a b", a=4), in1=psum[4][:, :].rearrange("p (a b) -> p a b", a=4), op=ALU.mult),
               reads=[cctB, mpB] + psB[4], writes=[mpB])
            op(DVE, lambda e: e.tensor_copy(out=mpv[:, :, 0:2], in_=mov[:, :, 128:130]), reads=[moB, mpB], writes=[mpB])
            ops_ = S.capture
            S.capture = None
            return ops_

        def capB(t):
            p = t % 2
            mp, mpB = mpc_b[p], mpcB_b[p]
            cbs, cbsB = cbs_b[p], cbsB_b[p]
            S.capture = []
            for j in range(4):
                op(DVE, (lambda j: lambda e: e.tensor_scalar(out=cacc[:, j * 128:(j + 1) * 128], in0=mp[:, j * 130:j * 130 + 128], scalar1=csw[:, j * 3:j * 3 + 1], scalar2=None, op0=ALU.mult))(j),
                   reads=[mpB, cpB], writes=[caccB])
                for k in range(1, 3):
                    op(DVE, (lambda j, k: lambda e: e.scalar_tensor_tensor(out=cacc[:, j * 128:(j + 1) * 128], in0=mp[:, j * 130 + k:j * 130 + k + 128], scalar=csw[:, j * 3 + k:j * 3 + k + 1], in1=cacc[:, j * 128:(j + 1) * 128], op0=ALU.mult, op1=ALU.add))(j, k),
                       reads=[mpB, cpB, caccB], writes=[caccB])
            op(DVE, lambda e: e.tensor_tensor(out=yv[:, :], in0=cacc[:, :], in1=cbs[:, :], op=ALU.mult), reads=[caccB, cbsB], writes=[yvB])
            op(ACT, lambda e: e.activation(out=sq[:, :], in_=yv[:, :], func=AF.Square), reads=[yvB], writes=[sqB])
            op(PE, lambda e: e.matmul(psum[5][:, :], lhsT=blk64, rhs=sq[:, :], start=True, stop=True), reads=[sqB, cpB], writes=psB[5])
            op(ACT, lambda e: e.activation(out=rs[:, :], in_=psum[5][:, :], func=AF.Ln, bias=cs("eps")), reads=psB[5] + [cpB], writes=[rsB])
            op(ACT, lambda e: e.activation(out=rs[:, :], in_=rs[:, :], func=AF.Exp, scale=-0.5), reads=[rsB], writes=[rsB])
            for j in range(4):
                op(DVE, (lambda j: lambda e: e.scalar_tensor_tensor(out=ycT[:, j * 128:(j + 1) * 128], in0=yv[:, j * 128:(j + 1) * 128], scalar=cgain[:, j:j + 1], in1=rs[:, j * 128:(j + 1) * 128], op0=ALU.mult, op1=ALU.mult))(j),
                   reads=[yvB, rsB, cpB], writes=[ycB])
            for hh in range(2):
                pb = 6 + hh
                for j in range(8):
                    if j < 4:
                        lhsT = ycT[:, j * 128:(j + 1) * 128]
                        rd = [ycB]
                    else:
                        lhsT = ygT[:, (j - 4) * T + t * 128:(j - 4) * T + (t + 1) * 128]
                        rd = [ygB[t]]
                    rhs = wo[:, j * 1024 + hh * 512: j * 1024 + (hh + 1) * 512]
                    op(PE, (lambda pb, lhsT, rhs, j: lambda e: e.matmul(psum[pb][:, :], lhsT=lhsT, rhs=rhs, start=(j == 0), stop=(j == 7)))(pb, lhsT, rhs, j),
                       reads=rd + [woB[j // 2]], writes=psB[pb])
                hap = h[:, t * D + hh * 512: t * D + (hh + 1) * 512]
                op(ACT, (lambda pb, hh: lambda e: e.activation(out=motmp[hh][:, :], in_=psum[pb][:, :], func=AF.Copy))(pb, hh), reads=psB[pb], writes=[motB[hh]])
                op(DVE, (lambda hh, hap: lambda e: e.tensor_tensor(out=hap, in0=hap, in1=motmp[hh][:, :], op=ALU.add))(hh, hap), reads=[motB[hh], hB[t]], writes=[hB[t]])
            ops_ = S.capture
            S.capture = None
            return ops_

        for it in capA(-1):
            S.replay(it)
        for it in capA(0):
            S.replay(it)
        for t in range(NT):
            A = capA(t + 1) if t + 1 < NT else []
            B_ = capB(t)
            na, nb = len(A), len(B_)
            ia = ib = 0
            while ia < na or ib < nb:
                if ib < nb and (ia >= na or ib * max(na, 1) <= ia * nb):
                    S.replay(B_[ib]); ib += 1
                else:
                    S.replay(A[ia]); ia += 1
        es.close()

    def final(self, out, fn_bc):
        S = self.S
        op = S.op
        cs, cpB = self.cs, self.cpB
        h, hB = self.h, self.hB
        es = contextlib.ExitStack()
        ot = [self.sb("ot%d" % i, D, F32, es) for i in range(2)]
        otB = [Buf(), Buf()]
        fs = self.sb("fs", 64, F32, es); fsB = Buf()
        junk = self.sb("junk_f", D, BF16, es); junkB = Buf()
        fnb = self.sb("fnb", D, F32, es); fnbB = Buf()
        new_bufs = otB + [fsB, junkB, fnbB]
        S.alias(new_bufs, getattr(self, "phase_bufs", []))
        self.phase_bufs = new_bufs
        S.dma(SP, lambda e: e.dma_start(out=fnb[:], in_=fn_bc), "const2", S.new_group(), writes=[fnbB])
        import os
        if os.environ.get("KRAWOUT"):
            for t in range(NT):
                S.dma(SP, (lambda t: lambda e: e.dma_start(out=out[t * 128:(t + 1) * 128, :], in_=h[:, t * D:(t + 1) * D]))(t), "out%d" % (t % 2), S.new_group(), reads=[hB[t]])
            es.close()
            return
        for t in range(NT):
            k = t % 2
            c0 = (t % 16) * 3
            hs = h[:, t * D:(t + 1) * D]
            op(ACT, (lambda hs, c0: lambda e: e.activation(out=junk[:, :], in_=hs, func=AF.Square, accum_out=fs[:, c0:c0 + 1]))(hs, c0), reads=[hB[t]], writes=[junkB, fsB])
            op(POOL, (lambda c0: lambda e: e.tensor_scalar(out=fs[:, c0 + 1:c0 + 2], in0=fs[:, c0:c0 + 1], scalar1=1.0 / D, scalar2=EPS, op0=ALU.mult, op1=ALU.add))(c0), reads=[fsB], writes=[fsB])
            op(POOL, (lambda c0: lambda e: e.tensor_tensor(out=fs[:, c0 + 2:c0 + 3], in0=fs[:, c0 + 1:c0 + 2], in1=cs("mhalf"), op=ALU.pow))(c0), reads=[fsB, cpB], writes=[fsB])
            op(DVE, (lambda hs, c0, k: lambda e: e.scalar_tensor_tensor(out=ot[k][:, :], in0=hs, scalar=fs[:, c0 + 2:c0 + 3], in1=fnb[:, :], op0=ALU.mult, op1=ALU.mult))(hs, c0, k),
               reads=[hB[t], fsB, fnbB], writes=[otB[k]])
            S.dma(SP, (lambda t, k: lambda e: e.dma_start(out=out[t * 128:(t + 1) * 128, :], in_=ot[k][:, :]))(t, k), "out%d" % k, S.new_group(), reads=[otB[k]])
        es.close()


def _pack_layout():
    names = [("ident", 128), ("ones", 128), ("triU", 128), ("maskL", 128), ("maskU", 128), ("blk64", 128),
             ("gon", 128), ("n1", 8), ("nm", 8), ("n2", 8), ("cwg", 48), ("csw", 12), ("cgain", 4),
             ("alog", 4), ("dtb", 4), ("mhalf", 1), ("eps", 1)]
    lay = {}
    off = 0
    for n, w in names:
        lay[n] = (off, off + w)
        off += w
    return lay, off


_CP, _CPK_COLS = _pack_layout()
Builder.CP = _CP
Builder.CPK_COLS = _CPK_COLS


def _pack_consts(inp):
    f = np.float32
    cp = np.zeros((128, _CPK_COLS), f)

    def put(name, arr):
        a, b = _CP[name]
        cp[:, a:b] = np.asarray(arr, f).reshape(128, b - a)

    idx = np.arange(128)
    same = (idx[:, None] // 64) == (idx[None, :] // 64)
    put("ident", np.eye(128))
    put("ones", np.ones((128, 128)))
    put("triU", (same & (idx[:, None] <= idx[None, :])))
    put("maskL", np.where(same & (idx[:, None] > idx[None, :]), 0.0, NEG))
    put("maskU", np.where(same & (idx[:, None] <= idx[None, :]), 0.0, NEG))
    put("blk64", same.astype(f) / 64.0)
    put("gon", np.broadcast_to(inp["gdn_out_norm"].reshape(1, 128), (128, 128)))
    put("n1", inp["ffn1_norm"].reshape(8, 128).T)
    put("nm", inp["mix_norm"].reshape(8, 128).T)
    put("n2", inp["ffn2_norm"].reshape(8, 128).T)
    put("cwg", inp["gdn_conv_w"].reshape(4, 12, 128).transpose(2, 1, 0).reshape(128, 48))
    put("csw", inp["conv_short_w"].reshape(3, 4, 128).transpose(2, 1, 0).reshape(128, 12))
    put("cgain", inp["conv_out_norm"].reshape(4, 128).T)
    put("alog", np.broadcast_to(inp["gdn_A_log"].reshape(1, 4), (128, 4)))
    put("dtb", np.broadcast_to(inp["gdn_dt_bias"].reshape(1, 4), (128, 4)))
    put("mhalf", np.full((128, 1), -0.5))
    put("eps", np.full((128, 1), EPS))
    return cp


_NC_CACHE = {}


def _get_nc(debug=False):
    if debug not in _NC_CACHE:
        b = Builder(debug=debug)
        b.spar_h = [0, 0, 0, 0]
        _NC_CACHE[debug] = (b.build(), b)
    return _NC_CACHE[debug]


def kernel(debug=False, **inputs):
    inp = {k: np.asarray(v) for k, v in inputs.items()}
    x = inp["x"].astype(np.float32, copy=False)
    nc, b = _get_nc(debug)
    cp = _pack_consts(inp)
    fn_bc = np.ascontiguousarray(np.broadcast_to(inp["final_norm"].reshape(1, D).astype(np.float32), (128, D)))
    shared = {
        "w1_in": np.ascontiguousarray(inp["ffn1_w_in"][0]), "w1_out": np.ascontiguousarray(inp["ffn1_w_out"][0]),
        "w2_in": np.ascontiguousarray(inp["ffn2_w_in"][0]), "w2_out": np.ascontiguousarray(inp["ffn2_w_out"][0]),
        "wm_in": np.ascontiguousarray(inp["w_mix_in"][0]), "wm_out": np.ascontiguousarray(inp["w_mix_out"][0]),
        "cpk": cp, "fn_bc": fn_bc,
    }
    zeros = np.zeros((T, D), np.float32)
    in_maps = []
    for c in range(8):
        bi, half = c // 2, c % 2
        m = dict(shared)
        m["x_own"] = np.ascontiguousarray(x[bi, half * T:(half + 1) * T])
        m["x_pre"] = zeros if half == 0 else np.ascontiguousarray(x[bi, 0:T])
        in_maps.append(m)
    import os
    ncores = int(os.environ.get("KCORES", 8))
    res = run_bass_kernel_spmd(nc, in_maps[:ncores], core_ids=list(range(ncores)))
    outp = np.zeros((4, 2 * T, D), np.float32)
    for c in range(ncores):
        outp[c // 2, (c % 2) * T:(c % 2 + 1) * T] = res.results[c]["out"]
    if debug:
        return outp, res.results
    return outp
```

```python
import contextlib
import numpy as np
import concourse.bass as bass
import concourse.mybir as mybir
from concourse.bass_utils import run_bass_kernel_spmd

F32 = mybir.dt.float32
BF16 = mybir.dt.bfloat16
AF = mybir.ActivationFunctionType
ALU = mybir.AluOpType

PE, ACT, DVE, POOL, SP = "pe", "act", "dve", "pool", "sp"
COMPUTE = (PE, ACT, DVE, POOL)

D = 1024
DFF = 2816
T = 2048
NT = T // 128
NB = T // 512
EPS = 1e-6
GW0 = 1536
NG = 2056
NEG = -1.0e30


class Buf:
    __slots__ = ("name", "last_w", "readers", "excl")

    def __init__(self, name="", excl=False):
        self.name = name
        self.last_w = None
        self.readers = []
        self.excl = excl


class Op:
    __slots__ = ("eng", "fn", "deps", "needs_inc", "cnt", "is_dma", "key", "grp", "idx")

    def __init__(self, eng, fn, is_dma=False, key=None, grp=None):
        self.eng = eng
        self.fn = fn
        self.deps = []
        self.needs_inc = False
        self.cnt = 0
        self.is_dma = is_dma
        self.key = key
        self.grp = grp


class Sched:
    def __init__(self):
        self.ops = []
        self.grp_ctr = 0

    def new_group(self):
        self.grp_ctr += 1
        return self.grp_ctr

    def _add(self, op, reads, writes):
        if getattr(self, "capture", None) is not None:
            self.capture.append((op, list(reads), list(writes)))
            return op
        return self._add_real(op, reads, writes)

    def replay(self, item):
        return self._add_real(*item)

    def _add_real(self, op, reads, writes):
        op.idx = len(self.ops)
        ex = [b for b in reads if b.excl]
        if ex:
            reads = [b for b in reads if not b.excl]
            writes = list(writes) + ex
        deps = {}
        for b in reads:
            if b.last_w is not None:
                deps[id(b.last_w)] = b.last_w
        for b in writes:
            if b.last_w is not None:
                deps[id(b.last_w)] = b.last_w
            for r in b.readers:
                deps[id(r)] = r
        latest = {}
        for d in deps.values():
            if d is op:
                continue
            if (not d.is_dma) and (not op.is_dma) and d.eng == PE and op.eng == PE:
                continue
            if d.is_dma:
                op.deps.append(d)
            else:
                cur = latest.get(d.eng)
                if cur is None or d.idx > cur.idx:
                    latest[d.eng] = d
        op.deps.extend(latest.values())
        for b in reads:
            if op.is_dma:
                b.readers.append(op)
            else:
                b.readers = [r for r in b.readers if r.is_dma or r.eng != op.eng]
                b.readers.append(op)
        for b in writes:
            b.last_w = op
            b.readers = []
        self.ops.append(op)
        return op

    def op(self, eng, fn, reads=(), writes=()):
        return self._add(Op(eng, fn), reads, writes)

    def dma(self, queue, fn, key, grp, reads=(), writes=()):
        return self._add(Op(queue, fn, is_dma=True, key=key, grp=grp), reads, writes)

    def alias(self, new_bufs, old_bufs):
        acc = {}
        for b in old_bufs:
            if b.last_w is not None:
                acc[id(b.last_w)] = b.last_w
            for r in b.readers:
                acc[id(r)] = r
        for nb in new_bufs:
            nb.readers = list(acc.values())

    def emit(self, nc, final_wait_keys=()):
        ops = self.ops
        for o in ops:
            for d in o.deps:
                d.needs_inc = True
        cnt = {}
        grp_end = {}
        for o in ops:
            if o.is_dma:
                k = ("dma", o.key)
                cnt[k] = cnt.get(k, 0) + 1
                o.cnt = cnt[k]
                grp_end[(o.key, o.grp)] = o.cnt
            elif o.needs_inc:
                cnt[o.eng] = cnt.get(o.eng, 0) + 1
                o.cnt = cnt[o.eng]
        dma_keys = sorted({o.key for o in ops if o.is_dma})
        streams = {e: [o for o in ops if o.eng == e] for e in (PE, ACT, DVE, POOL, SP)}
        self.stats = {e: len(s) for e, s in streams.items()}
        self.stats["incs"] = dict(cnt)

        import os
        SEG = int(os.environ.get('KSEG', 1500))
        with contextlib.ExitStack() as es:
            sems = {}
            for e in COMPUTE:
                nseg = (cnt.get(e, 0) + SEG - 1) // SEG + 1
                sems[e] = [es.enter_context(nc.semaphore("s_%s_%d" % (e, j))) for j in range(nseg)]
            for k in dma_keys:
                sems[("dma", k)] = es.enter_context(nc.semaphore("d_" + str(k)))
            block = es.enter_context(nc.Block())

            def run_stream(engname, eng):
                waited = {}
                for o in streams[engname]:
                    for d in o.deps:
                        if d.is_dma:
                            sk = ("dma", d.key)
                            val = 16 * grp_end[(d.key, d.grp)]
                            sem = sems[sk]
                        else:
                            seg = (d.cnt - 1) // SEG
                            sk = (d.eng, seg)
                            val = (d.cnt - 1) % SEG + 1
                            sem = sems[d.eng][seg]
                            if any(k2[0] == d.eng and k2[1] > seg for k2 in waited if isinstance(k2, tuple) and k2[0] == d.eng):
                                continue
                        if waited.get(sk, 0) >= val:
                            continue
                        waited[sk] = val
                        eng.wait_ge(sem, val)
                    ins = o.fn(eng)
                    if o.is_dma:
                        ins.then_inc(sems[("dma", o.key)], 16)
                    elif o.needs_inc:
                        ins.then_inc(sems[o.eng][(o.cnt - 1) // SEG], 1)
                if engname == SP:
                    for k in final_wait_keys:
                        eng.wait_ge(sems[("dma", k)], 16 * cnt[("dma", k)])

            @block.sync
            def _(e):
                run_stream(SP, e)

            @block.tensor
            def _(e):
                run_stream(PE, e)

            @block.scalar
            def _(e):
                run_stream(ACT, e)

            @block.vector
            def _(e):
                run_stream(DVE, e)

            @block.gpsimd
            def _(e):
                run_stream(POOL, e)


class Builder:
    def __init__(self, debug=False):
        self.debug = debug
        self.nc = bass.Bass("TRN2", target_bir_lowering=False)
        self.S = Sched()
        self.es = contextlib.ExitStack()
        self.dbg_outs = []
        self.dbg_keys = []
        self.rr = 0

    def sb(self, name, cols, dt=F32, es=None):
        return (es or self.es).enter_context(self.nc.sbuf_tensor(name, [128, cols], dt))

    def dram_in(self, name, shape, dt=F32):
        return self.nc.dram_tensor(name, list(shape), dt, kind="ExternalInput").ap()

    def dram_out(self, name, shape, dt=F32):
        return self.nc.dram_tensor(name, list(shape), dt, kind="ExternalOutput").ap()

    def dbg(self, name, ap, cols, bufs, dt=F32):
        if not self.debug:
            return
        o = self.dram_out("dbg_" + name, [128, cols], dt)
        self.dbg_keys.append("dbg_" + name)
        self.S.dma(SP, lambda e: e.dma_start(out=o, in_=ap), "dbg_" + name, self.S.new_group(), reads=bufs)

    def ew(self):
        self.rr += 1
        return ACT if (self.rr & 1) else DVE

    def build(self):
        nc, S = self.nc, self.S
        op = S.op
        x_pre = self.dram_in("x_pre", [T, D])
        x_own = self.dram_in("x_own", [T, D])
        w1_in = self.dram_in("w1_in", [D, 2 * DFF])
        w1_out = self.dram_in("w1_out", [DFF, D])
        w2_in = self.dram_in("w2_in", [D, 2 * DFF])
        w2_out = self.dram_in("w2_out", [DFF, D])
        wm_in = self.dram_in("wm_in", [D, 3592])
        wm_out = self.dram_in("wm_out", [D, D])
        cpk = self.dram_in("cpk", [128, self.CPK_COLS])
        fn_bc = self.dram_in("fn_bc", [128, D])
        out = self.dram_out("out", [T, D])
        self.out_grp = S.new_group()

        h = self.sb("h", NT * D)
        hB = [Buf("h%d" % t) for t in range(NT)]
        stage = [self.sb("stage%d" % i, 2048) for i in range(2)]
        stB = [Buf("st%d" % i) for i in range(2)]
        self.stage, self.stB, self.st_i = stage, stB, 0
        import os
        self.cast_order = os.environ.get('KCAST', 'dve,act,pool,dve,act').split(',')
        cp = self.sb("cp", self.CPK_COLS)
        cpB = Buf("cp")
        hhalo = self.sb("hhalo", D)
        hhB = Buf("hhalo")
        ygT = self.sb("ygT", 4 * T, BF16)
        ygB = [Buf("yg%d" % t) for t in range(NT)]
        pch = self.sb("pch", 36)
        pchB = Buf("pch")
        Sst = [self.sb("Sst%d" % i, 4 * 128) for i in range(2)]
        SsB = [[Buf("S%d_%d" % (i, hh)) for hh in range(4)] for i in range(2)]
        stat = self.sb("stat", 64)
        negA = self.sb("negA", 4)
        negAB = Buf("negA")
        psum = [self.es.enter_context(nc.psum_tensor("ps%d" % i, [128, 512], F32)) for i in range(8)]
        psB = [[Buf("ps%d" % i, excl=True)] * 4 for i in range(8)]
        self.psum, self.psB = psum, psB
        self.h, self.hB = h, hB

        C = self.CP
        g0 = S.new_group()
        S.dma(SP, lambda e: e.dma_start(out=cp[:], in_=cpk), "const", g0, writes=[cpB])
        self.cp, self.cpB = cp, cpB

        def cs(name, n=None):
            a, b = C[name]
            return cp[:, a:b]

        self.cs = cs
        ident = cs("ident")
        op(POOL, lambda e: e.memset(Sst[0][:], 0.0), writes=SsB[0])
        op(POOL, lambda e: e.memset(pch[:], 0.0), writes=[pchB])
        op(ACT, lambda e: e.activation(out=negA[:], in_=cs("alog"), func=AF.Exp), reads=[cpB], writes=[negAB])
        op(DVE, lambda e: e.tensor_scalar(out=negA[:], in0=negA[:], scalar1=-1.0, scalar2=None, op0=ALU.mult),
           reads=[negAB], writes=[negAB])
        self.negA, self.negAB = negA, negAB
        self.pch, self.pchB = pch, pchB
        self.Sst, self.SsB = Sst, SsB
        self.ygT, self.ygB = ygT, ygB
        self.spar = 0
        self.stat = stat
        self.statB = Buf("stat")

        import os
        PH = set(os.environ.get("KPH", "pf,pg,of,og,cv,f2").split(","))
        if "pf" in PH:
            self.load_x(x_pre)
            self.ffn(w1_in, w1_out, "n1", tag="p1")
        op(POOL, lambda e: e.tensor_copy(out=hhalo[:], in_=h[:, (NT - 1) * D:NT * D]), reads=[hB[NT - 1]], writes=[hhB])
        if "pg" in PH:
            self.gdn_phase(wm_in, full=False, tag="pg")
        self.load_x(x_own)
        if "of" in PH:
            self.ffn(w1_in, w1_out, "n1", tag="o1")
        self.dbg("h1", h[:, 0:D], D, [hB[0]])
        if "og" in PH:
            self.gdn_phase(wm_in, full=True, tag="og")
        self.dbg("yg", self.ygT[:, 0:T], T, self.ygB, BF16)
        if "cv" in PH:
            self.conv_phase(wm_in, wm_out, hhalo, hhB)
        self.dbg("h2", h[:, 0:D], D, [hB[0]])
        if "f2" in PH:
            self.ffn(w2_in, w2_out, "n2", tag="o2")
        self.final(out, fn_bc)
        S.emit(nc, final_wait_keys=["out0", "out1"] + self.dbg_keys)
        self.es.close()
        return nc

    def load_x(self, xd):
        S, h, hB = self.S, self.h, self.hB
        g = S.new_group()
        for t in range(NT):
            S.dma(SP, (lambda t: lambda e: e.dma_start(out=h[:, t * D:(t + 1) * D], in_=xd[t * 128:(t + 1) * 128, :]))(t),
                  "x%d" % (t % 4), g, writes=[hB[t]])

    def load_dma(self, dst_ap, dstB, src_ap, shape3):
        S = self.S
        n = len(self.stage)
        i = self.st_i % n
        self.st_i += 1
        st, sB = self.stage[i], self.stB[i]
        a, b = shape3
        sview = st[:, 0:a * b].rearrange("p (a b) -> p a b", a=a) if a > 1 else st[:, 0:b]
        sflat = st[:, 0:a * b]
        g = S.new_group()
        S.dma(SP, lambda e: e.dma_start(out=sview, in_=src_ap), "st%d" % i, g, writes=[sB])
        return (dst_ap, dstB, sflat, sB)

    def load_cast_do(self, hnd):
        S = self.S
        dst_ap, dstB, sflat, sB = hnd
        self.cast_i = getattr(self, "cast_i", 0) + 1
        eng = self.cast_order[self.cast_i % len(self.cast_order)]
        if eng == ACT:
            S.op(ACT, lambda e: e.activation(out=dst_ap, in_=sflat, func=AF.Copy), reads=[sB], writes=[dstB])
        else:
            S.op(eng, lambda e: e.tensor_copy(out=dst_ap, in_=sflat), reads=[sB], writes=[dstB])

    def load_cast(self, dst_ap, dstB, src_ap, shape3=None, key="w"):
        self.load_cast_do(self.load_dma(dst_ap, dstB, src_ap, shape3))

    def norm_transpose(self, src_ap, srcB, gain_name, dst, dst_off, dst_stride, dstB, xs, xsB, pbanks, only=None):
        S, cs = self.S, self.cs
        op = S.op
        stat = self.stat
        stB = self.statB
        op(ACT, lambda e: e.activation(out=xs[:, 0:D], in_=src_ap, func=AF.Square, accum_out=stat[:, 0:1]),
           reads=[srcB], writes=[xsB, stB])
        op(POOL, lambda e: e.tensor_scalar(out=stat[:, 1:2], in0=stat[:, 0:1], scalar1=1.0 / D, scalar2=EPS,
                                           op0=ALU.mult, op1=ALU.add), reads=[stB], writes=[stB])
        op(POOL, lambda e: e.tensor_tensor(out=stat[:, 2:3], in0=stat[:, 1:2], in1=cs("mhalf"), op=ALU.pow),
           reads=[stB, self.cpB], writes=[stB])
        op(DVE, lambda e: e.tensor_scalar(out=xs[:, 0:D], in0=src_ap, scalar1=stat[:, 2:3], scalar2=None, op0=ALU.mult),
           reads=[srcB, stB], writes=[xsB])
        if only == "front":
            return
        self._nt_back(gain_name, dst, dst_off, dst_stride, dstB, xs, xsB, pbanks)

    def _nt_back(self, gain_name, dst, dst_off, dst_stride, dstB, xs, xsB, pbanks):
        S, cs = self.S, self.cs
        op = S.op
        gain = cs(gain_name)
        ident = cs("ident")
        for half in range(2):
            pb = pbanks[half]
            pbuf = self.psum[pb]
            for q in range(4):
                c = half * 4 + q
                op(PE, (lambda c, q, pbuf: lambda e: e.transpose(pbuf[:, q * 128:(q + 1) * 128], xs[:, c * 128:(c + 1) * 128], ident))(c, q, pbuf),
                   reads=[xsB, self.cpB], writes=[self.psB[pb][q]])
            for q in range(4):
                c = half * 4 + q
                eng = self.ew()
                o_ap = dst[:, c * dst_stride + dst_off: c * dst_stride + dst_off + 128]
                i_ap = pbuf[:, q * 128:(q + 1) * 128]
                g_ap = gain[:, c:c + 1]
                if eng == ACT:
                    op(ACT, (lambda o_ap, i_ap, g_ap: lambda e: e.activation(out=o_ap, in_=i_ap, func=AF.Copy, scale=g_ap))(o_ap, i_ap, g_ap),
                       reads=[self.psB[pb][q], self.cpB], writes=[dstB])
                else:
                    op(DVE, (lambda o_ap, i_ap, g_ap: lambda e: e.tensor_scalar(out=o_ap, in0=i_ap, scalar1=g_ap, scalar2=None, op0=ALU.mult))(o_ap, i_ap, g_ap),
                       reads=[self.psB[pb][q], self.cpB], writes=[dstB])

    def ffn(self, w_in, w_out, gain_name, tag):
        import os
        nc, S = self.nc, self.S
        op = S.op
        h, hB = self.h, self.hB
        psum, psB = self.psum, self.psB
        es = contextlib.ExitStack()
        xnT = self.sb("xnT_" + tag, 8 * T, BF16, es)
        xnB = [Buf("xn%d" % t) for t in range(NT)]
        CPP = int(os.environ.get("KCPP", 4))
        nsub = CPP // 2
        wbi = [self.sb("wbi%d_%s" % (i, tag), 2 * nsub * 2048, BF16, es) for i in range(2)]
        wbo = [self.sb("wbo%d_%s" % (i, tag), CPP * 1024, BF16, es) for i in range(2)]
        assert CPP == 4
        wbiB = [[Buf() for _ in range(4)] for _ in range(2)]
        wboB = [[Buf() for _ in range(nsub)] for _ in range(2)]
        hid = [self.sb("hid%d_%s" % (i, tag), CPP * 512, BF16, es) for i in range(2)]
        hidB = [Buf(), Buf()]
        sg0 = self.sb("sg0_%s" % tag, 512, F32, es); sg = [sg0, sg0]
        sgB0 = Buf(); sgB = [sgB0, sgB0]
        xs = [self.sb("xs%d_%s" % (i, tag), D, F32, es) for i in range(2)]
        xsB = [Buf(), Buf()]
        ev = [xs[1][:, 0:512], xs[1][:, 512:1024]]
        evB = [xsB[1], xsB[1]]
        self.junkB = Buf()
        base_stage, base_stB = self.stage, self.stB
        nextra = int(os.environ.get("KXST", 0))
        xst = [self.sb("xst%d_%s" % (i, tag), 2048, F32, es) for i in range(nextra)]
        xstB = [Buf() for _ in range(nextra)]
        self.stage, self.stB = base_stage + xst, base_stB + xstB
        new_bufs = xnB + wbiB[0] + wbiB[1] + wboB[0] + wboB[1] + hidB + sgB + xsB + [self.junkB] + evB + xstB
        S.alias(new_bufs, getattr(self, "phase_bufs", []))
        self.phase_bufs = new_bufs

        w_in_v = w_in.rearrange("(c p) n -> p c n", p=128)
        w_out_v = w_out.rearrange("(c p) n -> p c n", p=128)
        pieces = []
        c0 = 0
        while c0 < DFF // 128:
            n = min(CPP, DFF // 128 - c0)
            pieces.append((c0, n))
            c0 += n
        NP = len(pieces)

        def piece_specs(p):
            s = p % 2
            ch0, n = pieces[p]
            W = n * 128
            col = ch0 * 128
            specs = []
            for which in range(2):
                for csub in range(2):
                    base = which * 4096 + csub * 2048
                    specs.append((wbi[s][:, base:base + 4 * W], wbiB[s][which * 2 + csub],
                                  w_in_v[:, csub * 4:csub * 4 + 4, which * DFF + col:which * DFF + col + W], (4, W)))
            for sub in range(n // 2):
                specs.append((wbo[s][:, sub * 2048:(sub + 1) * 2048], wboB[s][sub], w_out_v[:, ch0 + 2 * sub:ch0 + 2 * sub + 2, :], (2, 1024)))
            return specs

        def load_piece(p):
            for sp_ in piece_specs(p):
                self.load_cast(*sp_)

        load_piece(0)
        PBK = [(6, 7), (6, 7)]

        def stepA_front(t):
            self.norm_transpose(h[:, t * D:(t + 1) * D], hB[t], gain_name, xnT, t * 128, T, xnB[t],
                                xs[t % 2], xsB[t % 2], PBK[t % 2], only="front")

        def stepA_back(t):
            self._nt_back(gain_name, xnT, t * 128, T, xnB[t], xs[t % 2], xsB[t % 2], PBK[t % 2])

        def stepA_tiles(t0_, t1_):
            for t in range(t0_, t1_):
                if t == t0_:
                    stepA_front(t)
                if t + 1 < t1_:
                    stepA_front(t + 1)
                stepA_back(t)

        stepA_tiles(0, 4)
        blocks = [(p, tb) for p in range(NP) for tb in range(NB)]
        st = {"gi": 0, "oi": 0}

        def stage1(idx):
            p, tb = blocks[idx]
            s = p % 2
            hs = idx % 2
            n = pieces[p][1]
            for j in range(n):
                gi = st["gi"]
                for which in range(2):
                    pb = (0 if which == 0 else 2) + (gi % 2)
                    W = n * 128
                    for c in range(8):
                        off = which * 4096 + (c // 4) * 2048 + (c % 4) * W + j * 128
                        lhsT = wbi[s][:, off:off + 128]
                        rhs = xnT[:, c * T + tb * 512: c * T + (tb + 1) * 512]
                        op(PE, (lambda pb, lhsT, rhs, c: lambda e: e.matmul(psum[pb][:, :], lhsT=lhsT, rhs=rhs, start=(c == 0), stop=(c == 7)))(pb, lhsT, rhs, c),
                           reads=[wbiB[s][which * 2 + c // 4]] + xnB[tb * 4:(tb + 1) * 4], writes=psB[pb])
                pg, pu = (gi % 2), 2 + (gi % 2)
                k = gi % 2
                op(ACT, (lambda pg, k: lambda e: e.activation(out=sg[k][:, :], in_=psum[pg][:, :], func=AF.Silu))(pg, k),
                   reads=psB[pg], writes=[sgB[k]])
                op(DVE, (lambda pu, k, hs, j: lambda e: e.tensor_tensor(out=hid[hs][:, j * 512:(j + 1) * 512], in0=sg[k][:, :], in1=psum[pu][:, :], op=ALU.mult))(pu, k, hs, j),
                   reads=[sgB[k]] + psB[pu], writes=[hidB[hs]])
                st["gi"] += 1

        def stage2(idx):
            p, tb = blocks[idx]
            s = p % 2
            hs = idx % 2
            n = pieces[p][1]
            for tt in range(4):
                t = tb * 4 + tt
                for hh in range(2):
                    oi = st["oi"]
                    pb = 4 + (oi % 2)
                    for j in range(n):
                        lhsT = hid[hs][:, j * 512 + tt * 128: j * 512 + (tt + 1) * 128]
                        rhs = wbo[s][:, j * 1024 + hh * 512: j * 1024 + (hh + 1) * 512]
                        op(PE, (lambda pb, lhsT, rhs, j: lambda e: e.matmul(psum[pb][:, :], lhsT=lhsT, rhs=rhs, start=(j == 0), stop=(j == n - 1)))(pb, lhsT, rhs, j),
                           reads=[hidB[hs], wboB[s][j // 2]], writes=psB[pb])
                    hap = h[:, t * D + hh * 512: t * D + (hh + 1) * 512]
                    if oi % 2 == 0:
                        op(DVE, (lambda pb, hap: lambda e: e.scalar_tensor_tensor(out=hap, in0=psum[pb][:, :], scalar=0.5, in1=hap, op0=ALU.mult, op1=ALU.add))(pb, hap),
                           reads=psB[pb] + [hB[t]], writes=[hB[t]])
                    else:
                        k = (oi // 2) % 2
                        op(ACT, (lambda pb, k: lambda e: e.activation(out=ev[k][:, :], in_=psum[pb][:, :], func=AF.Copy, scale=0.5))(pb, k),
                           reads=psB[pb], writes=[evB[k]])
                        op(POOL, (lambda k, hap: lambda e: e.tensor_tensor(out=hap, in0=hap, in1=ev[k][:, :], op=ALU.add))(k, hap),
                           reads=[evB[k], hB[t]], writes=[hB[t]])
                    st["oi"] += 1

        pend_specs = []
        inflight = []
        for idx in range(len(blocks)):
            stage1(idx)
            if idx + 1 < NB:
                stepA_tiles((idx + 1) * 4, (idx + 2) * 4)
            if idx > 0:
                stage2(idx - 1)
            p, tb = blocks[idx]
            if tb == 0 and p + 1 < NP:
                pend_specs = piece_specs(p + 1)
            for hnd in inflight:
                self.load_cast_do(hnd)
            inflight = []
            if tb == NB - 1:
                for sp_ in pend_specs:
                    self.load_cast(*sp_)
                pend_specs = []
            else:
                for sp_ in pend_specs[:2]:
                    inflight.append(self.load_dma(*sp_))
                pend_specs = pend_specs[2:]
        stage2(len(blocks) - 1)
        self.stage, self.stB = base_stage, base_stB
        es.close()

    def gdn_phase(self, wm_in, full, tag):
        import os
        nc, S = self.nc, self.S
        op = S.op
        cs, cpB = self.cs, self.cpB
        h, hB = self.h, self.hB
        psum, psB = self.psum, self.psB
        es = contextlib.ExitStack()
        pc = self.sb("pc_" + tag, 12 * 131, F32, es); pcB = Buf()
        wg = self.sb("wg_" + tag, 8 * NG, BF16, es)
        wgB = [Buf() for _ in range(9)]
        xs = self.sb("xs_" + tag, D, F32, es); xsB = Buf()
        xn = self.sb("xn_" + tag, 8 * 128, BF16, es); xnB = Buf()
        qkv0 = self.sb("qkv_" + tag, 12 * 128, F32, es); qkvB0 = Buf()
        qkv_b = [qkv0, self.stage[0][:, 0:1536]]; qkvB_b = [qkvB0, self.stB[0]]
        cacc = self.sb("cacc_" + tag, 12 * 128, F32, es); caccB = Buf()
        etmp = self.sb("etmp_" + tag, 12 * 128, F32, es); etmpB = Buf()
        rs, rsB = etmp, etmpB
        zT0 = self.sb("zT_" + tag, 4 * 128, F32, es); zB0 = Buf()
        zT_b = [zT0, self.stage[1][:, 0:512]]; zB_b = [zB0, self.stB[1]]
        sm_b = [self.sb("sm%d_%s" % (i, tag), 64, F32, es) for i in range(2)]; smB_b = [Buf(), Buf()]
        Dg = self.sb("Dg_" + tag, 512, F32, es); DgB = Buf()
        glb_b = [self.sb("glb%d_%s" % (i, tag), 8, F32, es) for i in range(2)]; glB_b = [Buf(), Buf()]
        def mk(n, cols=128, dt=F32):
            return self.sb(n + "_" + tag, cols, dt, es), Buf(n)
        HB = []
        for hh in range(4):
            d_ = {}
            for n in ["kbg", "kdec", "vbeta", "tm1", "E1", "Lm", "AT", "X0", "X1", "Y0", "Y1", "P0", "P1", "wT", "um", "vnew"]:
                d_[n] = mk("%s%d" % (n, hh))
            HB.append(d_)
        new_bufs = wgB + [pcB, xsB, xnB, qkvB0, caccB, etmpB, zB0, DgB] + smB_b + glB_b + [b for d_ in HB for _, b in d_.values()]
        S.alias(new_bufs, getattr(self, "phase_bufs", []))
        self.phase_bufs = new_bufs

        wm_v = wm_in.rearrange("(c p) n -> p c n", p=128)
        col = 0
        while col < NG:
            w = min(256, NG - col)
            base = (col // 256) * 2048
            self.load_cast(wg[:, base:base + 8 * w], wgB[col // 256], wm_v[:, :, GW0 + col:GW0 + col + w], (8, w))
            col += w

        ident, ones, triU = cs("ident"), cs("ones"), cs("triU")
        maskL, maskU = cs("maskL"), cs("maskU")
        cwg = cs("cwg")
        pcv0 = pc[:, :].rearrange("p (a b) -> p a b", a=12)
        pchv = self.pch[:, :].rearrange("p (a b) -> p a b", a=12)
        op(DVE, lambda e: e.tensor_copy(out=pcv0[:, :, 0:3], in_=pchv), reads=[self.pchB], writes=[pcB])
        Sst, SsB = self.Sst, self.SsB
        nq = 16 if full else 12

        def pre_ops(t):
            pp = t % 2
            qkv, qkvB = qkv_b[pp], qkvB_b[pp]
            zT, zB = zT_b[pp], zB_b[pp]
            sm, smB = sm_b[pp], smB_b[pp]
            glb, glB = glb_b[pp], glB_b[pp]
            S.capture = []
            self.norm_transpose(h[:, t * D:(t + 1) * D], hB[t], "nm", xn, 0, 128, xnB, xs, xsB, (0, 1))
            for grp in range(nq // 4):
                if (not full) and grp == 0 and t != NT - 1:
                    continue
                pb = 2
                for q in range(4):
                    j = grp * 4 + q
                    for c in range(8):
                        lhsT = wg[:, (j // 2) * 2048 + c * 256 + (j % 2) * 128: (j // 2) * 2048 + c * 256 + (j % 2) * 128 + 128]
                        rhs = xn[:, c * 128:(c + 1) * 128]
                        op(PE, (lambda pb, q, lhsT, rhs, c: lambda e: e.matmul(psum[pb][:, q * 128:(q + 1) * 128], lhsT=lhsT, rhs=rhs, start=(c == 0), stop=(c == 7)))(pb, q, lhsT, rhs, c),
                           reads=[wgB[j // 2], xnB], writes=[psB[pb][q]])
                if grp < 3:
                    dstv = pc[:, grp * 4 * 131:(grp + 1) * 4 * 131].rearrange("p (a b) -> p a b", a=4)[:, :, 3:131]
                    srcv = psum[pb][:, :].rearrange("p (a b) -> p a b", a=4)
                    eng = self.ew()
                    if eng == ACT:
                        op(ACT, (lambda dstv, srcv: lambda e: e.activation(out=dstv, in_=srcv, func=AF.Copy))(dstv, srcv), reads=psB[pb], writes=[pcB])
                    else:
                        op(DVE, (lambda dstv, srcv: lambda e: e.tensor_copy(out=dstv, in_=srcv))(dstv, srcv), reads=psB[pb], writes=[pcB])
                else:
                    op(ACT, (lambda pb: lambda e: e.activation(out=zT[:, :], in_=psum[pb][:, :], func=AF.Copy))(pb), reads=psB[pb], writes=[zB])
            i3 = len(S.capture)
            for c in range(8):
                lhsT = xn[:, c * 128:(c + 1) * 128]
                rhs = wg[:, 8 * 2048 + c * 8: 8 * 2048 + c * 8 + 8]
                op(PE, (lambda lhsT, rhs, c: lambda e: e.matmul(psum[2][:, 0:8], lhsT=lhsT, rhs=rhs, start=(c == 0), stop=(c == 7)))(lhsT, rhs, c),
                   reads=[wgB[8], xnB], writes=[psB[2][0]])
            op(DVE, lambda e: e.tensor_copy(out=sm[:, 0:8], in_=psum[2][:, 0:8]), reads=[psB[2][0]], writes=[smB])
            op(ACT, lambda e: e.activation(out=sm[:, 8:12], in_=sm[:, 0:4], func=AF.Exp, scale=-1.0), reads=[smB], writes=[smB])
            op(DVE, lambda e: e.tensor_scalar(out=sm[:, 8:12], in0=sm[:, 8:12], scalar1=1.0, scalar2=None, op0=ALU.add), reads=[smB], writes=[smB])
            op(DVE, lambda e: e.reciprocal(out=sm[:, 8:12], in_=sm[:, 8:12]), reads=[smB], writes=[smB])
            op(DVE, lambda e: e.tensor_tensor(out=sm[:, 12:16], in0=sm[:, 4:8], in1=cs("dtb"), op=ALU.add), reads=[smB, cpB], writes=[smB])
            op(ACT, lambda e: e.activation(out=sm[:, 12:16], in_=sm[:, 12:16], func=AF.Exp), reads=[smB], writes=[smB])
            op(ACT, lambda e: e.activation(out=sm[:, 12:16], in_=sm[:, 12:16], func=AF.Ln, bias=1.0), reads=[smB], writes=[smB])
            op(DVE, lambda e: e.tensor_tensor(out=sm[:, 12:16], in0=sm[:, 12:16], in1=self.negA[:, :], op=ALU.mult), reads=[smB, self.negAB], writes=[smB])
            op(PE, lambda e: e.matmul(psum[2][:, 8:12], lhsT=triU, rhs=sm[:, 12:16], start=True, stop=True), reads=[smB, cpB], writes=[psB[2][0]])
            op(DVE, lambda e: e.tensor_copy(out=sm[:, 16:20], in_=psum[2][:, 8:12]), reads=[psB[2][0]], writes=[smB])
            op(ACT, lambda e: e.activation(out=sm[:, 20:24], in_=sm[:, 16:20], func=AF.Exp), reads=[smB], writes=[smB])
            op(DVE, lambda e: e.tensor_scalar(out=sm[:, 24:28], in0=sm[:, 16:20], scalar1=-1.0, scalar2=None, op0=ALU.mult), reads=[smB], writes=[smB])
            op(DVE, lambda e: e.tensor_tensor(out=sm[:, 28:32], in0=sm[:, 8:12], in1=sm[:, 20:24], op=ALU.mult), reads=[smB], writes=[smB])
            for hh in range(4):
                op(DVE, (lambda hh: lambda e: e.tensor_scalar(out=Dg[:, hh * 128:(hh + 1) * 128], in0=ident, scalar1=sm[:, 16 + hh:17 + hh], scalar2=None, op0=ALU.mult))(hh),
                   reads=[smB, cpB], writes=[DgB])
            op(PE, lambda e: e.matmul(psum[3][:, :], lhsT=ones, rhs=Dg[:, 0:512], start=True, stop=True), reads=[DgB, cpB], writes=psB[3])
            Gv = psum[3][:, :].rearrange("p (a b) -> p a b", a=4)
            op(ACT, lambda e: e.activation(out=glb[:, 0:8].rearrange("p (a b) -> p a b", a=4), in_=Gv[:, :, 63:128:64], func=AF.Exp), reads=psB[3], writes=[glB])
            op(DVE, lambda e: e.tensor_tensor(out=sm[0:64, 32:36], in0=Gv[0:64, :, 63], in1=sm[0:64, 16:20], op=ALU.subtract), reads=psB[3] + [smB], writes=[smB])
            op(DVE, lambda e: e.tensor_tensor(out=sm[64:128, 32:36], in0=Gv[64:128, :, 127], in1=sm[64:128, 16:20], op=ALU.subtract), reads=psB[3] + [smB], writes=[smB])
            op(ACT, lambda e: e.activation(out=sm[:, 32:36], in_=sm[:, 32:36], func=AF.Exp), reads=[smB], writes=[smB])
            i4 = len(S.capture)
            pcv = pc[:, :].rearrange("p (a b) -> p a b", a=12)
            for j in range(0 if full else 4, 12):
                op(DVE, (lambda j: lambda e: e.tensor_scalar(out=cacc[:, j * 128:(j + 1) * 128], in0=pc[:, j * 131:j * 131 + 128], scalar1=cwg[:, j * 4:j * 4 + 1], scalar2=None, op0=ALU.mult))(j),
                   reads=[pcB, cpB], writes=[caccB])
                for k in range(1, 4):
                    op(DVE, (lambda j, k: lambda e: e.scalar_tensor_tensor(out=cacc[:, j * 128:(j + 1) * 128], in0=pc[:, j * 131 + k:j * 131 + k + 128], scalar=cwg[:, j * 4 + k:j * 4 + k + 1], in1=cacc[:, j * 128:(j + 1) * 128], op0=ALU.mult, op1=ALU.add))(j, k),
                       reads=[pcB, cpB, caccB], writes=[caccB])
            op(DVE, lambda e: e.tensor_copy(out=pcv[:, :, 0:3], in_=pcv[:, :, 128:131]), reads=[pcB], writes=[pcB])
            i5 = len(S.capture)
            c_lo = 0 if full else 512
            op(ACT, lambda e: e.activation(out=etmp[:, c_lo:1536], in_=cacc[:, c_lo:1536], func=AF.Exp, scale=-1.0), reads=[caccB], writes=[etmpB])
            op(ACT, lambda e: e.activation(out=etmp[:, c_lo:1536], in_=etmp[:, c_lo:1536], func=AF.Ln, bias=1.0), reads=[etmpB], writes=[etmpB])
            op(ACT, lambda e: e.activation(out=etmp[:, c_lo:1536], in_=etmp[:, c_lo:1536], func=AF.Exp, scale=-1.0), reads=[etmpB], writes=[etmpB])
            op(DVE, lambda e: e.tensor_tensor(out=qkv[:, c_lo:1536], in0=cacc[:, c_lo:1536], in1=etmp[:, c_lo:1536], op=ALU.mult), reads=[etmpB, caccB], writes=[qkvB])
            op(ACT, lambda e: e.activation(out=etmp[:, c_lo:1024], in_=qkv[:, c_lo:1024], func=AF.Square), reads=[qkvB, etmpB], writes=[etmpB])
            for half in range(0 if full else 1, 2):
                op(PE, (lambda half: lambda e: e.matmul(psum[half][:, :], lhsT=ones, rhs=etmp[:, half * 512:(half + 1) * 512], start=True, stop=True))(half),
                   reads=[etmpB, cpB], writes=psB[half])
                op(ACT, (lambda half: lambda e: e.activation(out=rs[:, half * 512:(half + 1) * 512], in_=psum[half][:, :], func=AF.Ln, bias=cs("eps")))(half),
                   reads=psB[half] + [cpB], writes=[rsB])
            op(ACT, lambda e: e.activation(out=rs[:, c_lo:1024], in_=rs[:, c_lo:1024], func=AF.Exp, scale=-0.5), reads=[rsB], writes=[rsB])
            if full:
                op(DVE, lambda e: e.scalar_tensor_tensor(out=qkv[:, 0:512], in0=qkv[:, 0:512], scalar=128.0 ** -0.5, in1=rs[:, 0:512], op0=ALU.mult, op1=ALU.mult), reads=[qkvB, rsB], writes=[qkvB])
            op(DVE, lambda e: e.tensor_tensor(out=qkv[:, 512:1024], in0=qkv[:, 512:1024], in1=rs[:, 512:1024], op=ALU.mult), reads=[qkvB, rsB], writes=[qkvB])
            if full:
                op(ACT, lambda e: e.activation(out=etmp[:, 0:512], in_=zT[:, :], func=AF.Exp, scale=-1.0), reads=[zB, etmpB], writes=[etmpB])
                op(ACT, lambda e: e.activation(out=etmp[:, 0:512], in_=etmp[:, 0:512], func=AF.Ln, bias=1.0), reads=[etmpB], writes=[etmpB])
                op(ACT, lambda e: e.activation(out=etmp[:, 0:512], in_=etmp[:, 0:512], func=AF.Exp, scale=-1.0), reads=[etmpB], writes=[etmpB])
                op(DVE, lambda e: e.tensor_tensor(out=zT[:, :], in0=zT[:, :], in1=etmp[:, 0:512], op=ALU.mult), reads=[etmpB, zB], writes=[zB])
            cap_ = S.capture
            S.capture = None
            s3, s4 = cap_[i3:i4], cap_[i4:i5]
            mer = []
            i_, j_ = 0, 0
            while i_ < len(s3) or j_ < len(s4):
                if j_ < len(s4) and (i_ >= len(s3) or j_ * max(len(s3), 1) <= i_ * len(s4)):
                    mer.append(s4[j_]); j_ += 1
                else:
                    mer.append(s3[i_]); i_ += 1
            return cap_[:i3] + mer + cap_[i5:]

        def chain(hh, t):
            pp = t % 2
            qkv, qkvB = qkv_b[pp], qkvB_b[pp]
            zT, zB = zT_b[pp], zB_b[pp]
            sm, smB = sm_b[pp], smB_b[pp]
            glb, glB = glb_b[pp], glB_b[pp]
            B_ = HB[hh]
            kbg, kbgB = B_["kbg"]; kdec, kdecB = B_["kdec"]; vbeta, vbetaB = B_["vbeta"]
            tm1, tm1B = B_["tm1"]; E1, E1B = B_["E1"]; Lm, LmB = B_["Lm"]; AT, ATB = B_["AT"]
            X = [B_["X0"], B_["X1"]]; Y = [B_["Y0"], B_["Y1"]]; Pm = [B_["P0"], B_["P1"]]
            wT, wTB = B_["wT"]; um, umB = B_["um"]; vnew, vnewB = B_["vnew"]
            o1s, o1sB = tm1, tm1B
            om, omB = E1, E1B
            on, onB = Lm, LmB
            qT = qkv[:, hh * 128:(hh + 1) * 128]
            kT = qkv[:, 512 + hh * 128:512 + (hh + 1) * 128]
            vT = qkv[:, 1024 + hh * 128:1024 + (hh + 1) * 128]
            PH = psum[4 + hh]
            PB = psB[4 + hh][0]
            Q0, Q1, Q2, Q3 = PH[:, 0:128], PH[:, 128:256], PH[:, 256:384], PH[:, 384:512]
            Gh = psum[3][:, hh * 128:(hh + 1) * 128]
            GhB = psB[3]
            op(PE, lambda e: e.transpose(Q0, kT, ident), reads=[qkvB, cpB], writes=[PB])
            op(PE, lambda e: e.transpose(Q1, vT, ident), reads=[qkvB, cpB], writes=[PB])
            op(PE, lambda e: e.matmul(Q2, lhsT=kT, rhs=kT, start=True, stop=True), reads=[qkvB], writes=[PB])
            if full:
                op(PE, lambda e: e.matmul(Q3, lhsT=kT, rhs=qT, start=True, stop=True), reads=[qkvB], writes=[PB])
            yield
            op(DVE, lambda e: e.tensor_tensor(out=tm1[:, :], in0=maskL, in1=Gh, op=ALU.subtract), reads=GhB + [cpB], writes=[tm1B])
            yield
            op(ACT, lambda e: e.activation(out=E1[:, :], in_=tm1[:, :], func=AF.Exp, bias=sm[:, 16 + hh:17 + hh]), reads=[tm1B, smB], writes=[E1B])
            yield
            op(ACT, lambda e: e.activation(out=kbg[:, :], in_=Q0, func=AF.Copy, scale=sm[:, 28 + hh:29 + hh]), reads=[PB, smB], writes=[kbgB])
            op(ACT, lambda e: e.activation(out=vbeta[:, :], in_=Q1, func=AF.Copy, scale=sm[:, 8 + hh:9 + hh]), reads=[PB, smB], writes=[vbetaB])
            yield
            op(DVE, lambda e: e.tensor_scalar(out=kdec[:, :], in0=Q0, scalar1=sm[:, 32 + hh:33 + hh], scalar2=None, op0=ALU.mult), reads=[PB, smB], writes=[kdecB])
            op(DVE, lambda e: e.scalar_tensor_tensor(out=Lm[:, :], in0=Q2, scalar=sm[:, 8 + hh:9 + hh], in1=E1[:, :], op0=ALU.mult, op1=ALU.mult), reads=[PB, smB, E1B], writes=[LmB])
            yield
            if full:
                op(DVE, lambda e: e.tensor_tensor(out=tm1[:, :], in0=maskU, in1=Gh, op=ALU.add), reads=GhB + [cpB, tm1B], writes=[tm1B])
                yield
                op(ACT, lambda e: e.activation(out=E1[:, :], in_=tm1[:, :], func=AF.Exp, bias=sm[:, 24 + hh:25 + hh]), reads=[tm1B, smB, E1B], writes=[E1B])
                yield
                op(DVE, lambda e: e.tensor_tensor(out=AT[:, :], in0=Q3, in1=E1[:, :], op=ALU.mult), reads=[PB, E1B], writes=[ATB])
                yield
            op(PE, lambda e: e.transpose(Q0, Lm[:, :], ident), reads=[LmB, cpB], writes=[PB])
            yield
            X0, X0B = X[0]
            P0, P0B = Pm[0]
            op(ACT, lambda e: e.activation(out=X0[:, :], in_=Q0, func=AF.Copy), reads=[PB], writes=[X0B])
            op(DVE, lambda e: e.tensor_tensor(out=P0[:, :], in0=ident, in1=Q0, op=ALU.subtract), reads=[PB, cpB], writes=[P0B])
            yield
            Xc, XcB = X0, X0B
            Yc, YcB = Lm, LmB
            Pc, PcB = P0, P0B
            for k in range(1, 6):
                Yn, YnB = Y[k % 2]
                Xn, XnB = X[k % 2]
                Pn, PnB = Pm[k % 2]
                op(PE, (lambda Xc, Yc: lambda e: e.matmul(Q1, lhsT=Xc[:, :], rhs=Yc[:, :], start=True, stop=True))(Xc, Yc), reads=[XcB, YcB], writes=[PB])
                if k < 5:
                    op(PE, (lambda Xc, Yc: lambda e: e.matmul(Q2, lhsT=Yc[:, :], rhs=Xc[:, :], start=True, stop=True))(Xc, Yc), reads=[XcB, YcB], writes=[PB])
                yield
                op(ACT, (lambda Yn: lambda e: e.activation(out=Yn[:, :], in_=Q1, func=AF.Copy))(Yn), reads=[PB], writes=[YnB])
                if k < 5:
                    op(ACT, (lambda Xn: lambda e: e.activation(out=Xn[:, :], in_=Q2, func=AF.Copy))(Xn), reads=[PB], writes=[XnB])
                yield
                op(PE, (lambda Yn, Pc: lambda e: e.matmul(Q3, lhsT=Yn[:, :], rhs=Pc[:, :], start=True, stop=True))(Yn, Pc), reads=[YnB, PcB], writes=[PB])
                yield
                op(DVE, (lambda Pn, Pc: lambda e: e.tensor_tensor(out=Pn[:, :], in0=Pc[:, :], in1=Q3, op=ALU.add))(Pn, Pc), reads=[PcB, PB], writes=[PnB])
                yield
                Xc, XcB, Yc, YcB, Pc, PcB = Xn, XnB, Yn, YnB, Pn, PnB
            op(PE, (lambda Pc: lambda e: e.matmul(Q0, lhsT=kbg[:, :], rhs=Pc[:, :], start=True, stop=True))(Pc), reads=[kbgB, PcB], writes=[PB])
            op(PE, (lambda Pc: lambda e: e.matmul(Q1, lhsT=Pc[:, :], rhs=vbeta[:, :], start=True, stop=True))(Pc), reads=[vbetaB, PcB], writes=[PB])
            yield
            op(ACT, lambda e: e.activation(out=wT[:, :], in_=Q0, func=AF.Copy), reads=[PB], writes=[wTB])
            op(ACT, lambda e: e.activation(out=um[:, :], in_=Q1, func=AF.Copy), reads=[PB], writes=[umB])
            yield
            for half in range(2):
                r0, r1 = half * 64, half * 64 + 64
                sp_ = self.spar_h[hh]
                Scur = Sst[sp_][:, hh * 128:(hh + 1) * 128]; ScurB = SsB[sp_][hh]
                Snew = Sst[1 - sp_][:, hh * 128:(hh + 1) * 128]; SnewB = SsB[1 - sp_][hh]
                self.spar_h[hh] = 1 - sp_
                op(PE, (lambda r0, r1, Scur: lambda e: e.matmul(PH[r0:r1, 256:384], lhsT=wT[:, r0:r1], rhs=Scur, start=True, stop=True))(r0, r1, Scur), reads=[wTB, ScurB], writes=[PB])
                yield
                op(DVE, (lambda r0, r1: lambda e: e.tensor_tensor(out=vnew[r0:r1, :], in0=um[r0:r1, :], in1=PH[r0:r1, 256:384], op=ALU.subtract))(r0, r1), reads=[umB, PB], writes=[vnewB])
                yield
                if full:
                    op(PE, (lambda r0, r1, Scur: lambda e: e.matmul(PH[r0:r1, 0:128], lhsT=qT[:, r0:r1], rhs=Scur, start=True, stop=True))(r0, r1, Scur), reads=[qkvB, ScurB], writes=[PB])
                    op(PE, (lambda r0, r1: lambda e: e.matmul(PH[r0:r1, 128:256], lhsT=AT[r0:r1, r0:r1], rhs=vnew[r0:r1, :], start=True, stop=True))(r0, r1), reads=[ATB, vnewB], writes=[PB])
                op(PE, (lambda r0, r1: lambda e: e.matmul(Q3, lhsT=kdec[r0:r1, :], rhs=vnew[r0:r1, :], start=True, stop=True))(r0, r1), reads=[kdecB, vnewB], writes=[PB])
                yield
                op(DVE, (lambda Snew, Scur, half: lambda e: e.scalar_tensor_tensor(out=Snew, in0=Scur, scalar=glb[:, hh * 2 + half:hh * 2 + half + 1], in1=Q3, op0=ALU.mult, op1=ALU.add))(Snew, Scur, half),
                   reads=[ScurB, glB, PB], writes=[SnewB])
                yield
            if full:
                c0 = 40 + hh * 3
                op(ACT, lambda e: e.activation(out=o1s[:, :], in_=Q0, func=AF.Copy, scale=sm[:, 20 + hh:21 + hh]), reads=[PB, smB], writes=[o1sB])
                yield
                op(DVE, lambda e: e.tensor_tensor(out=om[:, :], in0=o1s[:, :], in1=Q1, op=ALU.add), reads=[o1sB, PB], writes=[omB])
                yield
                op(ACT, lambda e: e.activation(out=on[:, :], in_=om[:, :], func=AF.Square, accum_out=sm[:, c0:c0 + 1]), reads=[omB], writes=[onB, smB])
                yield
                op(POOL, lambda e: e.tensor_scalar(out=sm[:, c0 + 1:c0 + 2], in0=sm[:, c0:c0 + 1], scalar1=1.0 / 128, scalar2=EPS, op0=ALU.mult, op1=ALU.add), reads=[smB], writes=[smB])
                op(POOL, lambda e: e.tensor_tensor(out=sm[:, c0 + 2:c0 + 3], in0=sm[:, c0 + 1:c0 + 2], in1=cs("mhalf"), op=ALU.pow), reads=[smB, cpB], writes=[smB])
                yield
                op(DVE, lambda e: e.scalar_tensor_tensor(out=on[:, :], in0=om[:, :], scalar=sm[:, c0 + 2:c0 + 3], in1=cs("gon"), op0=ALU.mult, op1=ALU.mult), reads=[omB, smB, cpB], writes=[onB])
                yield
                op(PE, lambda e: e.transpose(Q2, on[:, :], ident), reads=[onB, cpB], writes=[PB])
                yield
                yg_ap = self.ygT[:, hh * T + t * 128: hh * T + (t + 1) * 128]
                op(DVE, lambda e: e.tensor_tensor(out=yg_ap, in0=Q2, in1=zT[:, hh * 128:(hh + 1) * 128], op=ALU.mult), reads=[PB, zB], writes=[self.ygB[t]])
                yield

        for it in pre_ops(0):
            S.replay(it)
        for t in range(NT):
            pend = pre_ops(t + 1) if t + 1 < NT else []
            per = (len(pend) + 39) // 40
            S0 = int(os.environ.get("KSTAG", 2))
            rounds_t = (47 if full else 37) + 3 * S0 - int(os.environ.get("KEARLY", 3))
            per = (len(pend) + rounds_t - 1) // rounds_t
            alive = [(hh, chain(hh, t)) for hh in range(4)]
            pi = 0
            rnd = 0
            while alive or pi < len(pend):
                nxt = []
                for hh, g_ in alive:
                    if rnd < hh * S0:
                        nxt.append((hh, g_))
                        continue
                    try:
                        next(g_)
                        nxt.append((hh, g_))
                    except StopIteration:
                        pass
                alive = nxt
                for it in pend[pi:pi + per]:
                    S.replay(it)
                pi += per
                rnd += 1
        op(DVE, lambda e: e.tensor_copy(out=pchv, in_=pcv0[:, :, 0:3]), reads=[pcB], writes=[self.pchB])
        es.close()

    def conv_phase(self, wm_in, wm_out, hhalo, hhB):
        import os
        nc, S = self.nc, self.S
        op = S.op
        cs, cpB = self.cs, self.cpB
        h, hB = self.h, self.hB
        psum, psB = self.psum, self.psB
        es = contextlib.ExitStack()
        tag = "cv"
        wc = self.sb("wc", 8 * 1536, BF16, es); wcB = [Buf() for _ in range(6)]
        wo = self.sb("wo", 8 * 1024, BF16, es); woB = [Buf() for _ in range(4)]
        xs = self.sb("xs_cv", D, F32, es); xsB = Buf()
        self.junk = self.sb("junk_cv", D, BF16, es); self.junkB = Buf()
        xn = self.sb("xn_cv", 8 * 128, BF16, es); xnB = Buf()
        mpc = self.sb("mpc", 4 * 130, F32, es); mpcB = Buf()
        mpc2 = self.sb("mpc2", 4 * 130, F32, es); mpc2B = Buf()
        cbs0 = self.sb("cbs0", 512, F32, es); cbs0B = Buf()
        cbs1 = self.sb("cbs1", 512, F32, es); cbs1B = Buf()
        cct = self.sb("cct", 512, F32, es); cctB = Buf()
        cacc = self.sb("cacc_cv", 512, F32, es); caccB = Buf()
        yv = self.sb("yv", 512, F32, es); yvB = Buf()
        sq = self.sb("sq_cv", 512, F32, es); sqB = Buf()
        rs = self.sb("rs_cv", 512, F32, es); rsB = Buf()
        ycT = self.sb("ycT", 512, BF16, es); ycB = Buf()
        motmp = [self.sb("motmp%d" % i, 512, F32, es) for i in range(2)]; motB = [Buf(), Buf()]
        new_bufs = wcB + woB + [mpc2B, cbs0B, cbs1B, xsB, self.junkB, xnB, mpcB, cctB, caccB, yvB, sqB, rsB, ycB] + motB
        S.alias(new_bufs, getattr(self, "phase_bufs", []))
        self.phase_bufs = new_bufs
        wm_v = wm_in.rearrange("(c p) n -> p c n", p=128)
        for col in range(0, 1536, 256):
            base = (col // 256) * 2048
            self.load_cast(wc[:, base:base + 2048], wcB[col // 256], wm_v[:, :, col:col + 256], (8, 256))
        wo_v = wm_out.rearrange("(c p) n -> p c n", p=128)
        for i in range(4):
            self.load_cast(wo[:, i * 2048:(i + 1) * 2048], woB[i], wo_v[:, 2 * i:2 * i + 2, :], (2, 1024))
        op(POOL, lambda e: e.memset(mpc[:, :], 0.0), writes=[mpcB])
        op(POOL, lambda e: e.memset(mpc2[:, :], 0.0), writes=[mpc2B])
        ident, blk64 = cs("ident"), cs("blk64")
        csw, cgain = cs("csw"), cs("cgain")
        ygT, ygB = self.ygT, self.ygB
        mpc_b = [mpc, mpc2]; mpcB_b = [mpcB, mpc2B]
        cbs_b = [cbs0, cbs1]; cbsB_b = [cbs0B, cbs1B]

        def capA(t):
            p = t % 2
            mp, mpB = mpc_b[p], mpcB_b[p]
            mo, moB = mpc_b[1 - p], mpcB_b[1 - p]
            mpv = mp[:, :].rearrange("p (a b) -> p a b", a=4)
            mov = mo[:, :].rearrange("p (a b) -> p a b", a=4)
            S.capture = []
            if t < 0:
                src, srcB = hhalo[:, :], hhB
            else:
                src, srcB = h[:, t * D:(t + 1) * D], hB[t]
            self.norm_transpose(src, srcB, "nm", xn, 0, 128, xnB, xs, xsB, (0, 1))
            for grp in range(3):
                if t < 0 and grp == 0:
                    continue
                pb = 2 + grp
                for q in range(4):
                    j = grp * 4 + q
                    for c in range(8):
                        lhsT = wc[:, (j // 2) * 2048 + c * 256 + (j % 2) * 128: (j // 2) * 2048 + c * 256 + (j % 2) * 128 + 128]
                        rhs = xn[:, c * 128:(c + 1) * 128]
                        op(PE, (lambda pb, q, lhsT, rhs, c: lambda e: e.matmul(psum[pb][:, q * 128:(q + 1) * 128], lhsT=lhsT, rhs=rhs, start=(c == 0), stop=(c == 7)))(pb, q, lhsT, rhs, c),
                           reads=[wcB[j // 2], xnB], writes=[psB[pb][q]])
                if grp == 0:
                    op(ACT, (lambda p: lambda e: e.activation(out=cbs_b[p][:, :], in_=psum[2][:, :], func=AF.Copy))(p), reads=psB[2], writes=[cbsB_b[p]])
                if grp == 1:
                    op(ACT, lambda e: e.activation(out=cct[:, :], in_=psum[3][:, :], func=AF.Copy), reads=psB[3], writes=[cctB])
            op(DVE, lambda e: e.tensor_tensor(out=mpv[:, :, 2:130], in0=cct[:, :].rearrange("p (a b) -> p a b", a=4), in1=psum[4][:, :].rearrange("p (a b) -> p a b", a=4), op=ALU.mult),
               reads=[cctB, mpB] + psB[4], writes=[mpB])
            op(DVE, lambda e: e.tensor_copy(out=mpv[:, :, 0:2], in_=mov[:, :, 128:130]), reads=[moB, mpB], writes=[mpB])
            ops_ = S.capture
            S.capture = None
            return ops_

        def capB(t):
            p = t % 2
            mp, mpB = mpc_b[p], mpcB_b[p]
            cbs, cbsB = cbs_b[p], cbsB_b[p]
            S.capture = []
            for j in range(4):
                op(DVE, (lambda j: lambda e: e.tensor_scalar(out=cacc[:, j * 128:(j + 1) * 128], in0=mp[:, j * 130:j * 130 + 128], scalar1=csw[:, j * 3:j * 3 + 1], scalar2=None, op0=ALU.mult))(j),
                   reads=[mpB, cpB], writes=[caccB])
                for k in range(1, 3):
                    op(DVE, (lambda j, k: lambda e: e.scalar_tensor_tensor(out=cacc[:, j * 128:(j + 1) * 128], in0=mp[:, j * 130 + k:j * 130 + k + 128], scalar=csw[:, j * 3 + k:j * 3 + k + 1], in1=cacc[:, j * 128:(j + 1) * 128], op0=ALU.mult, op1=ALU.add))(j, k),
                       reads=[mpB, cpB, caccB], writes=[caccB])
            op(DVE, lambda e: e.tensor_tensor(out=yv[:, :], in0=cacc[:, :], in1=cbs[:, :], op=ALU.mult), reads=[caccB, cbsB], writes=[yvB])
            op(ACT, lambda e: e.activation(out=sq[:, :], in_=yv[:, :], func=AF.Square), reads=[yvB], writes=[sqB])
            op(PE, lambda e: e.matmul(psum[5][:, :], lhsT=blk64, rhs=sq[:, :], start=True, stop=True), reads=[sqB, cpB], writes=psB[5])
            op(ACT, lambda e: e.activation(out=rs[:, :], in_=psum[5][:, :], func=AF.Ln, bias=cs("eps")), reads=psB[5] + [cpB], writes=[rsB])
            op(ACT, lambda e: e.activation(out=rs[:, :], in_=rs[:, :], func=AF.Exp, scale=-0.5), reads=[rsB], writes=[rsB])
            for j in range(4):
                op(DVE, (lambda j: lambda e: e.scalar_tensor_tensor(out=ycT[:, j * 128:(j + 1) * 128], in0=yv[:, j * 128:(j + 1) * 128], scalar=cgain[:, j:j + 1], in1=rs[:, j * 128:(j + 1) * 128], op0=ALU.mult, op1=ALU.mult))(j),
                   reads=[yvB, rsB, cpB], writes=[ycB])
            for hh in range(2):
                pb = 6 + hh
                for j in range(8):
                    if j < 4:
                        lhsT = ycT[:, j * 128:(j + 1) * 128]
                        rd = [ycB]
                    else:
                        lhsT = ygT[:, (j - 4) * T + t * 128:(j - 4) * T + (t + 1) * 128]
                        rd = [ygB[t]]
                    rhs = wo[:, j * 1024 + hh * 512: j * 1024 + (hh + 1) * 512]
                    op(PE, (lambda pb, lhsT, rhs, j: lambda e: e.matmul(psum[pb][:, :], lhsT=lhsT, rhs=rhs, start=(j == 0), stop=(j == 7)))(pb, lhsT, rhs, j),
                       reads=rd + [woB[j // 2]], writes=psB[pb])
                hap = h[:, t * D + hh * 512: t * D + (hh + 1) * 512]
                op(ACT, (lambda pb, hh: lambda e: e.activation(out=motmp[hh][:, :], in_=psum[pb][:, :], func=AF.Copy))(pb, hh), reads=psB[pb], writes=[motB[hh]])
                op(DVE, (lambda hh, hap: lambda e: e.tensor_tensor(out=hap, in0=hap, in1=motmp[hh][:, :], op=ALU.add))(hh, hap), reads=[motB[hh], hB[t]], writes=[hB[t]])
            ops_ = S.capture
            S.capture = None
            return ops_

        for it in capA(-1):
            S.replay(it)
        for it in capA(0):
            S.replay(it)
        for t in range(NT):
            A = capA(t + 1) if t + 1 < NT else []
            B_ = capB(t)
            na, nb = len(A), len(B_)
            ia = ib = 0
            while ia < na or ib < nb:
                if ib < nb and (ia >= na or ib * max(na, 1) <= ia * nb):
                    S.replay(B_[ib]); ib += 1
                else:
                    S.replay(A[ia]); ia += 1
        es.close()

    def final(self, out, fn_bc):
        S = self.S
        op = S.op
        cs, cpB = self.cs, self.cpB
        h, hB = self.h, self.hB
        es = contextlib.ExitStack()
        ot = [self.sb("ot%d" % i, D, F32, es) for i in range(2)]
        otB = [Buf(), Buf()]
        fs = self.sb("fs", 64, F32, es); fsB = Buf()
        junk = self.sb("junk_f", D, BF16, es); junkB = Buf()
        fnb = self.sb("fnb", D, F32, es); fnbB = Buf()
        new_bufs = otB + [fsB, junkB, fnbB]
        S.alias(new_bufs, getattr(self, "phase_bufs", []))
        self.phase_bufs = new_bufs
        S.dma(SP, lambda e: e.dma_start(out=fnb[:], in_=fn_bc), "const2", S.new_group(), writes=[fnbB])
        import os
        if os.environ.get("KRAWOUT"):
            for t in range(NT):
                S.dma(SP, (lambda t: lambda e: e.dma_start(out=out[t * 128:(t + 1) * 128, :], in_=h[:, t * D:(t + 1) * D]))(t), "out%d" % (t % 2), S.new_group(), reads=[hB[t]])
            es.close()
            return
        for t in range(NT):
            k = t % 2
            c0 = (t % 16) * 3
            hs = h[:, t * D:(t + 1) * D]
            op(ACT, (lambda hs, c0: lambda e: e.activation(out=junk[:, :], in_=hs, func=AF.Square, accum_out=fs[:, c0:c0 + 1]))(hs, c0), reads=[hB[t]], writes=[junkB, fsB])
            op(POOL, (lambda c0: lambda e: e.tensor_scalar(out=fs[:, c0 + 1:c0 + 2], in0=fs[:, c0:c0 + 1], scalar1=1.0 / D, scalar2=EPS, op0=ALU.mult, op1=ALU.add))(c0), reads=[fsB], writes=[fsB])
            op(POOL, (lambda c0: lambda e: e.tensor_tensor(out=fs[:, c0 + 2:c0 + 3], in0=fs[:, c0 + 1:c0 + 2], in1=cs("mhalf"), op=ALU.pow))(c0), reads=[fsB, cpB], writes=[fsB])
            op(DVE, (lambda hs, c0, k: lambda e: e.scalar_tensor_tensor(out=ot[k][:, :], in0=hs, scalar=fs[:, c0 + 2:c0 + 3], in1=fnb[:, :], op0=ALU.mult, op1=ALU.mult))(hs, c0, k),
               reads=[hB[t], fsB, fnbB], writes=[otB[k]])
            S.dma(SP, (lambda t, k: lambda e: e.dma_start(out=out[t * 128:(t + 1) * 128, :], in_=ot[k][:, :]))(t, k), "out%d" % k, S.new_group(), reads=[otB[k]])
        es.close()


def _pack_layout():
    names = [("ident", 128), ("ones", 128), ("triU", 128), ("maskL", 128), ("maskU", 128), ("blk64", 128),
             ("gon", 128), ("n1", 8), ("nm", 8), ("n2", 8), ("cwg", 48), ("csw", 12), ("cgain", 4),
             ("alog", 4), ("dtb", 4), ("mhalf", 1), ("eps", 1)]
    lay = {}
    off = 0
    for n, w in names:
        lay[n] = (off, off + w)
        off += w
    return lay, off


_CP, _CPK_COLS = _pack_layout()
Builder.CP = _CP
Builder.CPK_COLS = _CPK_COLS


def _pack_consts(inp):
    f = np.float32
    cp = np.zeros((128, _CPK_COLS), f)

    def put(name, arr):
        a, b = _CP[name]
        cp[:, a:b] = np.asarray(arr, f).reshape(128, b - a)

    idx = np.arange(128)
    same = (idx[:, None] // 64) == (idx[None, :] // 64)
    put("ident", np.eye(128))
    put("ones", np.ones((128, 128)))
    put("triU", (same & (idx[:, None] <= idx[None, :])))
    put("maskL", np.where(same & (idx[:, None] > idx[None, :]), 0.0, NEG))
    put("maskU", np.where(same & (idx[:, None] <= idx[None, :]), 0.0, NEG))
    put("blk64", same.astype(f) / 64.0)
    put("gon", np.broadcast_to(inp["gdn_out_norm"].reshape(1, 128), (128, 128)))
    put("n1", inp["ffn1_norm"].reshape(8, 128).T)
    put("nm", inp["mix_norm"].reshape(8, 128).T)
    put("n2", inp["ffn2_norm"].reshape(8, 128).T)
    put("cwg", inp["gdn_conv_w"].reshape(4, 12, 128).transpose(2, 1, 0).reshape(128, 48))
    put("csw", inp["conv_short_w"].reshape(3, 4, 128).transpose(2, 1, 0).reshape(128, 12))
    put("cgain", inp["conv_out_norm"].reshape(4, 128).T)
    put("alog", np.broadcast_to(inp["gdn_A_log"].reshape(1, 4), (128, 4)))
    put("dtb", np.broadcast_to(inp["gdn_dt_bias"].reshape(1, 4), (128, 4)))
    put("mhalf", np.full((128, 1), -0.5))
    put("eps", np.full((128, 1), EPS))
    return cp


_NC_CACHE = {}


def _get_nc(debug=False):
    if debug not in _NC_CACHE:
        b = Builder(debug=debug)
        b.spar_h = [0, 0, 0, 0]
        _NC_CACHE[debug] = (b.build(), b)
    return _NC_CACHE[debug]


def kernel(debug=False, **inputs):
    inp = {k: np.asarray(v) for k, v in inputs.items()}
    x = inp["x"].astype(np.float32, copy=False)
    nc, b = _get_nc(debug)
    cp = _pack_consts(inp)
    fn_bc = np.ascontiguousarray(np.broadcast_to(inp["final_norm"].reshape(1, D).astype(np.float32), (128, D)))
    shared = {
        "w1_in": np.ascontiguousarray(inp["ffn1_w_in"][0]), "w1_out": np.ascontiguousarray(inp["ffn1_w_out"][0]),
        "w2_in": np.ascontiguousarray(inp["ffn2_w_in"][0]), "w2_out": np.ascontiguousarray(inp["ffn2_w_out"][0]),
        "wm_in": np.ascontiguousarray(inp["w_mix_in"][0]), "wm_out": np.ascontiguousarray(inp["w_mix_out"][0]),
        "cpk": cp, "fn_bc": fn_bc,
    }
    zeros = np.zeros((T, D), np.float32)
    in_maps = []
    for c in range(8):
        bi, half = c // 2, c % 2
        m = dict(shared)
        m["x_own"] = np.ascontiguousarray(x[bi, half * T:(half + 1) * T])
        m["x_pre"] = zeros if half == 0 else np.ascontiguousarray(x[bi, 0:T])
        in_maps.append(m)
    import os
    ncores = int(os.environ.get("KCORES", 8))
    res = run_bass_kernel_spmd(nc, in_maps[:ncores], core_ids=list(range(ncores)))
    outp = np.zeros((4, 2 * T, D), np.float32)
    for c in range(ncores):
        outp[c // 2, (c % 2) * T:(c % 2 + 1) * T] = res.results[c]["out"]
    if debug:
        return outp, res.results
    return outp
```

```python
import contextlib
import numpy as np
import concourse.bass as bass
import concourse.mybir as mybir
from concourse.bass_utils import run_bass_kernel_spmd

F32 = mybir.dt.float32
BF16 = mybir.dt.bfloat16
AF = mybir.ActivationFunctionType
ALU = mybir.AluOpType

PE, ACT, DVE, POOL, SP = "pe", "act", "dve", "pool", "sp"
COMPUTE = (PE, ACT, DVE, POOL)

D = 1024
DFF = 2816
T = 2048
NT = T // 128
NB = T // 512
EPS = 1e-6
GW0 = 1536
NG = 2056
NEG = -1.0e30


class Buf:
    __slots__ = ("name", "last_w", "readers", "excl")

    def __init__(self, name="", excl=False):
        self.name = name
        self.last_w = None
        self.readers = []
        self.excl = excl


class Op:
    __slots__ = ("eng", "fn", "deps", "needs_inc", "cnt", "is_dma", "key", "grp", "idx")

    def __init__(self, eng, fn, is_dma=False, key=None, grp=None):
        self.eng = eng
        self.fn = fn
        self.deps = []
        self.needs_inc = False
        self.cnt = 0
        self.is_dma = is_dma
        self.key = key
        self.grp = grp


class Sched:
    def __init__(self):
        self.ops = []
        self.grp_ctr = 0

    def new_group(self):
        self.grp_ctr += 1
        return self.grp_ctr

    def _add(self, op, reads, writes):
        if getattr(self, "capture", None) is not None:
            self.capture.append((op, list(reads), list(writes)))
            return op
        return self._add_real(op, reads, writes)

    def replay(self, item):
        return self._add_real(*item)

    def _add_real(self, op, reads, writes):
        op.idx = len(self.ops)
        ex = [b for b in reads if b.excl]
        if ex:
            reads = [b for b in reads if not b.excl]
            writes = list(writes) + ex
        deps = {}
        for b in reads:
            if b.last_w is not None:
                deps[id(b.last_w)] = b.last_w
        for b in writes:
            if b.last_w is not None:
                deps[id(b.last_w)] = b.last_w
            for r in b.readers:
                deps[id(r)] = r
        latest = {}
        for d in deps.values():
            if d is op:
                continue
            if (not d.is_dma) and (not op.is_dma) and d.eng == PE and op.eng == PE:
                continue
            if d.is_dma:
                op.deps.append(d)
            else:
                cur = latest.get(d.eng)
                if cur is None or d.idx > cur.idx:
                    latest[d.eng] = d
        op.deps.extend(latest.values())
        for b in reads:
            if op.is_dma:
                b.readers.append(op)
            else:
                b.readers = [r for r in b.readers if r.is_dma or r.eng != op.eng]
                b.readers.append(op)
        for b in writes:
            b.last_w = op
            b.readers = []
        self.ops.append(op)
        return op

    def op(self, eng, fn, reads=(), writes=()):
        return self._add(Op(eng, fn), reads, writes)

    def dma(self, queue, fn, key, grp, reads=(), writes=()):
        return self._add(Op(queue, fn, is_dma=True, key=key, grp=grp), reads, writes)

    def alias(self, new_bufs, old_bufs):
        acc = {}
        for b in old_bufs:
            if b.last_w is not None:
                acc[id(b.last_w)] = b.last_w
            for r in b.readers:
                acc[id(r)] = r
        for nb in new_bufs:
            nb.readers = list(acc.values())

    def emit(self, nc, final_wait_keys=()):
        ops = self.ops
        for o in ops:
            for d in o.deps:
                d.needs_inc = True
        cnt = {}
        grp_end = {}
        for o in ops:
            if o.is_dma:
                k = ("dma", o.key)
                cnt[k] = cnt.get(k, 0) + 1
                o.cnt = cnt[k]
                grp_end[(o.key, o.grp)] = o.cnt
            elif o.needs_inc:
                cnt[o.eng] = cnt.get(o.eng, 0) + 1
                o.cnt = cnt[o.eng]
        dma_keys = sorted({o.key for o in ops if o.is_dma})
        streams = {e: [o for o in ops if o.eng == e] for e in (PE, ACT, DVE, POOL, SP)}
        self.stats = {e: len(s) for e, s in streams.items()}
        self.stats["incs"] = dict(cnt)

        import os
        SEG = int(os.environ.get('KSEG', 1500))
        with contextlib.ExitStack() as es:
            sems = {}
            for e in COMPUTE:
                nseg = (cnt.get(e, 0) + SEG - 1) // SEG + 1
                sems[e] = [es.enter_context(nc.semaphore("s_%s_%d" % (e, j))) for j in range(nseg)]
            for k in dma_keys:
                sems[("dma", k)] = es.enter_context(nc.semaphore("d_" + str(k)))
            block = es.enter_context(nc.Block())

            def run_stream(engname, eng):
                waited = {}
                for o in streams[engname]:
                    for d in o.deps:
                        if d.is_dma:
                            sk = ("dma", d.key)
                            val = 16 * grp_end[(d.key, d.grp)]
                            sem = sems[sk]
                        else:
                            seg = (d.cnt - 1) // SEG
                            sk = (d.eng, seg)
                            val = (d.cnt - 1) % SEG + 1
                            sem = sems[d.eng][seg]
                            if any(k2[0] == d.eng and k2[1] > seg for k2 in waited if isinstance(k2, tuple) and k2[0] == d.eng):
                                continue
                        if waited.get(sk, 0) >= val:
                            continue
                        waited[sk] = val
                        eng.wait_ge(sem, val)
                    ins = o.fn(eng)
                    if o.is_dma:
                        ins.then_inc(sems[("dma", o.key)], 16)
                    elif o.needs_inc:
                        ins.then_inc(sems[o.eng][(o.cnt - 1) // SEG], 1)
                if engname == SP:
                    for k in final_wait_keys:
                        eng.wait_ge(sems[("dma", k)], 16 * cnt[("dma", k)])

            @block.sync
            def _(e):
                run_stream(SP, e)

            @block.tensor
            def _(e):
                run_stream(PE, e)

            @block.scalar
            def _(e):
                run_stream(ACT, e)

            @block.vector
            def _(e):
                run_stream(DVE, e)

            @block.gpsimd
            def _(e):
                run_stream(POOL, e)


class Builder:
    def __init__(self, debug=False):
        self.debug = debug
        self.nc = bass.Bass("TRN2", target_bir_lowering=False)
        self.S = Sched()
        self.es = contextlib.ExitStack()
        self.dbg_outs = []
        self.dbg_keys = []
        self.rr = 0

    def sb(self, name, cols, dt=F32, es=None):
        return (es or self.es).enter_context(self.nc.sbuf_tensor(name, [128, cols], dt))

    def dram_in(self, name, shape, dt=F32):
        return self.nc.dram_tensor(name, list(shape), dt, kind="ExternalInput").ap()

    def dram_out(self, name, shape, dt=F32):
        return self.nc.dram_tensor(name, list(shape), dt, kind="ExternalOutput").ap()

    def dbg(self, name, ap, cols, bufs, dt=F32):
        if not self.debug:
            return
        o = self.dram_out("dbg_" + name, [128, cols], dt)
        self.dbg_keys.append("dbg_" + name)
        self.S.dma(SP, lambda e: e.dma_start(out=o, in_=ap), "dbg_" + name, self.S.new_group(), reads=bufs)

    def ew(self):
        self.rr += 1
        return ACT if (self.rr & 1) else DVE

    def build(self):
        nc, S = self.nc, self.S
        op = S.op
        x_pre = self.dram_in("x_pre", [T, D])
        x_own = self.dram_in("x_own", [T, D])
        w1_in = self.dram_in("w1_in", [D, 2 * DFF])
        w1_out = self.dram_in("w1_out", [DFF, D])
        w2_in = self.dram_in("w2_in", [D, 2 * DFF])
        w2_out = self.dram_in("w2_out", [DFF, D])
        wm_in = self.dram_in("wm_in", [D, 3592])
        wm_out = self.dram_in("wm_out", [D, D])
        cpk = self.dram_in("cpk", [128, self.CPK_COLS])
        fn_bc = self.dram_in("fn_bc", [128, D])
        out = self.dram_out("out", [T, D])
        self.out_grp = S.new_group()

        h = self.sb("h", NT * D)
        hB = [Buf("h%d" % t) for t in range(NT)]
        stage = [self.sb("stage%d" % i, 2048) for i in range(2)]
        stB = [Buf("st%d" % i) for i in range(2)]
        self.stage, self.stB, self.st_i = stage, stB, 0
        import os
        self.cast_order = os.environ.get('KCAST', 'dve').split(',')
        cp = self.sb("cp", self.CPK_COLS)
        cpB = Buf("cp")
        hhalo = self.sb("hhalo", D)
        hhB = Buf("hhalo")
        ygT = self.sb("ygT", 4 * T, BF16)
        ygB = [Buf("yg%d" % t) for t in range(NT)]
        pch = self.sb("pch", 36)
        pchB = Buf("pch")
        Sst = [self.sb("Sst%d" % i, 4 * 128) for i in range(2)]
        SsB = [[Buf("S%d_%d" % (i, hh)) for hh in range(4)] for i in range(2)]
        stat = self.sb("stat", 64)
        negA = self.sb("negA", 4)
        negAB = Buf("negA")
        psum = [self.es.enter_context(nc.psum_tensor("ps%d" % i, [128, 512], F32)) for i in range(8)]
        psB = [[Buf("ps%d" % i, excl=True)] * 4 for i in range(8)]
        self.psum, self.psB = psum, psB
        self.h, self.hB = h, hB

        C = self.CP
        g0 = S.new_group()
        S.dma(SP, lambda e: e.dma_start(out=cp[:], in_=cpk), "const", g0, writes=[cpB])
        self.cp, self.cpB = cp, cpB

        def cs(name, n=None):
            a, b = C[name]
            return cp[:, a:b]

        self.cs = cs
        ident = cs("ident")
        op(POOL, lambda e: e.memset(Sst[0][:], 0.0), writes=SsB[0])
        op(POOL, lambda e: e.memset(pch[:], 0.0), writes=[pchB])
        op(ACT, lambda e: e.activation(out=negA[:], in_=cs("alog"), func=AF.Exp), reads=[cpB], writes=[negAB])
        op(DVE, lambda e: e.tensor_scalar(out=negA[:], in0=negA[:], scalar1=-1.0, scalar2=None, op0=ALU.mult),
           reads=[negAB], writes=[negAB])
        self.negA, self.negAB = negA, negAB
        self.pch, self.pchB = pch, pchB
        self.Sst, self.SsB = Sst, SsB
        self.ygT, self.ygB = ygT, ygB
        self.spar = 0
        self.stat = stat
        self.statB = Buf("stat")

        import os
        PH = set(os.environ.get("KPH", "pf,pg,of,og,cv,f2").split(","))
        if "pf" in PH:
            self.load_x(x_pre)
            self.ffn(w1_in, w1_out, "n1", tag="p1")
        op(POOL, lambda e: e.tensor_copy(out=hhalo[:], in_=h[:, (NT - 1) * D:NT * D]), reads=[hB[NT - 1]], writes=[hhB])
        if "pg" in PH:
            self.gdn_phase(wm_in, full=False, tag="pg")
        self.load_x(x_own)
        if "of" in PH:
            self.ffn(w1_in, w1_out, "n1", tag="o1")
        self.dbg("h1", h[:, 0:D], D, [hB[0]])
        if "og" in PH:
            self.gdn_phase(wm_in, full=True, tag="og")
        self.dbg("yg", self.ygT[:, 0:T], T, self.ygB, BF16)
        if "cv" in PH:
            self.conv_phase(wm_in, wm_out, hhalo, hhB)
        self.dbg("h2", h[:, 0:D], D, [hB[0]])
        if "f2" in PH:
            self.ffn(w2_in, w2_out, "n2", tag="o2")
        self.final(out, fn_bc)
        S.emit(nc, final_wait_keys=["out0", "out1"] + self.dbg_keys)
        self.es.close()
        return nc

    def load_x(self, xd):
        S, h, hB = self.S, self.h, self.hB
        g = S.new_group()
        for t in range(NT):
            S.dma(SP, (lambda t: lambda e: e.dma_start(out=h[:, t * D:(t + 1) * D], in_=xd[t * 128:(t + 1) * 128, :]))(t),
                  "x%d" % (t % 4), g, writes=[hB[t]])

    def load_dma(self, dst_ap, dstB, src_ap, shape3):
        S = self.S
        n = len(self.stage)
        i = self.st_i % n
        self.st_i += 1
        st, sB = self.stage[i], self.stB[i]
        a, b = shape3
        sview = st[:, 0:a * b].rearrange("p (a b) -> p a b", a=a) if a > 1 else st[:, 0:b]
        sflat = st[:, 0:a * b]
        g = S.new_group()
        S.dma(SP, lambda e: e.dma_start(out=sview, in_=src_ap), "st%d" % i, g, writes=[sB])
        return (dst_ap, dstB, sflat, sB)

    def load_cast_do(self, hnd):
        S = self.S
        dst_ap, dstB, sflat, sB = hnd
        self.cast_i = getattr(self, "cast_i", 0) + 1
        eng = self.cast_order[self.cast_i % len(self.cast_order)]
        if eng == ACT:
            S.op(ACT, lambda e: e.activation(out=dst_ap, in_=sflat, func=AF.Copy), reads=[sB], writes=[dstB])
        else:
            S.op(eng, lambda e: e.tensor_copy(out=dst_ap, in_=sflat), reads=[sB], writes=[dstB])

    def load_cast(self, dst_ap, dstB, src_ap, shape3=None, key="w"):
        self.load_cast_do(self.load_dma(dst_ap, dstB, src_ap, shape3))

    def norm_transpose(self, src_ap, srcB, gain_name, dst, dst_off, dst_stride, dstB, xs, xsB, pbanks, only=None):
        S, cs = self.S, self.cs
        op = S.op
        stat = self.stat
        stB = self.statB
        op(ACT, lambda e: e.activation(out=xs[:, 0:D], in_=src_ap, func=AF.Square, accum_out=stat[:, 0:1]),
           reads=[srcB], writes=[xsB, stB])
        op(POOL, lambda e: e.tensor_scalar(out=stat[:, 1:2], in0=stat[:, 0:1], scalar1=1.0 / D, scalar2=EPS,
                                           op0=ALU.mult, op1=ALU.add), reads=[stB], writes=[stB])
        op(POOL, lambda e: e.tensor_tensor(out=stat[:, 2:3], in0=stat[:, 1:2], in1=cs("mhalf"), op=ALU.pow),
           reads=[stB, self.cpB], writes=[stB])
        op(DVE, lambda e: e.tensor_scalar(out=xs[:, 0:D], in0=src_ap, scalar1=stat[:, 2:3], scalar2=None, op0=ALU.mult),
           reads=[srcB, stB], writes=[xsB])
        if only == "front":
            return
        self._nt_back(gain_name, dst, dst_off, dst_stride, dstB, xs, xsB, pbanks)

    def _nt_back(self, gain_name, dst, dst_off, dst_stride, dstB, xs, xsB, pbanks):
        S, cs = self.S, self.cs
        op = S.op
        gain = cs(gain_name)
        ident = cs("ident")
        for half in range(2):
            pb = pbanks[half]
            pbuf = self.psum[pb]
            for q in range(4):
                c = half * 4 + q
                op(PE, (lambda c, q, pbuf: lambda e: e.transpose(pbuf[:, q * 128:(q + 1) * 128], xs[:, c * 128:(c + 1) * 128], ident))(c, q, pbuf),
                   reads=[xsB, self.cpB], writes=[self.psB[pb][q]])
            for q in range(4):
                c = half * 4 + q
                eng = self.ew()
                o_ap = dst[:, c * dst_stride + dst_off: c * dst_stride + dst_off + 128]
                i_ap = pbuf[:, q * 128:(q + 1) * 128]
                g_ap = gain[:, c:c + 1]
                if eng == ACT:
                    op(ACT, (lambda o_ap, i_ap, g_ap: lambda e: e.activation(out=o_ap, in_=i_ap, func=AF.Copy, scale=g_ap))(o_ap, i_ap, g_ap),
                       reads=[self.psB[pb][q], self.cpB], writes=[dstB])
                else:
                    op(DVE, (lambda o_ap, i_ap, g_ap: lambda e: e.tensor_scalar(out=o_ap, in0=i_ap, scalar1=g_ap, scalar2=None, op0=ALU.mult))(o_ap, i_ap, g_ap),
                       reads=[self.psB[pb][q], self.cpB], writes=[dstB])

    def ffn(self, w_in, w_out, gain_name, tag):
        import os
        nc, S = self.nc, self.S
        op = S.op
        h, hB = self.h, self.hB
        psum, psB = self.psum, self.psB
        es = contextlib.ExitStack()
        xnT = self.sb("xnT_" + tag, 8 * T, BF16, es)
        xnB = [Buf("xn%d" % t) for t in range(NT)]
        CPP = int(os.environ.get("KCPP", 4))
        nsub = CPP // 2
        wbi = [self.sb("wbi%d_%s" % (i, tag), 2 * nsub * 2048, BF16, es) for i in range(2)]
        wbo = [self.sb("wbo%d_%s" % (i, tag), CPP * 1024, BF16, es) for i in range(2)]
        assert CPP == 4
        wbiB = [[Buf() for _ in range(4)] for _ in range(2)]
        wboB = [[Buf() for _ in range(nsub)] for _ in range(2)]
        hid = [self.sb("hid%d_%s" % (i, tag), CPP * 512, BF16, es) for i in range(2)]
        hidB = [Buf(), Buf()]
        sg0 = self.sb("sg0_%s" % tag, 512, F32, es); sg = [sg0, sg0]
        sgB0 = Buf(); sgB = [sgB0, sgB0]
        xs = [self.sb("xs%d_%s" % (i, tag), D, F32, es) for i in range(2)]
        xsB = [Buf(), Buf()]
        ev = [xs[1][:, 0:512], xs[1][:, 512:1024]]
        evB = [xsB[1], xsB[1]]
        self.junkB = Buf()
        base_stage, base_stB = self.stage, self.stB
        nextra = int(os.environ.get("KXST", 0))
        xst = [self.sb("xst%d_%s" % (i, tag), 2048, F32, es) for i in range(nextra)]
        xstB = [Buf() for _ in range(nextra)]
        self.stage, self.stB = base_stage + xst, base_stB + xstB
        new_bufs = xnB + wbiB[0] + wbiB[1] + wboB[0] + wboB[1] + hidB + sgB + xsB + [self.junkB] + evB + xstB
        S.alias(new_bufs, getattr(self, "phase_bufs", []))
        self.phase_bufs = new_bufs

        w_in_v = w_in.rearrange("(c p) n -> p c n", p=128)
        w_out_v = w_out.rearrange("(c p) n -> p c n", p=128)
        pieces = []
        c0 = 0
        while c0 < DFF // 128:
            n = min(CPP, DFF // 128 - c0)
            pieces.append((c0, n))
            c0 += n
        NP = len(pieces)

        def piece_specs(p):
            s = p % 2
            ch0, n = pieces[p]
            W = n * 128
            col = ch0 * 128
            specs = []
            for which in range(2):
                for csub in range(2):
                    base = which * 4096 + csub * 2048
                    specs.append((wbi[s][:, base:base + 4 * W], wbiB[s][which * 2 + csub],
                                  w_in_v[:, csub * 4:csub * 4 + 4, which * DFF + col:which * DFF + col + W], (4, W)))
            for sub in range(n // 2):
                specs.append((wbo[s][:, sub * 2048:(sub + 1) * 2048], wboB[s][sub], w_out_v[:, ch0 + 2 * sub:ch0 + 2 * sub + 2, :], (2, 1024)))
            return specs

        def load_piece(p):
            for sp_ in piece_specs(p):
                self.load_cast(*sp_)

        load_piece(0)
        PBK = [(0, 1), (2, 3), (4, 5), (6, 7)]
        self.norm_transpose(h[:, 0:D], hB[0], gain_name, xnT, 0, T, xnB[0], xs[0], xsB[0], PBK[0], only="front")
        for t in range(NT):
            if t + 1 < NT:
                self.norm_transpose(h[:, (t + 1) * D:(t + 2) * D], hB[t + 1], gain_name, xnT, (t + 1) * 128, T, xnB[t + 1],
                                    xs[(t + 1) % 2], xsB[(t + 1) % 2], PBK[(t + 1) % 4], only="front")
            self._nt_back(gain_name, xnT, t * 128, T, xnB[t], xs[t % 2], xsB[t % 2], PBK[t % 4])
        blocks = [(p, tb) for p in range(NP) for tb in range(NB)]
        st = {"gi": 0, "oi": 0}

        def stage1(idx):
            p, tb = blocks[idx]
            s = p % 2
            hs = idx % 2
            n = pieces[p][1]
            for j in range(n):
                gi = st["gi"]
                for which in range(2):
                    pb = (0 if which == 0 else 2) + (gi % 2)
                    W = n * 128
                    for c in range(8):
                        off = which * 4096 + (c // 4) * 2048 + (c % 4) * W + j * 128
                        lhsT = wbi[s][:, off:off + 128]
                        rhs = xnT[:, c * T + tb * 512: c * T + (tb + 1) * 512]
                        op(PE, (lambda pb, lhsT, rhs, c: lambda e: e.matmul(psum[pb][:, :], lhsT=lhsT, rhs=rhs, start=(c == 0), stop=(c == 7)))(pb, lhsT, rhs, c),
                           reads=[wbiB[s][which * 2 + c // 4]] + xnB[tb * 4:(tb + 1) * 4], writes=psB[pb])
                pg, pu = (gi % 2), 2 + (gi % 2)
                k = gi % 2
                op(ACT, (lambda pg, k: lambda e: e.activation(out=sg[k][:, :], in_=psum[pg][:, :], func=AF.Silu))(pg, k),
                   reads=psB[pg], writes=[sgB[k]])
                op(DVE, (lambda pu, k, hs, j: lambda e: e.tensor_tensor(out=hid[hs][:, j * 512:(j + 1) * 512], in0=sg[k][:, :], in1=psum[pu][:, :], op=ALU.mult))(pu, k, hs, j),
                   reads=[sgB[k]] + psB[pu], writes=[hidB[hs]])
                st["gi"] += 1

        def stage2(idx):
            p, tb = blocks[idx]
            s = p % 2
            hs = idx % 2
            n = pieces[p][1]
            for tt in range(4):
                t = tb * 4 + tt
                for hh in range(2):
                    oi = st["oi"]
                    pb = 4 + (oi % 2)
                    for j in range(n):
                        lhsT = hid[hs][:, j * 512 + tt * 128: j * 512 + (tt + 1) * 128]
                        rhs = wbo[s][:, j * 1024 + hh * 512: j * 1024 + (hh + 1) * 512]
                        op(PE, (lambda pb, lhsT, rhs, j: lambda e: e.matmul(psum[pb][:, :], lhsT=lhsT, rhs=rhs, start=(j == 0), stop=(j == n - 1)))(pb, lhsT, rhs, j),
                           reads=[hidB[hs], wboB[s][j // 2]], writes=psB[pb])
                    hap = h[:, t * D + hh * 512: t * D + (hh + 1) * 512]
                    if oi % 2 == 0:
                        op(DVE, (lambda pb, hap: lambda e: e.scalar_tensor_tensor(out=hap, in0=psum[pb][:, :], scalar=0.5, in1=hap, op0=ALU.mult, op1=ALU.add))(pb, hap),
                           reads=psB[pb] + [hB[t]], writes=[hB[t]])
                    else:
                        k = (oi // 2) % 2
                        op(ACT, (lambda pb, k: lambda e: e.activation(out=ev[k][:, :], in_=psum[pb][:, :], func=AF.Copy, scale=0.5))(pb, k),
                           reads=psB[pb], writes=[evB[k]])
                        op(POOL, (lambda k, hap: lambda e: e.tensor_tensor(out=hap, in0=hap, in1=ev[k][:, :], op=ALU.add))(k, hap),
                           reads=[evB[k], hB[t]], writes=[hB[t]])
                    st["oi"] += 1

        pend_specs = []
        inflight = []
        for idx in range(len(blocks)):
            stage1(idx)
            if idx > 0:
                stage2(idx - 1)
            p, tb = blocks[idx]
            if tb == 0 and p + 1 < NP:
                pend_specs = piece_specs(p + 1)
            for hnd in inflight:
                self.load_cast_do(hnd)
            inflight = []
            if tb == NB - 1:
                for sp_ in pend_specs:
                    self.load_cast(*sp_)
                pend_specs = []
            else:
                for sp_ in pend_specs[:2]:
                    inflight.append(self.load_dma(*sp_))
                pend_specs = pend_specs[2:]
        stage2(len(blocks) - 1)
        self.stage, self.stB = base_stage, base_stB
        es.close()

    def gdn_phase(self, wm_in, full, tag):
        import os
        nc, S = self.nc, self.S
        op = S.op
        cs, cpB = self.cs, self.cpB
        h, hB = self.h, self.hB
        psum, psB = self.psum, self.psB
        es = contextlib.ExitStack()
        pc = self.sb("pc_" + tag, 12 * 131, F32, es); pcB = Buf()
        wg = self.sb("wg_" + tag, 8 * NG, BF16, es)
        wgB = [Buf() for _ in range(9)]
        xs = self.sb("xs_" + tag, D, F32, es); xsB = Buf()
        xn = self.sb("xn_" + tag, 8 * 128, BF16, es); xnB = Buf()
        qkv0 = self.sb("qkv_" + tag, 12 * 128, F32, es); qkvB0 = Buf()
        qkv_b = [qkv0, self.stage[0][:, 0:1536]]; qkvB_b = [qkvB0, self.stB[0]]
        cacc = self.sb("cacc_" + tag, 12 * 128, F32, es); caccB = Buf()
        etmp = self.sb("etmp_" + tag, 12 * 128, F32, es); etmpB = Buf()
        rs, rsB = etmp, etmpB
        zT0 = self.sb("zT_" + tag, 4 * 128, F32, es); zB0 = Buf()
        zT_b = [zT0, self.stage[1][:, 0:512]]; zB_b = [zB0, self.stB[1]]
        sm_b = [self.sb("sm%d_%s" % (i, tag), 64, F32, es) for i in range(2)]; smB_b = [Buf(), Buf()]
        Dg = self.sb("Dg_" + tag, 512, F32, es); DgB = Buf()
        glb_b = [self.sb("glb%d_%s" % (i, tag), 8, F32, es) for i in range(2)]; glB_b = [Buf(), Buf()]
        def mk(n, cols=128, dt=F32):
            return self.sb(n + "_" + tag, cols, dt, es), Buf(n)
        HB = []
        for hh in range(4):
            d_ = {}
            for n in ["kbg", "kdec", "vbeta", "tm1", "E1", "Lm", "AT", "X0", "X1", "Y0", "Y1", "P0", "P1", "wT", "um", "vnew"]:
                d_[n] = mk("%s%d" % (n, hh))
            HB.append(d_)
        new_bufs = wgB + [pcB, xsB, xnB, qkvB0, caccB, etmpB, zB0, DgB] + smB_b + glB_b + [b for d_ in HB for _, b in d_.values()]
        S.alias(new_bufs, getattr(self, "phase_bufs", []))
        self.phase_bufs = new_bufs

        wm_v = wm_in.rearrange("(c p) n -> p c n", p=128)
        col = 0
        while col < NG:
            w = min(256, NG - col)
            base = (col // 256) * 2048
            self.load_cast(wg[:, base:base + 8 * w], wgB[col // 256], wm_v[:, :, GW0 + col:GW0 + col + w], (8, w))
            col += w

        ident, ones, triU = cs("ident"), cs("ones"), cs("triU")
        maskL, maskU = cs("maskL"), cs("maskU")
        cwg = cs("cwg")
        pcv0 = pc[:, :].rearrange("p (a b) -> p a b", a=12)
        pchv = self.pch[:, :].rearrange("p (a b) -> p a b", a=12)
        op(DVE, lambda e: e.tensor_copy(out=pcv0[:, :, 0:3], in_=pchv), reads=[self.pchB], writes=[pcB])
        Sst, SsB = self.Sst, self.SsB
        nq = 16 if full else 12

        def pre_ops(t):
            pp = t % 2
            qkv, qkvB = qkv_b[pp], qkvB_b[pp]
            zT, zB = zT_b[pp], zB_b[pp]
            sm, smB = sm_b[pp], smB_b[pp]
            glb, glB = glb_b[pp], glB_b[pp]
            S.capture = []
            self.norm_transpose(h[:, t * D:(t + 1) * D], hB[t], "nm", xn, 0, 128, xnB, xs, xsB, (0, 1))
            for grp in range(nq // 4):
                if (not full) and grp == 0 and t != NT - 1:
                    continue
                pb = 2
                for q in range(4):
                    j = grp * 4 + q
                    for c in range(8):
                        lhsT = wg[:, (j // 2) * 2048 + c * 256 + (j % 2) * 128: (j // 2) * 2048 + c * 256 + (j % 2) * 128 + 128]
                        rhs = xn[:, c * 128:(c + 1) * 128]
                        op(PE, (lambda pb, q, lhsT, rhs, c: lambda e: e.matmul(psum[pb][:, q * 128:(q + 1) * 128], lhsT=lhsT, rhs=rhs, start=(c == 0), stop=(c == 7)))(pb, q, lhsT, rhs, c),
                           reads=[wgB[j // 2], xnB], writes=[psB[pb][q]])
                if grp < 3:
                    dstv = pc[:, grp * 4 * 131:(grp + 1) * 4 * 131].rearrange("p (a b) -> p a b", a=4)[:, :, 3:131]
                    srcv = psum[pb][:, :].rearrange("p (a b) -> p a b", a=4)
                    eng = self.ew()
                    if eng == ACT:
                        op(ACT, (lambda dstv, srcv: lambda e: e.activation(out=dstv, in_=srcv, func=AF.Copy))(dstv, srcv), reads=psB[pb], writes=[pcB])
                    else:
                        op(DVE, (lambda dstv, srcv: lambda e: e.tensor_copy(out=dstv, in_=srcv))(dstv, srcv), reads=psB[pb], writes=[pcB])
                else:
                    op(ACT, (lambda pb: lambda e: e.activation(out=zT[:, :], in_=psum[pb][:, :], func=AF.Copy))(pb), reads=psB[pb], writes=[zB])
            i3 = len(S.capture)
            for c in range(8):
                lhsT = xn[:, c * 128:(c + 1) * 128]
                rhs = wg[:, 8 * 2048 + c * 8: 8 * 2048 + c * 8 + 8]
                op(PE, (lambda lhsT, rhs, c: lambda e: e.matmul(psum[2][:, 0:8], lhsT=lhsT, rhs=rhs, start=(c == 0), stop=(c == 7)))(lhsT, rhs, c),
                   reads=[wgB[8], xnB], writes=[psB[2][0]])
            op(DVE, lambda e: e.tensor_copy(out=sm[:, 0:8], in_=psum[2][:, 0:8]), reads=[psB[2][0]], writes=[smB])
            op(ACT, lambda e: e.activation(out=sm[:, 8:12], in_=sm[:, 0:4], func=AF.Exp, scale=-1.0), reads=[smB], writes=[smB])
            op(DVE, lambda e: e.tensor_scalar(out=sm[:, 8:12], in0=sm[:, 8:12], scalar1=1.0, scalar2=None, op0=ALU.add), reads=[smB], writes=[smB])
            op(DVE, lambda e: e.reciprocal(out=sm[:, 8:12], in_=sm[:, 8:12]), reads=[smB], writes=[smB])
            op(DVE, lambda e: e.tensor_tensor(out=sm[:, 12:16], in0=sm[:, 4:8], in1=cs("dtb"), op=ALU.add), reads=[smB, cpB], writes=[smB])
            op(ACT, lambda e: e.activation(out=sm[:, 12:16], in_=sm[:, 12:16], func=AF.Exp), reads=[smB], writes=[smB])
            op(ACT, lambda e: e.activation(out=sm[:, 12:16], in_=sm[:, 12:16], func=AF.Ln, bias=1.0), reads=[smB], writes=[smB])
            op(DVE, lambda e: e.tensor_tensor(out=sm[:, 12:16], in0=sm[:, 12:16], in1=self.negA[:, :], op=ALU.mult), reads=[smB, self.negAB], writes=[smB])
            op(PE, lambda e: e.matmul(psum[2][:, 8:12], lhsT=triU, rhs=sm[:, 12:16], start=True, stop=True), reads=[smB, cpB], writes=[psB[2][0]])
            op(DVE, lambda e: e.tensor_copy(out=sm[:, 16:20], in_=psum[2][:, 8:12]), reads=[psB[2][0]], writes=[smB])
            op(ACT, lambda e: e.activation(out=sm[:, 20:24], in_=sm[:, 16:20], func=AF.Exp), reads=[smB], writes=[smB])
            op(DVE, lambda e: e.tensor_scalar(out=sm[:, 24:28], in0=sm[:, 16:20], scalar1=-1.0, scalar2=None, op0=ALU.mult), reads=[smB], writes=[smB])
            op(DVE, lambda e: e.tensor_tensor(out=sm[:, 28:32], in0=sm[:, 8:12], in1=sm[:, 20:24], op=ALU.mult), reads=[smB], writes=[smB])
            for hh in range(4):
                op(DVE, (lambda hh: lambda e: e.tensor_scalar(out=Dg[:, hh * 128:(hh + 1) * 128], in0=ident, scalar1=sm[:, 16 + hh:17 + hh], scalar2=None, op0=ALU.mult))(hh),
                   reads=[smB, cpB], writes=[DgB])
            op(PE, lambda e: e.matmul(psum[3][:, :], lhsT=ones, rhs=Dg[:, 0:512], start=True, stop=True), reads=[DgB, cpB], writes=psB[3])
            Gv = psum[3][:, :].rearrange("p (a b) -> p a b", a=4)
            op(ACT, lambda e: e.activation(out=glb[:, 0:8].rearrange("p (a b) -> p a b", a=4), in_=Gv[:, :, 63:128:64], func=AF.Exp), reads=psB[3], writes=[glB])
            op(DVE, lambda e: e.tensor_tensor(out=sm[0:64, 32:36], in0=Gv[0:64, :, 63], in1=sm[0:64, 16:20], op=ALU.subtract), reads=psB[3] + [smB], writes=[smB])
            op(DVE, lambda e: e.tensor_tensor(out=sm[64:128, 32:36], in0=Gv[64:128, :, 127], in1=sm[64:128, 16:20], op=ALU.subtract), reads=psB[3] + [smB], writes=[smB])
            op(ACT, lambda e: e.activation(out=sm[:, 32:36], in_=sm[:, 32:36], func=AF.Exp), reads=[smB], writes=[smB])
            i4 = len(S.capture)
            pcv = pc[:, :].rearrange("p (a b) -> p a b", a=12)
            for j in range(0 if full else 4, 12):
                op(DVE, (lambda j: lambda e: e.tensor_scalar(out=cacc[:, j * 128:(j + 1) * 128], in0=pc[:, j * 131:j * 131 + 128], scalar1=cwg[:, j * 4:j * 4 + 1], scalar2=None, op0=ALU.mult))(j),
                   reads=[pcB, cpB], writes=[caccB])
                for k in range(1, 4):
                    op(DVE, (lambda j, k: lambda e: e.scalar_tensor_tensor(out=cacc[:, j * 128:(j + 1) * 128], in0=pc[:, j * 131 + k:j * 131 + k + 128], scalar=cwg[:, j * 4 + k:j * 4 + k + 1], in1=cacc[:, j * 128:(j + 1) * 128], op0=ALU.mult, op1=ALU.add))(j, k),
                       reads=[pcB, cpB, caccB], writes=[caccB])
            op(DVE, lambda e: e.tensor_copy(out=pcv[:, :, 0:3], in_=pcv[:, :, 128:131]), reads=[pcB], writes=[pcB])
            i5 = len(S.capture)
            c_lo = 0 if full else 512
            op(ACT, lambda e: e.activation(out=etmp[:, c_lo:1536], in_=cacc[:, c_lo:1536], func=AF.Exp, scale=-1.0), reads=[caccB], writes=[etmpB])
            op(ACT, lambda e: e.activation(out=etmp[:, c_lo:1536], in_=etmp[:, c_lo:1536], func=AF.Ln, bias=1.0), reads=[etmpB], writes=[etmpB])
            op(ACT, lambda e: e.activation(out=etmp[:, c_lo:1536], in_=etmp[:, c_lo:1536], func=AF.Exp, scale=-1.0), reads=[etmpB], writes=[etmpB])
            op(DVE, lambda e: e.tensor_tensor(out=qkv[:, c_lo:1536], in0=cacc[:, c_lo:1536], in1=etmp[:, c_lo:1536], op=ALU.mult), reads=[etmpB, caccB], writes=[qkvB])
            op(ACT, lambda e: e.activation(out=etmp[:, c_lo:1024], in_=qkv[:, c_lo:1024], func=AF.Square), reads=[qkvB, etmpB], writes=[etmpB])
            for half in range(0 if full else 1, 2):
                op(PE, (lambda half: lambda e: e.matmul(psum[half][:, :], lhsT=ones, rhs=etmp[:, half * 512:(half + 1) * 512], start=True, stop=True))(half),
                   reads=[etmpB, cpB], writes=psB[half])
                op(ACT, (lambda half: lambda e: e.activation(out=rs[:, half * 512:(half + 1) * 512], in_=psum[half][:, :], func=AF.Ln, bias=cs("eps")))(half),
                   reads=psB[half] + [cpB], writes=[rsB])
            op(ACT, lambda e: e.activation(out=rs[:, c_lo:1024], in_=rs[:, c_lo:1024], func=AF.Exp, scale=-0.5), reads=[rsB], writes=[rsB])
            if full:
                op(DVE, lambda e: e.scalar_tensor_tensor(out=qkv[:, 0:512], in0=qkv[:, 0:512], scalar=128.0 ** -0.5, in1=rs[:, 0:512], op0=ALU.mult, op1=ALU.mult), reads=[qkvB, rsB], writes=[qkvB])
            op(DVE, lambda e: e.tensor_tensor(out=qkv[:, 512:1024], in0=qkv[:, 512:1024], in1=rs[:, 512:1024], op=ALU.mult), reads=[qkvB, rsB], writes=[qkvB])
            if full:
                op(ACT, lambda e: e.activation(out=etmp[:, 0:512], in_=zT[:, :], func=AF.Exp, scale=-1.0), reads=[zB, etmpB], writes=[etmpB])
                op(ACT, lambda e: e.activation(out=etmp[:, 0:512], in_=etmp[:, 0:512], func=AF.Ln, bias=1.0), reads=[etmpB], writes=[etmpB])
                op(ACT, lambda e: e.activation(out=etmp[:, 0:512], in_=etmp[:, 0:512], func=AF.Exp, scale=-1.0), reads=[etmpB], writes=[etmpB])
                op(DVE, lambda e: e.tensor_tensor(out=zT[:, :], in0=zT[:, :], in1=etmp[:, 0:512], op=ALU.mult), reads=[etmpB, zB], writes=[zB])
            cap_ = S.capture
            S.capture = None
            s3, s4 = cap_[i3:i4], cap_[i4:i5]
            mer = []
            i_, j_ = 0, 0
            while i_ < len(s3) or j_ < len(s4):
                if j_ < len(s4) and (i_ >= len(s3) or j_ * max(len(s3), 1) <= i_ * len(s4)):
                    mer.append(s4[j_]); j_ += 1
                else:
                    mer.append(s3[i_]); i_ += 1
            return cap_[:i3] + mer + cap_[i5:]

        def chain(hh, t):
            pp = t % 2
            qkv, qkvB = qkv_b[pp], qkvB_b[pp]
            zT, zB = zT_b[pp], zB_b[pp]
            sm, smB = sm_b[pp], smB_b[pp]
            glb, glB = glb_b[pp], glB_b[pp]
            B_ = HB[hh]
            kbg, kbgB = B_["kbg"]; kdec, kdecB = B_["kdec"]; vbeta, vbetaB = B_["vbeta"]
            tm1, tm1B = B_["tm1"]; E1, E1B = B_["E1"]; Lm, LmB = B_["Lm"]; AT, ATB = B_["AT"]
            X = [B_["X0"], B_["X1"]]; Y = [B_["Y0"], B_["Y1"]]; Pm = [B_["P0"], B_["P1"]]
            wT, wTB = B_["wT"]; um, umB = B_["um"]; vnew, vnewB = B_["vnew"]
            o1s, o1sB = tm1, tm1B
            om, omB = E1, E1B
            on, onB = Lm, LmB
            qT = qkv[:, hh * 128:(hh + 1) * 128]
            kT = qkv[:, 512 + hh * 128:512 + (hh + 1) * 128]
            vT = qkv[:, 1024 + hh * 128:1024 + (hh + 1) * 128]
            PH = psum[4 + hh]
            PB = psB[4 + hh][0]
            Q0, Q1, Q2, Q3 = PH[:, 0:128], PH[:, 128:256], PH[:, 256:384], PH[:, 384:512]
            Gh = psum[3][:, hh * 128:(hh + 1) * 128]
            GhB = psB[3]
            op(PE, lambda e: e.transpose(Q0, kT, ident), reads=[qkvB, cpB], writes=[PB])
            op(PE, lambda e: e.transpose(Q1, vT, ident), reads=[qkvB, cpB], writes=[PB])
            op(PE, lambda e: e.matmul(Q2, lhsT=kT, rhs=kT, start=True, stop=True), reads=[qkvB], writes=[PB])
            if full:
                op(PE, lambda e: e.matmul(Q3, lhsT=kT, rhs=qT, start=True, stop=True), reads=[qkvB], writes=[PB])
            yield
            op(DVE, lambda e: e.tensor_tensor(out=tm1[:, :], in0=maskL, in1=Gh, op=ALU.subtract), reads=GhB + [cpB], writes=[tm1B])
            yield
            op(ACT, lambda e: e.activation(out=E1[:, :], in_=tm1[:, :], func=AF.Exp, bias=sm[:, 16 + hh:17 + hh]), reads=[tm1B, smB], writes=[E1B])
            yield
            op(ACT, lambda e: e.activation(out=kbg[:, :], in_=Q0, func=AF.Copy, scale=sm[:, 28 + hh:29 + hh]), reads=[PB, smB], writes=[kbgB])
            op(ACT, lambda e: e.activation(out=vbeta[:, :], in_=Q1, func=AF.Copy, scale=sm[:, 8 + hh:9 + hh]), reads=[PB, smB], writes=[vbetaB])
            yield
            op(DVE, lambda e: e.tensor_scalar(out=kdec[:, :], in0=Q0, scalar1=sm[:, 32 + hh:33 + hh], scalar2=None, op0=ALU.mult), reads=[PB, smB], writes=[kdecB])
            op(DVE, lambda e: e.scalar_tensor_tensor(out=Lm[:, :], in0=Q2, scalar=sm[:, 8 + hh:9 + hh], in1=E1[:, :], op0=ALU.mult, op1=ALU.mult), reads=[PB, smB, E1B], writes=[LmB])
            yield
            if full:
                op(DVE, lambda e: e.tensor_tensor(out=tm1[:, :], in0=maskU, in1=Gh, op=ALU.add), reads=GhB + [cpB, tm1B], writes=[tm1B])
                yield
                op(ACT, lambda e: e.activation(out=E1[:, :], in_=tm1[:, :], func=AF.Exp, bias=sm[:, 24 + hh:25 + hh]), reads=[tm1B, smB, E1B], writes=[E1B])
                yield
                op(DVE, lambda e: e.tensor_tensor(out=AT[:, :], in0=Q3, in1=E1[:, :], op=ALU.mult), reads=[PB, E1B], writes=[ATB])
                yield
            op(PE, lambda e: e.transpose(Q0, Lm[:, :], ident), reads=[LmB, cpB], writes=[PB])
            yield
            X0, X0B = X[0]
            P0, P0B = Pm[0]
            op(ACT, lambda e: e.activation(out=X0[:, :], in_=Q0, func=AF.Copy), reads=[PB], writes=[X0B])
            op(DVE, lambda e: e.tensor_tensor(out=P0[:, :], in0=ident, in1=Q0, op=ALU.subtract), reads=[PB, cpB], writes=[P0B])
            yield
            Xc, XcB = X0, X0B
            Yc, YcB = Lm, LmB
            Pc, PcB = P0, P0B
            for k in range(1, 6):
                Yn, YnB = Y[k % 2]
                Xn, XnB = X[k % 2]
                Pn, PnB = Pm[k % 2]
                op(PE, (lambda Xc, Yc: lambda e: e.matmul(Q1, lhsT=Xc[:, :], rhs=Yc[:, :], start=True, stop=True))(Xc, Yc), reads=[XcB, YcB], writes=[PB])
                if k < 5:
                    op(PE, (lambda Xc, Yc: lambda e: e.matmul(Q2, lhsT=Yc[:, :], rhs=Xc[:, :], start=True, stop=True))(Xc, Yc), reads=[XcB, YcB], writes=[PB])
                yield
                op(ACT, (lambda Yn: lambda e: e.activation(out=Yn[:, :], in_=Q1, func=AF.Copy))(Yn), reads=[PB], writes=[YnB])
                if k < 5:
                    op(ACT, (lambda Xn: lambda e: e.activation(out=Xn[:, :], in_=Q2, func=AF.Copy))(Xn), reads=[PB], writes=[XnB])
                yield
                op(PE, (lambda Yn, Pc: lambda e: e.matmul(Q3, lhsT=Yn[:, :], rhs=Pc[:, :], start=True, stop=True))(Yn, Pc), reads=[YnB, PcB], writes=[PB])
                yield
                op(DVE, (lambda Pn, Pc: lambda e: e.tensor_tensor(out=Pn[:, :], in0=Pc[:, :], in1=Q3, op=ALU.add))(Pn, Pc), reads=[PcB, PB], writes=[PnB])
                yield
                Xc, XcB, Yc, YcB, Pc, PcB = Xn, XnB, Yn, YnB, Pn, PnB
            op(PE, (lambda Pc: lambda e: e.matmul(Q0, lhsT=kbg[:, :], rhs=Pc[:, :], start=True, stop=True))(Pc), reads=[kbgB, PcB], writes=[PB])
            op(PE, (lambda Pc: lambda e: e.matmul(Q1, lhsT=Pc[:, :], rhs=vbeta[:, :], start=True, stop=True))(Pc), reads=[vbetaB, PcB], writes=[PB])
            yield
            op(ACT, lambda e: e.activation(out=wT[:, :], in_=Q0, func=AF.Copy), reads=[PB], writes=[wTB])
            op(ACT, lambda e: e.activation(out=um[:, :], in_=Q1, func=AF.Copy), reads=[PB], writes=[umB])
            yield
            for half in range(2):
                r0, r1 = half * 64, half * 64 + 64
                sp_ = self.spar_h[hh]
                Scur = Sst[sp_][:, hh * 128:(hh + 1) * 128]; ScurB = SsB[sp_][hh]
                Snew = Sst[1 - sp_][:, hh * 128:(hh + 1) * 128]; SnewB = SsB[1 - sp_][hh]
                self.spar_h[hh] = 1 - sp_
                op(PE, (lambda r0, r1, Scur: lambda e: e.matmul(PH[r0:r1, 256:384], lhsT=wT[:, r0:r1], rhs=Scur, start=True, stop=True))(r0, r1, Scur), reads=[wTB, ScurB], writes=[PB])
                yield
                op(DVE, (lambda r0, r1: lambda e: e.tensor_tensor(out=vnew[r0:r1, :], in0=um[r0:r1, :], in1=PH[r0:r1, 256:384], op=ALU.subtract))(r0, r1), reads=[umB, PB], writes=[vnewB])
                yield
                if full:
                    op(PE, (lambda r0, r1, Scur: lambda e: e.matmul(PH[r0:r1, 0:128], lhsT=qT[:, r0:r1], rhs=Scur, start=True, stop=True))(r0, r1, Scur), reads=[qkvB, ScurB], writes=[PB])
                    op(PE, (lambda r0, r1: lambda e: e.matmul(PH[r0:r1, 128:256], lhsT=AT[r0:r1, r0:r1], rhs=vnew[r0:r1, :], start=True, stop=True))(r0, r1), reads=[ATB, vnewB], writes=[PB])
                op(PE, (lambda r0, r1: lambda e: e.matmul(Q3, lhsT=kdec[r0:r1, :], rhs=vnew[r0:r1, :], start=True, stop=True))(r0, r1), reads=[kdecB, vnewB], writes=[PB])
                yield
                op(DVE, (lambda Snew, Scur, half: lambda e: e.scalar_tensor_tensor(out=Snew, in0=Scur, scalar=glb[:, hh * 2 + half:hh * 2 + half + 1], in1=Q3, op0=ALU.mult, op1=ALU.add))(Snew, Scur, half),
                   reads=[ScurB, glB, PB], writes=[SnewB])
                yield
            if full:
                c0 = 40 + hh * 3
                op(ACT, lambda e: e.activation(out=o1s[:, :], in_=Q0, func=AF.Copy, scale=sm[:, 20 + hh:21 + hh]), reads=[PB, smB], writes=[o1sB])
                yield
                op(DVE, lambda e: e.tensor_tensor(out=om[:, :], in0=o1s[:, :], in1=Q1, op=ALU.add), reads=[o1sB, PB], writes=[omB])
                yield
                op(ACT, lambda e: e.activation(out=on[:, :], in_=om[:, :], func=AF.Square, accum_out=sm[:, c0:c0 + 1]), reads=[omB], writes=[onB, smB])
                yield
                op(POOL, lambda e: e.tensor_scalar(out=sm[:, c0 + 1:c0 + 2], in0=sm[:, c0:c0 + 1], scalar1=1.0 / 128, scalar2=EPS, op0=ALU.mult, op1=ALU.add), reads=[smB], writes=[smB])
                op(POOL, lambda e: e.tensor_tensor(out=sm[:, c0 + 2:c0 + 3], in0=sm[:, c0 + 1:c0 + 2], in1=cs("mhalf"), op=ALU.pow), reads=[smB, cpB], writes=[smB])
                yield
                op(DVE, lambda e: e.scalar_tensor_tensor(out=on[:, :], in0=om[:, :], scalar=sm[:, c0 + 2:c0 + 3], in1=cs("gon"), op0=ALU.mult, op1=ALU.mult), reads=[omB, smB, cpB], writes=[onB])
                yield
                op(PE, lambda e: e.transpose(Q2, on[:, :], ident), reads=[onB, cpB], writes=[PB])
                yield
                yg_ap = self.ygT[:, hh * T + t * 128: hh * T + (t + 1) * 128]
                op(DVE, lambda e: e.tensor_tensor(out=yg_ap, in0=Q2, in1=zT[:, hh * 128:(hh + 1) * 128], op=ALU.mult), reads=[PB, zB], writes=[self.ygB[t]])
                yield

        for it in pre_ops(0):
            S.replay(it)
        for t in range(NT):
            pend = pre_ops(t + 1) if t + 1 < NT else []
            per = (len(pend) + 39) // 40
            S0 = int(os.environ.get("KSTAG", 2))
            rounds_t = (47 if full else 37) + 3 * S0 - int(os.environ.get("KEARLY", 3))
            per = (len(pend) + rounds_t - 1) // rounds_t
            alive = [(hh, chain(hh, t)) for hh in range(4)]
            pi = 0
            rnd = 0
            while alive or pi < len(pend):
                nxt = []
                for hh, g_ in alive:
                    if rnd < hh * S0:
                        nxt.append((hh, g_))
                        continue
                    try:
                        next(g_)
                        nxt.append((hh, g_))
                    except StopIteration:
                        pass
                alive = nxt
                for it in pend[pi:pi + per]:
                    S.replay(it)
                pi += per
                rnd += 1
        op(DVE, lambda e: e.tensor_copy(out=pchv, in_=pcv0[:, :, 0:3]), reads=[pcB], writes=[self.pchB])
        es.close()

    def conv_phase(self, wm_in, wm_out, hhalo, hhB):
        import os
        nc, S = self.nc, self.S
        op = S.op
        cs, cpB = self.cs, self.cpB
        h, hB = self.h, self.hB
        psum, psB = self.psum, self.psB
        es = contextlib.ExitStack()
        tag = "cv"
        wc = self.sb("wc", 8 * 1536, BF16, es); wcB = [Buf() for _ in range(6)]
        wo = self.sb("wo", 8 * 1024, BF16, es); woB = [Buf() for _ in range(4)]
        xs = self.sb("xs_cv", D, F32, es); xsB = Buf()
        self.junk = self.sb("junk_cv", D, BF16, es); self.junkB = Buf()
        xn = self.sb("xn_cv", 8 * 128, BF16, es); xnB = Buf()
        mpc = self.sb("mpc", 4 * 130, F32, es); mpcB = Buf()
        mpc2 = self.sb("mpc2", 4 * 130, F32, es); mpc2B = Buf()
        cbs0 = self.sb("cbs0", 512, F32, es); cbs0B = Buf()
        cbs1 = self.sb("cbs1", 512, F32, es); cbs1B = Buf()
        cct = self.sb("cct", 512, F32, es); cctB = Buf()
        cacc = self.sb("cacc_cv", 512, F32, es); caccB = Buf()
        yv = self.sb("yv", 512, F32, es); yvB = Buf()
        sq = self.sb("sq_cv", 512, F32, es); sqB = Buf()
        rs = self.sb("rs_cv", 512, F32, es); rsB = Buf()
        ycT = self.sb("ycT", 512, BF16, es); ycB = Buf()
        motmp = [self.sb("motmp%d" % i, 512, F32, es) for i in range(2)]; motB = [Buf(), Buf()]
        new_bufs = wcB + woB + [mpc2B, cbs0B, cbs1B, xsB, self.junkB, xnB, mpcB, cctB, caccB, yvB, sqB, rsB, ycB] + motB
        S.alias(new_bufs, getattr(self, "phase_bufs", []))
        self.phase_bufs = new_bufs
        wm_v = wm_in.rearrange("(c p) n -> p c n", p=128)
        for col in range(0, 1536, 256):
            base = (col // 256) * 2048
            self.load_cast(wc[:, base:base + 2048], wcB[col // 256], wm_v[:, :, col:col + 256], (8, 256))
        wo_v = wm_out.rearrange("(c p) n -> p c n", p=128)
        for i in range(4):
            self.load_cast(wo[:, i * 2048:(i + 1) * 2048], woB[i], wo_v[:, 2 * i:2 * i + 2, :], (2, 1024))
        op(POOL, lambda e: e.memset(mpc[:, :], 0.0), writes=[mpcB])
        op(POOL, lambda e: e.memset(mpc2[:, :], 0.0), writes=[mpc2B])
        ident, blk64 = cs("ident"), cs("blk64")
        csw, cgain = cs("csw"), cs("cgain")
        ygT, ygB = self.ygT, self.ygB
        mpc_b = [mpc, mpc2]; mpcB_b = [mpcB, mpc2B]
        cbs_b = [cbs0, cbs1]; cbsB_b = [cbs0B, cbs1B]

        def capA(t):
            p = t % 2
            mp, mpB = mpc_b[p], mpcB_b[p]
            mo, moB = mpc_b[1 - p], mpcB_b[1 - p]
            mpv = mp[:, :].rearrange("p (a b) -> p a b", a=4)
            mov = mo[:, :].rearrange("p (a b) -> p a b", a=4)
            S.capture = []
            if t < 0:
                src, srcB = hhalo[:, :], hhB
            else:
                src, srcB = h[:, t * D:(t + 1) * D], hB[t]
            self.norm_transpose(src, srcB, "nm", xn, 0, 128, xnB, xs, xsB, (0, 1))
            for grp in range(3):
                if t < 0 and grp == 0:
                    continue
                pb = 2 + grp
                for q in range(4):
                    j = grp * 4 + q
                    for c in range(8):
                        lhsT = wc[:, (j // 2) * 2048 + c * 256 + (j % 2) * 128: (j // 2) * 2048 + c * 256 + (j % 2) * 128 + 128]
                        rhs = xn[:, c * 128:(c + 1) * 128]
                        op(PE, (lambda pb, q, lhsT, rhs, c: lambda e: e.matmul(psum[pb][:, q * 128:(q + 1) * 128], lhsT=lhsT, rhs=rhs, start=(c == 0), stop=(c == 7)))(pb, q, lhsT, rhs, c),
                           reads=[wcB[j // 2], xnB], writes=[psB[pb][q]])
                if grp == 0:
                    op(ACT, (lambda p: lambda e: e.activation(out=cbs_b[p][:, :], in_=psum[2][:, :], func=AF.Copy))(p), reads=psB[2], writes=[cbsB_b[p]])
                if grp == 1:
                    op(ACT, lambda e: e.activation(out=cct[:, :], in_=psum[3][:, :], func=AF.Copy), reads=psB[3], writes=[cctB])
            op(DVE, lambda e: e.tensor_tensor(out=mpv[:, :, 2:130], in0=cct[:, :].rearrange("p (a b) -> p a b", a=4), in1=psum[4][:, :].rearrange("p (a b) -> p a b", a=4), op=ALU.mult),
               reads=[cctB, mpB] + psB[4], writes=[mpB])
            op(DVE, lambda e: e.tensor_copy(out=mpv[:, :, 0:2], in_=mov[:, :, 128:130]), reads=[moB, mpB], writes=[mpB])
            ops_ = S.capture
            S.capture = None
            return ops_

        def capB(t):
            p = t % 2
            mp, mpB = mpc_b[p], mpcB_b[p]
            cbs, cbsB = cbs_b[p], cbsB_b[p]
            S.capture = []
            for j in range(4):
                op(DVE, (lambda j: lambda e: e.tensor_scalar(out=cacc[:, j * 128:(j + 1) * 128], in0=mp[:, j * 130:j * 130 + 128], scalar1=csw[:, j * 3:j * 3 + 1], scalar2=None, op0=ALU.mult))(j),
                   reads=[mpB, cpB], writes=[caccB])
                for k in range(1, 3):
                    op(DVE, (lambda j, k: lambda e: e.scalar_tensor_tensor(out=cacc[:, j * 128:(j + 1) * 128], in0=mp[:, j * 130 + k:j * 130 + k + 128], scalar=csw[:, j * 3 + k:j * 3 + k + 1], in1=cacc[:, j * 128:(j + 1) * 128], op0=ALU.mult, op1=ALU.add))(j, k),
                       reads=[mpB, cpB, caccB], writes=[caccB])
            op(DVE, lambda e: e.tensor_tensor(out=yv[:, :], in0=cacc[:, :], in1=cbs[:, :], op=ALU.mult), reads=[caccB, cbsB], writes=[yvB])
            op(ACT, lambda e: e.activation(out=sq[:, :], in_=yv[:, :], func=AF.Square), reads=[yvB], writes=[sqB])
            op(PE, lambda e: e.matmul(psum[5][:, :], lhsT=blk64, rhs=sq[:, :], start=True, stop=True), reads=[sqB, cpB], writes=psB[5])
            op(ACT, lambda e: e.activation(out=rs[:, :], in_=psum[5][:, :], func=AF.Ln, bias=cs("eps")), reads=psB[5] + [cpB], writes=[rsB])
            op(ACT, lambda e: e.activation(out=rs[:, :], in_=rs[:, :], func=AF.Exp, scale=-0.5), reads=[rsB], writes=[rsB])
            for j in range(4):
                op(DVE, (lambda j: lambda e: e.scalar_tensor_tensor(out=ycT[:, j * 128:(j + 1) * 128], in0=yv[:, j * 128:(j + 1) * 128], scalar=cgain[:, j:j + 1], in1=rs[:, j * 128:(j + 1) * 128], op0=ALU.mult, op1=ALU.mult))(j),
                   reads=[yvB, rsB, cpB], writes=[ycB])
            for hh in range(2):
                pb = 6 + hh
                for j in range(8):
                    if j < 4:
                        lhsT = ycT[:, j * 128:(j + 1) * 128]
                        rd = [ycB]
                    else:
                        lhsT = ygT[:, (j - 4) * T + t * 128:(j - 4) * T + (t + 1) * 128]
                        rd = [ygB[t]]
                    rhs = wo[:, j * 1024 + hh * 512: j * 1024 + (hh + 1) * 512]
                    op(PE, (lambda pb, lhsT, rhs, j: lambda e: e.matmul(psum[pb][:, :], lhsT=lhsT, rhs=rhs, start=(j == 0), stop=(j == 7)))(pb, lhsT, rhs, j),
                       reads=rd + [woB[j // 2]], writes=psB[pb])
                hap = h[:, t * D + hh * 512: t * D + (hh + 1) * 512]
                op(ACT, (lambda pb, hh: lambda e: e.activation(out=motmp[hh][:, :], in_=psum[pb][:, :], func=AF.Copy))(pb, hh), reads=psB[pb], writes=[motB[hh]])
                op(DVE, (lambda hh, hap: lambda e: e.tensor_tensor(out=hap, in0=hap, in1=motmp[hh][:, :], op=ALU.add))(hh, hap), reads=[motB[hh], hB[t]], writes=[hB[t]])
            ops_ = S.capture
            S.capture = None
            return ops_

        for it in capA(-1):
            S.replay(it)
        for it in capA(0):
            S.replay(it)
        for t in range(NT):
            A = capA(t + 1) if t + 1 < NT else []
            B_ = capB(t)
            na, nb = len(A), len(B_)
            ia = ib = 0
            while ia < na or ib < nb:
                if ib < nb and (ia >= na or ib * max(na, 1) <= ia * nb):
                    S.replay(B_[ib]); ib += 1
                else:
                    S.replay(A[ia]); ia += 1
        es.close()

    def final(self, out, fn_bc):
        S = self.S
        op = S.op
        cs, cpB = self.cs, self.cpB
        h, hB = self.h, self.hB
        es = contextlib.ExitStack()
        ot = [self.sb("ot%d" % i, D, F32, es) for i in range(2)]
        otB = [Buf(), Buf()]
        fs = self.sb("fs", 64, F32, es); fsB = Buf()
        junk = self.sb("junk_f", D, BF16, es); junkB = Buf()
        fnb = self.sb("fnb", D, F32, es); fnbB = Buf()
        new_bufs = otB + [fsB, junkB, fnbB]
        S.alias(new_bufs, getattr(self, "phase_bufs", []))
        self.phase_bufs = new_bufs
        S.dma(SP, lambda e: e.dma_start(out=fnb[:], in_=fn_bc), "const2", S.new_group(), writes=[fnbB])
        import os
        if os.environ.get("KRAWOUT"):
            for t in range(NT):
                S.dma(SP, (lambda t: lambda e: e.dma_start(out=out[t * 128:(t + 1) * 128, :], in_=h[:, t * D:(t + 1) * D]))(t), "out%d" % (t % 2), S.new_group(), reads=[hB[t]])
            es.close()
            return
        for t in range(NT):
            k = t % 2
            c0 = (t % 16) * 3
            hs = h[:, t * D:(t + 1) * D]
            op(ACT, (lambda hs, c0: lambda e: e.activation(out=junk[:, :], in_=hs, func=AF.Square, accum_out=fs[:, c0:c0 + 1]))(hs, c0), reads=[hB[t]], writes=[junkB, fsB])
            op(POOL, (lambda c0: lambda e: e.tensor_scalar(out=fs[:, c0 + 1:c0 + 2], in0=fs[:, c0:c0 + 1], scalar1=1.0 / D, scalar2=EPS, op0=ALU.mult, op1=ALU.add))(c0), reads=[fsB], writes=[fsB])
            op(POOL, (lambda c0: lambda e: e.tensor_tensor(out=fs[:, c0 + 2:c0 + 3], in0=fs[:, c0 + 1:c0 + 2], in1=cs("mhalf"), op=ALU.pow))(c0), reads=[fsB, cpB], writes=[fsB])
            op(DVE, (lambda hs, c0, k: lambda e: e.scalar_tensor_tensor(out=ot[k][:, :], in0=hs, scalar=fs[:, c0 + 2:c0 + 3], in1=fnb[:, :], op0=ALU.mult, op1=ALU.mult))(hs, c0, k),
               reads=[hB[t], fsB, fnbB], writes=[otB[k]])
            S.dma(SP, (lambda t, k: lambda e: e.dma_start(out=out[t * 128:(t + 1) * 128, :], in_=ot[k][:, :]))(t, k), "out%d" % k, S.new_group(), reads=[otB[k]])
        es.close()


def _pack_layout():
    names = [("ident", 128), ("ones", 128), ("triU", 128), ("maskL", 128), ("maskU", 128), ("blk64", 128),
             ("gon", 128), ("n1", 8), ("nm", 8), ("n2", 8), ("cwg", 48), ("csw", 12), ("cgain", 4),
             ("alog", 4), ("dtb", 4), ("mhalf", 1), ("eps", 1)]
    lay = {}
    off = 0
    for n, w in names:
        lay[n] = (off, off + w)
        off += w
    return lay, off


_CP, _CPK_COLS = _pack_layout()
Builder.CP = _CP
Builder.CPK_COLS = _CPK_COLS


def _pack_consts(inp):
    f = np.float32
    cp = np.zeros((128, _CPK_COLS), f)

    def put(name, arr):
        a, b = _CP[name]
        cp[:, a:b] = np.asarray(arr, f).reshape(128, b - a)

    idx = np.arange(128)
    same = (idx[:, None] // 64) == (idx[None, :] // 64)
    put("ident", np.eye(128))
    put("ones", np.ones((128, 128)))
    put("triU", (same & (idx[:, None] <= idx[None, :])))
    put("maskL", np.where(same & (idx[:, None] > idx[None, :]), 0.0, NEG))
    put("maskU", np.where(same & (idx[:, None] <= idx[None, :]), 0.0, NEG))
    put("blk64", same.astype(f) / 64.0)
    put("gon", np.broadcast_to(inp["gdn_out_norm"].reshape(1, 128), (128, 128)))
    put("n1", inp["ffn1_norm"].reshape(8, 128).T)
    put("nm", inp["mix_norm"].reshape(8, 128).T)
    put("n2", inp["ffn2_norm"].reshape(8, 128).T)
    put("cwg", inp["gdn_conv_w"].reshape(4, 12, 128).transpose(2, 1, 0).reshape(128, 48))
    put("csw", inp["conv_short_w"].reshape(3, 4, 128).transpose(2, 1, 0).reshape(128, 12))
    put("cgain", inp["conv_out_norm"].reshape(4, 128).T)
    put("alog", np.broadcast_to(inp["gdn_A_log"].reshape(1, 4), (128, 4)))
    put("dtb", np.broadcast_to(inp["gdn_dt_bias"].reshape(1, 4), (128, 4)))
    put("mhalf", np.full((128, 1), -0.5))
    put("eps", np.full((128, 1), EPS))
    return cp


_NC_CACHE = {}


def _get_nc(debug=False):
    if debug not in _NC_CACHE:
        b = Builder(debug=debug)
        b.spar_h = [0, 0, 0, 0]
        _NC_CACHE[debug] = (b.build(), b)
    return _NC_CACHE[debug]


def kernel(debug=False, **inputs):
    inp = {k: np.asarray(v) for k, v in inputs.items()}
    x = inp["x"].astype(np.float32, copy=False)
    nc, b = _get_nc(debug)
    cp = _pack_consts(inp)
    fn_bc = np.ascontiguousarray(np.broadcast_to(inp["final_norm"].reshape(1, D).astype(np.float32), (128, D)))
    shared = {
        "w1_in": np.ascontiguousarray(inp["ffn1_w_in"][0]), "w1_out": np.ascontiguousarray(inp["ffn1_w_out"][0]),
        "w2_in": np.ascontiguousarray(inp["ffn2_w_in"][0]), "w2_out": np.ascontiguousarray(inp["ffn2_w_out"][0]),
        "wm_in": np.ascontiguousarray(inp["w_mix_in"][0]), "wm_out": np.ascontiguousarray(inp["w_mix_out"][0]),
        "cpk": cp, "fn_bc": fn_bc,
    }
    zeros = np.zeros((T, D), np.float32)
    in_maps = []
    for c in range(8):
        bi, half = c // 2, c % 2
        m = dict(shared)
        m["x_own"] = np.ascontiguousarray(x[bi, half * T:(half + 1) * T])
        m["x_pre"] = zeros if half == 0 else np.ascontiguousarray(x[bi, 0:T])
        in_maps.append(m)
    import os
    ncores = int(os.environ.get("KCORES", 8))
    res = run_bass_kernel_spmd(nc, in_maps[:ncores], core_ids=list(range(ncores)))
    outp = np.zeros((4, 2 * T, D), np.float32)
    for c in range(ncores):
        outp[c // 2, (c % 2) * T:(c % 2 + 1) * T] = res.results[c]["out"]
    if debug:
        return outp, res.results
    return outp
```

```python
import contextlib
import numpy as np
import concourse.bass as bass
import concourse.mybir as mybir
from concourse.bass_utils import run_bass_kernel_spmd

F32 = mybir.dt.float32
BF16 = mybir.dt.bfloat16
AF = mybir.ActivationFunctionType
ALU = mybir.AluOpType

PE, ACT, DVE, POOL, SP = "pe", "act", "dve", "pool", "sp"
COMPUTE = (PE, ACT, DVE, POOL)

D = 1024
DFF = 2816
T = 2048
NT = T // 128
NB = T // 512
EPS = 1e-6
GW0 = 1536
NG = 2056
NEG = -1.0e30


class Buf:
    __slots__ = ("name", "last_w", "readers", "excl")

    def __init__(self, name="", excl=False):
        self.name = name
        self.last_w = None
        self.readers = []
        self.excl = excl


class Op:
    __slots__ = ("eng", "fn", "deps", "needs_inc", "cnt", "is_dma", "key", "grp", "idx")

    def __init__(self, eng, fn, is_dma=False, key=None, grp=None):
        self.eng = eng
        self.fn = fn
        self.deps = []
        self.needs_inc = False
        self.cnt = 0
        self.is_dma = is_dma
        self.key = key
        self.grp = grp


class Sched:
    def __init__(self):
        self.ops = []
        self.grp_ctr = 0

    def new_group(self):
        self.grp_ctr += 1
        return self.grp_ctr

    def _add(self, op, reads, writes):
        if getattr(self, "capture", None) is not None:
            self.capture.append((op, list(reads), list(writes)))
            return op
        return self._add_real(op, reads, writes)

    def replay(self, item):
        return self._add_real(*item)

    def _add_real(self, op, reads, writes):
        op.idx = len(self.ops)
        ex = [b for b in reads if b.excl]
        if ex:
            reads = [b for b in reads if not b.excl]
            writes = list(writes) + ex
        deps = {}
        for b in reads:
            if b.last_w is not None:
                deps[id(b.last_w)] = b.last_w
        for b in writes:
            if b.last_w is not None:
                deps[id(b.last_w)] = b.last_w
            for r in b.readers:
                deps[id(r)] = r
        latest = {}
        for d in deps.values():
            if d is op:
                continue
            if (not d.is_dma) and (not op.is_dma) and d.eng == PE and op.eng == PE:
                continue
            if d.is_dma:
                op.deps.append(d)
            else:
                cur = latest.get(d.eng)
                if cur is None or d.idx > cur.idx:
                    latest[d.eng] = d
        op.deps.extend(latest.values())
        for b in reads:
            if op.is_dma:
                b.readers.append(op)
            else:
                b.readers = [r for r in b.readers if r.is_dma or r.eng != op.eng]
                b.readers.append(op)
        for b in writes:
            b.last_w = op
            b.readers = []
        self.ops.append(op)
        return op

    def op(self, eng, fn, reads=(), writes=()):
        return self._add(Op(eng, fn), reads, writes)

    def dma(self, queue, fn, key, grp, reads=(), writes=()):
        return self._add(Op(queue, fn, is_dma=True, key=key, grp=grp), reads, writes)

    def alias(self, new_bufs, old_bufs):
        acc = {}
        for b in old_bufs:
            if b.last_w is not None:
                acc[id(b.last_w)] = b.last_w
            for r in b.readers:
                acc[id(r)] = r
        for nb in new_bufs:
            nb.readers = list(acc.values())

    def emit(self, nc, final_wait_keys=()):
        ops = self.ops
        for o in ops:
            for d in o.deps:
                d.needs_inc = True
        cnt = {}
        grp_end = {}
        for o in ops:
            if o.is_dma:
                k = ("dma", o.key)
                cnt[k] = cnt.get(k, 0) + 1
                o.cnt = cnt[k]
                grp_end[(o.key, o.grp)] = o.cnt
            elif o.needs_inc:
                cnt[o.eng] = cnt.get(o.eng, 0) + 1
                o.cnt = cnt[o.eng]
        dma_keys = sorted({o.key for o in ops if o.is_dma})
        streams = {e: [o for o in ops if o.eng == e] for e in (PE, ACT, DVE, POOL, SP)}
        self.stats = {e: len(s) for e, s in streams.items()}
        self.stats["incs"] = dict(cnt)

        import os
        SEG = int(os.environ.get('KSEG', 1500))
        with contextlib.ExitStack() as es:
            sems = {}
            for e in COMPUTE:
                nseg = (cnt.get(e, 0) + SEG - 1) // SEG + 1
                sems[e] = [es.enter_context(nc.semaphore("s_%s_%d" % (e, j))) for j in range(nseg)]
            for k in dma_keys:
                sems[("dma", k)] = es.enter_context(nc.semaphore("d_" + str(k)))
            block = es.enter_context(nc.Block())

            def run_stream(engname, eng):
                waited = {}
                for o in streams[engname]:
                    for d in o.deps:
                        if d.is_dma:
                            sk = ("dma", d.key)
                            val = 16 * grp_end[(d.key, d.grp)]
                            sem = sems[sk]
                        else:
                            seg = (d.cnt - 1) // SEG
                            sk = (d.eng, seg)
                            val = (d.cnt - 1) % SEG + 1
                            sem = sems[d.eng][seg]
                            if any(k2[0] == d.eng and k2[1] > seg for k2 in waited if isinstance(k2, tuple) and k2[0] == d.eng):
                                continue
                        if waited.get(sk, 0) >= val:
                            continue
                        waited[sk] = val
                        eng.wait_ge(sem, val)
                    ins = o.fn(eng)
                    if o.is_dma:
                        ins.then_inc(sems[("dma", o.key)], 16)
                    elif o.needs_inc:
                        ins.then_inc(sems[o.eng][(o.cnt - 1) // SEG], 1)
                if engname == SP:
                    for k in final_wait_keys:
                        eng.wait_ge(sems[("dma", k)], 16 * cnt[("dma", k)])

            @block.sync
            def _(e):
                run_stream(SP, e)

            @block.tensor
            def _(e):
                run_stream(PE, e)

            @block.scalar
            def _(e):
                run_stream(ACT, e)

            @block.vector
            def _(e):
                run_stream(DVE, e)

            @block.gpsimd
            def _(e):
                run_stream(POOL, e)


class Builder:
    def __init__(self, debug=False):
        self.debug = debug
        self.nc = bass.Bass("TRN2", target_bir_lowering=False)
        self.S = Sched()
        self.es = contextlib.ExitStack()
        self.dbg_outs = []
        self.dbg_keys = []
        self.rr = 0

    def sb(self, name, cols, dt=F32, es=None):
        return (es or self.es).enter_context(self.nc.sbuf_tensor(name, [128, cols], dt))

    def dram_in(self, name, shape, dt=F32):
        return self.nc.dram_tensor(name, list(shape), dt, kind="ExternalInput").ap()

    def dram_out(self, name, shape, dt=F32):
        return self.nc.dram_tensor(name, list(shape), dt, kind="ExternalOutput").ap()

    def dbg(self, name, ap, cols, bufs, dt=F32):
        if not self.debug:
            return
        o = self.dram_out("dbg_" + name, [128, cols], dt)
        self.dbg_keys.append("dbg_" + name)
        self.S.dma(SP, lambda e: e.dma_start(out=o, in_=ap), "dbg_" + name, self.S.new_group(), reads=bufs)

    def ew(self):
        self.rr += 1
        return ACT if (self.rr & 1) else DVE

    def build(self):
        nc, S = self.nc, self.S
        op = S.op
        x_pre = self.dram_in("x_pre", [T, D])
        x_own = self.dram_in("x_own", [T, D])
        w1_in = self.dram_in("w1_in", [D, 2 * DFF])
        w1_out = self.dram_in("w1_out", [DFF, D])
        w2_in = self.dram_in("w2_in", [D, 2 * DFF])
        w2_out = self.dram_in("w2_out", [DFF, D])
        wm_in = self.dram_in("wm_in", [D, 3592])
        wm_out = self.dram_in("wm_out", [D, D])
        cpk = self.dram_in("cpk", [128, self.CPK_COLS])
        fn_bc = self.dram_in("fn_bc", [128, D])
        out = self.dram_out("out", [T, D])
        self.out_grp = S.new_group()

        h = self.sb("h", NT * D)
        hB = [Buf("h%d" % t) for t in range(NT)]
        stage = [self.sb("stage%d" % i, 2048) for i in range(2)]
        stB = [Buf("st%d" % i) for i in range(2)]
        self.stage, self.stB, self.st_i = stage, stB, 0
        import os
        self.cast_order = os.environ.get('KCAST', 'dve').split(',')
        cp = self.sb("cp", self.CPK_COLS)
        cpB = Buf("cp")
        hhalo = self.sb("hhalo", D)
        hhB = Buf("hhalo")
        ygT = self.sb("ygT", 4 * T, BF16)
        ygB = [Buf("yg%d" % t) for t in range(NT)]
        pch = self.sb("pch", 36)
        pchB = Buf("pch")
        Sst = [self.sb("Sst%d" % i, 4 * 128) for i in range(2)]
        SsB = [[Buf("S%d_%d" % (i, hh)) for hh in range(4)] for i in range(2)]
        stat = self.sb("stat", 64)
        negA = self.sb("negA", 4)
        negAB = Buf("negA")
        psum = [self.es.enter_context(nc.psum_tensor("ps%d" % i, [128, 512], F32)) for i in range(8)]
        psB = [[Buf("ps%d" % i, excl=True)] * 4 for i in range(8)]
        self.psum, self.psB = psum, psB
        self.h, self.hB = h, hB

        C = self.CP
        g0 = S.new_group()
        S.dma(SP, lambda e: e.dma_start(out=cp[:], in_=cpk), "const", g0, writes=[cpB])
        self.cp, self.cpB = cp, cpB

        def cs(name, n=None):
            a, b = C[name]
            return cp[:, a:b]

        self.cs = cs
        ident = cs("ident")
        op(POOL, lambda e: e.memset(Sst[0][:], 0.0), writes=SsB[0])
        op(POOL, lambda e: e.memset(pch[:], 0.0), writes=[pchB])
        op(ACT, lambda e: e.activation(out=negA[:], in_=cs("alog"), func=AF.Exp), reads=[cpB], writes=[negAB])
        op(DVE, lambda e: e.tensor_scalar(out=negA[:], in0=negA[:], scalar1=-1.0, scalar2=None, op0=ALU.mult),
           reads=[negAB], writes=[negAB])
        self.negA, self.negAB = negA, negAB
        self.pch, self.pchB = pch, pchB
        self.Sst, self.SsB = Sst, SsB
        self.ygT, self.ygB = ygT, ygB
        self.spar = 0
        self.stat = stat
        self.statB = Buf("stat")

        import os
        PH = set(os.environ.get("KPH", "pf,pg,of,og,cv,f2").split(","))
        if "pf" in PH:
            self.load_x(x_pre)
            self.ffn(w1_in, w1_out, "n1", tag="p1")
        op(POOL, lambda e: e.tensor_copy(out=hhalo[:], in_=h[:, (NT - 1) * D:NT * D]), reads=[hB[NT - 1]], writes=[hhB])
        if "pg" in PH:
            self.gdn_phase(wm_in, full=False, tag="pg")
        self.load_x(x_own)
        if "of" in PH:
            self.ffn(w1_in, w1_out, "n1", tag="o1")
        self.dbg("h1", h[:, 0:D], D, [hB[0]])
        if "og" in PH:
            self.gdn_phase(wm_in, full=True, tag="og")
        self.dbg("yg", self.ygT[:, 0:T], T, self.ygB, BF16)
        if "cv" in PH:
            self.conv_phase(wm_in, wm_out, hhalo, hhB)
        self.dbg("h2", h[:, 0:D], D, [hB[0]])
        if "f2" in PH:
            self.ffn(w2_in, w2_out, "n2", tag="o2")
        self.final(out, fn_bc)
        S.emit(nc, final_wait_keys=["out0", "out1"] + self.dbg_keys)
        self.es.close()
        return nc

    def load_x(self, xd):
        S, h, hB = self.S, self.h, self.hB
        g = S.new_group()
        for t in range(NT):
            S.dma(SP, (lambda t: lambda e: e.dma_start(out=h[:, t * D:(t + 1) * D], in_=xd[t * 128:(t + 1) * 128, :]))(t),
                  "x%d" % (t % 4), g, writes=[hB[t]])

    def load_dma(self, dst_ap, dstB, src_ap, shape3):
        S = self.S
        n = len(self.stage)
        i = self.st_i % n
        self.st_i += 1
        st, sB = self.stage[i], self.stB[i]
        a, b = shape3
        sview = st[:, 0:a * b].rearrange("p (a b) -> p a b", a=a) if a > 1 else st[:, 0:b]
        sflat = st[:, 0:a * b]
        g = S.new_group()
        S.dma(SP, lambda e: e.dma_start(out=sview, in_=src_ap), "st%d" % i, g, writes=[sB])
        return (dst_ap, dstB, sflat, sB)

    def load_cast_do(self, hnd):
        S = self.S
        dst_ap, dstB, sflat, sB = hnd
        self.cast_i = getattr(self, "cast_i", 0) + 1
        eng = self.cast_order[self.cast_i % len(self.cast_order)]
        if eng == ACT:
            S.op(ACT, lambda e: e.activation(out=dst_ap, in_=sflat, func=AF.Copy), reads=[sB], writes=[dstB])
        else:
            S.op(eng, lambda e: e.tensor_copy(out=dst_ap, in_=sflat), reads=[sB], writes=[dstB])

    def load_cast(self, dst_ap, dstB, src_ap, shape3=None, key="w"):
        self.load_cast_do(self.load_dma(dst_ap, dstB, src_ap, shape3))

    def norm_transpose(self, src_ap, srcB, gain_name, dst, dst_off, dst_stride, dstB, xs, xsB, pbanks, only=None):
        S, cs = self.S, self.cs
        op = S.op
        stat = self.stat
        stB = self.statB
        op(ACT, lambda e: e.activation(out=xs[:, 0:D], in_=src_ap, func=AF.Square, accum_out=stat[:, 0:1]),
           reads=[srcB], writes=[xsB, stB])
        op(POOL, lambda e: e.tensor_scalar(out=stat[:, 1:2], in0=stat[:, 0:1], scalar1=1.0 / D, scalar2=EPS,
                                           op0=ALU.mult, op1=ALU.add), reads=[stB], writes=[stB])
        op(POOL, lambda e: e.tensor_tensor(out=stat[:, 2:3], in0=stat[:, 1:2], in1=cs("mhalf"), op=ALU.pow),
           reads=[stB, self.cpB], writes=[stB])
        op(DVE, lambda e: e.tensor_scalar(out=xs[:, 0:D], in0=src_ap, scalar1=stat[:, 2:3], scalar2=None, op0=ALU.mult),
           reads=[srcB, stB], writes=[xsB])
        if only == "front":
            return
        self._nt_back(gain_name, dst, dst_off, dst_stride, dstB, xs, xsB, pbanks)

    def _nt_back(self, gain_name, dst, dst_off, dst_stride, dstB, xs, xsB, pbanks):
        S, cs = self.S, self.cs
        op = S.op
        gain = cs(gain_name)
        ident = cs("ident")
        for half in range(2):
            pb = pbanks[half]
            pbuf = self.psum[pb]
            for q in range(4):
                c = half * 4 + q
                op(PE, (lambda c, q, pbuf: lambda e: e.transpose(pbuf[:, q * 128:(q + 1) * 128], xs[:, c * 128:(c + 1) * 128], ident))(c, q, pbuf),
                   reads=[xsB, self.cpB], writes=[self.psB[pb][q]])
            for q in range(4):
                c = half * 4 + q
                eng = self.ew()
                o_ap = dst[:, c * dst_stride + dst_off: c * dst_stride + dst_off + 128]
                i_ap = pbuf[:, q * 128:(q + 1) * 128]
                g_ap = gain[:, c:c + 1]
                if eng == ACT:
                    op(ACT, (lambda o_ap, i_ap, g_ap: lambda e: e.activation(out=o_ap, in_=i_ap, func=AF.Copy, scale=g_ap))(o_ap, i_ap, g_ap),
                       reads=[self.psB[pb][q], self.cpB], writes=[dstB])
                else:
                    op(DVE, (lambda o_ap, i_ap, g_ap: lambda e: e.tensor_scalar(out=o_ap, in0=i_ap, scalar1=g_ap, scalar2=None, op0=ALU.mult))(o_ap, i_ap, g_ap),
                       reads=[self.psB[pb][q], self.cpB], writes=[dstB])

    def ffn(self, w_in, w_out, gain_name, tag):
        import os
        nc, S = self.nc, self.S
        op = S.op
        h, hB = self.h, self.hB
        psum, psB = self.psum, self.psB
        es = contextlib.ExitStack()
        xnT = self.sb("xnT_" + tag, 8 * T, BF16, es)
        xnB = [Buf("xn%d" % t) for t in range(NT)]
        CPP = int(os.environ.get("KCPP", 4))
        nsub = CPP // 2
        wbi = [self.sb("wbi%d_%s" % (i, tag), 2 * nsub * 2048, BF16, es) for i in range(2)]
        wbo = [self.sb("wbo%d_%s" % (i, tag), CPP * 1024, BF16, es) for i in range(2)]
        assert CPP == 4
        wbiB = [[Buf() for _ in range(4)] for _ in range(2)]
        wboB = [[Buf() for _ in range(nsub)] for _ in range(2)]
        hid = [self.sb("hid%d_%s" % (i, tag), CPP * 512, BF16, es) for i in range(2)]
        hidB = [Buf(), Buf()]
        sg0 = self.sb("sg0_%s" % tag, 512, F32, es); sg = [sg0, sg0]
        sgB0 = Buf(); sgB = [sgB0, sgB0]
        xs = [self.sb("xs%d_%s" % (i, tag), D, F32, es) for i in range(2)]
        xsB = [Buf(), Buf()]
        ev = [xs[1][:, 0:512], xs[1][:, 512:1024]]
        evB = [xsB[1], xsB[1]]
        self.junkB = Buf()
        base_stage, base_stB = self.stage, self.stB
        nextra = int(os.environ.get("KXST", 0))
        xst = [self.sb("xst%d_%s" % (i, tag), 2048, F32, es) for i in range(nextra)]
        xstB = [Buf() for _ in range(nextra)]
        self.stage, self.stB = base_stage + xst, base_stB + xstB
        new_bufs = xnB + wbiB[0] + wbiB[1] + wboB[0] + wboB[1] + hidB + sgB + xsB + [self.junkB] + evB + xstB
        S.alias(new_bufs, getattr(self, "phase_bufs", []))
        self.phase_bufs = new_bufs

        w_in_v = w_in.rearrange("(c p) n -> p c n", p=128)
        w_out_v = w_out.rearrange("(c p) n -> p c n", p=128)
        pieces = []
        c0 = 0
        while c0 < DFF // 128:
            n = min(CPP, DFF // 128 - c0)
            pieces.append((c0, n))
            c0 += n
        NP = len(pieces)

        def piece_specs(p):
            s = p % 2
            ch0, n = pieces[p]
            W = n * 128
            col = ch0 * 128
            specs = []
            for which in range(2):
                for csub in range(2):
                    base = which * 4096 + csub * 2048
                    specs.append((wbi[s][:, base:base + 4 * W], wbiB[s][which * 2 + csub],
                                  w_in_v[:, csub * 4:csub * 4 + 4, which * DFF + col:which * DFF + col + W], (4, W)))
            for sub in range(n // 2):
                specs.append((wbo[s][:, sub * 2048:(sub + 1) * 2048], wboB[s][sub], w_out_v[:, ch0 + 2 * sub:ch0 + 2 * sub + 2, :], (2, 1024)))
            return specs

        def load_piece(p):
            for sp_ in piece_specs(p):
                self.load_cast(*sp_)

        load_piece(0)
        PBK = [(0, 1), (2, 3), (4, 5), (6, 7)]
        self.norm_transpose(h[:, 0:D], hB[0], gain_name, xnT, 0, T, xnB[0], xs[0], xsB[0], PBK[0], only="front")
        for t in range(NT):
            if t + 1 < NT:
                self.norm_transpose(h[:, (t + 1) * D:(t + 2) * D], hB[t + 1], gain_name, xnT, (t + 1) * 128, T, xnB[t + 1],
                                    xs[(t + 1) % 2], xsB[(t + 1) % 2], PBK[(t + 1) % 4], only="front")
            self._nt_back(gain_name, xnT, t * 128, T, xnB[t], xs[t % 2], xsB[t % 2], PBK[t % 4])
        blocks = [(p, tb) for p in range(NP) for tb in range(NB)]
        st = {"gi": 0, "oi": 0}

        def stage1(idx):
            p, tb = blocks[idx]
            s = p % 2
            hs = idx % 2
            n = pieces[p][1]
            for j in range(n):
                gi = st["gi"]
                for which in range(2):
                    pb = (0 if which == 0 else 2) + (gi % 2)
                    W = n * 128
                    for c in range(8):
                        off = which * 4096 + (c // 4) * 2048 + (c % 4) * W + j * 128
                        lhsT = wbi[s][:, off:off + 128]
                        rhs = xnT[:, c * T + tb * 512: c * T + (tb + 1) * 512]
                        op(PE, (lambda pb, lhsT, rhs, c: lambda e: e.matmul(psum[pb][:, :], lhsT=lhsT, rhs=rhs, start=(c == 0), stop=(c == 7)))(pb, lhsT, rhs, c),
                           reads=[wbiB[s][which * 2 + c // 4]] + xnB[tb * 4:(tb + 1) * 4], writes=psB[pb])
                pg, pu = (gi % 2), 2 + (gi % 2)
                k = gi % 2
                op(ACT, (lambda pg, k: lambda e: e.activation(out=sg[k][:, :], in_=psum[pg][:, :], func=AF.Silu))(pg, k),
                   reads=psB[pg], writes=[sgB[k]])
                op(DVE, (lambda pu, k, hs, j: lambda e: e.tensor_tensor(out=hid[hs][:, j * 512:(j + 1) * 512], in0=sg[k][:, :], in1=psum[pu][:, :], op=ALU.mult))(pu, k, hs, j),
                   reads=[sgB[k]] + psB[pu], writes=[hidB[hs]])
                st["gi"] += 1

        def stage2(idx):
            p, tb = blocks[idx]
            s = p % 2
            hs = idx % 2
            n = pieces[p][1]
            for tt in range(4):
                t = tb * 4 + tt
                for hh in range(2):
                    oi = st["oi"]
                    pb = 4 + (oi % 2)
                    for j in range(n):
                        lhsT = hid[hs][:, j * 512 + tt * 128: j * 512 + (tt + 1) * 128]
                        rhs = wbo[s][:, j * 1024 + hh * 512: j * 1024 + (hh + 1) * 512]
                        op(PE, (lambda pb, lhsT, rhs, j: lambda e: e.matmul(psum[pb][:, :], lhsT=lhsT, rhs=rhs, start=(j == 0), stop=(j == n - 1)))(pb, lhsT, rhs, j),
                           reads=[hidB[hs], wboB[s][j // 2]], writes=psB[pb])
                    hap = h[:, t * D + hh * 512: t * D + (hh + 1) * 512]
                    if oi % 2 == 0:
                        op(DVE, (lambda pb, hap: lambda e: e.scalar_tensor_tensor(out=hap, in0=psum[pb][:, :], scalar=0.5, in1=hap, op0=ALU.mult, op1=ALU.add))(pb, hap),
                           reads=psB[pb] + [hB[t]], writes=[hB[t]])
                    else:
                        k = (oi // 2) % 2
                        op(ACT, (lambda pb, k: lambda e: e.activation(out=ev[k][:, :], in_=psum[pb][:, :], func=AF.Copy, scale=0.5))(pb, k),
                           reads=psB[pb], writes=[evB[k]])
                        op(POOL, (lambda k, hap: lambda e: e.tensor_tensor(out=hap, in0=hap, in1=ev[k][:, :], op=ALU.add))(k, hap),
                           reads=[evB[k], hB[t]], writes=[hB[t]])
                    st["oi"] += 1

        pend_specs = []
        inflight = []
        for idx in range(len(blocks)):
            stage1(idx)
            if idx > 0:
                stage2(idx - 1)
            p, tb = blocks[idx]
            if tb == 0 and p + 1 < NP:
                pend_specs = piece_specs(p + 1)
            for hnd in inflight:
                self.load_cast_do(hnd)
            inflight = []
            if tb == NB - 1:
                for sp_ in pend_specs:
                    self.load_cast(*sp_)
                pend_specs = []
            else:
                for sp_ in pend_specs[:2]:
                    inflight.append(self.load_dma(*sp_))
                pend_specs = pend_specs[2:]
        stage2(len(blocks) - 1)
        self.stage, self.stB = base_stage, base_stB
        es.close()

    def gdn_phase(self, wm_in, full, tag):
        import os
        nc, S = self.nc, self.S
        op = S.op
        cs, cpB = self.cs, self.cpB
        h, hB = self.h, self.hB
        psum, psB = self.psum, self.psB
        es = contextlib.ExitStack()
        pc = self.sb("pc_" + tag, 12 * 131, F32, es); pcB = Buf()
        wg = self.sb("wg_" + tag, 8 * NG, BF16, es)
        wgB = [Buf() for _ in range(9)]
        xs = self.sb("xs_" + tag, D, F32, es); xsB = Buf()
        xn = self.sb("xn_" + tag, 8 * 128, BF16, es); xnB = Buf()
        qkv0 = self.sb("qkv_" + tag, 12 * 128, F32, es); qkvB0 = Buf()
        qkv_b = [qkv0, self.stage[0][:, 0:1536]]; qkvB_b = [qkvB0, self.stB[0]]
        cacc = self.sb("cacc_" + tag, 12 * 128, F32, es); caccB = Buf()
        etmp = self.sb("etmp_" + tag, 12 * 128, F32, es); etmpB = Buf()
        rs, rsB = etmp, etmpB
        zT0 = self.sb("zT_" + tag, 4 * 128, F32, es); zB0 = Buf()
        zT_b = [zT0, self.stage[1][:, 0:512]]; zB_b = [zB0, self.stB[1]]
        sm_b = [self.sb("sm%d_%s" % (i, tag), 64, F32, es) for i in range(2)]; smB_b = [Buf(), Buf()]
        Dg = self.sb("Dg_" + tag, 512, F32, es); DgB = Buf()
        glb_b = [self.sb("glb%d_%s" % (i, tag), 8, F32, es) for i in range(2)]; glB_b = [Buf(), Buf()]
        def mk(n, cols=128, dt=F32):
            return self.sb(n + "_" + tag, cols, dt, es), Buf(n)
        HB = []
        for hh in range(4):
            d_ = {}
            for n in ["kbg", "kdec", "vbeta", "tm1", "E1", "Lm", "AT", "X0", "X1", "Y0", "Y1", "P0", "P1", "wT", "um", "vnew"]:
                d_[n] = mk("%s%d" % (n, hh))
            HB.append(d_)
        new_bufs = wgB + [pcB, xsB, xnB, qkvB0, caccB, etmpB, zB0, DgB] + smB_b + glB_b + [b for d_ in HB for _, b in d_.values()]
        S.alias(new_bufs, getattr(self, "phase_bufs", []))
        self.phase_bufs = new_bufs

        wm_v = wm_in.rearrange("(c p) n -> p c n", p=128)
        col = 0
        while col < NG:
            w = min(256, NG - col)
            base = (col // 256) * 2048
            self.load_cast(wg[:, base:base + 8 * w], wgB[col // 256], wm_v[:, :, GW0 + col:GW0 + col + w], (8, w))
            col += w

        ident, ones, triU = cs("ident"), cs("ones"), cs("triU")
        maskL, maskU = cs("maskL"), cs("maskU")
        cwg = cs("cwg")
        pcv0 = pc[:, :].rearrange("p (a b) -> p a b", a=12)
        pchv = self.pch[:, :].rearrange("p (a b) -> p a b", a=12)
        op(DVE, lambda e: e.tensor_copy(out=pcv0[:, :, 0:3], in_=pchv), reads=[self.pchB], writes=[pcB])
        Sst, SsB = self.Sst, self.SsB
        nq = 16 if full else 12

        def pre_ops(t):
            pp = t % 2
            qkv, qkvB = qkv_b[pp], qkvB_b[pp]
            zT, zB = zT_b[pp], zB_b[pp]
            sm, smB = sm_b[pp], smB_b[pp]
            glb, glB = glb_b[pp], glB_b[pp]
            S.capture = []
            self.norm_transpose(h[:, t * D:(t + 1) * D], hB[t], "nm", xn, 0, 128, xnB, xs, xsB, (0, 1))
            for grp in range(nq // 4):
                if (not full) and grp == 0 and t != NT - 1:
                    continue
                pb = 2
                for q in range(4):
                    j = grp * 4 + q
                    for c in range(8):
                        lhsT = wg[:, (j // 2) * 2048 + c * 256 + (j % 2) * 128: (j // 2) * 2048 + c * 256 + (j % 2) * 128 + 128]
                        rhs = xn[:, c * 128:(c + 1) * 128]
                        op(PE, (lambda pb, q, lhsT, rhs, c: lambda e: e.matmul(psum[pb][:, q * 128:(q + 1) * 128], lhsT=lhsT, rhs=rhs, start=(c == 0), stop=(c == 7)))(pb, q, lhsT, rhs, c),
                           reads=[wgB[j // 2], xnB], writes=[psB[pb][q]])
                if grp < 3:
                    dstv = pc[:, grp * 4 * 131:(grp + 1) * 4 * 131].rearrange("p (a b) -> p a b", a=4)[:, :, 3:131]
                    srcv = psum[pb][:, :].rearrange("p (a b) -> p a b", a=4)
                    eng = self.ew()
                    if eng == ACT:
                        op(ACT, (lambda dstv, srcv: lambda e: e.activation(out=dstv, in_=srcv, func=AF.Copy))(dstv, srcv), reads=psB[pb], writes=[pcB])
                    else:
                        op(DVE, (lambda dstv, srcv: lambda e: e.tensor_copy(out=dstv, in_=srcv))(dstv, srcv), reads=psB[pb], writes=[pcB])
                else:
                    op(ACT, (lambda pb: lambda e: e.activation(out=zT[:, :], in_=psum[pb][:, :], func=AF.Copy))(pb), reads=psB[pb], writes=[zB])
            i3 = len(S.capture)
            for c in range(8):
                lhsT = xn[:, c * 128:(c + 1) * 128]
                rhs = wg[:, 8 * 2048 + c * 8: 8 * 2048 + c * 8 + 8]
                op(PE, (lambda lhsT, rhs, c: lambda e: e.matmul(psum[2][:, 0:8], lhsT=lhsT, rhs=rhs, start=(c == 0), stop=(c == 7)))(lhsT, rhs, c),
                   reads=[wgB[8], xnB], writes=[psB[2][0]])
            op(DVE, lambda e: e.tensor_copy(out=sm[:, 0:8], in_=psum[2][:, 0:8]), reads=[psB[2][0]], writes=[smB])
            op(ACT, lambda e: e.activation(out=sm[:, 8:12], in_=sm[:, 0:4], func=AF.Exp, scale=-1.0), reads=[smB], writes=[smB])
            op(DVE, lambda e: e.tensor_scalar(out=sm[:, 8:12], in0=sm[:, 8:12], scalar1=1.0, scalar2=None, op0=ALU.add), reads=[smB], writes=[smB])
            op(DVE, lambda e: e.reciprocal(out=sm[:, 8:12], in_=sm[:, 8:12]), reads=[smB], writes=[smB])
            op(DVE, lambda e: e.tensor_tensor(out=sm[:, 12:16], in0=sm[:, 4:8], in1=cs("dtb"), op=ALU.add), reads=[smB, cpB], writes=[smB])
            op(ACT, lambda e: e.activation(out=sm[:, 12:16], in_=sm[:, 12:16], func=AF.Exp), reads=[smB], writes=[smB])
            op(ACT, lambda e: e.activation(out=sm[:, 12:16], in_=sm[:, 12:16], func=AF.Ln, bias=1.0), reads=[smB], writes=[smB])
            op(DVE, lambda e: e.tensor_tensor(out=sm[:, 12:16], in0=sm[:, 12:16], in1=self.negA[:, :], op=ALU.mult), reads=[smB, self.negAB], writes=[smB])
            op(PE, lambda e: e.matmul(psum[2][:, 8:12], lhsT=triU, rhs=sm[:, 12:16], start=True, stop=True), reads=[smB, cpB], writes=[psB[2][0]])
            op(DVE, lambda e: e.tensor_copy(out=sm[:, 16:20], in_=psum[2][:, 8:12]), reads=[psB[2][0]], writes=[smB])
            op(ACT, lambda e: e.activation(out=sm[:, 20:24], in_=sm[:, 16:20], func=AF.Exp), reads=[smB], writes=[smB])
            op(DVE, lambda e: e.tensor_scalar(out=sm[:, 24:28], in0=sm[:, 16:20], scalar1=-1.0, scalar2=None, op0=ALU.mult), reads=[smB], writes=[smB])
            op(DVE, lambda e: e.tensor_tensor(out=sm[:, 28:32], in0=sm[:, 8:12], in1=sm[:, 20:24], op=ALU.mult), reads=[smB], writes=[smB])
            for hh in range(4):
                op(DVE, (lambda hh: lambda e: e.tensor_scalar(out=Dg[:, hh * 128:(hh + 1) * 128], in0=ident, scalar1=sm[:, 16 + hh:17 + hh], scalar2=None, op0=ALU.mult))(hh),
                   reads=[smB, cpB], writes=[DgB])
            op(PE, lambda e: e.matmul(psum[3][:, :], lhsT=ones, rhs=Dg[:, 0:512], start=True, stop=True), reads=[DgB, cpB], writes=psB[3])
            Gv = psum[3][:, :].rearrange("p (a b) -> p a b", a=4)
            op(ACT, lambda e: e.activation(out=glb[:, 0:8].rearrange("p (a b) -> p a b", a=4), in_=Gv[:, :, 63:128:64], func=AF.Exp), reads=psB[3], writes=[glB])
            op(DVE, lambda e: e.tensor_tensor(out=sm[0:64, 32:36], in0=Gv[0:64, :, 63], in1=sm[0:64, 16:20], op=ALU.subtract), reads=psB[3] + [smB], writes=[smB])
            op(DVE, lambda e: e.tensor_tensor(out=sm[64:128, 32:36], in0=Gv[64:128, :, 127], in1=sm[64:128, 16:20], op=ALU.subtract), reads=psB[3] + [smB], writes=[smB])
            op(ACT, lambda e: e.activation(out=sm[:, 32:36], in_=sm[:, 32:36], func=AF.Exp), reads=[smB], writes=[smB])
            i4 = len(S.capture)
            pcv = pc[:, :].rearrange("p (a b) -> p a b", a=12)
            for j in range(0 if full else 4, 12):
                op(DVE, (lambda j: lambda e: e.tensor_scalar(out=cacc[:, j * 128:(j + 1) * 128], in0=pc[:, j * 131:j * 131 + 128], scalar1=cwg[:, j * 4:j * 4 + 1], scalar2=None, op0=ALU.mult))(j),
                   reads=[pcB, cpB], writes=[caccB])
                for k in range(1, 4):
                    op(DVE, (lambda j, k: lambda e: e.scalar_tensor_tensor(out=cacc[:, j * 128:(j + 1) * 128], in0=pc[:, j * 131 + k:j * 131 + k + 128], scalar=cwg[:, j * 4 + k:j * 4 + k + 1], in1=cacc[:, j * 128:(j + 1) * 128], op0=ALU.mult, op1=ALU.add))(j, k),
                       reads=[pcB, cpB, caccB], writes=[caccB])
            op(DVE, lambda e: e.tensor_copy(out=pcv[:, :, 0:3], in_=pcv[:, :, 128:131]), reads=[pcB], writes=[pcB])
            i5 = len(S.capture)
            c_lo = 0 if full else 512
            op(ACT, lambda e: e.activation(out=etmp[:, c_lo:1536], in_=cacc[:, c_lo:1536], func=AF.Exp, scale=-1.0), reads=[caccB], writes=[etmpB])
            op(ACT, lambda e: e.activation(out=etmp[:, c_lo:1536], in_=etmp[:, c_lo:1536], func=AF.Ln, bias=1.0), reads=[etmpB], writes=[etmpB])
            op(ACT, lambda e: e.activation(out=etmp[:, c_lo:1536], in_=etmp[:, c_lo:1536], func=AF.Exp, scale=-1.0), reads=[etmpB], writes=[etmpB])
            op(DVE, lambda e: e.tensor_tensor(out=qkv[:, c_lo:1536], in0=cacc[:, c_lo:1536], in1=etmp[:, c_lo:1536], op=ALU.mult), reads=[etmpB, caccB], writes=[qkvB])
            op(ACT, lambda e: e.activation(out=etmp[:, c_lo:1024], in_=qkv[:, c_lo:1024], func=AF.Square), reads=[qkvB, etmpB], writes=[etmpB])
            for half in range(0 if full else 1, 2):
                op(PE, (lambda half: lambda e: e.matmul(psum[half][:, :], lhsT=ones, rhs=etmp[:, half * 512:(half + 1) * 512], start=True, stop=True))(half),
                   reads=[etmpB, cpB], writes=psB[half])
                op(ACT, (lambda half: lambda e: e.activation(out=rs[:, half * 512:(half + 1) * 512], in_=psum[half][:, :], func=AF.Ln, bias=cs("eps")))(half),
                   reads=psB[half] + [cpB], writes=[rsB])
            op(ACT, lambda e: e.activation(out=rs[:, c_lo:1024], in_=rs[:, c_lo:1024], func=AF.Exp, scale=-0.5), reads=[rsB], writes=[rsB])
            if full:
                op(DVE, lambda e: e.scalar_tensor_tensor(out=qkv[:, 0:512], in0=qkv[:, 0:512], scalar=128.0 ** -0.5, in1=rs[:, 0:512], op0=ALU.mult, op1=ALU.mult), reads=[qkvB, rsB], writes=[qkvB])
            op(DVE, lambda e: e.tensor_tensor(out=qkv[:, 512:1024], in0=qkv[:, 512:1024], in1=rs[:, 512:1024], op=ALU.mult), reads=[qkvB, rsB], writes=[qkvB])
            if full:
                op(ACT, lambda e: e.activation(out=etmp[:, 0:512], in_=zT[:, :], func=AF.Exp, scale=-1.0), reads=[zB, etmpB], writes=[etmpB])
                op(ACT, lambda e: e.activation(out=etmp[:, 0:512], in_=etmp[:, 0:512], func=AF.Ln, bias=1.0), reads=[etmpB], writes=[etmpB])
                op(ACT, lambda e: e.activation(out=etmp[:, 0:512], in_=etmp[:, 0:512], func=AF.Exp, scale=-1.0), reads=[etmpB], writes=[etmpB])
                op(DVE, lambda e: e.tensor_tensor(out=zT[:, :], in0=zT[:, :], in1=etmp[:, 0:512], op=ALU.mult), reads=[etmpB, zB], writes=[zB])
            cap_ = S.capture
            S.capture = None
            s3, s4 = cap_[i3:i4], cap_[i4:i5]
            mer = []
            i_, j_ = 0, 0
            while i_ < len(s3) or j_ < len(s4):
                if j_ < len(s4) and (i_ >= len(s3) or j_ * max(len(s3), 1) <= i_ * len(s4)):
                    mer.append(s4[j_]); j_ += 1
                else:
                    mer.append(s3[i_]); i_ += 1
            return cap_[:i3] + mer + cap_[i5:]

        def chain(hh, t):
            pp = t % 2
            qkv, qkvB = qkv_b[pp], qkvB_b[pp]
            zT, zB = zT_b[pp], zB_b[pp]
            sm, smB = sm_b[pp], smB_b[pp]
            glb, glB = glb_b[pp], glB_b[pp]
            B_ = HB[hh]
            kbg, kbgB = B_["kbg"]; kdec, kdecB = B_["kdec"]; vbeta, vbetaB = B_["vbeta"]
            tm1, tm1B = B_["tm1"]; E1, E1B = B_["E1"]; Lm, LmB = B_["Lm"]; AT, ATB = B_["AT"]
            X = [B_["X0"], B_["X1"]]; Y = [B_["Y0"], B_["Y1"]]; Pm = [B_["P0"], B_["P1"]]
            wT, wTB = B_["wT"]; um, umB = B_["um"]; vnew, vnewB = B_["vnew"]
            o1s, o1sB = tm1, tm1B
            om, omB = E1, E1B
            on, onB = Lm, LmB
            qT = qkv[:, hh * 128:(hh + 1) * 128]
            kT = qkv[:, 512 + hh * 128:512 + (hh + 1) * 128]
            vT = qkv[:, 1024 + hh * 128:1024 + (hh + 1) * 128]
            PH = psum[4 + hh]
            PB = psB[4 + hh][0]
            Q0, Q1, Q2, Q3 = PH[:, 0:128], PH[:, 128:256], PH[:, 256:384], PH[:, 384:512]
            Gh = psum[3][:, hh * 128:(hh + 1) * 128]
            GhB = psB[3]
            op(PE, lambda e: e.transpose(Q0, kT, ident), reads=[qkvB, cpB], writes=[PB])
            op(PE, lambda e: e.transpose(Q1, vT, ident), reads=[qkvB, cpB], writes=[PB])
            op(PE, lambda e: e.matmul(Q2, lhsT=kT, rhs=kT, start=True, stop=True), reads=[qkvB], writes=[PB])
            if full:
                op(PE, lambda e: e.matmul(Q3, lhsT=kT, rhs=qT, start=True, stop=True), reads=[qkvB], writes=[PB])
            yield
            op(DVE, lambda e: e.tensor_tensor(out=tm1[:, :], in0=maskL, in1=Gh, op=ALU.subtract), reads=GhB + [cpB], writes=[tm1B])
            yield
            op(ACT, lambda e: e.activation(out=E1[:, :], in_=tm1[:, :], func=AF.Exp, bias=sm[:, 16 + hh:17 + hh]), reads=[tm1B, smB], writes=[E1B])
            yield
            op(ACT, lambda e: e.activation(out=kbg[:, :], in_=Q0, func=AF.Copy, scale=sm[:, 28 + hh:29 + hh]), reads=[PB, smB], writes=[kbgB])
            op(ACT, lambda e: e.activation(out=vbeta[:, :], in_=Q1, func=AF.Copy, scale=sm[:, 8 + hh:9 + hh]), reads=[PB, smB], writes=[vbetaB])
            yield
            op(DVE, lambda e: e.tensor_scalar(out=kdec[:, :], in0=Q0, scalar1=sm[:, 32 + hh:33 + hh], scalar2=None, op0=ALU.mult), reads=[PB, smB], writes=[kdecB])
            op(DVE, lambda e: e.scalar_tensor_tensor(out=Lm[:, :], in0=Q2, scalar=sm[:, 8 + hh:9 + hh], in1=E1[:, :], op0=ALU.mult, op1=ALU.mult), reads=[PB, smB, E1B], writes=[LmB])
            yield
            if full:
                op(DVE, lambda e: e.tensor_tensor(out=tm1[:, :], in0=maskU, in1=Gh, op=ALU.add), reads=GhB + [cpB, tm1B], writes=[tm1B])
                yield
                op(ACT, lambda e: e.activation(out=E1[:, :], in_=tm1[:, :], func=AF.Exp, bias=sm[:, 24 + hh:25 + hh]), reads=[tm1B, smB, E1B], writes=[E1B])
                yield
                op(DVE, lambda e: e.tensor_tensor(out=AT[:, :], in0=Q3, in1=E1[:, :], op=ALU.mult), reads=[PB, E1B], writes=[ATB])
                yield
            op(PE, lambda e: e.transpose(Q0, Lm[:, :], ident), reads=[LmB, cpB], writes=[PB])
            yield
            X0, X0B = X[0]
            P0, P0B = Pm[0]
            op(ACT, lambda e: e.activation(out=X0[:, :], in_=Q0, func=AF.Copy), reads=[PB], writes=[X0B])
            op(DVE, lambda e: e.tensor_tensor(out=P0[:, :], in0=ident, in1=Q0, op=ALU.subtract), reads=[PB, cpB], writes=[P0B])
            yield
            Xc, XcB = X0, X0B
            Yc, YcB = Lm, LmB
            Pc, PcB = P0, P0B
            for k in range(1, 6):
                Yn, YnB = Y[k % 2]
                Xn, XnB = X[k % 2]
                Pn, PnB = Pm[k % 2]
                op(PE, (lambda Xc, Yc: lambda e: e.matmul(Q1, lhsT=Xc[:, :], rhs=Yc[:, :], start=True, stop=True))(Xc, Yc), reads=[XcB, YcB], writes=[PB])
                if k < 5:
                    op(PE, (lambda Xc, Yc: lambda e: e.matmul(Q2, lhsT=Yc[:, :], rhs=Xc[:, :], start=True, stop=True))(Xc, Yc), reads=[XcB, YcB], writes=[PB])
                yield
                op(ACT, (lambda Yn: lambda e: e.activation(out=Yn[:, :], in_=Q1, func=AF.Copy))(Yn), reads=[PB], writes=[YnB])
                if k < 5:
                    op(ACT, (lambda Xn: lambda e: e.activation(out=Xn[:, :], in_=Q2, func=AF.Copy))(Xn), reads=[PB], writes=[XnB])
                yield
                op(PE, (lambda Yn, Pc: lambda e: e.matmul(Q3, lhsT=Yn[:, :], rhs=Pc[:, :], start=True, stop=True))(Yn, Pc), reads=[YnB, PcB], writes=[PB])
                yield
                op(DVE, (lambda Pn, Pc: lambda e: e.tensor_tensor(out=Pn[:, :], in0=Pc[:, :], in1=Q3, op=ALU.add))(Pn, Pc), reads=[PcB, PB], writes=[PnB])
                yield
                Xc, XcB, Yc, YcB, Pc, PcB = Xn, XnB, Yn, YnB, Pn, PnB
            op(PE, (lambda Pc: lambda e: e.matmul(Q0, lhsT=kbg[:, :], rhs=Pc[:, :], start=True, stop=True))(Pc), reads=[kbgB, PcB], writes=[PB])
            op(PE, (lambda Pc: lambda e: e.matmul(Q1, lhsT=Pc[:, :], rhs=vbeta[:, :], start=True, stop=True))(Pc), reads=[vbetaB, PcB], writes=[PB])
            yield
            op(ACT, lambda e: e.activation(out=wT[:, :], in_=Q0, func=AF.Copy), reads=[PB], writes=[wTB])
            op(ACT, lambda e: e.activation(out=um[:, :], in_=Q1, func=AF.Copy), reads=[PB], writes=[umB])
            yield
            for half in range(2):
                r0, r1 = half * 64, half * 64 + 64
                sp_ = self.spar_h[hh]
                Scur = Sst[sp_][:, hh * 128:(hh + 1) * 128]; ScurB = SsB[sp_][hh]
                Snew = Sst[1 - sp_][:, hh * 128:(hh + 1) * 128]; SnewB = SsB[1 - sp_][hh]
                self.spar_h[hh] = 1 - sp_
                op(PE, (lambda r0, r1, Scur: lambda e: e.matmul(PH[r0:r1, 256:384], lhsT=wT[:, r0:r1], rhs=Scur, start=True, stop=True))(r0, r1, Scur), reads=[wTB, ScurB], writes=[PB])
                yield
                op(DVE, (lambda r0, r1: lambda e: e.tensor_tensor(out=vnew[r0:r1, :], in0=um[r0:r1, :], in1=PH[r0:r1, 256:384], op=ALU.subtract))(r0, r1), reads=[umB, PB], writes=[vnewB])
                yield
                if full:
                    op(PE, (lambda r0, r1, Scur: lambda e: e.matmul(PH[r0:r1, 0:128], lhsT=qT[:, r0:r1], rhs=Scur, start=True, stop=True))(r0, r1, Scur), reads=[qkvB, ScurB], writes=[PB])
                    op(PE, (lambda r0, r1: lambda e: e.matmul(PH[r0:r1, 128:256], lhsT=AT[r0:r1, r0:r1], rhs=vnew[r0:r1, :], start=True, stop=True))(r0, r1), reads=[ATB, vnewB], writes=[PB])
                op(PE, (lambda r0, r1: lambda e: e.matmul(Q3, lhsT=kdec[r0:r1, :], rhs=vnew[r0:r1, :], start=True, stop=True))(r0, r1), reads=[kdecB, vnewB], writes=[PB])
                yield
                op(DVE, (lambda Snew, Scur, half: lambda e: e.scalar_tensor_tensor(out=Snew, in0=Scur, scalar=glb[:, hh * 2 + half:hh * 2 + half + 1], in1=Q3, op0=ALU.mult, op1=ALU.add))(Snew, Scur, half),
                   reads=[ScurB, glB, PB], writes=[SnewB])
                yield
            if full:
                c0 = 40 + hh * 3
                op(ACT, lambda e: e.activation(out=o1s[:, :], in_=Q0, func=AF.Copy, scale=sm[:, 20 + hh:21 + hh]), reads=[PB, smB], writes=[o1sB])
                yield
                op(DVE, lambda e: e.tensor_tensor(out=om[:, :], in0=o1s[:, :], in1=Q1, op=ALU.add), reads=[o1sB, PB], writes=[omB])
                yield
                op(ACT, lambda e: e.activation(out=on[:, :], in_=om[:, :], func=AF.Square, accum_out=sm[:, c0:c0 + 1]), reads=[omB], writes=[onB, smB])
                yield
                op(POOL, lambda e: e.tensor_scalar(out=sm[:, c0 + 1:c0 + 2], in0=sm[:, c0:c0 + 1], scalar1=1.0 / 128, scalar2=EPS, op0=ALU.mult, op1=ALU.add), reads=[smB], writes=[smB])
                op(POOL, lambda e: e.tensor_tensor(out=sm[:, c0 + 2:c0 + 3], in0=sm[:, c0 + 1:c0 + 2], in1=cs("mhalf"), op=ALU.pow), reads=[smB, cpB], writes=[smB])
                yield
                op(DVE, lambda e: e.scalar_tensor_tensor(out=on[:, :], in0=om[:, :], scalar=sm[:, c0 + 2:c0 + 3], in1=cs("gon"), op0=ALU.mult, op1=ALU.mult), reads=[omB, smB, cpB], writes=[onB])
                yield
                op(PE, lambda e: e.transpose(Q2, on[:, :], ident), reads=[onB, cpB], writes=[PB])
                yield
                yg_ap = self.ygT[:, hh * T + t * 128: hh * T + (t + 1) * 128]
                op(DVE, lambda e: e.tensor_tensor(out=yg_ap, in0=Q2, in1=zT[:, hh * 128:(hh + 1) * 128], op=ALU.mult), reads=[PB, zB], writes=[self.ygB[t]])
                yield

        for it in pre_ops(0):
            S.replay(it)
        for t in range(NT):
            pend = pre_ops(t + 1) if t + 1 < NT else []
            per = (len(pend) + 39) // 40
            S0 = int(os.environ.get("KSTAG", 3))
            rounds_t = (47 if full else 37) + 3 * S0 - int(os.environ.get("KEARLY", 3))
            per = (len(pend) + rounds_t - 1) // rounds_t
            alive = [(hh, chain(hh, t)) for hh in range(4)]
            pi = 0
            rnd = 0
            while alive or pi < len(pend):
                nxt = []
                for hh, g_ in alive:
                    if rnd < hh * S0:
                        nxt.append((hh, g_))
                        continue
                    try:
                        next(g_)
                        nxt.append((hh, g_))
                    except StopIteration:
                        pass
                alive = nxt
                for it in pend[pi:pi + per]:
                    S.replay(it)
                pi += per
                rnd += 1
        op(DVE, lambda e: e.tensor_copy(out=pchv, in_=pcv0[:, :, 0:3]), reads=[pcB], writes=[self.pchB])
        es.close()

    def conv_phase(self, wm_in, wm_out, hhalo, hhB):
        import os
        nc, S = self.nc, self.S
        op = S.op
        cs, cpB = self.cs, self.cpB
        h, hB = self.h, self.hB
        psum, psB = self.psum, self.psB
        es = contextlib.ExitStack()
        tag = "cv"
        wc = self.sb("wc", 8 * 1536, BF16, es); wcB = [Buf() for _ in range(6)]
        wo = self.sb("wo", 8 * 1024, BF16, es); woB = [Buf() for _ in range(4)]
        xs = self.sb("xs_cv", D, F32, es); xsB = Buf()
        self.junk = self.sb("junk_cv", D, BF16, es); self.junkB = Buf()
        xn = self.sb("xn_cv", 8 * 128, BF16, es); xnB = Buf()
        mpc = self.sb("mpc", 4 * 130, F32, es); mpcB = Buf()
        mpc2 = self.sb("mpc2", 4 * 130, F32, es); mpc2B = Buf()
        cbs0 = self.sb("cbs0", 512, F32, es); cbs0B = Buf()
        cbs1 = self.sb("cbs1", 512, F32, es); cbs1B = Buf()
        cct = self.sb("cct", 512, F32, es); cctB = Buf()
        cacc = self.sb("cacc_cv", 512, F32, es); caccB = Buf()
        yv = self.sb("yv", 512, F32, es); yvB = Buf()
        sq = self.sb("sq_cv", 512, F32, es); sqB = Buf()
        rs = self.sb("rs_cv", 512, F32, es); rsB = Buf()
        ycT = self.sb("ycT", 512, BF16, es); ycB = Buf()
        motmp = [self.sb("motmp%d" % i, 512, F32, es) for i in range(2)]; motB = [Buf(), Buf()]
        new_bufs = wcB + woB + [mpc2B, cbs0B, cbs1B, xsB, self.junkB, xnB, mpcB, cctB, caccB, yvB, sqB, rsB, ycB] + motB
        S.alias(new_bufs, getattr(self, "phase_bufs", []))
        self.phase_bufs = new_bufs
        wm_v = wm_in.rearrange("(c p) n -> p c n", p=128)
        for col in range(0, 1536, 256):
            base = (col // 256) * 2048
            self.load_cast(wc[:, base:base + 2048], wcB[col // 256], wm_v[:, :, col:col + 256], (8, 256))
        wo_v = wm_out.rearrange("(c p) n -> p c n", p=128)
        for i in range(4):
            self.load_cast(wo[:, i * 2048:(i + 1) * 2048], woB[i], wo_v[:, 2 * i:2 * i + 2, :], (2, 1024))
        op(POOL, lambda e: e.memset(mpc[:, :], 0.0), writes=[mpcB])
        op(POOL, lambda e: e.memset(mpc2[:, :], 0.0), writes=[mpc2B])
        ident, blk64 = cs("ident"), cs("blk64")
        csw, cgain = cs("csw"), cs("cgain")
        ygT, ygB = self.ygT, self.ygB
        mpc_b = [mpc, mpc2]; mpcB_b = [mpcB, mpc2B]
        cbs_b = [cbs0, cbs1]; cbsB_b = [cbs0B, cbs1B]

        def capA(t):
            p = t % 2
            mp, mpB = mpc_b[p], mpcB_b[p]
            mo, moB = mpc_b[1 - p], mpcB_b[1 - p]
            mpv = mp[:, :].rearrange("p (a b) -> p a b", a=4)
            mov = mo[:, :].rearrange("p (a b) -> p a b", a=4)
            S.capture = []
            if t < 0:
                src, srcB = hhalo[:, :], hhB
            else:
                src, srcB = h[:, t * D:(t + 1) * D], hB[t]
            self.norm_transpose(src, srcB, "nm", xn, 0, 128, xnB, xs, xsB, (0, 1))
            for grp in range(3):
                if t < 0 and grp == 0:
                    continue
                pb = 2 + grp
                for q in range(4):
                    j = grp * 4 + q
                    for c in range(8):
                        lhsT = wc[:, (j // 2) * 2048 + c * 256 + (j % 2) * 128: (j // 2) * 2048 + c * 256 + (j % 2) * 128 + 128]
                        rhs = xn[:, c * 128:(c + 1) * 128]
                        op(PE, (lambda pb, q, lhsT, rhs, c: lambda e: e.matmul(psum[pb][:, q * 128:(q + 1) * 128], lhsT=lhsT, rhs=rhs, start=(c == 0), stop=(c == 7)))(pb, q, lhsT, rhs, c),
                           reads=[wcB[j // 2], xnB], writes=[psB[pb][q]])
                if grp == 0:
                    op(ACT, (lambda p: lambda e: e.activation(out=cbs_b[p][:, :], in_=psum[2][:, :], func=AF.Copy))(p), reads=psB[2], writes=[cbsB_b[p]])
                if grp == 1:
                    op(ACT, lambda e: e.activation(out=cct[:, :], in_=psum[3][:, :], func=AF.Copy), reads=psB[3], writes=[cctB])
            op(DVE, lambda e: e.tensor_tensor(out=mpv[:, :, 2:130], in0=cct[:, :].rearrange("p (a b) -> p a b", a=4), in1=psum[4][:, :].rearrange("p (a b) -> p a b", a=4), op=ALU.mult),
               reads=[cctB, mpB] + psB[4], writes=[mpB])
            op(DVE, lambda e: e.tensor_copy(out=mpv[:, :, 0:2], in_=mov[:, :, 128:130]), reads=[moB, mpB], writes=[mpB])
            ops_ = S.capture
            S.capture = None
            return ops_

        def capB(t):
            p = t % 2
            mp, mpB = mpc_b[p], mpcB_b[p]
            cbs, cbsB = cbs_b[p], cbsB_b[p]
            S.capture = []
            for j in range(4):
                op(DVE, (lambda j: lambda e: e.tensor_scalar(out=cacc[:, j * 128:(j + 1) * 128], in0=mp[:, j * 130:j * 130 + 128], scalar1=csw[:, j * 3:j * 3 + 1], scalar2=None, op0=ALU.mult))(j),
                   reads=[mpB, cpB], writes=[caccB])
                for k in range(1, 3):
                    op(DVE, (lambda j, k: lambda e: e.scalar_tensor_tensor(out=cacc[:, j * 128:(j + 1) * 128], in0=mp[:, j * 130 + k:j * 130 + k + 128], scalar=csw[:, j * 3 + k:j * 3 + k + 1], in1=cacc[:, j * 128:(j + 1) * 128], op0=ALU.mult, op1=ALU.add))(j, k),
                       reads=[mpB, cpB, caccB], writes=[caccB])
            op(DVE, lambda e: e.tensor_tensor(out=yv[:, :], in0=cacc[:, :], in1=cbs[:, :], op=ALU.mult), reads=[caccB, cbsB], writes=[yvB])
            op(ACT, lambda e: e.activation(out=sq[:, :], in_=yv[:, :], func=AF.Square), reads=[yvB], writes=[sqB])
            op(PE, lambda e: e.matmul(psum[5][:, :], lhsT=blk64, rhs=sq[:, :], start=True, stop=True), reads=[sqB, cpB], writes=psB[5])
            op(ACT, lambda e: e.activation(out=rs[:, :], in_=psum[5][:, :], func=AF.Ln, bias=cs("eps")), reads=psB[5] + [cpB], writes=[rsB])
            op(ACT, lambda e: e.activation(out=rs[:, :], in_=rs[:, :], func=AF.Exp, scale=-0.5), reads=[rsB], writes=[rsB])
            for j in range(4):
                op(DVE, (lambda j: lambda e: e.scalar_tensor_tensor(out=ycT[:, j * 128:(j + 1) * 128], in0=yv[:, j * 128:(j + 1) * 128], scalar=cgain[:, j:j + 1], in1=rs[:, j * 128:(j + 1) * 128], op0=ALU.mult, op1=ALU.mult))(j),
                   reads=[yvB, rsB, cpB], writes=[ycB])
            for hh in range(2):
                pb = 6 + hh
                for j in range(8):
                    if j < 4:
                        lhsT = ycT[:, j * 128:(j + 1) * 128]
                        rd = [ycB]
                    else:
                        lhsT = ygT[:, (j - 4) * T + t * 128:(j - 4) * T + (t + 1) * 128]
                        rd = [ygB[t]]
                    rhs = wo[:, j * 1024 + hh * 512: j * 1024 + (hh + 1) * 512]
                    op(PE, (lambda pb, lhsT, rhs, j: lambda e: e.matmul(psum[pb][:, :], lhsT=lhsT, rhs=rhs, start=(j == 0), stop=(j == 7)))(pb, lhsT, rhs, j),
                       reads=rd + [woB[j // 2]], writes=psB[pb])
                hap = h[:, t * D + hh * 512: t * D + (hh + 1) * 512]
                op(ACT, (lambda pb, hh: lambda e: e.activation(out=motmp[hh][:, :], in_=psum[pb][:, :], func=AF.Copy))(pb, hh), reads=psB[pb], writes=[motB[hh]])
                op(DVE, (lambda hh, hap: lambda e: e.tensor_tensor(out=hap, in0=hap, in1=motmp[hh][:, :], op=ALU.add))(hh, hap), reads=[motB[hh], hB[t]], writes=[hB[t]])
            ops_ = S.capture
            S.capture = None
            return ops_

        for it in capA(-1):
            S.replay(it)
        for it in capA(0):
            S.replay(it)
        for t in range(NT):
            A = capA(t + 1) if t + 1 < NT else []
            B_ = capB(t)
            na, nb = len(A), len(B_)
            ia = ib = 0
            while ia < na or ib < nb:
                if ib < nb and (ia >= na or ib * max(na, 1) <= ia * nb):
                    S.replay(B_[ib]); ib += 1
                else:
                    S.replay(A[ia]); ia += 1
        es.close()

    def final(self, out, fn_bc):
        S = self.S
        op = S.op
        cs, cpB = self.cs, self.cpB
        h, hB = self.h, self.hB
        es = contextlib.ExitStack()
        ot = [self.sb("ot%d" % i, D, F32, es) for i in range(2)]
        otB = [Buf(), Buf()]
        fs = self.sb("fs", 64, F32, es); fsB = Buf()
        junk = self.sb("junk_f", D, BF16, es); junkB = Buf()
        fnb = self.sb("fnb", D, F32, es); fnbB = Buf()
        new_bufs = otB + [fsB, junkB, fnbB]
        S.alias(new_bufs, getattr(self, "phase_bufs", []))
        self.phase_bufs = new_bufs
        S.dma(SP, lambda e: e.dma_start(out=fnb[:], in_=fn_bc), "const2", S.new_group(), writes=[fnbB])
        import os
        if os.environ.get("KRAWOUT"):
            for t in range(NT):
                S.dma(SP, (lambda t: lambda e: e.dma_start(out=out[t * 128:(t + 1) * 128, :], in_=h[:, t * D:(t + 1) * D]))(t), "out%d" % (t % 2), S.new_group(), reads=[hB[t]])
            es.close()
            return
        for t in range(NT):
            k = t % 2
            c0 = (t % 16) * 3
            hs = h[:, t * D:(t + 1) * D]
            op(ACT, (lambda hs, c0: lambda e: e.activation(out=junk[:, :], in_=hs, func=AF.Square, accum_out=fs[:, c0:c0 + 1]))(hs, c0), reads=[hB[t]], writes=[junkB, fsB])
            op(POOL, (lambda c0: lambda e: e.tensor_scalar(out=fs[:, c0 + 1:c0 + 2], in0=fs[:, c0:c0 + 1], scalar1=1.0 / D, scalar2=EPS, op0=ALU.mult, op1=ALU.add))(c0), reads=[fsB], writes=[fsB])
            op(POOL, (lambda c0: lambda e: e.tensor_tensor(out=fs[:, c0 + 2:c0 + 3], in0=fs[:, c0 + 1:c0 + 2], in1=cs("mhalf"), op=ALU.pow))(c0), reads=[fsB, cpB], writes=[fsB])
            op(DVE, (lambda hs, c0, k: lambda e: e.scalar_tensor_tensor(out=ot[k][:, :], in0=hs, scalar=fs[:, c0 + 2:c0 + 3], in1=fnb[:, :], op0=ALU.mult, op1=ALU.mult))(hs, c0, k),
               reads=[hB[t], fsB, fnbB], writes=[otB[k]])
            S.dma(SP, (lambda t, k: lambda e: e.dma_start(out=out[t * 128:(t + 1) * 128, :], in_=ot[k][:, :]))(t, k), "out%d" % k, S.new_group(), reads=[otB[k]])
        es.close()


def _pack_layout():
    names = [("ident", 128), ("ones", 128), ("triU", 128), ("maskL", 128), ("maskU", 128), ("blk64", 128),
             ("gon", 128), ("n1", 8), ("nm", 8), ("n2", 8), ("cwg", 48), ("csw", 12), ("cgain", 4),
             ("alog", 4), ("dtb", 4), ("mhalf", 1), ("eps", 1)]
    lay = {}
    off = 0
    for n, w in names:
        lay[n] = (off, off + w)
        off += w
    return lay, off


_CP, _CPK_COLS = _pack_layout()
Builder.CP = _CP
Builder.CPK_COLS = _CPK_COLS


def _pack_consts(inp):
    f = np.float32
    cp = np.zeros((128, _CPK_COLS), f)

    def put(name, arr):
        a, b = _CP[name]
        cp[:, a:b] = np.asarray(arr, f).reshape(128, b - a)

    idx = np.arange(128)
    same = (idx[:, None] // 64) == (idx[None, :] // 64)
    put("ident", np.eye(128))
    put("ones", np.ones((128, 128)))
    put("triU", (same & (idx[:, None] <= idx[None, :])))
    put("maskL", np.where(same & (idx[:, None] > idx[None, :]), 0.0, NEG))
    put("maskU", np.where(same & (idx[:, None] <= idx[None, :]), 0.0, NEG))
    put("blk64", same.astype(f) / 64.0)
    put("gon", np.broadcast_to(inp["gdn_out_norm"].reshape(1, 128), (128, 128)))
    put("n1", inp["ffn1_norm"].reshape(8, 128).T)
    put("nm", inp["mix_norm"].reshape(8, 128).T)
    put("n2", inp["ffn2_norm"].reshape(8, 128).T)
    put("cwg", inp["gdn_conv_w"].reshape(4, 12, 128).transpose(2, 1, 0).reshape(128, 48))
    put("csw", inp["conv_short_w"].reshape(3, 4, 128).transpose(2, 1, 0).reshape(128, 12))
    put("cgain", inp["conv_out_norm"].reshape(4, 128).T)
    put("alog", np.broadcast_to(inp["gdn_A_log"].reshape(1, 4), (128, 4)))
    put("dtb", np.broadcast_to(inp["gdn_dt_bias"].reshape(1, 4), (128, 4)))
    put("mhalf", np.full((128, 1), -0.5))
    put("eps", np.full((128, 1), EPS))
    return cp


_NC_CACHE = {}


def _get_nc(debug=False):
    if debug not in _NC_CACHE:
        b = Builder(debug=debug)
        b.spar_h = [0, 0, 0, 0]
        _NC_CACHE[debug] = (b.build(), b)
    return _NC_CACHE[debug]


def kernel(debug=False, **inputs):
    inp = {k: np.asarray(v) for k, v in inputs.items()}
    x = inp["x"].astype(np.float32, copy=False)
    nc, b = _get_nc(debug)
    cp = _pack_consts(inp)
    fn_bc = np.ascontiguousarray(np.broadcast_to(inp["final_norm"].reshape(1, D).astype(np.float32), (128, D)))
    shared = {
        "w1_in": np.ascontiguousarray(inp["ffn1_w_in"][0]), "w1_out": np.ascontiguousarray(inp["ffn1_w_out"][0]),
        "w2_in": np.ascontiguousarray(inp["ffn2_w_in"][0]), "w2_out": np.ascontiguousarray(inp["ffn2_w_out"][0]),
        "wm_in": np.ascontiguousarray(inp["w_mix_in"][0]), "wm_out": np.ascontiguousarray(inp["w_mix_out"][0]),
        "cpk": cp, "fn_bc": fn_bc,
    }
    zeros = np.zeros((T, D), np.float32)
    in_maps = []
    for c in range(8):
        bi, half = c // 2, c % 2
        m = dict(shared)
        m["x_own"] = np.ascontiguousarray(x[bi, half * T:(half + 1) * T])
        m["x_pre"] = zeros if half == 0 else np.ascontiguousarray(x[bi, 0:T])
        in_maps.append(m)
    import os
    ncores = int(os.environ.get("KCORES", 8))
    res = run_bass_kernel_spmd(nc, in_maps[:ncores], core_ids=list(range(ncores)))
    outp = np.zeros((4, 2 * T, D), np.float32)
    for c in range(ncores):
        outp[c // 2, (c % 2) * T:(c % 2 + 1) * T] = res.results[c]["out"]
    if debug:
        return outp, res.results
    return outp
```

```python
import contextlib
import numpy as np
import concourse.bass as bass
import concourse.mybir as mybir
from concourse.bass_utils import run_bass_kernel_spmd

F32 = mybir.dt.float32
BF16 = mybir.dt.bfloat16
AF = mybir.ActivationFunctionType
ALU = mybir.AluOpType

PE, ACT, DVE, POOL, SP = "pe", "act", "dve", "pool", "sp"
COMPUTE = (PE, ACT, DVE, POOL)

D = 1024
DFF = 2816
T = 2048
NT = T // 128
NB = T // 512
EPS = 1e-6
GW0 = 1536
NG = 2056
NEG = -1.0e30


class Buf:
    __slots__ = ("name", "last_w", "readers", "excl")

    def __init__(self, name="", excl=False):
        self.name = name
        self.last_w = None
        self.readers = []
        self.excl = excl


class Op:
    __slots__ = ("eng", "fn", "deps", "needs_inc", "cnt", "is_dma", "key", "grp", "idx")

    def __init__(self, eng, fn, is_dma=False, key=None, grp=None):
        self.eng = eng
        self.fn = fn
        self.deps = []
        self.needs_inc = False
        self.cnt = 0
        self.is_dma = is_dma
        self.key = key
        self.grp = grp


class Sched:
    def __init__(self):
        self.ops = []
        self.grp_ctr = 0

    def new_group(self):
        self.grp_ctr += 1
        return self.grp_ctr

    def _add(self, op, reads, writes):
        if getattr(self, "capture", None) is not None:
            self.capture.append((op, list(reads), list(writes)))
            return op
        return self._add_real(op, reads, writes)

    def replay(self, item):
        return self._add_real(*item)

    def _add_real(self, op, reads, writes):
        op.idx = len(self.ops)
        ex = [b for b in reads if b.excl]
        if ex:
            reads = [b for b in reads if not b.excl]
            writes = list(writes) + ex
        deps = {}
        for b in reads:
            if b.last_w is not None:
                deps[id(b.last_w)] = b.last_w
        for b in writes:
            if b.last_w is not None:
                deps[id(b.last_w)] = b.last_w
            for r in b.readers:
                deps[id(r)] = r
        latest = {}
        for d in deps.values():
            if d is op:
                continue
            if (not d.is_dma) and (not op.is_dma) and d.eng == PE and op.eng == PE:
                continue
            if d.is_dma:
                op.deps.append(d)
            else:
                cur = latest.get(d.eng)
                if cur is None or d.idx > cur.idx:
                    latest[d.eng] = d
        op.deps.extend(latest.values())
        for b in reads:
            if op.is_dma:
                b.readers.append(op)
            else:
                b.readers = [r for r in b.readers if r.is_dma or r.eng != op.eng]
                b.readers.append(op)
        for b in writes:
            b.last_w = op
            b.readers = []
        self.ops.append(op)
        return op

    def op(self, eng, fn, reads=(), writes=()):
        return self._add(Op(eng, fn), reads, writes)

    def dma(self, queue, fn, key, grp, reads=(), writes=()):
        return self._add(Op(queue, fn, is_dma=True, key=key, grp=grp), reads, writes)

    def alias(self, new_bufs, old_bufs):
        acc = {}
        for b in old_bufs:
            if b.last_w is not None:
                acc[id(b.last_w)] = b.last_w
            for r in b.readers:
                acc[id(r)] = r
        for nb in new_bufs:
            nb.readers = list(acc.values())

    def emit(self, nc, final_wait_keys=()):
        ops = self.ops
        for o in ops:
            for d in o.deps:
                d.needs_inc = True
        cnt = {}
        grp_end = {}
        for o in ops:
            if o.is_dma:
                k = ("dma", o.key)
                cnt[k] = cnt.get(k, 0) + 1
                o.cnt = cnt[k]
                grp_end[(o.key, o.grp)] = o.cnt
            elif o.needs_inc:
                cnt[o.eng] = cnt.get(o.eng, 0) + 1
                o.cnt = cnt[o.eng]
        dma_keys = sorted({o.key for o in ops if o.is_dma})
        streams = {e: [o for o in ops if o.eng == e] for e in (PE, ACT, DVE, POOL, SP)}
        self.stats = {e: len(s) for e, s in streams.items()}
        self.stats["incs"] = dict(cnt)

        import os
        SEG = int(os.environ.get('KSEG', 1500))
        with contextlib.ExitStack() as es:
            sems = {}
            for e in COMPUTE:
                nseg = (cnt.get(e, 0) + SEG - 1) // SEG + 1
                sems[e] = [es.enter_context(nc.semaphore("s_%s_%d" % (e, j))) for j in range(nseg)]
            for k in dma_keys:
                sems[("dma", k)] = es.enter_context(nc.semaphore("d_" + str(k)))
            block = es.enter_context(nc.Block())

            def run_stream(engname, eng):
                waited = {}
                for o in streams[engname]:
                    for d in o.deps:
                        if d.is_dma:
                            sk = ("dma", d.key)
                            val = 16 * grp_end[(d.key, d.grp)]
                            sem = sems[sk]
                        else:
                            seg = (d.cnt - 1) // SEG
                            sk = (d.eng, seg)
                            val = (d.cnt - 1) % SEG + 1
                            sem = sems[d.eng][seg]
                            if any(k2[0] == d.eng and k2[1] > seg for k2 in waited if isinstance(k2, tuple) and k2[0] == d.eng):
                                continue
                        if waited.get(sk, 0) >= val:
                            continue
                        waited[sk] = val
                        eng.wait_ge(sem, val)
                    ins = o.fn(eng)
                    if o.is_dma:
                        ins.then_inc(sems[("dma", o.key)], 16)
                    elif o.needs_inc:
                        ins.then_inc(sems[o.eng][(o.cnt - 1) // SEG], 1)
                if engname == SP:
                    for k in final_wait_keys:
                        eng.wait_ge(sems[("dma", k)], 16 * cnt[("dma", k)])

            @block.sync
            def _(e):
                run_stream(SP, e)

            @block.tensor
            def _(e):
                run_stream(PE, e)

            @block.scalar
            def _(e):
                run_stream(ACT, e)

            @block.vector
            def _(e):
                run_stream(DVE, e)

            @block.gpsimd
            def _(e):
                run_stream(POOL, e)


class Builder:
    def __init__(self, debug=False):
        self.debug = debug
        self.nc = bass.Bass("TRN2", target_bir_lowering=False)
        self.S = Sched()
        self.es = contextlib.ExitStack()
        self.dbg_outs = []
        self.dbg_keys = []
        self.rr = 0

    def sb(self, name, cols, dt=F32, es=None):
        return (es or self.es).enter_context(self.nc.sbuf_tensor(name, [128, cols], dt))

    def dram_in(self, name, shape, dt=F32):
        return self.nc.dram_tensor(name, list(shape), dt, kind="ExternalInput").ap()

    def dram_out(self, name, shape, dt=F32):
        return self.nc.dram_tensor(name, list(shape), dt, kind="ExternalOutput").ap()

    def dbg(self, name, ap, cols, bufs, dt=F32):
        if not self.debug:
            return
        o = self.dram_out("dbg_" + name, [128, cols], dt)
        self.dbg_keys.append("dbg_" + name)
        self.S.dma(SP, lambda e: e.dma_start(out=o, in_=ap), "dbg_" + name, self.S.new_group(), reads=bufs)

    def ew(self):
        self.rr += 1
        return ACT if (self.rr & 1) else DVE

    def build(self):
        nc, S = self.nc, self.S
        op = S.op
        x_pre = self.dram_in("x_pre", [T, D])
        x_own = self.dram_in("x_own", [T, D])
        w1_in = self.dram_in("w1_in", [D, 2 * DFF])
        w1_out = self.dram_in("w1_out", [DFF, D])
        w2_in = self.dram_in("w2_in", [D, 2 * DFF])
        w2_out = self.dram_in("w2_out", [DFF, D])
        wm_in = self.dram_in("wm_in", [D, 3592])
        wm_out = self.dram_in("wm_out", [D, D])
        cpk = self.dram_in("cpk", [128, self.CPK_COLS])
        fn_bc = self.dram_in("fn_bc", [128, D])
        out = self.dram_out("out", [T, D])
        self.out_grp = S.new_group()

        h = self.sb("h", NT * D)
        hB = [Buf("h%d" % t) for t in range(NT)]
        stage = [self.sb("stage%d" % i, 2048) for i in range(2)]
        stB = [Buf("st%d" % i) for i in range(2)]
        self.stage, self.stB, self.st_i = stage, stB, 0
        import os
        self.cast_order = os.environ.get('KCAST', 'dve').split(',')
        cp = self.sb("cp", self.CPK_COLS)
        cpB = Buf("cp")
        hhalo = self.sb("hhalo", D)
        hhB = Buf("hhalo")
        ygT = self.sb("ygT", 4 * T, BF16)
        ygB = [Buf("yg%d" % t) for t in range(NT)]
        pch = self.sb("pch", 36)
        pchB = Buf("pch")
        Sst = [self.sb("Sst%d" % i, 4 * 128) for i in range(2)]
        SsB = [[Buf("S%d_%d" % (i, hh)) for hh in range(4)] for i in range(2)]
        stat = self.sb("stat", 64)
        negA = self.sb("negA", 4)
        negAB = Buf("negA")
        psum = [self.es.enter_context(nc.psum_tensor("ps%d" % i, [128, 512], F32)) for i in range(8)]
        psB = [[Buf("ps%d" % i, excl=True)] * 4 for i in range(8)]
        self.psum, self.psB = psum, psB
        self.h, self.hB = h, hB

        C = self.CP
        g0 = S.new_group()
        S.dma(SP, lambda e: e.dma_start(out=cp[:], in_=cpk), "const", g0, writes=[cpB])
        self.cp, self.cpB = cp, cpB

        def cs(name, n=None):
            a, b = C[name]
            return cp[:, a:b]

        self.cs = cs
        ident = cs("ident")
        op(POOL, lambda e: e.memset(Sst[0][:], 0.0), writes=SsB[0])
        op(POOL, lambda e: e.memset(pch[:], 0.0), writes=[pchB])
        op(ACT, lambda e: e.activation(out=negA[:], in_=cs("alog"), func=AF.Exp), reads=[cpB], writes=[negAB])
        op(DVE, lambda e: e.tensor_scalar(out=negA[:], in0=negA[:], scalar1=-1.0, scalar2=None, op0=ALU.mult),
           reads=[negAB], writes=[negAB])
        self.negA, self.negAB = negA, negAB
        self.pch, self.pchB = pch, pchB
        self.Sst, self.SsB = Sst, SsB
        self.ygT, self.ygB = ygT, ygB
        self.spar = 0
        self.stat = stat
        self.statB = Buf("stat")

        import os
        PH = set(os.environ.get("KPH", "pf,pg,of,og,cv,f2").split(","))
        if "pf" in PH:
            self.load_x(x_pre)
            self.ffn(w1_in, w1_out, "n1", tag="p1")
        op(POOL, lambda e: e.tensor_copy(out=hhalo[:], in_=h[:, (NT - 1) * D:NT * D]), reads=[hB[NT - 1]], writes=[hhB])
        if "pg" in PH:
            self.gdn_phase(wm_in, full=False, tag="pg")
        self.load_x(x_own)
        if "of" in PH:
            self.ffn(w1_in, w1_out, "n1", tag="o1")
        self.dbg("h1", h[:, 0:D], D, [hB[0]])
        if "og" in PH:
            self.gdn_phase(wm_in, full=True, tag="og")
        self.dbg("yg", self.ygT[:, 0:T], T, self.ygB, BF16)
        if "cv" in PH:
            self.conv_phase(wm_in, wm_out, hhalo, hhB)
        self.dbg("h2", h[:, 0:D], D, [hB[0]])
        if "f2" in PH:
            self.ffn(w2_in, w2_out, "n2", tag="o2")
        self.final(out, fn_bc)
        S.emit(nc, final_wait_keys=["out0", "out1", "out2", "out3"] + self.dbg_keys)
        self.es.close()
        return nc

    def load_x(self, xd):
        S, h, hB = self.S, self.h, self.hB
        g = S.new_group()
        for t in range(NT):
            S.dma(SP, (lambda t: lambda e: e.dma_start(out=h[:, t * D:(t + 1) * D], in_=xd[t * 128:(t + 1) * 128, :]))(t),
                  "x%d" % (t % 4), g, writes=[hB[t]])

    def load_dma(self, dst_ap, dstB, src_ap, shape3):
        S = self.S
        n = len(self.stage)
        i = self.st_i % n
        self.st_i += 1
        st, sB = self.stage[i], self.stB[i]
        a, b = shape3
        sview = st[:, 0:a * b].rearrange("p (a b) -> p a b", a=a) if a > 1 else st[:, 0:b]
        sflat = st[:, 0:a * b]
        g = S.new_group()
        S.dma(SP, lambda e: e.dma_start(out=sview, in_=src_ap), "st%d" % i, g, writes=[sB])
        return (dst_ap, dstB, sflat, sB)

    def load_cast_do(self, hnd):
        S = self.S
        dst_ap, dstB, sflat, sB = hnd
        self.cast_i = getattr(self, "cast_i", 0) + 1
        eng = self.cast_order[self.cast_i % len(self.cast_order)]
        if eng == ACT:
            S.op(ACT, lambda e: e.activation(out=dst_ap, in_=sflat, func=AF.Copy), reads=[sB], writes=[dstB])
        else:
            S.op(eng, lambda e: e.tensor_copy(out=dst_ap, in_=sflat), reads=[sB], writes=[dstB])

    def load_cast(self, dst_ap, dstB, src_ap, shape3=None, key="w"):
        self.load_cast_do(self.load_dma(dst_ap, dstB, src_ap, shape3))

    def norm_transpose(self, src_ap, srcB, gain_name, dst, dst_off, dst_stride, dstB, xs, xsB, pbanks, only=None):
        S, cs = self.S, self.cs
        op = S.op
        stat = self.stat
        stB = self.statB
        op(ACT, lambda e: e.activation(out=xs[:, 0:D], in_=src_ap, func=AF.Square, accum_out=stat[:, 0:1]),
           reads=[srcB], writes=[xsB, stB])
        op(POOL, lambda e: e.tensor_scalar(out=stat[:, 1:2], in0=stat[:, 0:1], scalar1=1.0 / D, scalar2=EPS,
                                           op0=ALU.mult, op1=ALU.add), reads=[stB], writes=[stB])
        op(POOL, lambda e: e.tensor_tensor(out=stat[:, 2:3], in0=stat[:, 1:2], in1=cs("mhalf"), op=ALU.pow),
           reads=[stB, self.cpB], writes=[stB])
        op(DVE, lambda e: e.tensor_scalar(out=xs[:, 0:D], in0=src_ap, scalar1=stat[:, 2:3], scalar2=None, op0=ALU.mult),
           reads=[srcB, stB], writes=[xsB])
        if only == "front":
            return
        self._nt_back(gain_name, dst, dst_off, dst_stride, dstB, xs, xsB, pbanks)

    def _nt_back(self, gain_name, dst, dst_off, dst_stride, dstB, xs, xsB, pbanks):
        S, cs = self.S, self.cs
        op = S.op
        gain = cs(gain_name)
        ident = cs("ident")
        for half in range(2):
            pb = pbanks[half]
            pbuf = self.psum[pb]
            for q in range(4):
                c = half * 4 + q
                op(PE, (lambda c, q, pbuf: lambda e: e.transpose(pbuf[:, q * 128:(q + 1) * 128], xs[:, c * 128:(c + 1) * 128], ident))(c, q, pbuf),
                   reads=[xsB, self.cpB], writes=[self.psB[pb][q]])
            for q in range(4):
                c = half * 4 + q
                eng = self.ew()
                o_ap = dst[:, c * dst_stride + dst_off: c * dst_stride + dst_off + 128]
                i_ap = pbuf[:, q * 128:(q + 1) * 128]
                g_ap = gain[:, c:c + 1]
                if eng == ACT:
                    op(ACT, (lambda o_ap, i_ap, g_ap: lambda e: e.activation(out=o_ap, in_=i_ap, func=AF.Copy, scale=g_ap))(o_ap, i_ap, g_ap),
                       reads=[self.psB[pb][q], self.cpB], writes=[dstB])
                else:
                    op(DVE, (lambda o_ap, i_ap, g_ap: lambda e: e.tensor_scalar(out=o_ap, in0=i_ap, scalar1=g_ap, scalar2=None, op0=ALU.mult))(o_ap, i_ap, g_ap),
                       reads=[self.psB[pb][q], self.cpB], writes=[dstB])

    def ffn(self, w_in, w_out, gain_name, tag):
        import os
        nc, S = self.nc, self.S
        op = S.op
        h, hB = self.h, self.hB
        psum, psB = self.psum, self.psB
        es = contextlib.ExitStack()
        xnT = self.sb("xnT_" + tag, 8 * T, BF16, es)
        xnB = [Buf("xn%d" % t) for t in range(NT)]
        CPP = int(os.environ.get("KCPP", 4))
        nsub = CPP // 2
        wbi = [self.sb("wbi%d_%s" % (i, tag), 2 * nsub * 2048, BF16, es) for i in range(2)]
        wbo = [self.sb("wbo%d_%s" % (i, tag), CPP * 1024, BF16, es) for i in range(2)]
        assert CPP == 4
        wbiB = [[Buf() for _ in range(4)] for _ in range(2)]
        wboB = [[Buf() for _ in range(nsub)] for _ in range(2)]
        hid = [self.sb("hid%d_%s" % (i, tag), CPP * 512, BF16, es) for i in range(2)]
        hidB = [Buf(), Buf()]
        sg0 = self.sb("sg0_%s" % tag, 512, F32, es); sg = [sg0, sg0]
        sgB0 = Buf(); sgB = [sgB0, sgB0]
        xs = [self.sb("xs%d_%s" % (i, tag), D, F32, es) for i in range(2)]
        xsB = [Buf(), Buf()]
        ev = [xs[1][:, 0:512], xs[1][:, 512:1024]]
        evB = [xsB[1], xsB[1]]
        self.junkB = Buf()
        base_stage, base_stB = self.stage, self.stB
        nextra = int(os.environ.get("KXST", 0))
        xst = [self.sb("xst%d_%s" % (i, tag), 2048, F32, es) for i in range(nextra)]
        xstB = [Buf() for _ in range(nextra)]
        self.stage, self.stB = base_stage + xst, base_stB + xstB
        new_bufs = xnB + wbiB[0] + wbiB[1] + wboB[0] + wboB[1] + hidB + sgB + xsB + [self.junkB] + evB + xstB
        S.alias(new_bufs, getattr(self, "phase_bufs", []))
        self.phase_bufs = new_bufs

        w_in_v = w_in.rearrange("(c p) n -> p c n", p=128)
        w_out_v = w_out.rearrange("(c p) n -> p c n", p=128)
        pieces = []
        c0 = 0
        while c0 < DFF // 128:
            n = min(CPP, DFF // 128 - c0)
            pieces.append((c0, n))
            c0 += n
        NP = len(pieces)

        def piece_specs(p):
            s = p % 2
            ch0, n = pieces[p]
            W = n * 128
            col = ch0 * 128
            specs = []
            for which in range(2):
                for csub in range(2):
                    base = which * 4096 + csub * 2048
                    specs.append((wbi[s][:, base:base + 4 * W], wbiB[s][which * 2 + csub],
                                  w_in_v[:, csub * 4:csub * 4 + 4, which * DFF + col:which * DFF + col + W], (4, W)))
            for sub in range(n // 2):
                specs.append((wbo[s][:, sub * 2048:(sub + 1) * 2048], wboB[s][sub], w_out_v[:, ch0 + 2 * sub:ch0 + 2 * sub + 2, :], (2, 1024)))
            return specs

        def load_piece(p):
            for sp_ in piece_specs(p):
                self.load_cast(*sp_)

        load_piece(0)
        PBK = [(0, 1), (2, 3), (4, 5), (6, 7)]
        self.norm_transpose(h[:, 0:D], hB[0], gain_name, xnT, 0, T, xnB[0], xs[0], xsB[0], PBK[0], only="front")
        for t in range(NT):
            if t + 1 < NT:
                self.norm_transpose(h[:, (t + 1) * D:(t + 2) * D], hB[t + 1], gain_name, xnT, (t + 1) * 128, T, xnB[t + 1],
                                    xs[(t + 1) % 2], xsB[(t + 1) % 2], PBK[(t + 1) % 4], only="front")
            self._nt_back(gain_name, xnT, t * 128, T, xnB[t], xs[t % 2], xsB[t % 2], PBK[t % 4])
        blocks = [(p, tb) for p in range(NP) for tb in range(NB)]
        st = {"gi": 0, "oi": 0}

        def stage1(idx):
            p, tb = blocks[idx]
            s = p % 2
            hs = idx % 2
            n = pieces[p][1]
            for j in range(n):
                gi = st["gi"]
                for which in range(2):
                    pb = (0 if which == 0 else 2) + (gi % 2)
                    W = n * 128
                    for c in range(8):
                        off = which * 4096 + (c // 4) * 2048 + (c % 4) * W + j * 128
                        lhsT = wbi[s][:, off:off + 128]
                        rhs = xnT[:, c * T + tb * 512: c * T + (tb + 1) * 512]
                        op(PE, (lambda pb, lhsT, rhs, c: lambda e: e.matmul(psum[pb][:, :], lhsT=lhsT, rhs=rhs, start=(c == 0), stop=(c == 7)))(pb, lhsT, rhs, c),
                           reads=[wbiB[s][which * 2 + c // 4]] + xnB[tb * 4:(tb + 1) * 4], writes=psB[pb])
                pg, pu = (gi % 2), 2 + (gi % 2)
                k = gi % 2
                op(ACT, (lambda pg, k: lambda e: e.activation(out=sg[k][:, :], in_=psum[pg][:, :], func=AF.Silu))(pg, k),
                   reads=psB[pg], writes=[sgB[k]])
                op(DVE, (lambda pu, k, hs, j: lambda e: e.tensor_tensor(out=hid[hs][:, j * 512:(j + 1) * 512], in0=sg[k][:, :], in1=psum[pu][:, :], op=ALU.mult))(pu, k, hs, j),
                   reads=[sgB[k]] + psB[pu], writes=[hidB[hs]])
                st["gi"] += 1

        def stage2(idx):
            p, tb = blocks[idx]
            s = p % 2
            hs = idx % 2
            n = pieces[p][1]
            for tt in range(4):
                t = tb * 4 + tt
                for hh in range(2):
                    oi = st["oi"]
                    pb = 4 + (oi % 2)
                    for j in range(n):
                        lhsT = hid[hs][:, j * 512 + tt * 128: j * 512 + (tt + 1) * 128]
                        rhs = wbo[s][:, j * 1024 + hh * 512: j * 1024 + (hh + 1) * 512]
                        op(PE, (lambda pb, lhsT, rhs, j: lambda e: e.matmul(psum[pb][:, :], lhsT=lhsT, rhs=rhs, start=(j == 0), stop=(j == n - 1)))(pb, lhsT, rhs, j),
                           reads=[hidB[hs], wboB[s][j // 2]], writes=psB[pb])
                    hap = h[:, t * D + hh * 512: t * D + (hh + 1) * 512]
                    if oi % 2 == 0:
                        op(DVE, (lambda pb, hap: lambda e: e.scalar_tensor_tensor(out=hap, in0=psum[pb][:, :], scalar=0.5, in1=hap, op0=ALU.mult, op1=ALU.add))(pb, hap),
                           reads=psB[pb] + [hB[t]], writes=[hB[t]])
                    else:
                        k = (oi // 2) % 2
                        op(ACT, (lambda pb, k: lambda e: e.activation(out=ev[k][:, :], in_=psum[pb][:, :], func=AF.Copy, scale=0.5))(pb, k),
                           reads=psB[pb], writes=[evB[k]])
                        op(POOL, (lambda k, hap: lambda e: e.tensor_tensor(out=hap, in0=hap, in1=ev[k][:, :], op=ALU.add))(k, hap),
                           reads=[evB[k], hB[t]], writes=[hB[t]])
                    st["oi"] += 1

        pend_specs = []
        inflight = []
        for idx in range(len(blocks)):
            stage1(idx)
            if idx > 0:
                stage2(idx - 1)
            p, tb = blocks[idx]
            if tb == 0 and p + 1 < NP:
                pend_specs = piece_specs(p + 1)
            for hnd in inflight:
                self.load_cast_do(hnd)
            inflight = []
            if tb == NB - 1:
                for sp_ in pend_specs:
                    self.load_cast(*sp_)
                pend_specs = []
            else:
                for sp_ in pend_specs[:2]:
                    inflight.append(self.load_dma(*sp_))
                pend_specs = pend_specs[2:]
        stage2(len(blocks) - 1)
        self.stage, self.stB = base_stage, base_stB
        es.close()

    def gdn_phase(self, wm_in, full, tag):
        import os
        nc, S = self.nc, self.S
        op = S.op
        cs, cpB = self.cs, self.cpB
        h, hB = self.h, self.hB
        psum, psB = self.psum, self.psB
        es = contextlib.ExitStack()
        pc = self.sb("pc_" + tag, 12 * 131, F32, es); pcB = Buf()
        wg = self.sb("wg_" + tag, 8 * NG, BF16, es)
        wgB = [Buf() for _ in range(9)]
        xs = self.sb("xs_" + tag, D, F32, es); xsB = Buf()
        xn = self.sb("xn_" + tag, 8 * 128, BF16, es); xnB = Buf()
        qkv0 = self.sb("qkv_" + tag, 12 * 128, F32, es); qkvB0 = Buf()
        qkv_b = [qkv0, self.stage[0][:, 0:1536]]; qkvB_b = [qkvB0, self.stB[0]]
        cacc = self.sb("cacc_" + tag, 12 * 128, F32, es); caccB = Buf()
        etmp = self.sb("etmp_" + tag, 12 * 128, F32, es); etmpB = Buf()
        rs, rsB = etmp, etmpB
        zT0 = self.sb("zT_" + tag, 4 * 128, F32, es); zB0 = Buf()
        zT_b = [zT0, self.stage[1][:, 0:512]]; zB_b = [zB0, self.stB[1]]
        sm_b = [self.sb("sm%d_%s" % (i, tag), 64, F32, es) for i in range(2)]; smB_b = [Buf(), Buf()]
        Dg = self.sb("Dg_" + tag, 512, F32, es); DgB = Buf()
        glb_b = [self.sb("glb%d_%s" % (i, tag), 8, F32, es) for i in range(2)]; glB_b = [Buf(), Buf()]
        def mk(n, cols=128, dt=F32):
            return self.sb(n + "_" + tag, cols, dt, es), Buf(n)
        HB = []
        for hh in range(4):
            d_ = {}
            for n in ["kbg", "kdec", "vbeta", "tm1", "E1", "Lm", "AT", "X0", "X1", "Y0", "Y1", "P0", "P1", "wT", "um", "vnew"]:
                d_[n] = mk("%s%d" % (n, hh))
            HB.append(d_)
        new_bufs = wgB + [pcB, xsB, xnB, qkvB0, caccB, etmpB, zB0, DgB] + smB_b + glB_b + [b for d_ in HB for _, b in d_.values()]
        S.alias(new_bufs, getattr(self, "phase_bufs", []))
        self.phase_bufs = new_bufs

        wm_v = wm_in.rearrange("(c p) n -> p c n", p=128)
        col = 0
        while col < NG:
            w = min(256, NG - col)
            base = (col // 256) * 2048
            self.load_cast(wg[:, base:base + 8 * w], wgB[col // 256], wm_v[:, :, GW0 + col:GW0 + col + w], (8, w))
            col += w

        ident, ones, triU = cs("ident"), cs("ones"), cs("triU")
        maskL, maskU = cs("maskL"), cs("maskU")
        cwg = cs("cwg")
        pcv0 = pc[:, :].rearrange("p (a b) -> p a b", a=12)
        pchv = self.pch[:, :].rearrange("p (a b) -> p a b", a=12)
        op(DVE, lambda e: e.tensor_copy(out=pcv0[:, :, 0:3], in_=pchv), reads=[self.pchB], writes=[pcB])
        Sst, SsB = self.Sst, self.SsB
        nq = 16 if full else 12

        def pre_ops(t):
            pp = t % 2
            qkv, qkvB = qkv_b[pp], qkvB_b[pp]
            zT, zB = zT_b[pp], zB_b[pp]
            sm, smB = sm_b[pp], smB_b[pp]
            glb, glB = glb_b[pp], glB_b[pp]
            S.capture = []
            self.norm_transpose(h[:, t * D:(t + 1) * D], hB[t], "nm", xn, 0, 128, xnB, xs, xsB, (0, 1))
            for grp in range(nq // 4):
                if (not full) and grp == 0 and t != NT - 1:
                    continue
                pb = 2
                for q in range(4):
                    j = grp * 4 + q
                    for c in range(8):
                        lhsT = wg[:, (j // 2) * 2048 + c * 256 + (j % 2) * 128: (j // 2) * 2048 + c * 256 + (j % 2) * 128 + 128]
                        rhs = xn[:, c * 128:(c + 1) * 128]
                        op(PE, (lambda pb, q, lhsT, rhs, c: lambda e: e.matmul(psum[pb][:, q * 128:(q + 1) * 128], lhsT=lhsT, rhs=rhs, start=(c == 0), stop=(c == 7)))(pb, q, lhsT, rhs, c),
                           reads=[wgB[j // 2], xnB], writes=[psB[pb][q]])
                if grp < 3:
                    dstv = pc[:, grp * 4 * 131:(grp + 1) * 4 * 131].rearrange("p (a b) -> p a b", a=4)[:, :, 3:131]
                    srcv = psum[pb][:, :].rearrange("p (a b) -> p a b", a=4)
                    eng = self.ew()
                    if eng == ACT:
                        op(ACT, (lambda dstv, srcv: lambda e: e.activation(out=dstv, in_=srcv, func=AF.Copy))(dstv, srcv), reads=psB[pb], writes=[pcB])
                    else:
                        op(DVE, (lambda dstv, srcv: lambda e: e.tensor_copy(out=dstv, in_=srcv))(dstv, srcv), reads=psB[pb], writes=[pcB])
                else:
                    op(ACT, (lambda pb: lambda e: e.activation(out=zT[:, :], in_=psum[pb][:, :], func=AF.Copy))(pb), reads=psB[pb], writes=[zB])
            i3 = len(S.capture)
            for c in range(8):
                lhsT = xn[:, c * 128:(c + 1) * 128]
                rhs = wg[:, 8 * 2048 + c * 8: 8 * 2048 + c * 8 + 8]
                op(PE, (lambda lhsT, rhs, c: lambda e: e.matmul(psum[2][:, 0:8], lhsT=lhsT, rhs=rhs, start=(c == 0), stop=(c == 7)))(lhsT, rhs, c),
                   reads=[wgB[8], xnB], writes=[psB[2][0]])
            op(DVE, lambda e: e.tensor_copy(out=sm[:, 0:8], in_=psum[2][:, 0:8]), reads=[psB[2][0]], writes=[smB])
            op(ACT, lambda e: e.activation(out=sm[:, 8:12], in_=sm[:, 0:4], func=AF.Exp, scale=-1.0), reads=[smB], writes=[smB])
            op(DVE, lambda e: e.tensor_scalar(out=sm[:, 8:12], in0=sm[:, 8:12], scalar1=1.0, scalar2=None, op0=ALU.add), reads=[smB], writes=[smB])
            op(DVE, lambda e: e.reciprocal(out=sm[:, 8:12], in_=sm[:, 8:12]), reads=[smB], writes=[smB])
            op(DVE, lambda e: e.tensor_tensor(out=sm[:, 12:16], in0=sm[:, 4:8], in1=cs("dtb"), op=ALU.add), reads=[smB, cpB], writes=[smB])
            op(ACT, lambda e: e.activation(out=sm[:, 12:16], in_=sm[:, 12:16], func=AF.Exp), reads=[smB], writes=[smB])
            op(ACT, lambda e: e.activation(out=sm[:, 12:16], in_=sm[:, 12:16], func=AF.Ln, bias=1.0), reads=[smB], writes=[smB])
            op(DVE, lambda e: e.tensor_tensor(out=sm[:, 12:16], in0=sm[:, 12:16], in1=self.negA[:, :], op=ALU.mult), reads=[smB, self.negAB], writes=[smB])
            op(PE, lambda e: e.matmul(psum[2][:, 8:12], lhsT=triU, rhs=sm[:, 12:16], start=True, stop=True), reads=[smB, cpB], writes=[psB[2][0]])
            op(DVE, lambda e: e.tensor_copy(out=sm[:, 16:20], in_=psum[2][:, 8:12]), reads=[psB[2][0]], writes=[smB])
            op(ACT, lambda e: e.activation(out=sm[:, 20:24], in_=sm[:, 16:20], func=AF.Exp), reads=[smB], writes=[smB])
            op(DVE, lambda e: e.tensor_scalar(out=sm[:, 24:28], in0=sm[:, 16:20], scalar1=-1.0, scalar2=None, op0=ALU.mult), reads=[smB], writes=[smB])
            op(DVE, lambda e: e.tensor_tensor(out=sm[:, 28:32], in0=sm[:, 8:12], in1=sm[:, 20:24], op=ALU.mult), reads=[smB], writes=[smB])
            for hh in range(4):
                op(DVE, (lambda hh: lambda e: e.tensor_scalar(out=Dg[:, hh * 128:(hh + 1) * 128], in0=ident, scalar1=sm[:, 16 + hh:17 + hh], scalar2=None, op0=ALU.mult))(hh),
                   reads=[smB, cpB], writes=[DgB])
            op(PE, lambda e: e.matmul(psum[3][:, :], lhsT=ones, rhs=Dg[:, 0:512], start=True, stop=True), reads=[DgB, cpB], writes=psB[3])
            Gv = psum[3][:, :].rearrange("p (a b) -> p a b", a=4)
            op(ACT, lambda e: e.activation(out=glb[:, 0:8].rearrange("p (a b) -> p a b", a=4), in_=Gv[:, :, 63:128:64], func=AF.Exp), reads=psB[3], writes=[glB])
            op(DVE, lambda e: e.tensor_tensor(out=sm[0:64, 32:36], in0=Gv[0:64, :, 63], in1=sm[0:64, 16:20], op=ALU.subtract), reads=psB[3] + [smB], writes=[smB])
            op(DVE, lambda e: e.tensor_tensor(out=sm[64:128, 32:36], in0=Gv[64:128, :, 127], in1=sm[64:128, 16:20], op=ALU.subtract), reads=psB[3] + [smB], writes=[smB])
            op(ACT, lambda e: e.activation(out=sm[:, 32:36], in_=sm[:, 32:36], func=AF.Exp), reads=[smB], writes=[smB])
            i4 = len(S.capture)
            pcv = pc[:, :].rearrange("p (a b) -> p a b", a=12)
            for j in range(0 if full else 4, 12):
                op(DVE, (lambda j: lambda e: e.tensor_scalar(out=cacc[:, j * 128:(j + 1) * 128], in0=pc[:, j * 131:j * 131 + 128], scalar1=cwg[:, j * 4:j * 4 + 1], scalar2=None, op0=ALU.mult))(j),
                   reads=[pcB, cpB], writes=[caccB])
                for k in range(1, 4):
                    op(DVE, (lambda j, k: lambda e: e.scalar_tensor_tensor(out=cacc[:, j * 128:(j + 1) * 128], in0=pc[:, j * 131 + k:j * 131 + k + 128], scalar=cwg[:, j * 4 + k:j * 4 + k + 1], in1=cacc[:, j * 128:(j + 1) * 128], op0=ALU.mult, op1=ALU.add))(j, k),
                       reads=[pcB, cpB, caccB], writes=[caccB])
            op(DVE, lambda e: e.tensor_copy(out=pcv[:, :, 0:3], in_=pcv[:, :, 128:131]), reads=[pcB], writes=[pcB])
            i5 = len(S.capture)
            c_lo = 0 if full else 512
            op(ACT, lambda e: e.activation(out=etmp[:, c_lo:1536], in_=cacc[:, c_lo:1536], func=AF.Exp, scale=-1.0), reads=[caccB], writes=[etmpB])
            op(ACT, lambda e: e.activation(out=etmp[:, c_lo:1536], in_=etmp[:, c_lo:1536], func=AF.Ln, bias=1.0), reads=[etmpB], writes=[etmpB])
            op(ACT, lambda e: e.activation(out=etmp[:, c_lo:1536], in_=etmp[:, c_lo:1536], func=AF.Exp, scale=-1.0), reads=[etmpB], writes=[etmpB])
            op(DVE, lambda e: e.tensor_tensor(out=qkv[:, c_lo:1536], in0=cacc[:, c_lo:1536], in1=etmp[:, c_lo:1536], op=ALU.mult), reads=[etmpB, caccB], writes=[qkvB])
            op(ACT, lambda e: e.activation(out=etmp[:, c_lo:1024], in_=qkv[:, c_lo:1024], func=AF.Square), reads=[qkvB, etmpB], writes=[etmpB])
            for half in range(0 if full else 1, 2):
                op(PE, (lambda half: lambda e: e.matmul(psum[half][:, :], lhsT=ones, rhs=etmp[:, half * 512:(half + 1) * 512], start=True, stop=True))(half),
                   reads=[etmpB, cpB], writes=psB[half])
                op(ACT, (lambda half: lambda e: e.activation(out=rs[:, half * 512:(half + 1) * 512], in_=psum[half][:, :], func=AF.Ln, bias=cs("eps")))(half),
                   reads=psB[half] + [cpB], writes=[rsB])
            op(ACT, lambda e: e.activation(out=rs[:, c_lo:1024], in_=rs[:, c_lo:1024], func=AF.Exp, scale=-0.5), reads=[rsB], writes=[rsB])
            if full:
                op(DVE, lambda e: e.scalar_tensor_tensor(out=qkv[:, 0:512], in0=qkv[:, 0:512], scalar=128.0 ** -0.5, in1=rs[:, 0:512], op0=ALU.mult, op1=ALU.mult), reads=[qkvB, rsB], writes=[qkvB])
            op(DVE, lambda e: e.tensor_tensor(out=qkv[:, 512:1024], in0=qkv[:, 512:1024], in1=rs[:, 512:1024], op=ALU.mult), reads=[qkvB, rsB], writes=[qkvB])
            if full:
                op(ACT, lambda e: e.activation(out=etmp[:, 0:512], in_=zT[:, :], func=AF.Exp, scale=-1.0), reads=[zB, etmpB], writes=[etmpB])
                op(ACT, lambda e: e.activation(out=etmp[:, 0:512], in_=etmp[:, 0:512], func=AF.Ln, bias=1.0), reads=[etmpB], writes=[etmpB])
                op(ACT, lambda e: e.activation(out=etmp[:, 0:512], in_=etmp[:, 0:512], func=AF.Exp, scale=-1.0), reads=[etmpB], writes=[etmpB])
                op(DVE, lambda e: e.tensor_tensor(out=zT[:, :], in0=zT[:, :], in1=etmp[:, 0:512], op=ALU.mult), reads=[etmpB, zB], writes=[zB])
            cap_ = S.capture
            S.capture = None
            s3, s4 = cap_[i3:i4], cap_[i4:i5]
            mer = []
            i_, j_ = 0, 0
            while i_ < len(s3) or j_ < len(s4):
                if j_ < len(s4) and (i_ >= len(s3) or j_ * max(len(s3), 1) <= i_ * len(s4)):
                    mer.append(s4[j_]); j_ += 1
                else:
                    mer.append(s3[i_]); i_ += 1
            return cap_[:i3] + mer + cap_[i5:]

        def chain(hh, t):
            pp = t % 2
            qkv, qkvB = qkv_b[pp], qkvB_b[pp]
            zT, zB = zT_b[pp], zB_b[pp]
            sm, smB = sm_b[pp], smB_b[pp]
            glb, glB = glb_b[pp], glB_b[pp]
            B_ = HB[hh]
            kbg, kbgB = B_["kbg"]; kdec, kdecB = B_["kdec"]; vbeta, vbetaB = B_["vbeta"]
            tm1, tm1B = B_["tm1"]; E1, E1B = B_["E1"]; Lm, LmB = B_["Lm"]; AT, ATB = B_["AT"]
            X = [B_["X0"], B_["X1"]]; Y = [B_["Y0"], B_["Y1"]]; Pm = [B_["P0"], B_["P1"]]
            wT, wTB = B_["wT"]; um, umB = B_["um"]; vnew, vnewB = B_["vnew"]
            o1s, o1sB = tm1, tm1B
            om, omB = E1, E1B
            on, onB = Lm, LmB
            qT = qkv[:, hh * 128:(hh + 1) * 128]
            kT = qkv[:, 512 + hh * 128:512 + (hh + 1) * 128]
            vT = qkv[:, 1024 + hh * 128:1024 + (hh + 1) * 128]
            PH = psum[4 + hh]
            PB = psB[4 + hh][0]
            Q0, Q1, Q2, Q3 = PH[:, 0:128], PH[:, 128:256], PH[:, 256:384], PH[:, 384:512]
            Gh = psum[3][:, hh * 128:(hh + 1) * 128]
            GhB = psB[3]
            op(PE, lambda e: e.transpose(Q0, kT, ident), reads=[qkvB, cpB], writes=[PB])
            op(PE, lambda e: e.transpose(Q1, vT, ident), reads=[qkvB, cpB], writes=[PB])
            op(PE, lambda e: e.matmul(Q2, lhsT=kT, rhs=kT, start=True, stop=True), reads=[qkvB], writes=[PB])
            if full:
                op(PE, lambda e: e.matmul(Q3, lhsT=kT, rhs=qT, start=True, stop=True), reads=[qkvB], writes=[PB])
            yield
            op(DVE, lambda e: e.tensor_tensor(out=tm1[:, :], in0=maskL, in1=Gh, op=ALU.subtract), reads=GhB + [cpB], writes=[tm1B])
            yield
            op(ACT, lambda e: e.activation(out=E1[:, :], in_=tm1[:, :], func=AF.Exp, bias=sm[:, 16 + hh:17 + hh]), reads=[tm1B, smB], writes=[E1B])
            yield
            op(ACT, lambda e: e.activation(out=kbg[:, :], in_=Q0, func=AF.Copy, scale=sm[:, 28 + hh:29 + hh]), reads=[PB, smB], writes=[kbgB])
            op(ACT, lambda e: e.activation(out=vbeta[:, :], in_=Q1, func=AF.Copy, scale=sm[:, 8 + hh:9 + hh]), reads=[PB, smB], writes=[vbetaB])
            yield
            op(DVE, lambda e: e.tensor_scalar(out=kdec[:, :], in0=Q0, scalar1=sm[:, 32 + hh:33 + hh], scalar2=None, op0=ALU.mult), reads=[PB, smB], writes=[kdecB])
            op(DVE, lambda e: e.scalar_tensor_tensor(out=Lm[:, :], in0=Q2, scalar=sm[:, 8 + hh:9 + hh], in1=E1[:, :], op0=ALU.mult, op1=ALU.mult), reads=[PB, smB, E1B], writes=[LmB])
            yield
            if full:
                op(DVE, lambda e: e.tensor_tensor(out=tm1[:, :], in0=maskU, in1=Gh, op=ALU.add), reads=GhB + [cpB, tm1B], writes=[tm1B])
                yield
                op(ACT, lambda e: e.activation(out=E1[:, :], in_=tm1[:, :], func=AF.Exp, bias=sm[:, 24 + hh:25 + hh]), reads=[tm1B, smB, E1B], writes=[E1B])
                yield
                op(DVE, lambda e: e.tensor_tensor(out=AT[:, :], in0=Q3, in1=E1[:, :], op=ALU.mult), reads=[PB, E1B], writes=[ATB])
                yield
            op(PE, lambda e: e.transpose(Q0, Lm[:, :], ident), reads=[LmB, cpB], writes=[PB])
            yield
            X0, X0B = X[0]
            P0, P0B = Pm[0]
            op(ACT, lambda e: e.activation(out=X0[:, :], in_=Q0, func=AF.Copy), reads=[PB], writes=[X0B])
            op(DVE, lambda e: e.tensor_tensor(out=P0[:, :], in0=ident, in1=Q0, op=ALU.subtract), reads=[PB, cpB], writes=[P0B])
            yield
            Xc, XcB = X0, X0B
            Yc, YcB = Lm, LmB
            Pc, PcB = P0, P0B
            for k in range(1, 6):
                Yn, YnB = Y[k % 2]
                Xn, XnB = X[k % 2]
                Pn, PnB = Pm[k % 2]
                op(PE, (lambda Xc, Yc: lambda e: e.matmul(Q1, lhsT=Xc[:, :], rhs=Yc[:, :], start=True, stop=True))(Xc, Yc), reads=[XcB, YcB], writes=[PB])
                if k < 5:
                    op(PE, (lambda Xc, Yc: lambda e: e.matmul(Q2, lhsT=Yc[:, :], rhs=Xc[:, :], start=True, stop=True))(Xc, Yc), reads=[XcB, YcB], writes=[PB])
                yield
                op(ACT, (lambda Yn: lambda e: e.activation(out=Yn[:, :], in_=Q1, func=AF.Copy))(Yn), reads=[PB], writes=[YnB])
                if k < 5:
                    op(ACT, (lambda Xn: lambda e: e.activation(out=Xn[:, :], in_=Q2, func=AF.Copy))(Xn), reads=[PB], writes=[XnB])
                yield
                op(PE, (lambda Yn, Pc: lambda e: e.matmul(Q3, lhsT=Yn[:, :], rhs=Pc[:, :], start=True, stop=True))(Yn, Pc), reads=[YnB, PcB], writes=[PB])
                yield
                op(DVE, (lambda Pn, Pc: lambda e: e.tensor_tensor(out=Pn[:, :], in0=Pc[:, :], in1=Q3, op=ALU.add))(Pn, Pc), reads=[PcB, PB], writes=[PnB])
                yield
                Xc, XcB, Yc, YcB, Pc, PcB = Xn, XnB, Yn, YnB, Pn, PnB
            op(PE, (lambda Pc: lambda e: e.matmul(Q0, lhsT=kbg[:, :], rhs=Pc[:, :], start=True, stop=True))(Pc), reads=[kbgB, PcB], writes=[PB])
            op(PE, (lambda Pc: lambda e: e.matmul(Q1, lhsT=Pc[:, :], rhs=vbeta[:, :], start=True, stop=True))(Pc), reads=[vbetaB, PcB], writes=[PB])
            yield
            op(ACT, lambda e: e.activation(out=wT[:, :], in_=Q0, func=AF.Copy), reads=[PB], writes=[wTB])
            op(ACT, lambda e: e.activation(out=um[:, :], in_=Q1, func=AF.Copy), reads=[PB], writes=[umB])
            yield
            for half in range(2):
                r0, r1 = half * 64, half * 64 + 64
                sp_ = self.spar_h[hh]
                Scur = Sst[sp_][:, hh * 128:(hh + 1) * 128]; ScurB = SsB[sp_][hh]
                Snew = Sst[1 - sp_][:, hh * 128:(hh + 1) * 128]; SnewB = SsB[1 - sp_][hh]
                self.spar_h[hh] = 1 - sp_
                op(PE, (lambda r0, r1, Scur: lambda e: e.matmul(PH[r0:r1, 256:384], lhsT=wT[:, r0:r1], rhs=Scur, start=True, stop=True))(r0, r1, Scur), reads=[wTB, ScurB], writes=[PB])
                yield
                op(DVE, (lambda r0, r1: lambda e: e.tensor_tensor(out=vnew[r0:r1, :], in0=um[r0:r1, :], in1=PH[r0:r1, 256:384], op=ALU.subtract))(r0, r1), reads=[umB, PB], writes=[vnewB])
                yield
                if full:
                    op(PE, (lambda r0, r1, Scur: lambda e: e.matmul(PH[r0:r1, 0:128], lhsT=qT[:, r0:r1], rhs=Scur, start=True, stop=True))(r0, r1, Scur), reads=[qkvB, ScurB], writes=[PB])
                    op(PE, (lambda r0, r1: lambda e: e.matmul(PH[r0:r1, 128:256], lhsT=AT[r0:r1, r0:r1], rhs=vnew[r0:r1, :], start=True, stop=True))(r0, r1), reads=[ATB, vnewB], writes=[PB])
                op(PE, (lambda r0, r1: lambda e: e.matmul(Q3, lhsT=kdec[r0:r1, :], rhs=vnew[r0:r1, :], start=True, stop=True))(r0, r1), reads=[kdecB, vnewB], writes=[PB])
                yield
                op(DVE, (lambda Snew, Scur, half: lambda e: e.scalar_tensor_tensor(out=Snew, in0=Scur, scalar=glb[:, hh * 2 + half:hh * 2 + half + 1], in1=Q3, op0=ALU.mult, op1=ALU.add))(Snew, Scur, half),
                   reads=[ScurB, glB, PB], writes=[SnewB])
                yield
            if full:
                c0 = 40 + hh * 3
                op(ACT, lambda e: e.activation(out=o1s[:, :], in_=Q0, func=AF.Copy, scale=sm[:, 20 + hh:21 + hh]), reads=[PB, smB], writes=[o1sB])
                yield
                op(DVE, lambda e: e.tensor_tensor(out=om[:, :], in0=o1s[:, :], in1=Q1, op=ALU.add), reads=[o1sB, PB], writes=[omB])
                yield
                op(ACT, lambda e: e.activation(out=on[:, :], in_=om[:, :], func=AF.Square, accum_out=sm[:, c0:c0 + 1]), reads=[omB], writes=[onB, smB])
                yield
                op(POOL, lambda e: e.tensor_scalar(out=sm[:, c0 + 1:c0 + 2], in0=sm[:, c0:c0 + 1], scalar1=1.0 / 128, scalar2=EPS, op0=ALU.mult, op1=ALU.add), reads=[smB], writes=[smB])
                op(POOL, lambda e: e.tensor_tensor(out=sm[:, c0 + 2:c0 + 3], in0=sm[:, c0 + 1:c0 + 2], in1=cs("mhalf"), op=ALU.pow), reads=[smB, cpB], writes=[smB])
                yield
                op(DVE, lambda e: e.scalar_tensor_tensor(out=on[:, :], in0=om[:, :], scalar=sm[:, c0 + 2:c0 + 3], in1=cs("gon"), op0=ALU.mult, op1=ALU.mult), reads=[omB, smB, cpB], writes=[onB])
                yield
                op(PE, lambda e: e.transpose(Q2, on[:, :], ident), reads=[onB, cpB], writes=[PB])
                yield
                yg_ap = self.ygT[:, hh * T + t * 128: hh * T + (t + 1) * 128]
                op(DVE, lambda e: e.tensor_tensor(out=yg_ap, in0=Q2, in1=zT[:, hh * 128:(hh + 1) * 128], op=ALU.mult), reads=[PB, zB], writes=[self.ygB[t]])
                yield

        for it in pre_ops(0):
            S.replay(it)
        for t in range(NT):
            pend = pre_ops(t + 1) if t + 1 < NT else []
            per = (len(pend) + 39) // 40
            S0 = int(os.environ.get("KSTAG", 3))
            rounds_t = (47 if full else 37) + 3 * S0 - int(os.environ.get("KEARLY", 3))
            per = (len(pend) + rounds_t - 1) // rounds_t
            alive = [(hh, chain(hh, t)) for hh in range(4)]
            pi = 0
            rnd = 0
            while alive or pi < len(pend):
                nxt = []
                for hh, g_ in alive:
                    if rnd < hh * S0:
                        nxt.append((hh, g_))
                        continue
                    try:
                        next(g_)
                        nxt.append((hh, g_))
                    except StopIteration:
                        pass
                alive = nxt
                for it in pend[pi:pi + per]:
                    S.replay(it)
                pi += per
                rnd += 1
        op(DVE, lambda e: e.tensor_copy(out=pchv, in_=pcv0[:, :, 0:3]), reads=[pcB], writes=[self.pchB])
        es.close()

    def conv_phase(self, wm_in, wm_out, hhalo, hhB):
        import os
        nc, S = self.nc, self.S
        op = S.op
        cs, cpB = self.cs, self.cpB
        h, hB = self.h, self.hB
        psum, psB = self.psum, self.psB
        es = contextlib.ExitStack()
        tag = "cv"
        wc = self.sb("wc", 8 * 1536, BF16, es); wcB = [Buf() for _ in range(6)]
        wo = self.sb("wo", 8 * 1024, BF16, es); woB = [Buf() for _ in range(4)]
        xs = self.sb("xs_cv", D, F32, es); xsB = Buf()
        self.junk = self.sb("junk_cv", D, BF16, es); self.junkB = Buf()
        xn = self.sb("xn_cv", 8 * 128, BF16, es); xnB = Buf()
        mpc = self.sb("mpc", 4 * 130, F32, es); mpcB = Buf()
        mpc2 = self.sb("mpc2", 4 * 130, F32, es); mpc2B = Buf()
        cbs0 = self.sb("cbs0", 512, F32, es); cbs0B = Buf()
        cbs1 = self.sb("cbs1", 512, F32, es); cbs1B = Buf()
        cct = self.sb("cct", 512, F32, es); cctB = Buf()
        cacc = self.sb("cacc_cv", 512, F32, es); caccB = Buf()
        yv = self.sb("yv", 512, F32, es); yvB = Buf()
        sq = self.sb("sq_cv", 512, F32, es); sqB = Buf()
        rs = self.sb("rs_cv", 512, F32, es); rsB = Buf()
        ycT = self.sb("ycT", 512, BF16, es); ycB = Buf()
        motmp = [self.sb("motmp%d" % i, 512, F32, es) for i in range(2)]; motB = [Buf(), Buf()]
        new_bufs = wcB + woB + [mpc2B, cbs0B, cbs1B, xsB, self.junkB, xnB, mpcB, cctB, caccB, yvB, sqB, rsB, ycB] + motB
        S.alias(new_bufs, getattr(self, "phase_bufs", []))
        self.phase_bufs = new_bufs
        wm_v = wm_in.rearrange("(c p) n -> p c n", p=128)
        for col in range(0, 1536, 256):
            base = (col // 256) * 2048
            self.load_cast(wc[:, base:base + 2048], wcB[col // 256], wm_v[:, :, col:col + 256], (8, 256))
        wo_v = wm_out.rearrange("(c p) n -> p c n", p=128)
        for i in range(4):
            self.load_cast(wo[:, i * 2048:(i + 1) * 2048], woB[i], wo_v[:, 2 * i:2 * i + 2, :], (2, 1024))
        op(POOL, lambda e: e.memset(mpc[:, :], 0.0), writes=[mpcB])
        op(POOL, lambda e: e.memset(mpc2[:, :], 0.0), writes=[mpc2B])
        ident, blk64 = cs("ident"), cs("blk64")
        csw, cgain = cs("csw"), cs("cgain")
        ygT, ygB = self.ygT, self.ygB
        mpc_b = [mpc, mpc2]; mpcB_b = [mpcB, mpc2B]
        cbs_b = [cbs0, cbs1]; cbsB_b = [cbs0B, cbs1B]

        def capA(t):
            p = t % 2
            mp, mpB = mpc_b[p], mpcB_b[p]
            mo, moB = mpc_b[1 - p], mpcB_b[1 - p]
            mpv = mp[:, :].rearrange("p (a b) -> p a b", a=4)
            mov = mo[:, :].rearrange("p (a b) -> p a b", a=4)
            S.capture = []
            if t < 0:
                src, srcB = hhalo[:, :], hhB
            else:
                src, srcB = h[:, t * D:(t + 1) * D], hB[t]
            self.norm_transpose(src, srcB, "nm", xn, 0, 128, xnB, xs, xsB, (0, 1))
            for grp in range(3):
                if t < 0 and grp == 0:
                    continue
                pb = 2 + grp
                for q in range(4):
                    j = grp * 4 + q
                    for c in range(8):
                        lhsT = wc[:, (j // 2) * 2048 + c * 256 + (j % 2) * 128: (j // 2) * 2048 + c * 256 + (j % 2) * 128 + 128]
                        rhs = xn[:, c * 128:(c + 1) * 128]
                        op(PE, (lambda pb, q, lhsT, rhs, c: lambda e: e.matmul(psum[pb][:, q * 128:(q + 1) * 128], lhsT=lhsT, rhs=rhs, start=(c == 0), stop=(c == 7)))(pb, q, lhsT, rhs, c),
                           reads=[wcB[j // 2], xnB], writes=[psB[pb][q]])
                if grp == 0:
                    op(ACT, (lambda p: lambda e: e.activation(out=cbs_b[p][:, :], in_=psum[2][:, :], func=AF.Copy))(p), reads=psB[2], writes=[cbsB_b[p]])
                if grp == 1:
                    op(ACT, lambda e: e.activation(out=cct[:, :], in_=psum[3][:, :], func=AF.Copy), reads=psB[3], writes=[cctB])
            op(DVE, lambda e: e.tensor_tensor(out=mpv[:, :, 2:130], in0=cct[:, :].rearrange("p (a b) -> p a b", a=4), in1=psum[4][:, :].rearrange("p (a b) -> p a b", a=4), op=ALU.mult),
               reads=[cctB, mpB] + psB[4], writes=[mpB])
            op(DVE, lambda e: e.tensor_copy(out=mpv[:, :, 0:2], in_=mov[:, :, 128:130]), reads=[moB, mpB], writes=[mpB])
            ops_ = S.capture
            S.capture = None
            return ops_

        def capB(t):
            p = t % 2
            mp, mpB = mpc_b[p], mpcB_b[p]
            cbs, cbsB = cbs_b[p], cbsB_b[p]
            S.capture = []
            for j in range(4):
                op(DVE, (lambda j: lambda e: e.tensor_scalar(out=cacc[:, j * 128:(j + 1) * 128], in0=mp[:, j * 130:j * 130 + 128], scalar1=csw[:, j * 3:j * 3 + 1], scalar2=None, op0=ALU.mult))(j),
                   reads=[mpB, cpB], writes=[caccB])
                for k in range(1, 3):
                    op(DVE, (lambda j, k: lambda e: e.scalar_tensor_tensor(out=cacc[:, j * 128:(j + 1) * 128], in0=mp[:, j * 130 + k:j * 130 + k + 128], scalar=csw[:, j * 3 + k:j * 3 + k + 1], in1=cacc[:, j * 128:(j + 1) * 128], op0=ALU.mult, op1=ALU.add))(j, k),
                       reads=[mpB, cpB, caccB], writes=[caccB])
            op(DVE, lambda e: e.tensor_tensor(out=yv[:, :], in0=cacc[:, :], in1=cbs[:, :], op=ALU.mult), reads=[caccB, cbsB], writes=[yvB])
            op(ACT, lambda e: e.activation(out=sq[:, :], in_=yv[:, :], func=AF.Square), reads=[yvB], writes=[sqB])
            op(PE, lambda e: e.matmul(psum[5][:, :], lhsT=blk64, rhs=sq[:, :], start=True, stop=True), reads=[sqB, cpB], writes=psB[5])
            op(ACT, lambda e: e.activation(out=rs[:, :], in_=psum[5][:, :], func=AF.Ln, bias=cs("eps")), reads=psB[5] + [cpB], writes=[rsB])
            op(ACT, lambda e: e.activation(out=rs[:, :], in_=rs[:, :], func=AF.Exp, scale=-0.5), reads=[rsB], writes=[rsB])
            for j in range(4):
                op(DVE, (lambda j: lambda e: e.scalar_tensor_tensor(out=ycT[:, j * 128:(j + 1) * 128], in0=yv[:, j * 128:(j + 1) * 128], scalar=cgain[:, j:j + 1], in1=rs[:, j * 128:(j + 1) * 128], op0=ALU.mult, op1=ALU.mult))(j),
                   reads=[yvB, rsB, cpB], writes=[ycB])
            for hh in range(2):
                pb = 6 + hh
                for j in range(8):
                    if j < 4:
                        lhsT = ycT[:, j * 128:(j + 1) * 128]
                        rd = [ycB]
                    else:
                        lhsT = ygT[:, (j - 4) * T + t * 128:(j - 4) * T + (t + 1) * 128]
                        rd = [ygB[t]]
                    rhs = wo[:, j * 1024 + hh * 512: j * 1024 + (hh + 1) * 512]
                    op(PE, (lambda pb, lhsT, rhs, j: lambda e: e.matmul(psum[pb][:, :], lhsT=lhsT, rhs=rhs, start=(j == 0), stop=(j == 7)))(pb, lhsT, rhs, j),
                       reads=rd + [woB[j // 2]], writes=psB[pb])
                hap = h[:, t * D + hh * 512: t * D + (hh + 1) * 512]
                op(ACT, (lambda pb, hh: lambda e: e.activation(out=motmp[hh][:, :], in_=psum[pb][:, :], func=AF.Copy))(pb, hh), reads=psB[pb], writes=[motB[hh]])
                op(DVE, (lambda hh, hap: lambda e: e.tensor_tensor(out=hap, in0=hap, in1=motmp[hh][:, :], op=ALU.add))(hh, hap), reads=[motB[hh], hB[t]], writes=[hB[t]])
            ops_ = S.capture
            S.capture = None
            return ops_

        for it in capA(-1):
            S.replay(it)
        for it in capA(0):
            S.replay(it)
        for t in range(NT):
            A = capA(t + 1) if t + 1 < NT else []
            B_ = capB(t)
            na, nb = len(A), len(B_)
            ia = ib = 0
            while ia < na or ib < nb:
                if ib < nb and (ia >= na or ib * max(na, 1) <= ia * nb):
                    S.replay(B_[ib]); ib += 1
                else:
                    S.replay(A[ia]); ia += 1
        es.close()

    def final(self, out, fn_bc):
        S = self.S
        op = S.op
        cs, cpB = self.cs, self.cpB
        h, hB = self.h, self.hB
        es = contextlib.ExitStack()
        ot = [self.sb("ot%d" % i, D, F32, es) for i in range(4)]
        otB = [Buf() for _ in range(4)]
        fs = self.sb("fs", 64, F32, es); fsB = Buf()
        junk = self.sb("junk_f", D, BF16, es); junkB = Buf()
        fnb = self.sb("fnb", D, F32, es); fnbB = Buf()
        new_bufs = otB + [fsB, junkB, fnbB]
        S.alias(new_bufs, getattr(self, "phase_bufs", []))
        self.phase_bufs = new_bufs
        S.dma(SP, lambda e: e.dma_start(out=fnb[:], in_=fn_bc), "const2", S.new_group(), writes=[fnbB])
        import os
        if os.environ.get("KRAWOUT"):
            for t in range(NT):
                S.dma(SP, (lambda t: lambda e: e.dma_start(out=out[t * 128:(t + 1) * 128, :], in_=h[:, t * D:(t + 1) * D]))(t), "out%d" % (t % 2), S.new_group(), reads=[hB[t]])
            es.close()
            return
        for t in range(NT):
            k = t % 4
            c0 = (t % 16) * 3
            hs = h[:, t * D:(t + 1) * D]
            op(ACT, (lambda hs, c0: lambda e: e.activation(out=junk[:, :], in_=hs, func=AF.Square, accum_out=fs[:, c0:c0 + 1]))(hs, c0), reads=[hB[t]], writes=[junkB, fsB])
            op(POOL, (lambda c0: lambda e: e.tensor_scalar(out=fs[:, c0 + 1:c0 + 2], in0=fs[:, c0:c0 + 1], scalar1=1.0 / D, scalar2=EPS, op0=ALU.mult, op1=ALU.add))(c0), reads=[fsB], writes=[fsB])
            op(POOL, (lambda c0: lambda e: e.tensor_tensor(out=fs[:, c0 + 2:c0 + 3], in0=fs[:, c0 + 1:c0 + 2], in1=cs("mhalf"), op=ALU.pow))(c0), reads=[fsB, cpB], writes=[fsB])
            op(DVE, (lambda hs, c0, k: lambda e: e.scalar_tensor_tensor(out=ot[k][:, :], in0=hs, scalar=fs[:, c0 + 2:c0 + 3], in1=fnb[:, :], op0=ALU.mult, op1=ALU.mult))(hs, c0, k),
               reads=[hB[t], fsB, fnbB], writes=[otB[k]])
            S.dma(SP, (lambda t, k: lambda e: e.dma_start(out=out[t * 128:(t + 1) * 128, :], in_=ot[k][:, :]))(t, k), "out%d" % k, S.new_group(), reads=[otB[k]])
        es.close()


def _pack_layout():
    names = [("ident", 128), ("ones", 128), ("triU", 128), ("maskL", 128), ("maskU", 128), ("blk64", 128),
             ("gon", 128), ("n1", 8), ("nm", 8), ("n2", 8), ("cwg", 48), ("csw", 12), ("cgain", 4),
             ("alog", 4), ("dtb", 4), ("mhalf", 1), ("eps", 1)]
    lay = {}
    off = 0
    for n, w in names:
        lay[n] = (off, off + w)
        off += w
    return lay, off


_CP, _CPK_COLS = _pack_layout()
Builder.CP = _CP
Builder.CPK_COLS = _CPK_COLS


def _pack_consts(inp):
    f = np.float32
    cp = np.zeros((128, _CPK_COLS), f)

    def put(name, arr):
        a, b = _CP[name]
        cp[:, a:b] = np.asarray(arr, f).reshape(128, b - a)

    idx = np.arange(128)
    same = (idx[:, None] // 64) == (idx[None, :] // 64)
    put("ident", np.eye(128))
    put("ones", np.ones((128, 128)))
    put("triU", (same & (idx[:, None] <= idx[None, :])))
    put("maskL", np.where(same & (idx[:, None] > idx[None, :]), 0.0, NEG))
    put("maskU", np.where(same & (idx[:, None] <= idx[None, :]), 0.0, NEG))
    put("blk64", same.astype(f) / 64.0)
    put("gon", np.broadcast_to(inp["gdn_out_norm"].reshape(1, 128), (128, 128)))
    put("n1", inp["ffn1_norm"].reshape(8, 128).T)
    put("nm", inp["mix_norm"].reshape(8, 128).T)
    put("n2", inp["ffn2_norm"].reshape(8, 128).T)
    put("cwg", inp["gdn_conv_w"].reshape(4, 12, 128).transpose(2, 1, 0).reshape(128, 48))
    put("csw", inp["conv_short_w"].reshape(3, 4, 128).transpose(2, 1, 0).reshape(128, 12))
    put("cgain", inp["conv_out_norm"].reshape(4, 128).T)
    put("alog", np.broadcast_to(inp["gdn_A_log"].reshape(1, 4), (128, 4)))
    put("dtb", np.broadcast_to(inp["gdn_dt_bias"].reshape(1, 4), (128, 4)))
    put("mhalf", np.full((128, 1), -0.5))
    put("eps", np.full((128, 1), EPS))
    return cp


_NC_CACHE = {}


def _get_nc(debug=False):
    if debug not in _NC_CACHE:
        b = Builder(debug=debug)
        b.spar_h = [0, 0, 0, 0]
        _NC_CACHE[debug] = (b.build(), b)
    return _NC_CACHE[debug]


def kernel(debug=False, **inputs):
    inp = {k: np.asarray(v) for k, v in inputs.items()}
    x = inp["x"].astype(np.float32, copy=False)
    nc, b = _get_nc(debug)
    cp = _pack_consts(inp)
    fn_bc = np.ascontiguousarray(np.broadcast_to(inp["final_norm"].reshape(1, D).astype(np.float32), (128, D)))
    shared = {
        "w1_in": np.ascontiguousarray(inp["ffn1_w_in"][0]), "w1_out": np.ascontiguousarray(inp["ffn1_w_out"][0]),
        "w2_in": np.ascontiguousarray(inp["ffn2_w_in"][0]), "w2_out": np.ascontiguousarray(inp["ffn2_w_out"][0]),
        "wm_in": np.ascontiguousarray(inp["w_mix_in"][0]), "wm_out": np.ascontiguousarray(inp["w_mix_out"][0]),
        "cpk": cp, "fn_bc": fn_bc,
    }
    zeros = np.zeros((T, D), np.float32)
    in_maps = []
    for c in range(8):
        bi, half = c // 2, c % 2
        m = dict(shared)
        m["x_own"] = np.ascontiguousarray(x[bi, half * T:(half + 1) * T])
        m["x_pre"] = zeros if half == 0 else np.ascontiguousarray(x[bi, 0:T])
        in_maps.append(m)
    import os
    ncores = int(os.environ.get("KCORES", 8))
    res = run_bass_kernel_spmd(nc, in_maps[:ncores], core_ids=list(range(ncores)))
    outp = np.zeros((4, 2 * T, D), np.float32)
    for c in range(ncores):
        outp[c // 2, (c % 2) * T:(c % 2 + 1) * T] = res.results[c]["out"]
    if debug:
        return outp, res.results
    return outp
```
